# Optimizing a Trainium2 kernel written in Bass

```python
import math
import jax, jax.numpy as jnp
from jax import lax
import numpy as np

D_MODEL = 1024
BATCH = 8
SEQ = 4096
DEPTH = 4

N_MIXERS = 3
BLOCK = 128
NORM_EPS = 1e-6

SB_HEADS = 16
SB_HEAD_DIM = 64
SB_WIDTH = SB_HEADS * SB_HEAD_DIM

MLA_HEADS = 8
MLA_NOPE_DIM = 128
MLA_ROPE_DIM = 64
MLA_QK_DIM = MLA_NOPE_DIM + MLA_ROPE_DIM
MLA_V_DIM = 128
MLA_Q_RANK = 256
MLA_KV_RANK = 128
MLA_WIDTH = MLA_HEADS * MLA_V_DIM
ROPE_THETA = 10000.0

SWA_HEADS = 16
SWA_KV_HEADS = 4
SWA_GROUP = SWA_HEADS // SWA_KV_HEADS
SWA_HEAD_DIM = 64
SWA_WINDOW = 128
SWA_WIDTH = SWA_HEADS * SWA_HEAD_DIM

kernel_name = 'hybrid_sb_mla_swa_gated_trunk'


def rms_norm(x, g):
    xf = x.astype(jnp.float32)
    y = xf * lax.rsqrt(jnp.mean(xf * xf, axis=-1, keepdims=True) + NORM_EPS)
    return (y * g.astype(jnp.float32)).astype(x.dtype)


def rope(x, pos):
    half = x.shape[-1] // 2
    inv_freq = ROPE_THETA ** (-jnp.arange(half, dtype=jnp.float32) / half)
    ang = pos.astype(jnp.float32)[:, None] * inv_freq[None, :]
    cos = jnp.cos(ang)[None, :, None, :]
    sin = jnp.sin(ang)[None, :, None, :]
    xf = x.astype(jnp.float32)
    x1, x2 = xf[..., :half], xf[..., half:]
    out = jnp.concatenate([x1 * cos - x2 * sin, x2 * cos + x1 * sin], axis=-1)
    return out.astype(x.dtype)


def alibi_slopes(n_heads):
    return 2.0 ** (-8.0 * jnp.arange(1, n_heads + 1, dtype=jnp.float32) / n_heads)


def stick_breaking_attention(q, k, v):
    B, S, H, d = q.shape
    scale = 1.0 / math.sqrt(d)
    outs = []
    for i in range(S // BLOCK):
        t0 = i * BLOCK
        kl = t0 + BLOCK
        z = jnp.einsum('bthd,bshd->bhts', q[:, t0:kl], k[:, :kl]).astype(jnp.float32) * scale
        t_idx = t0 + jnp.arange(BLOCK)[:, None]
        s_idx = jnp.arange(kl)[None, :]
        mask = s_idx < t_idx
        log_fail = jnp.where(mask, jax.nn.log_sigmoid(-z), 0.0)
        later = lax.cumsum(log_fail, axis=3, reverse=True) - log_fail
        a = jnp.where(mask, jnp.exp(jax.nn.log_sigmoid(z) + later), 0.0)
        outs.append(jnp.einsum('bhts,bshd->bthd', a.astype(v.dtype), v[:, :kl]))
    return jnp.concatenate(outs, axis=1)


def causal_softmax_attention(q, k, v, scale):
    B, S, H, _ = q.shape
    outs = []
    for i in range(S // BLOCK):
        t0 = i * BLOCK
        kl = t0 + BLOCK
        s = jnp.einsum('bthd,bshd->bhts', q[:, t0:kl], k[:, :kl]).astype(jnp.float32) * scale
        mask = jnp.arange(kl)[None, :] <= (t0 + jnp.arange(BLOCK)[:, None])
        p = jax.nn.softmax(jnp.where(mask, s, -jnp.inf), axis=-1)
        outs.append(jnp.einsum('bhts,bshd->bthd', p.astype(v.dtype), v[:, :kl]))
    return jnp.concatenate(outs, axis=1)


def sliding_window_sink_attention(q, k, v, sinks):
    B, S, H, d = q.shape
    nb = S // BLOCK
    qb = q.reshape(B, nb, BLOCK, SWA_KV_HEADS, SWA_GROUP, d)

    def band(t):
        prev = jnp.pad(t, ((0, 0), (BLOCK, 0), (0, 0), (0, 0)))[:, :S]
        return jnp.concatenate([prev.reshape(B, nb, BLOCK, SWA_KV_HEADS, d),
                                t.reshape(B, nb, BLOCK, SWA_KV_HEADS, d)], axis=2)

    kb, vb = band(k), band(v)
    s = jnp.einsum('bnqkgd,bnskd->bnkgqs', qb, kb).astype(jnp.float32) / math.sqrt(d)
    rel = jnp.arange(BLOCK)[:, None] + BLOCK - jnp.arange(2 * BLOCK)[None, :]
    abs_s = jnp.arange(nb)[:, None] * BLOCK - BLOCK + jnp.arange(2 * BLOCK)[None, :]
    valid = ((rel >= 0) & (rel < SWA_WINDOW))[None, :, :] & (abs_s >= 0)[:, None, :]
    slopes = alibi_slopes(SWA_HEADS).reshape(SWA_KV_HEADS, SWA_GROUP)
    s = s - slopes[:, :, None, None] * rel.astype(jnp.float32)
    s = jnp.where(valid[None, :, None, None, :, :], s, -jnp.inf)
    sink = sinks.astype(jnp.float32).reshape(SWA_KV_HEADS, SWA_GROUP)[None, None, :, :, None, None]
    m = jnp.maximum(jnp.max(s, axis=-1, keepdims=True), sink)
    e = jnp.exp(s - m)
    p = e / (jnp.sum(e, axis=-1, keepdims=True) + jnp.exp(sink - m))
    o = jnp.einsum('bnkgqs,bnskd->bnqkgd', p.astype(v.dtype), vb)
    return o.reshape(B, S, H, d)


def stick_breaking_layer(x, norm_g, w_in, w_out):
    B, S, _ = x.shape
    proj = rms_norm(x, norm_g) @ w_in
    q, k, v, gate = jnp.split(proj, 4, axis=-1)
    q = q.reshape(B, S, SB_HEADS, SB_HEAD_DIM)
    k = k.reshape(B, S, SB_HEADS, SB_HEAD_DIM)
    v = v.reshape(B, S, SB_HEADS, SB_HEAD_DIM)
    o = stick_breaking_attention(q, k, v).reshape(B, S, SB_WIDTH)
    return x + (o * jax.nn.silu(gate)) @ w_out


def mla_layer(x, norm_g, w_in, q_a_norm, w_uq, kv_a_norm, w_ukv, q_head_norm, k_head_norm, w_out):
    B, S, _ = x.shape
    proj = rms_norm(x, norm_g) @ w_in
    c1 = MLA_Q_RANK
    c2 = c1 + MLA_KV_RANK
    c3 = c2 + MLA_ROPE_DIM
    q_lat, kv_lat, k_pe, gate = proj[..., :c1], proj[..., c1:c2], proj[..., c2:c3], proj[..., c3:]
    q = (rms_norm(q_lat, q_a_norm) @ w_uq).reshape(B, S, MLA_HEADS, MLA_QK_DIM)
    kv = (rms_norm(kv_lat, kv_a_norm) @ w_ukv).reshape(B, S, MLA_HEADS, MLA_NOPE_DIM + MLA_V_DIM)
    k_nope, v = kv[..., :MLA_NOPE_DIM], kv[..., MLA_NOPE_DIM:]
    k_pe = jnp.broadcast_to(k_pe[:, :, None, :], (B, S, MLA_HEADS, MLA_ROPE_DIM))
    k = jnp.concatenate([k_nope, k_pe], axis=-1)
    q = rms_norm(q, q_head_norm)
    k = rms_norm(k, k_head_norm)
    pos = jnp.arange(S)
    q = jnp.concatenate([q[..., :MLA_NOPE_DIM], rope(q[..., MLA_NOPE_DIM:], pos)], axis=-1)
    k = jnp.concatenate([k[..., :MLA_NOPE_DIM], rope(k[..., MLA_NOPE_DIM:], pos)], axis=-1)
    o = causal_softmax_attention(q, k, v, 1.0 / math.sqrt(MLA_QK_DIM)).reshape(B, S, MLA_WIDTH)
    return x + (o * jax.nn.silu(gate)) @ w_out


def swa_layer(x, norm_g, w_in, q_head_norm, k_head_norm, sinks, w_out):
    B, S, _ = x.shape
    proj = rms_norm(x, norm_g) @ w_in
    kv_w = SWA_KV_HEADS * SWA_HEAD_DIM
    c1 = SWA_WIDTH
    c2 = c1 + kv_w
    c3 = c2 + kv_w
    q = proj[..., :c1].reshape(B, S, SWA_HEADS, SWA_HEAD_DIM)
    k = proj[..., c1:c2].reshape(B, S, SWA_KV_HEADS, SWA_HEAD_DIM)
    v = proj[..., c2:c3].reshape(B, S, SWA_KV_HEADS, SWA_HEAD_DIM)
    gate = proj[..., c3:]
    q = rms_norm(q, q_head_norm)
    k = rms_norm(k, k_head_norm)
    o = sliding_window_sink_attention(q, k, v, sinks).reshape(B, S, SWA_WIDTH)
    return x + (o * jax.nn.silu(gate)) @ w_out


def _dense(key, fan_in, fan_out):
    return jax.random.normal(key, (fan_in, fan_out), jnp.float32) * fan_in ** -0.5


def _gain(key, n):
    return 1.0 + 0.02 * jax.random.normal(key, (n,), jnp.float32)


def setup_inputs(seed: int = 0) -> dict:
    key = jax.random.key(seed)
    ks = jax.random.split(key, 24)
    sb_in = 4 * SB_WIDTH
    mla_in = MLA_Q_RANK + MLA_KV_RANK + MLA_ROPE_DIM + MLA_WIDTH
    swa_in = SWA_WIDTH + 2 * SWA_KV_HEADS * SWA_HEAD_DIM + SWA_WIDTH
    return {
        'x': jax.random.normal(ks[0], (BATCH, SEQ, D_MODEL), jnp.float32),
        'l0_norm': _gain(ks[1], D_MODEL),
        'l0_w_in': _dense(ks[2], D_MODEL, sb_in),
        'l0_w_out': _dense(ks[3], SB_WIDTH, D_MODEL),
        'l1_norm': _gain(ks[4], D_MODEL),
        'l1_w_in': _dense(ks[5], D_MODEL, mla_in),
        'l1_q_a_norm': _gain(ks[6], MLA_Q_RANK),
        'l1_w_uq': _dense(ks[7], MLA_Q_RANK, MLA_HEADS * MLA_QK_DIM),
        'l1_kv_a_norm': _gain(ks[8], MLA_KV_RANK),
        'l1_w_ukv': _dense(ks[9], MLA_KV_RANK, MLA_HEADS * (MLA_NOPE_DIM + MLA_V_DIM)),
        'l1_q_head_norm': _gain(ks[10], MLA_QK_DIM),
        'l1_k_head_norm': _gain(ks[11], MLA_QK_DIM),
        'l1_w_out': _dense(ks[12], MLA_WIDTH, D_MODEL),
        'l2_norm': _gain(ks[13], D_MODEL),
        'l2_w_in': _dense(ks[14], D_MODEL, swa_in),
        'l2_q_head_norm': _gain(ks[15], SWA_HEAD_DIM),
        'l2_k_head_norm': _gain(ks[16], SWA_HEAD_DIM),
        'l2_sinks': 0.5 * jax.random.normal(ks[17], (SWA_HEADS,), jnp.float32),
        'l2_w_out': _dense(ks[18], SWA_WIDTH, D_MODEL),
        'l3_norm': _gain(ks[19], D_MODEL),
        'l3_w_in': _dense(ks[20], D_MODEL, sb_in),
        'l3_w_out': _dense(ks[21], SB_WIDTH, D_MODEL),
    }


def reference(x, l0_norm, l0_w_in, l0_w_out,
              l1_norm, l1_w_in, l1_q_a_norm, l1_w_uq, l1_kv_a_norm, l1_w_ukv,
              l1_q_head_norm, l1_k_head_norm, l1_w_out,
              l2_norm, l2_w_in, l2_q_head_norm, l2_k_head_norm, l2_sinks, l2_w_out,
              l3_norm, l3_w_in, l3_w_out):
    layer_params = [
        (l0_norm, l0_w_in, l0_w_out),
        (l1_norm, l1_w_in, l1_q_a_norm, l1_w_uq, l1_kv_a_norm, l1_w_ukv,
         l1_q_head_norm, l1_k_head_norm, l1_w_out),
        (l2_norm, l2_w_in, l2_q_head_norm, l2_k_head_norm, l2_sinks, l2_w_out),
        (l3_norm, l3_w_in, l3_w_out),
    ]
    mixers = (stick_breaking_layer, mla_layer, swa_layer)
    for i in range(DEPTH):
        x = mixers[i % N_MIXERS](x, *layer_params[i])
    return x
```

```python
import math
import numpy as np
import ml_dtypes
import concourse.bass as bass
import concourse.mybir as mybir
from concourse.bass_utils import run_bass_kernel_spmd

F32 = mybir.dt.float32
BF16 = mybir.dt.bfloat16
AF = mybir.ActivationFunctionType
ALU = mybir.AluOpType
AX = mybir.AxisListType

S_LEN = 4096
D = 1024
NT = S_LEN // 128
NG = S_LEN // 512
EPS = 1e-6

LAYER_KIND = ["sb", "mla", "swa", "sb"]
LAYER_WEIGHTS = [
    ["l0_norm", "l0_w_in", "l0_w_out"],
    ["l1_norm", "l1_w_in", "l1_q_a_norm", "l1_w_uq", "l1_kv_a_norm", "l1_w_ukv",
     "l1_q_head_norm", "l1_k_head_norm", "l1_w_out"],
    ["l2_norm", "l2_w_in", "l2_q_head_norm", "l2_k_head_norm", "l2_sinks", "l2_w_out"],
    ["l3_norm", "l3_w_in", "l3_w_out"],
]
WSHAPES = {
    "l0_norm": [1024], "l0_w_in": [1024, 4096], "l0_w_out": [1024, 1024],
    "l1_norm": [1024], "l1_w_in": [1024, 1472], "l1_q_a_norm": [256], "l1_w_uq": [256, 1536],
    "l1_kv_a_norm": [128], "l1_w_ukv": [128, 2048], "l1_q_head_norm": [192], "l1_k_head_norm": [192],
    "l1_w_out": [1024, 1024],
    "l2_norm": [1024], "l2_w_in": [1024, 2560], "l2_q_head_norm": [64], "l2_k_head_norm": [64],
    "l2_sinks": [16], "l2_w_out": [1024, 1024],
    "l3_norm": [1024], "l3_w_in": [1024, 4096], "l3_w_out": [1024, 1024],
}


class Res:
    __slots__ = ("name", "w", "r", "excl")

    def __init__(self, name, excl=False):
        self.name = name
        self.w = None
        self.r = {}
        self.excl = excl


class Sched:
    ENGS = ("pe", "act", "dve", "pool", "sp")

    def __init__(self, sems, dma_pools):
        self.sems = sems
        self.cnt = {k: 0 for k in sems}
        self.ops = {e: [] for e in self.ENGS}
        self.seen = {e: {} for e in self.ENGS}
        self.dma_pools = dma_pools
        self.dma_rr = {q: 0 for q in dma_pools}
        self.nwaits = 0

    def _wait(self, e, key, val):
        if val <= 0 or val <= self.seen[e].get(key, 0):
            return
        self.seen[e][key] = val
        sem = self.sems[key]
        self.ops[e].append(lambda eng, sem=sem, val=val: eng.wait_ge(sem, val))
        self.nwaits += 1

    def _sync(self, e, reads, writes, dma):
        rd = [r for r in reads if not r.excl]
        wr = list(writes) + [r for r in reads if r.excl]
        need = []
        for r in rd:
            if r.w is not None:
                need.append((r.w, True))
        for r in wr:
            if r.w is not None:
                need.append((r.w, False))
            for ev in r.r.values():
                need.append((ev, False))
        for (key, val, eng), raw in need:
            if eng == e and not dma:
                if e == "pe" or not raw:
                    continue
            self._wait(e, key, val)
        return rd, wr

    def _update(self, rd, wr, ev):
        for r in rd:
            r.r[ev[0]] = ev
        for r in wr:
            r.w = ev
            r.r = {}

    def op(self, e, fn, reads=(), writes=()):
        rd, wr = self._sync(e, reads, writes, False)
        self.cnt[e] += 1
        sem = self.sems[e]
        self.ops[e].append(lambda eng, fn=fn, sem=sem: fn(eng).then_inc(sem, 1))
        self._update(rd, wr, (e, self.cnt[e], e))

    def dma(self, q, fn, reads=(), writes=()):
        pool = self.dma_pools[q]
        k = pool[self.dma_rr[q] % len(pool)]
        self.dma_rr[q] += 1
        self._wait(q, k, self.cnt[k])
        rd, wr = self._sync(q, reads, writes, True)
        self.cnt[k] += 16
        sem = self.sems[k]
        self.ops[q].append(lambda eng, fn=fn, sem=sem: fn(eng).then_inc(sem, 16))
        self._update(rd, wr, (k, self.cnt[k], None))

    def barrier(self, engines=None):
        for e in (engines or self.ENGS):
            for k, v in self.cnt.items():
                if k != e:
                    self._wait(e, k, v)


    def mm(self, out, lhsT, rhs, start, stop, reads, writes, skip=False):
        if skip:
            self.op("pe", lambda e: e.matmul(out, lhsT=lhsT, rhs=rhs, start=start, stop=stop, skip_group_check=True),
                    reads, writes)
        else:
            self.op("pe", lambda e: e.matmul(out, lhsT=lhsT, rhs=rhs, start=start, stop=stop), reads, writes)

    def tr(self, out, in_, ident, reads, writes):
        self.op("pe", lambda e: e.transpose(out=out, in_=in_, identity=ident), reads, writes)

    def act(self, out, in_, func, reads, writes, **kw):
        self.op("act", lambda e: e.activation(out=out, in_=in_, func=func, **kw), reads, writes)

    def tt(self, eng, out, in0, in1, op, reads, writes):
        self.op(eng, lambda e: e.tensor_tensor(out=out, in0=in0, in1=in1, op=op), reads, writes)

    def stt(self, out, in0, scalar, in1, op0, op1, reads, writes):
        self.op("dve", lambda e: e.scalar_tensor_tensor(out=out, in0=in0, scalar=scalar, in1=in1, op0=op0, op1=op1),
                reads, writes)

    def ts(self, eng, out, in0, s1, s2, op0, op1, reads, writes):
        if op1 is None:
            self.op(eng, lambda e: e.tensor_scalar(out=out, in0=in0, scalar1=s1, scalar2=None, op0=op0), reads, writes)
        else:
            self.op(eng, lambda e: e.tensor_scalar(out=out, in0=in0, scalar1=s1, scalar2=s2, op0=op0, op1=op1),
                    reads, writes)

    def cp(self, eng, out, in_, reads, writes):
        if eng == "act":
            self.op("act", lambda e: e.copy(out=out, in_=in_), reads, writes)
        else:
            self.op(eng, lambda e: e.tensor_copy(out=out, in_=in_), reads, writes)

    def ld(self, q, out, in_, reads=(), writes=()):
        self.dma(q, lambda e: e.dma_start(out=out, in_=in_), reads, writes)

    def emit(self, block):
        def mk(e):
            def body(eng):
                for f in self.ops[e]:
                    f(eng)
            return body
        block.tensor(mk("pe"))
        block.scalar(mk("act"))
        block.vector(mk("dve"))
        block.gpsimd(mk("pool"))
        block.sync(mk("sp"))


def make_consts():
    j = np.arange(128)[:, None]
    s = np.arange(128)[None, :]
    ident = (j == s).astype(np.float32)
    tinc = -(j >= s).astype(np.float32)
    tcar = -(j < s).astype(np.float32)
    ones = np.ones((128, 128), np.float32)
    cbf = np.concatenate([ident, tinc, tcar, ones], axis=1)
    mstrict = (s > j).astype(np.float32)
    mincl = (s >= j).astype(np.float32)
    half = 32
    inv_freq = (10000.0 ** (-np.arange(half, dtype=np.float32) / half)).astype(np.float32)
    pos = np.arange(S_LEN, dtype=np.float32)
    ang = (pos[:, None] * inv_freq[None, :]).astype(np.float32)
    cos = np.cos(ang).astype(np.float32)
    sin = np.sin(ang).astype(np.float32)
    cosT = cos.reshape(NT, 128, 32).transpose(1, 0, 2).reshape(128, NT * 32)
    sinT = sin.reshape(NT, 128, 32).transpose(1, 0, 2).reshape(128, NT * 32)
    slopes = (2.0 ** (-8.0 * np.arange(1, 17, dtype=np.float32) / 16)).astype(np.float32)
    NEG = -30000.0
    bias = np.zeros((128, 2, 16, 128), np.float32)
    for kbi in range(2):
        rel = (s + 128 - (j + 128 * kbi)).astype(np.float32)
        valid = (rel >= 0) & (rel < 128)
        for h in range(16):
            bias[:, kbi, h, :] = np.where(valid, -slopes[h] * rel, NEG)
    cf = np.concatenate([mstrict, mincl, cosT, sinT, bias.reshape(128, -1)], axis=1).astype(np.float32)
    return cbf.astype(np.float32), cf


CF_MSTRICT = 0
CF_MINCL = 128
CF_COS = 256
CF_SIN = 256 + NT * 32
CF_BIAS = 256 + 2 * NT * 32
CF_TOTAL = CF_BIAS + 2 * 16 * 128


class Ctx:
    pass


def build_program(layers, dbg=None):
    nc = bass.Bass("TRN2", target_bir_lowering=False)
    x_in = nc.dram_tensor("x", [S_LEN, D], F32, kind="ExternalInput").ap()
    y_out = nc.dram_tensor("y", [S_LEN, D], F32, kind="ExternalOutput").ap()
    cbf_d = nc.dram_tensor("cbf", [128, 512], F32, kind="ExternalInput").ap()
    cf_d = nc.dram_tensor("cf", [128, CF_TOTAL], F32, kind="ExternalInput").ap()
    W = {}
    for li in layers:
        for n in LAYER_WEIGHTS[li]:
            W[n] = nc.dram_tensor(n, WSHAPES[n], F32, kind="ExternalInput").ap()
    xs = [x_in]
    for i in range(len(layers) - 1):
        xs.append(nc.dram_tensor(f"xmid{i}", [S_LEN, D], F32, kind="Internal").ap())
    xs.append(y_out)

    from contextlib import ExitStack
    with ExitStack() as st:
        def sb(name, shape, dt):
            return st.enter_context(nc.sbuf_tensor(name, shape, dt))
        c = Ctx()
        c.nc = nc
        c.xnT = sb("xnT", [128, 8 * S_LEN], BF16)
        c.ogT = sb("ogT", [128, 8, S_LEN], BF16)
        c.hbuf = sb("hbuf", [128, 3 * S_LEN], BF16)
        c.wbuf = sb("wbuf", [128, 8192], BF16)
        c.scr = sb("scr", [128, 6144], F32)
        c.gbc = sb("gbc", [128, 1024], F32)
        c.cbf = sb("cbfs", [128, 512], BF16)
        c.cmask = sb("cmask", [128, 256], F32)
        c.small = sb("small", [128, 512], F32)
        c.ps = st.enter_context(nc.psum_tensor("ps", [128, 8, 512], F32))
        sem_names = list(Sched.ENGS) + [f"d{i}" for i in range(20)]
        sems = {k: st.enter_context(nc.semaphore(f"s_{k}")) for k in sem_names}
        S = Sched(sems, {"sp": [f"d{i}" for i in range(0, 8)],
                         "pool": [f"d{i}" for i in range(8, 16)],
                         "act": [f"d{i}" for i in range(16, 20)]})
        c.S = S
        c.W = W
        c.cf_d = cf_d
        c.dbg = dbg
        c.bank = [Res(f"bank{b}", excl=True) for b in range(8)]
        c.ident = c.cbf[:, 0:128]
        c.tinc = c.cbf[:, 128:256]
        c.tcar = c.cbf[:, 256:384]
        c.ones = c.cbf[:, 384:512]
        c.mstrict = c.cmask[:, 0:128]
        c.mincl = c.cmask[:, 128:256]
        c.Rconst = Res("const")
        S.ld("pool", c.cbf[:], cbf_d[:, :], writes=[c.Rconst])
        S.ld("sp", c.cmask[:], cf_d[:, 0:256], writes=[c.Rconst])
        S.barrier()

        block = st.enter_context(nc.Block())
        for idx, li in enumerate(layers):
            kind = LAYER_KIND[li]
            pre = f"l{li}_"
            phase_norm(c, xs[idx], W[pre + "norm"])
            S.barrier()
            if kind == "sb":
                layer_sb(c, W[pre + "w_in"])
            elif kind == "mla":
                layer_mla(c, {k[3:]: v for k, v in W.items() if k.startswith(pre)})
            else:
                layer_swa(c, {k[3:]: v for k, v in W.items() if k.startswith(pre)})
            S.barrier()
            phase_out(c, xs[idx], xs[idx + 1], W[pre + "w_out"])
            S.barrier()
        S.emit(block)
    return nc


def phase_norm(c, x_d, g_d):
    S = c.S
    xnT3 = c.xnT[:, :].rearrange("p (c t) -> p c t", c=8)
    Rg = Res("gbc")
    S.ld("sp", c.gbc[:], g_d.partition_broadcast(128), writes=[Rg])
    xin = [c.scr[:, 0:1024], c.scr[:, 1024:2048]]
    Rxin = [Res("xin0"), Res("xin1")]
    junk = c.scr[:, 2048:3072]
    Rjunk = Res("junk")
    xn = [c.scr[:, 3072:3584].bitcast(BF16), c.scr[:, 3584:4096].bitcast(BF16)]
    Rxn = [Res("xn0"), Res("xn1")]
    st = c.small
    Rst = [Res("st0"), Res("st1")]
    c.RxnT = [Res(f"xnT{i}") for i in range(NT)]
    for i in range(NT):
        b = i % 2
        S.ld("sp", xin[b], x_d[i * 128:(i + 1) * 128, :], writes=[Rxin[b]])
        ss = st[:, 4 * b:4 * b + 1]
        lg = st[:, 4 * b + 1:4 * b + 2]
        rs = st[:, 4 * b + 2:4 * b + 3]
        S.act(junk, xin[b], AF.Square, [Rxin[b]], [Rjunk, Rst[b]], accum_out=ss)
        S.act(lg, ss, AF.Ln, [Rst[b]], [Rst[b]], scale=1.0 / D, bias=EPS)
        S.act(rs, lg, AF.Exp, [Rst[b]], [Rst[b]], scale=-0.5)
        S.stt(xn[b], xin[b], rs, c.gbc[:], ALU.mult, ALU.mult, [Rxin[b], Rst[b], Rg], [Rxn[b]])
        bank = 4 + b
        psT = c.ps[:, bank, :].bitcast(BF16)
        for ch in range(8):
            S.tr(psT[:, ch * 128:(ch + 1) * 128], xn[b][:, ch * 128:(ch + 1) * 128], c.ident,
                 [Rxn[b], c.Rconst], [c.bank[bank]])
        dst = xnT3[:, :, i * 128:(i + 1) * 128]
        src = psT.rearrange("p (c t) -> p c t", c=8)
        S.cp("act" if b == 0 else "dve", dst, src, [c.bank[bank]], [c.RxnT[i]])


def phase_out(c, x_d, y_d, wout_d):
    S = c.S
    wo = c.wbuf[:, :].rearrange("p (c n) -> p c n", c=8)
    Rwo = Res("wo")
    wv = wout_d.rearrange("(c p) n -> p c n", p=128)
    for h in range(2):
        S.ld("pool", wo[:, 4 * h:4 * h + 4, :], wv[:, 4 * h:4 * h + 4, :], writes=[Rwo])
    xin = [c.scr[:, 0:1024], c.scr[:, 1024:2048]]
    Rxin = [Res("cxin0"), Res("cxin1")]
    yo = [c.scr[:, 2048:3072], c.scr[:, 3072:4096]]
    Ryo = [Res("yo0"), Res("yo1")]
    for i in range(NT):
        b = i % 2
        S.ld("sp", xin[b], x_d[i * 128:(i + 1) * 128, :], writes=[Rxin[b]])
        for h in range(2):
            bank = 2 * b + h
            for ch in range(8):
                S.mm(c.ps[:, bank, :], c.ogT[:, ch, i * 128:(i + 1) * 128], wo[:, ch, h * 512:(h + 1) * 512],
                     ch == 0, ch == 7, [Rwo, c.Rog[ch][i // 4]], [c.bank[bank]])
            S.tt("dve", yo[b][:, h * 512:(h + 1) * 512], c.ps[:, bank, :], xin[b][:, h * 512:(h + 1) * 512],
                 ALU.add, [c.bank[bank], Rxin[b]], [Ryo[b]])
        S.ld("sp", y_d[i * 128:(i + 1) * 128, :], yo[b], reads=[Ryo[b]])


def gate_phase(c, w_in, col0):
    S = c.S
    xnT3 = c.xnT[:, :].rearrange("p (c t) -> p c t", c=8)
    wv = w_in.rearrange("(c p) n -> p c n", p=128)
    wg = [c.wbuf[:, 6144 + 1024 * b: 6144 + 1024 * (b + 1)].rearrange("p (c n) -> p c n", c=8) for b in range(2)]
    Rwg = [Res("wg0"), Res("wg1")]
    c.Rog = [[Res(f"og{ch}_{tg}") for tg in range(NG)] for ch in range(8)]
    k = 0
    for ch in range(8):
        b = ch % 2
        S.ld("pool", wg[b], wv[:, :, col0 + ch * 128: col0 + (ch + 1) * 128], writes=[Rwg[b]])
        for tg in range(NG):
            bank = k % 4
            k += 1
            for cc in range(8):
                S.mm(c.ps[:, bank, :], wg[b][:, cc, :], xnT3[:, cc, tg * 512:(tg + 1) * 512], cc == 0, cc == 7,
                     [Rwg[b]] + c.RxnT[4 * tg:4 * tg + 4], [c.bank[bank]])
            S.act(c.ogT[:, ch, tg * 512:(tg + 1) * 512], c.ps[:, bank, :], AF.Silu, [c.bank[bank]], [c.Rog[ch][tg]])


def layer_sb(c, w_in):
    S = c.S
    gate_phase(c, w_in, 3072)
    xnT3 = c.xnT[:, :].rearrange("p (c t) -> p c t", c=8)
    wv = w_in.rearrange("(c p) n -> p c n", p=128)
    qT = c.hbuf[:, 0:S_LEN]
    kT = c.hbuf[:, S_LEN:2 * S_LEN]
    v = c.hbuf[:, 2 * S_LEN:3 * S_LEN].rearrange("p (i f) -> p i f", f=128)
    RqT = [Res(f"qT{g}") for g in range(NG)]
    RkT = [Res(f"kT{g}") for g in range(NG)]
    Rv = [Res(f"v{g}") for g in range(NG)]
    wsl = [c.wbuf[:, 3072 * b:3072 * (b + 1)].rearrange("p (s c n) -> p s c n", s=3, c=8) for b in range(2)]
    Rw = [Res("wsl0"), Res("wsl1")]
    NE, NL, NW, NA = 5, 4, 2, 3
    off = 0
    E, Wb, L, Ab = [], [], [], []
    for i in range(NE):
        E.append(c.scr[:, off:off + 512]); off += 512
    for i in range(NW):
        Wb.append(c.scr[:, off:off + 512]); off += 512
    for i in range(NL):
        L.append(c.scr[:, off:off + 256].bitcast(BF16)); off += 256
    for i in range(NA):
        Ab.append(c.scr[:, off:off + 256].bitcast(BF16)); off += 256
    assert off <= 6144
    RE = [Res(f"E{i}") for i in range(NE)]
    RW = [Res(f"W{i}") for i in range(NW)]
    RL = [Res(f"L{i}") for i in range(NL)]
    RA = [Res(f"A{i}") for i in range(NA)]
    NZ = 4
    ZB = [0, 1, 2, 3]
    AB = [4, 5]
    OB = [6, 7]
    gcount = 0

    def load_w(hp):
        b = hp % 2
        for s in range(3):
            col = s * 1024 + hp * 128
            S.ld("pool", wsl[b][:, s, :, :], wv[:, :, col:col + 128], writes=[Rw[b]])

    load_w(0)
    for hp in range(8):
        b = hp % 2
        if hp + 1 < 8:
            load_w(hp + 1)
        k = 0
        for tg in range(NG):
            for s, (dst, Rd) in enumerate(((qT, RqT), (kT, RkT))):
                bank = k % 4
                k += 1
                for cc in range(8):
                    S.mm(c.ps[:, bank, :], wsl[b][:, s, cc, :], xnT3[:, cc, tg * 512:(tg + 1) * 512], cc == 0, cc == 7,
                         [Rw[b]] + c.RxnT[4 * tg:4 * tg + 4], [c.bank[bank]])
                S.cp("dve" if s == 0 else "act", dst[:, tg * 512:(tg + 1) * 512], c.ps[:, bank, :],
                     [c.bank[bank]], [Rd[tg]])
            bank = k % 4
            k += 1
            for j in range(4):
                i = 4 * tg + j
                for cc in range(8):
                    S.mm(c.ps[:, bank, j * 128:(j + 1) * 128], xnT3[:, cc, i * 128:(i + 1) * 128], wsl[b][:, 2, cc, :],
                         cc == 0, cc == 7, [Rw[b], c.RxnT[i]], [c.bank[bank]])
            S.cp("dve", v[:, 4 * tg:4 * tg + 4, :], c.ps[:, bank, :].rearrange("p (j f) -> p j f", f=128),
                 [c.bank[bank]], [Rv[tg]])
        G = []
        for qg in range(NG):
            nkb = 4 * qg + 4
            for kb in reversed(range(nkb)):
                for hd in (0, 1):
                    G.append((qg, kb, hd))
        n_g = len(G)

        def info(gi):
            qg, kb, hd = G[gi]
            r = kb - 4 * qg
            c0 = r * 128 if r >= 0 else 0
            return qg, kb, hd, r, c0, kb == 4 * qg + 3, kb == 0, gcount + gi

        def st_z(gi):
            qg, kb, hd, r, c0, first, last, gid = info(gi)
            zb = ZB[gid % NZ]
            rows = slice(hd * 64, hd * 64 + 64)
            S.mm(c.ps[:, zb, c0:512], kT[rows, kb * 128:(kb + 1) * 128], qT[rows, qg * 512 + c0:(qg + 1) * 512],
                 True, True, [RkT[kb // 4], RqT[qg]], [c.bank[zb]])

        def st_e(gi):
            qg, kb, hd, r, c0, first, last, gid = info(gi)
            zb = ZB[gid % NZ]
            eb = gid % NE
            S.act(E[eb][:, c0:512], c.ps[:, zb, c0:512], AF.Exp, [c.bank[zb]], [RE[eb]], scale=0.125)
            if r >= 0:
                S.tt("dve", E[eb][:, c0:c0 + 128], E[eb][:, c0:c0 + 128], c.mstrict, ALU.mult,
                     [RE[eb], c.Rconst], [RE[eb]])

        def st_l(gi):
            qg, kb, hd, r, c0, first, last, gid = info(gi)
            eb = gid % NE
            lb = gid % NL
            S.act(L[lb][:, c0:512], E[eb][:, c0:512], AF.Ln, [RE[eb]], [RL[lb]], bias=1.0, scale=1.0)

        def st_cum(gi):
            qg, kb, hd, r, c0, first, last, gid = info(gi)
            lb = gid % NL
            ab = AB[hd]
            S.mm(c.ps[:, ab, c0:512], c.tinc, L[lb][:, c0:512], first, False, [RL[lb], c.Rconst], [c.bank[ab]],
                 skip=True)

        def st_w(gi):
            qg, kb, hd, r, c0, first, last, gid = info(gi)
            ab = AB[hd]
            wb = gid % NW
            eb = gid % NE
            a_i = gid % NA
            S.act(Wb[wb][:, c0:512], c.ps[:, ab, c0:512], AF.Exp, [c.bank[ab]], [RW[wb]])
            S.tt("dve", Ab[a_i][:, c0:512], E[eb][:, c0:512], Wb[wb][:, c0:512], ALU.mult,
                 [RE[eb], RW[wb]], [RA[a_i]])

        def st_av(gi):
            qg, kb, hd, r, c0, first, last, gid = info(gi)
            lb = gid % NL
            ab = AB[hd]
            ob = OB[hd]
            a_i = gid % NA
            rows = slice(hd * 64, hd * 64 + 64)
            if not last:
                S.mm(c.ps[:, ab, c0:512], c.tcar, L[lb][:, c0:512], False, False, [RL[lb], c.Rconst], [c.bank[ab]],
                     skip=True)
            S.mm(c.ps[rows, ob, c0:512], v[:, kb, hd * 64:(hd + 1) * 64], Ab[a_i][:, c0:512], first, last,
                 [RA[a_i], Rv[kb // 4]], [c.bank[ob]], skip=True)
            if last:
                og = c.ogT[rows, hp, qg * 512:(qg + 1) * 512]
                S.tt("dve", og, c.ps[rows, ob, :], og, ALU.mult, [c.bank[ob], c.Rog[hp][qg]], [c.Rog[hp][qg]])

        stages = [(st_z, 0), (st_e, 1), (st_l, 2), (st_cum, 3), (st_w, 4), (st_av, 5)]
        for n in range(n_g + 5):
            for fn, d in reversed(stages):
                gi = n - d
                if 0 <= gi < n_g:
                    fn(gi)
        gcount += n_g


def layer_mla(c, w):
    S = c.S
    w_in = w["w_in"]
    gate_phase(c, w_in, 448)
    S.barrier()
    xnT3 = c.xnT[:, :].rearrange("p (c t) -> p c t", c=8)
    wv = w_in.rearrange("(c p) n -> p c n", p=128)
    sm = c.small
    scr = c.scr
    Rgn = Res("mla_g")
    gqa = c.gbc[:, 0:256]
    gkva = c.gbc[:, 256:384]
    gq192 = c.gbc[:, 384:576]
    gk192 = c.gbc[:, 576:768]
    gkpe = c.gbc[:, 704:768]
    S.ld("sp", gqa, w["q_a_norm"].partition_broadcast(128), writes=[Rgn])
    S.ld("sp", gkva, w["kv_a_norm"].partition_broadcast(128), writes=[Rgn])
    S.ld("sp", gq192, w["q_head_norm"].partition_broadcast(128), writes=[Rgn])
    S.ld("sp", gk192, w["k_head_norm"].partition_broadcast(128), writes=[Rgn])
    S.tt("dve", gq192[:, 0:128], gq192[:, 0:128], gk192[:, 0:128], ALU.mult, [Rgn], [Rgn])
    cosT = scr[:, 0:1024].rearrange("p (i f) -> p i f", f=32)
    sinT = scr[:, 1024:2048].rearrange("p (i f) -> p i f", f=32)
    Rtab = Res("ropetab")
    S.ld("sp", scr[:, 0:2048], c.cf_d[:, CF_COS:CF_COS + 2048], writes=[Rtab])
    kpeT = scr[:, 2048:4096].bitcast(BF16)
    qlnT3 = c.hbuf[:, 0:2 * S_LEN].rearrange("p (a t) -> p a t", a=2)
    kvlnT = c.hbuf[:, 2 * S_LEN:3 * S_LEN]
    sskpe = sm[:, 128:160]
    rstdk = sm[:, 160:192]
    Rqln = [Res(f"qln{g}") for g in range(NG)]
    Rkvln = [Res(f"kvln{g}") for g in range(NG)]
    Rkpe = [Res(f"kpe{g}") for g in range(NG)]
    Rsskpe = [Res(f"sskpe{g}") for g in range(NG)]

    def rope(x, out, i, Rx, Rout, t1, t2, Rt):
        cb = cosT[:, i, :].unsqueeze(1).to_broadcast([128, 2, 32])
        S.tt("dve", t1.rearrange("p (a f) -> p a f", a=2), x.rearrange("p (a f) -> p a f", a=2), cb, ALU.mult,
             [Rx, Rtab], [Rt])
        S.tt("dve", t2[:, 0:32], x[:, 32:64], sinT[:, i, :], ALU.mult, [Rx, Rtab], [Rt])
        S.tt("dve", t2[:, 32:64], x[:, 0:32], sinT[:, i, :], ALU.mult, [Rx, Rtab], [Rt])
        S.tt("dve", out[:, 0:32], t1[:, 0:32], t2[:, 0:32], ALU.subtract, [Rt], [Rout])
        S.tt("dve", out[:, 32:64], t1[:, 32:64], t2[:, 32:64], ALU.add, [Rt], [Rout])

    wlat = c.wbuf[:, 0:3584].rearrange("p (c n) -> p c n", c=8)
    Rwlat = Res("wlat")
    S.ld("pool", wlat[:, 0:4, :], wv[:, 0:4, 0:448], writes=[Rwlat])
    S.ld("pool", wlat[:, 4:8, :], wv[:, 4:8, 0:448], writes=[Rwlat])
    o = 4096
    junk = scr[:, o:o + 448]; o += 448
    lnb = [scr[:, o:o + 192].bitcast(BF16), scr[:, o + 192:o + 384].bitcast(BF16)]; o += 384
    kp = scr[:, o:o + 64]; o += 64
    t1 = scr[:, o:o + 64]; o += 64
    t2 = scr[:, o:o + 64]; o += 64
    kr = [scr[:, o:o + 32].bitcast(BF16), scr[:, o + 32:o + 64].bitcast(BF16)]; o += 64
    assert o <= 6144
    Rjunk, Rkp, Rt = Res("junk"), Res("kp"), Res("ropet")
    Rlnb = [Res("lnb0"), Res("lnb1")]
    Rkr = [Res("kr0"), Res("kr1")]
    Rst = [Res("mst0"), Res("mst1")]
    for i in range(NT):
        tg = i // 4
        pb = i % 2
        bank = 6 + pb
        psL = c.ps[:, bank, 0:448]
        for cc in range(8):
            S.mm(psL, xnT3[:, cc, i * 128:(i + 1) * 128], wlat[:, cc, :], cc == 0, cc == 7,
                 [Rwlat, c.RxnT[i]], [c.bank[bank]])
        sb_ = 192 + 16 * pb
        ss2 = sm[:, sb_:sb_ + 2]
        lg2 = sm[:, sb_ + 2:sb_ + 4]
        rs2 = sm[:, sb_ + 4:sb_ + 6]
        S.act(junk[:, 0:256], c.ps[:, bank, 0:256], AF.Square, [c.bank[bank]], [Rjunk, Rst[pb]], accum_out=ss2[:, 0:1])
        S.act(junk[:, 256:384], c.ps[:, bank, 256:384], AF.Square, [c.bank[bank]], [Rjunk, Rst[pb]], accum_out=ss2[:, 1:2])
        S.act(junk[:, 384:448], c.ps[:, bank, 384:448], AF.Square, [c.bank[bank]], [Rjunk, Rsskpe[tg]],
              accum_out=sskpe[:, i:i + 1])
        S.act(lg2[:, 0:1], ss2[:, 0:1], AF.Ln, [Rst[pb]], [Rst[pb]], scale=1.0 / 256, bias=EPS)
        S.act(lg2[:, 1:2], ss2[:, 1:2], AF.Ln, [Rst[pb]], [Rst[pb]], scale=1.0 / 128, bias=EPS)
        S.act(rs2, lg2, AF.Exp, [Rst[pb]], [Rst[pb]], scale=-0.5)
        S.stt(lnb[pb][:, 0:256], c.ps[:, bank, 0:256], rs2[:, 0:1], gqa, ALU.mult, ALU.mult,
              [c.bank[bank], Rst[pb], Rgn], [Rlnb[pb]])
        S.stt(lnb[pb][:, 256:384], c.ps[:, bank, 256:384], rs2[:, 1:2], gkva, ALU.mult, ALU.mult,
              [c.bank[bank], Rst[pb], Rgn], [Rlnb[pb]])
        S.tt("dve", kp, c.ps[:, bank, 384:448], gkpe, ALU.mult, [c.bank[bank], Rgn], [Rkp])
        rope(kp, kr[pb], i, Rkp, Rkr[pb], t1, t2, Rt)
        tbank = 4 + pb
        psT = c.ps[:, tbank, :].bitcast(BF16)
        for a in range(3):
            S.tr(psT[:, a * 128:(a + 1) * 128], lnb[pb][:, a * 128:(a + 1) * 128], c.ident,
                 [Rlnb[pb], c.Rconst], [c.bank[tbank]])
        S.tr(psT[0:64, 384:512], kr[pb], c.ident, [Rkr[pb], c.Rconst], [c.bank[tbank]])
        S.cp("act", qlnT3[:, :, i * 128:(i + 1) * 128], psT[:, 0:256].rearrange("p (a t) -> p a t", a=2),
             [c.bank[tbank]], [Rqln[tg]])
        S.cp("dve", kvlnT[:, i * 128:(i + 1) * 128], psT[:, 256:384], [c.bank[tbank]], [Rkvln[tg]])
        S.cp("dve", kpeT[0:64, i * 128:(i + 1) * 128], psT[0:64, 384:512], [c.bank[tbank]], [Rkpe[tg]])
    S.barrier()
    wuq = c.wbuf[:, 0:3072].rearrange("p (a n) -> p a n", a=2)
    wukv = c.wbuf[:, 3072:5120]
    Rwu = Res("wu")
    S.ld("pool", wuq, w["w_uq"].rearrange("(a p) n -> p a n", p=128), writes=[Rwu])
    S.ld("pool", wukv, w["w_ukv"], writes=[Rwu])
    X = c.xnT
    qTn = X[:, 0:4096]
    qTp = X[:, 4096:8192]
    kTn = X[:, 8192:12288]
    vh = X[:, 12288:16384].rearrange("p (i f) -> p i f", f=128)
    NP = 3
    Pb = [X[:, 16384 + 512 * j:16384 + 512 * (j + 1)] for j in range(NP)]
    qr = [X[:, 18432 + 256 * j:18432 + 256 * j + 192] for j in range(2)]
    XF = X[:, 20480:24576].bitcast(F32)
    rcb = XF[:, 0:512]
    tbuf = XF[:, 512:1024]
    qpe = XF[:, 1024:1088]
    u1 = XF[:, 1088:1152]
    u2 = XF[:, 1152:1216]
    junk2 = XF[:, 1216:1536]
    RqTn = [Res(f"qTn{g}") for g in range(NG)]
    RqTp = [Res(f"qTp{g}") for g in range(NG)]
    RkTn = [Res(f"kTn{g}") for g in range(NG)]
    Rvh = [Res(f"vh{g}") for g in range(NG)]
    Rrk = [Res(f"rstdk{g}") for g in range(NG)]
    RPb = [Res(f"mPb{j}") for j in range(NP)]
    Rqr = [Res("qr0"), Res("qr1")]
    Rrc, Rtb, Rqpe, Ru, Rj2 = Res("rcb"), Res("tbuf"), Res("qpe"), Res("u"), Res("junk2")
    Rs2 = [Res("hst0"), Res("hst1")]
    gid = 0
    sw = 0
    for h in range(8):
        for tg in range(NG):
            bank = tg % 4
            S.mm(c.ps[:, bank, :], wukv[:, h * 256:h * 256 + 128], kvlnT[:, tg * 512:(tg + 1) * 512], True, True,
                 [Rwu, Rkvln[tg]], [c.bank[bank]])
            S.cp("act", kTn[:, tg * 512:(tg + 1) * 512], c.ps[:, bank, :], [c.bank[bank]], [RkTn[tg]])
        for i in range(NT):
            tg, j = divmod(i, 4)
            pb = i % 2
            bank = 6 + pb
            tok = slice(i * 128, (i + 1) * 128)
            S.mm(c.ps[:, bank, 0:192], qlnT3[:, 0, tok], wuq[:, 0, h * 192:(h + 1) * 192], True, False,
                 [Rwu, Rqln[tg]], [c.bank[bank]])
            S.mm(c.ps[:, bank, 0:192], qlnT3[:, 1, tok], wuq[:, 1, h * 192:(h + 1) * 192], False, True,
                 [Rwu, Rqln[tg]], [c.bank[bank]])
            S.mm(c.ps[:, bank, 192:448], kvlnT[:, tok], wukv[:, h * 256:(h + 1) * 256], True, True,
                 [Rwu, Rkvln[tg]], [c.bank[bank]])
            sb_ = 224 + 16 * pb
            ss2 = sm[:, sb_:sb_ + 2]
            lg2 = sm[:, sb_ + 2:sb_ + 4]
            rsq = sm[:, sb_ + 4:sb_ + 5]
            S.act(junk2[:, 0:192], c.ps[:, bank, 0:192], AF.Square, [c.bank[bank]], [Rj2, Rs2[pb]], accum_out=ss2[:, 0:1])
            S.act(junk2[:, 192:320], c.ps[:, bank, 192:320], AF.Square, [c.bank[bank]], [Rj2, Rs2[pb]],
                  accum_out=ss2[:, 1:2])
            S.tt("dve", ss2[:, 1:2], ss2[:, 1:2], sskpe[:, i:i + 1], ALU.add, [Rs2[pb], Rsskpe[tg]], [Rs2[pb]])
            S.act(lg2, ss2, AF.Ln, [Rs2[pb]], [Rs2[pb]], scale=1.0 / 192, bias=EPS)
            S.act(rsq, lg2[:, 0:1], AF.Exp, [Rs2[pb]], [Rs2[pb]], scale=-0.5)
            S.act(rstdk[:, i:i + 1], lg2[:, 1:2], AF.Exp, [Rs2[pb]], [Rrk[tg]], scale=-0.5, bias=-0.5 * math.log(192.0))
            S.cp("act", vh[:, i, :], c.ps[:, bank, 320:448], [c.bank[bank]], [Rvh[tg]])
            S.stt(qr[pb][:, 0:128], c.ps[:, bank, 0:128], rsq, gq192[:, 0:128], ALU.mult, ALU.mult,
                  [c.bank[bank], Rs2[pb], Rgn], [Rqr[pb]])
            S.stt(qpe, c.ps[:, bank, 128:192], rsq, gq192[:, 128:192], ALU.mult, ALU.mult,
                  [c.bank[bank], Rs2[pb], Rgn], [Rqpe])
            rope(qpe, qr[pb][:, 128:192], i, Rqpe, Rqr[pb], u1, u2, Ru)
            tbank = 4 + tg % 2
            psT = c.ps[:, tbank, :].bitcast(BF16)
            S.tr(psT[:, j * 128:(j + 1) * 128], qr[pb][:, 0:128], c.ident, [Rqr[pb], c.Rconst], [c.bank[tbank]])
            S.tr(psT[0:64, 512 + j * 128:512 + (j + 1) * 128], qr[pb][:, 128:192], c.ident,
                 [Rqr[pb], c.Rconst], [c.bank[tbank]])
            if j == 3:
                S.cp("dve", qTn[:, tg * 512:(tg + 1) * 512], psT[:, 0:512], [c.bank[tbank]], [RqTn[tg]])
                S.cp("dve", qTp[0:64, tg * 512:(tg + 1) * 512], psT[0:64, 512:1024], [c.bank[tbank]], [RqTp[tg]])
        G = []
        for qg in range(NG):
            for kb in reversed(range(4 * qg + 4)):
                G.append((qg, kb))
        n_g = len(G)

        def info(gi):
            qg, kb = G[gi]
            r = kb - 4 * qg
            c0 = r * 128 if r >= 0 else 0
            return qg, kb, r, c0, kb == 4 * qg + 3, kb == 0, gid + gi, sw + qg

        def st_z(gi):
            qg, kb, r, c0, first, last, g_, sw_ = info(gi)
            zb = g_ % 4
            S.mm(c.ps[:, zb, c0:512], kTn[:, kb * 128:(kb + 1) * 128], qTn[:, qg * 512 + c0:(qg + 1) * 512], True, False,
                 [RkTn[kb // 4], RqTn[qg]], [c.bank[zb]])
            S.mm(c.ps[:, zb, c0:512], kpeT[0:64, kb * 128:(kb + 1) * 128], qTp[0:64, qg * 512 + c0:(qg + 1) * 512],
                 False, True, [Rkpe[kb // 4], RqTp[qg]], [c.bank[zb]])

        def st_e(gi):
            qg, kb, r, c0, first, last, g_, sw_ = info(gi)
            zb = g_ % 4
            pi = g_ % NP
            S.act(Pb[pi][:, c0:512], c.ps[:, zb, c0:512], AF.Exp, [c.bank[zb], Rrk[kb // 4]], [RPb[pi]],
                  scale=rstdk[:, kb:kb + 1])
            if r >= 0:
                S.tt("dve", Pb[pi][:, c0:c0 + 128], Pb[pi][:, c0:c0 + 128], c.mincl, ALU.mult,
                     [RPb[pi], c.Rconst], [RPb[pi]])

        def st_av(gi):
            qg, kb, r, c0, first, last, g_, sw_ = info(gi)
            pi = g_ % NP
            ob = 4 + sw_ % 2
            db = 6 + sw_ % 2
            S.mm(c.ps[:, ob, c0:512], vh[:, kb, :], Pb[pi][:, c0:512], first, last, [Rvh[kb // 4], RPb[pi]],
                 [c.bank[ob]], skip=True)
            S.mm(c.ps[:, db, c0:512], c.ones, Pb[pi][:, c0:512], first, last, [c.Rconst, RPb[pi]],
                 [c.bank[db]], skip=True)
            if last:
                S.op("dve", lambda e, o_=rcb, i_=c.ps[:, db, :]: e.reciprocal(out=o_, in_=i_), [c.bank[db]], [Rrc])
                S.tt("dve", tbuf, c.ps[:, ob, :], rcb, ALU.mult, [c.bank[ob], Rrc], [Rtb])
                og = c.ogT[:, h, qg * 512:(qg + 1) * 512]
                S.tt("dve", og, tbuf, og, ALU.mult, [Rtb, c.Rog[h][qg]], [c.Rog[h][qg]])

        stages = [(st_z, 0), (st_e, 1), (st_av, 2)]
        for n in range(n_g + 2):
            for fn, d in reversed(stages):
                gi = n - d
                if 0 <= gi < n_g:
                    fn(gi)
        gid += n_g
        sw += NG


def layer_swa(c, w):
    S = c.S
    w_in = w["w_in"]
    gate_phase(c, w_in, 1536)
    S.barrier()
    xnT3 = c.xnT[:, :].rearrange("p (c t) -> p c t", c=8)
    wv = w_in.rearrange("(c p) n -> p c n", p=128)
    qT3 = c.hbuf[:, 0:2 * S_LEN].rearrange("p (a t) -> p a t", a=2)
    kT2 = c.hbuf[:, 2 * S_LEN:3 * S_LEN]
    scr = c.scr
    off = 0
    vg = scr[:, off:off + 1024].bitcast(BF16).rearrange("p (i f) -> p i f", f=64); off += 1024
    junk = scr[:, off:off + 320]; off += 320
    tmpq = scr[:, off:off + 256]; off += 256
    qn = [scr[:, off:off + 128].bitcast(BF16), scr[:, off + 128:off + 256].bitcast(BF16)]; off += 256
    biasg = scr[:, off:off + 1024]; off += 1024
    Tb = scr[:, off:off + 1024]; off += 1024
    Pb = [scr[:, off:off + 512].bitcast(BF16), scr[:, off + 512:off + 1024].bitcast(BF16)]; off += 1024
    dnb = scr[:, off:off + 256]; off += 256
    rcb = scr[:, off:off + 256]; off += 256
    tb = scr[:, off:off + 256]; off += 256
    gqk4 = scr[:, off:off + 256]; off += 256
    assert off <= 6144
    sm = c.small
    kscale = sm[:, 16:48]
    es16 = sm[:, 48:64]
    Rjunk, Rtmpq, Rbias, RTb, Rdn, Rrc, Rtb, Rgqk, Res16 = (Res(n) for n in
                                                          ("junk", "tmpq", "biasg", "Tb", "dnb", "rcb", "tb", "gqk", "es16"))
    Rqn = [Res("qn0"), Res("qn1")]
    RPb = [Res("Pb0"), Res("Pb1")]
    Rst = [Res("sst0"), Res("sst1")]
    RqT = [Res(f"qT{g}") for g in range(NG)]
    RkT = [Res(f"kT{g}") for g in range(NG)]
    Rvg = [Res(f"vg{g}") for g in range(NG)]
    Rks = [Res(f"ks{g}") for g in range(NG)]
    wtm = [c.wbuf[:, 4096 * b:4096 * b + 3072].rearrange("p (c n) -> p c n", c=8) for b in range(2)]
    wk2 = [c.wbuf[:, 4096 * b + 3072:4096 * (b + 1)].rearrange("p (c n) -> p c n", c=8) for b in range(2)]
    Rw = [Res("swaw0"), Res("swaw1")]
    gk4 = junk[:, 0:256]
    for j in range(4):
        S.ld("sp", gqk4[:, j * 64:(j + 1) * 64], w["q_head_norm"].partition_broadcast(128), writes=[Rgqk])
        S.ld("sp", gk4[:, j * 64:(j + 1) * 64], w["k_head_norm"].partition_broadcast(128), writes=[Rjunk])
    S.tt("dve", gqk4, gqk4, gk4, ALU.mult, [Rjunk, Rgqk], [Rgqk])
    S.ld("sp", es16, w["sinks"].partition_broadcast(128), writes=[Res16])
    S.act(es16, es16, AF.Exp, [Res16], [Res16])

    def load_w(g):
        b = g % 2
        S.ld("pool", wtm[b][:, :, 0:256], wv[:, :, g * 256:(g + 1) * 256], writes=[Rw[b]])
        S.ld("pool", wtm[b][:, :, 256:320], wv[:, :, 1024 + g * 64:1024 + (g + 1) * 64], writes=[Rw[b]])
        S.ld("pool", wtm[b][:, :, 320:384], wv[:, :, 1280 + g * 64:1280 + (g + 1) * 64], writes=[Rw[b]])
        for d in range(2):
            S.ld("pool", wk2[b][:, :, d * 64:(d + 1) * 64], wv[:, :, 1024 + g * 64:1024 + (g + 1) * 64], writes=[Rw[b]])

    load_w(0)
    for g in range(4):
        b = g % 2
        if g + 1 < 4:
            load_w(g + 1)
        for kbi in range(2):
            S.ld("sp", biasg[:, kbi * 512:(kbi + 1) * 512],
                 c.cf_d[:, CF_BIAS + kbi * 2048 + g * 512:CF_BIAS + kbi * 2048 + (g + 1) * 512], writes=[Rbias])
        for tg in range(NG):
            bank = tg % 4
            for cc in range(8):
                S.mm(c.ps[:, bank, :], wk2[b][:, cc, :], xnT3[:, cc, tg * 512:(tg + 1) * 512], cc == 0, cc == 7,
                     [Rw[b]] + c.RxnT[4 * tg:4 * tg + 4], [c.bank[bank]])
            S.cp("act", kT2[:, tg * 512:(tg + 1) * 512], c.ps[:, bank, :], [c.bank[bank]], [RkT[tg]])
        for i in range(NT):
            tg, j = divmod(i, 4)
            pb = i % 2
            bank = 6 + pb
            psTM = c.ps[:, bank, 0:384]
            for cc in range(8):
                S.mm(psTM, xnT3[:, cc, i * 128:(i + 1) * 128], wtm[b][:, cc, :], cc == 0, cc == 7,
                     [Rw[b], c.RxnT[i]], [c.bank[bank]])
            ss5 = sm[:, 64 + 16 * pb:64 + 16 * pb + 5]
            lg5 = sm[:, 72 + 16 * pb:72 + 16 * pb + 5]
            rs4 = sm[:, 96 + 8 * pb:96 + 8 * pb + 4]
            S.act(junk, c.ps[:, bank, 0:320], AF.Square, [c.bank[bank]], [Rjunk])
            S.op("dve", lambda e, o=ss5, i_=junk.rearrange("p (h f) -> p h f", f=64): e.reduce_sum(out=o, in_=i_, axis=AX.X),
                 [Rjunk], [Rst[pb]])
            S.act(lg5, ss5, AF.Ln, [Rst[pb]], [Rst[pb]], scale=1.0 / 64, bias=EPS)
            S.act(rs4, lg5[:, 0:4], AF.Exp, [Rst[pb]], [Rst[pb]], scale=-0.5)
            S.act(kscale[:, i:i + 1], lg5[:, 4:5], AF.Exp, [Rst[pb]], [Rks[tg]], scale=-0.5, bias=math.log(0.125))
            S.tt("dve", tmpq.rearrange("p (h f) -> p h f", f=64),
                 c.ps[:, bank, 0:256].rearrange("p (h f) -> p h f", f=64),
                 rs4.unsqueeze(2).to_broadcast([128, 4, 64]), ALU.mult, [c.bank[bank], Rst[pb]], [Rtmpq])
            S.tt("dve", qn[pb], tmpq, gqk4, ALU.mult, [Rtmpq, Rgqk], [Rqn[pb]])
            S.cp("act", vg[:, i, :], c.ps[:, bank, 320:384], [c.bank[bank]], [Rvg[tg]])
            tbank = 4 + tg % 2
            psT = c.ps[:, tbank, :].bitcast(BF16)
            for a in range(2):
                S.tr(psT[:, a * 512 + j * 128:a * 512 + (j + 1) * 128], qn[pb][:, a * 128:(a + 1) * 128], c.ident,
                     [Rqn[pb], c.Rconst], [c.bank[tbank]])
            if j == 3:
                S.cp("dve", qT3[:, :, tg * 512:(tg + 1) * 512], psT.rearrange("p (a t) -> p a t", a=2),
                     [c.bank[tbank]], [RqT[tg]])
        for qb in range(NT):
            kbs = [(0, qb - 1), (1, qb)] if qb > 0 else [(1, qb)]
            zb0 = 2 * (qb % 2)
            for kbi, kb in kbs:
                for hq in range(4):
                    p = hq % 2
                    rows = slice(p * 64, p * 64 + 64)
                    col = (kbi * 2 + hq // 2) * 128
                    S.mm(c.ps[:, zb0 + p, col:col + 128], kT2[rows, kb * 128:(kb + 1) * 128],
                         qT3[rows, hq // 2, qb * 128:(qb + 1) * 128], True, True,
                         [RkT[kb // 4], RqT[qb // 4]], [c.bank[zb0 + p]])
            for kbi, kb in kbs:
                for p in range(2):
                    src = c.ps[:, zb0 + p, kbi * 256:(kbi + 1) * 256].rearrange("q (a t) -> q a t", a=2)
                    dst = Tb[:, kbi * 512:(kbi + 1) * 512].rearrange("q (a p t) -> q p a t", a=2, p=2)[:, p]
                    bia = biasg[:, kbi * 512:(kbi + 1) * 512].rearrange("q (a p t) -> q p a t", a=2, p=2)[:, p]
                    S.stt(dst, src, kscale[:, kb:kb + 1], bia, ALU.mult, ALU.add,
                          [c.bank[zb0 + p], Rks[kb // 4], Rbias], [RTb])
            pbi = qb % 2
            lo = 0 if qb > 0 else 512
            S.act(Pb[pbi][:, lo:1024], Tb[:, lo:1024], AF.Exp, [RTb], [RPb[pbi]])
            ob = 4 + qb % 2
            for hq in range(4):
                rows = slice((hq % 2) * 64, (hq % 2) * 64 + 64)
                col = (hq // 2) * 128
                for n_, (kbi, kb) in enumerate(kbs):
                    S.mm(c.ps[rows, ob, col:col + 128], vg[:, kb, :], Pb[pbi][:, (kbi * 4 + hq) * 128:(kbi * 4 + hq + 1) * 128],
                         n_ == 0, n_ == len(kbs) - 1, [Rvg[kb // 4], RPb[pbi]], [c.bank[ob]])
                for n_, (kbi, kb) in enumerate(kbs):
                    S.mm(c.ps[rows, ob, 256 + col:256 + col + 128], c.ones[:, 0:64],
                         Pb[pbi][:, (kbi * 4 + hq) * 128:(kbi * 4 + hq + 1) * 128],
                         n_ == 0, n_ == len(kbs) - 1, [c.Rconst, RPb[pbi]], [c.bank[ob]])
            for hq in range(4):
                rows = slice((hq % 2) * 64, (hq % 2) * 64 + 64)
                col = (hq // 2) * 128
                h = 4 * g + hq
                S.ts("dve", dnb[rows, col:col + 128], c.ps[rows, ob, 256 + col:256 + col + 128], es16[rows, h:h + 1],
                     None, ALU.add, None, [c.bank[ob], Res16], [Rdn])
            S.op("dve", lambda e, o=rcb, i_=dnb: e.reciprocal(out=o, in_=i_), [Rdn], [Rrc])
            S.tt("dve", tb, c.ps[:, ob, 0:256], rcb, ALU.mult, [c.bank[ob], Rrc], [Rtb])
            og = c.ogT[:, 2 * g:2 * g + 2, qb * 128:(qb + 1) * 128]
            Rogs = [c.Rog[2 * g][qb // 4], c.Rog[2 * g + 1][qb // 4]]
            S.tt("dve", og, tb.rearrange("p (a t) -> p a t", a=2), og, ALU.mult, [Rtb] + Rogs, Rogs)


LAUNCH_GROUPS = [[0], [1], [2], [3]]
_CONSTS = None


def run_layers(layers, xs, inputs):
    global _CONSTS
    if _CONSTS is None:
        _CONSTS = make_consts()
    cbf, cf = _CONSTS
    nc = build_program(layers)
    names = [n for li in layers for n in LAYER_WEIGHTS[li]]
    in_maps = []
    for b in range(len(xs)):
        m = {"x": np.ascontiguousarray(xs[b], dtype=np.float32), "cbf": cbf, "cf": cf}
        for n in names:
            m[n] = np.ascontiguousarray(inputs[n], dtype=np.float32)
        in_maps.append(m)
    res = run_bass_kernel_spmd(nc, in_maps, core_ids=list(range(len(xs))))
    return [r["y"] for r in res.results]


def kernel(**inputs):
    x = np.asarray(inputs["x"])
    xs = [x[b] for b in range(x.shape[0])]
    for grp in LAUNCH_GROUPS:
        xs = run_layers(grp, xs, inputs)
    return np.stack(xs, axis=0).astype(np.float32)
```

```python
import math
import numpy as np
import ml_dtypes
import concourse.bass as bass
import concourse.mybir as mybir
from concourse.bass_utils import run_bass_kernel_spmd

F32 = mybir.dt.float32
BF16 = mybir.dt.bfloat16
AF = mybir.ActivationFunctionType
ALU = mybir.AluOpType
AX = mybir.AxisListType

S_LEN = 4096
D = 1024
NT = S_LEN // 128
NG = S_LEN // 512
EPS = 1e-6

LAYER_KIND = ["sb", "mla", "swa", "sb"]
LAYER_WEIGHTS = [
    ["l0_norm", "l0_w_in", "l0_w_out"],
    ["l1_norm", "l1_w_in", "l1_q_a_norm", "l1_w_uq", "l1_kv_a_norm", "l1_w_ukv",
     "l1_q_head_norm", "l1_k_head_norm", "l1_w_out"],
    ["l2_norm", "l2_w_in", "l2_q_head_norm", "l2_k_head_norm", "l2_sinks", "l2_w_out"],
    ["l3_norm", "l3_w_in", "l3_w_out"],
]
WSHAPES = {
    "l0_norm": [1024], "l0_w_in": [1024, 4096], "l0_w_out": [1024, 1024],
    "l1_norm": [1024], "l1_w_in": [1024, 1472], "l1_q_a_norm": [256], "l1_w_uq": [256, 1536],
    "l1_kv_a_norm": [128], "l1_w_ukv": [128, 2048], "l1_q_head_norm": [192], "l1_k_head_norm": [192],
    "l1_w_out": [1024, 1024],
    "l2_norm": [1024], "l2_w_in": [1024, 2560], "l2_q_head_norm": [64], "l2_k_head_norm": [64],
    "l2_sinks": [16], "l2_w_out": [1024, 1024],
    "l3_norm": [1024], "l3_w_in": [1024, 4096], "l3_w_out": [1024, 1024],
}


class Res:
    __slots__ = ("name", "w", "r", "excl")

    def __init__(self, name, excl=False):
        self.name = name
        self.w = None
        self.r = {}
        self.excl = excl


class Sched:
    ENGS = ("pe", "act", "dve", "pool", "sp")

    def __init__(self, sems, dma_pools):
        self.sems = sems
        self.cnt = {k: 0 for k in sems}
        self.ops = {e: [] for e in self.ENGS}
        self.seen = {e: {} for e in self.ENGS}
        self.dma_pools = dma_pools
        self.dma_rr = {q: 0 for q in dma_pools}
        self.nwaits = 0

    def _wait(self, e, key, val):
        if val <= 0 or val <= self.seen[e].get(key, 0):
            return
        self.seen[e][key] = val
        sem = self.sems[key]
        self.ops[e].append(lambda eng, sem=sem, val=val: eng.wait_ge(sem, val))
        self.nwaits += 1

    def _sync(self, e, reads, writes, dma):
        rd = [r for r in reads if not r.excl]
        wr = list(writes) + [r for r in reads if r.excl]
        need = []
        for r in rd:
            if r.w is not None:
                need.append((r.w, True))
        for r in wr:
            if r.w is not None:
                need.append((r.w, False))
            for ev in r.r.values():
                need.append((ev, False))
        for (key, val, eng), raw in need:
            if eng == e and not dma:
                if e == "pe" or not raw:
                    continue
            self._wait(e, key, val)
        return rd, wr

    def _update(self, rd, wr, ev):
        for r in rd:
            r.r[ev[0]] = ev
        for r in wr:
            r.w = ev
            r.r = {}

    def op(self, e, fn, reads=(), writes=()):
        rd, wr = self._sync(e, reads, writes, False)
        self.cnt[e] += 1
        sem = self.sems[e]
        self.ops[e].append(lambda eng, fn=fn, sem=sem: fn(eng).then_inc(sem, 1))
        self._update(rd, wr, (e, self.cnt[e], e))

    def dma(self, q, fn, reads=(), writes=()):
        pool = self.dma_pools[q]
        k = pool[self.dma_rr[q] % len(pool)]
        self.dma_rr[q] += 1
        self._wait(q, k, self.cnt[k])
        rd, wr = self._sync(q, reads, writes, True)
        self.cnt[k] += 16
        sem = self.sems[k]
        self.ops[q].append(lambda eng, fn=fn, sem=sem: fn(eng).then_inc(sem, 16))
        self._update(rd, wr, (k, self.cnt[k], None))

    def barrier(self, engines=None):
        for e in (engines or self.ENGS):
            for k, v in self.cnt.items():
                if k != e:
                    self._wait(e, k, v)


    def mm(self, out, lhsT, rhs, start, stop, reads, writes, skip=False):
        if skip:
            self.op("pe", lambda e: e.matmul(out, lhsT=lhsT, rhs=rhs, start=start, stop=stop, skip_group_check=True),
                    reads, writes)
        else:
            self.op("pe", lambda e: e.matmul(out, lhsT=lhsT, rhs=rhs, start=start, stop=stop), reads, writes)

    def tr(self, out, in_, ident, reads, writes):
        self.op("pe", lambda e: e.transpose(out=out, in_=in_, identity=ident), reads, writes)

    def act(self, out, in_, func, reads, writes, **kw):
        self.op("act", lambda e: e.activation(out=out, in_=in_, func=func, **kw), reads, writes)

    def tt(self, eng, out, in0, in1, op, reads, writes):
        self.op(eng, lambda e: e.tensor_tensor(out=out, in0=in0, in1=in1, op=op), reads, writes)

    def stt(self, out, in0, scalar, in1, op0, op1, reads, writes):
        self.op("dve", lambda e: e.scalar_tensor_tensor(out=out, in0=in0, scalar=scalar, in1=in1, op0=op0, op1=op1),
                reads, writes)

    def ts(self, eng, out, in0, s1, s2, op0, op1, reads, writes):
        if op1 is None:
            self.op(eng, lambda e: e.tensor_scalar(out=out, in0=in0, scalar1=s1, scalar2=None, op0=op0), reads, writes)
        else:
            self.op(eng, lambda e: e.tensor_scalar(out=out, in0=in0, scalar1=s1, scalar2=s2, op0=op0, op1=op1),
                    reads, writes)

    def cp(self, eng, out, in_, reads, writes):
        if eng == "act":
            self.op("act", lambda e: e.copy(out=out, in_=in_), reads, writes)
        else:
            self.op(eng, lambda e: e.tensor_copy(out=out, in_=in_), reads, writes)

    def ld(self, q, out, in_, reads=(), writes=()):
        self.dma(q, lambda e: e.dma_start(out=out, in_=in_), reads, writes)

    def emit(self, block):
        def mk(e):
            def body(eng):
                for f in self.ops[e]:
                    f(eng)
            return body
        block.tensor(mk("pe"))
        block.scalar(mk("act"))
        block.vector(mk("dve"))
        block.gpsimd(mk("pool"))
        block.sync(mk("sp"))


def make_consts():
    j = np.arange(128)[:, None]
    s = np.arange(128)[None, :]
    ident = (j == s).astype(np.float32)
    tinc = -(j >= s).astype(np.float32)
    tcar = -(j < s).astype(np.float32)
    ones = np.ones((128, 128), np.float32)
    cbf = np.concatenate([ident, tinc, tcar, ones], axis=1)
    mstrict = (s > j).astype(np.float32)
    mincl = (s >= j).astype(np.float32)
    half = 32
    inv_freq = (10000.0 ** (-np.arange(half, dtype=np.float32) / half)).astype(np.float32)
    pos = np.arange(S_LEN, dtype=np.float32)
    ang = (pos[:, None] * inv_freq[None, :]).astype(np.float32)
    cos = np.cos(ang).astype(np.float32)
    sin = np.sin(ang).astype(np.float32)
    cosT = cos.reshape(NT, 128, 32).transpose(1, 0, 2).reshape(128, NT * 32)
    sinT = sin.reshape(NT, 128, 32).transpose(1, 0, 2).reshape(128, NT * 32)
    slopes = (2.0 ** (-8.0 * np.arange(1, 17, dtype=np.float32) / 16)).astype(np.float32)
    NEG = -30000.0
    bias = np.zeros((128, 2, 16, 128), np.float32)
    for kbi in range(2):
        rel = (s + 128 - (j + 128 * kbi)).astype(np.float32)
        valid = (rel >= 0) & (rel < 128)
        for h in range(16):
            bias[:, kbi, h, :] = np.where(valid, -slopes[h] * rel, NEG)
    cf = np.concatenate([mstrict, mincl, cosT, sinT, bias.reshape(128, -1)], axis=1).astype(np.float32)
    return cbf.astype(np.float32), cf


CF_MSTRICT = 0
CF_MINCL = 128
CF_COS = 256
CF_SIN = 256 + NT * 32
CF_BIAS = 256 + 2 * NT * 32
CF_TOTAL = CF_BIAS + 2 * 16 * 128


class Ctx:
    pass


def build_program(layers, dbg=None):
    nc = bass.Bass("TRN2", target_bir_lowering=False)
    x_in = nc.dram_tensor("x", [S_LEN, D], F32, kind="ExternalInput").ap()
    y_out = nc.dram_tensor("y", [S_LEN, D], F32, kind="ExternalOutput").ap()
    cbf_d = nc.dram_tensor("cbf", [128, 512], F32, kind="ExternalInput").ap()
    cf_d = nc.dram_tensor("cf", [128, CF_TOTAL], F32, kind="ExternalInput").ap()
    W = {}
    for li in layers:
        for n in LAYER_WEIGHTS[li]:
            W[n] = nc.dram_tensor(n, WSHAPES[n], F32, kind="ExternalInput").ap()
    xs = [x_in]
    for i in range(len(layers) - 1):
        xs.append(nc.dram_tensor(f"xmid{i}", [S_LEN, D], F32, kind="Internal").ap())
    xs.append(y_out)

    from contextlib import ExitStack
    with ExitStack() as st:
        def sb(name, shape, dt):
            return st.enter_context(nc.sbuf_tensor(name, shape, dt))
        c = Ctx()
        c.nc = nc
        c.xnT = sb("xnT", [128, 8 * S_LEN], BF16)
        c.ogT = sb("ogT", [128, 8, S_LEN], BF16)
        c.hbuf = sb("hbuf", [128, 3 * S_LEN], BF16)
        c.wbuf = sb("wbuf", [128, 8192], BF16)
        c.scr = sb("scr", [128, 6144], F32)
        c.gbc = sb("gbc", [128, 1024], F32)
        c.cbf = sb("cbfs", [128, 512], BF16)
        c.cmask = sb("cmask", [128, 256], F32)
        c.small = sb("small", [128, 512], F32)
        c.ps = st.enter_context(nc.psum_tensor("ps", [128, 8, 512], F32))
        sem_names = list(Sched.ENGS) + [f"d{i}" for i in range(20)]
        sems = {k: st.enter_context(nc.semaphore(f"s_{k}")) for k in sem_names}
        S = Sched(sems, {"sp": [f"d{i}" for i in range(0, 8)],
                         "pool": [f"d{i}" for i in range(8, 16)],
                         "act": [f"d{i}" for i in range(16, 20)]})
        c.S = S
        c.W = W
        c.cf_d = cf_d
        c.dbg = dbg
        c.bank = [Res(f"bank{b}", excl=True) for b in range(8)]
        c.ident = c.cbf[:, 0:128]
        c.tinc = c.cbf[:, 128:256]
        c.tcar = c.cbf[:, 256:384]
        c.ones = c.cbf[:, 384:512]
        c.mstrict = c.cmask[:, 0:128]
        c.mincl = c.cmask[:, 128:256]
        c.Rconst = Res("const")
        S.ld("pool", c.cbf[:], cbf_d[:, :], writes=[c.Rconst])
        S.ld("sp", c.cmask[:], cf_d[:, 0:256], writes=[c.Rconst])
        S.barrier()

        block = st.enter_context(nc.Block())
        for idx, li in enumerate(layers):
            kind = LAYER_KIND[li]
            pre = f"l{li}_"
            phase_norm(c, xs[idx], W[pre + "norm"])
            S.barrier()
            if kind == "sb":
                layer_sb(c, W[pre + "w_in"])
            elif kind == "mla":
                layer_mla(c, {k[3:]: v for k, v in W.items() if k.startswith(pre)})
            else:
                layer_swa(c, {k[3:]: v for k, v in W.items() if k.startswith(pre)})
            S.barrier()
            phase_out(c, xs[idx], xs[idx + 1], W[pre + "w_out"])
            S.barrier()
        S.emit(block)
    return nc


def phase_norm(c, x_d, g_d):
    S = c.S
    xnT3 = c.xnT[:, :].rearrange("p (c t) -> p c t", c=8)
    Rg = Res("gbc")
    S.ld("sp", c.gbc[:], g_d.partition_broadcast(128), writes=[Rg])
    xin = [c.scr[:, 0:1024], c.scr[:, 1024:2048]]
    Rxin = [Res("xin0"), Res("xin1")]
    junk = c.scr[:, 2048:3072]
    Rjunk = Res("junk")
    xn = [c.scr[:, 3072:3584].bitcast(BF16), c.scr[:, 3584:4096].bitcast(BF16)]
    Rxn = [Res("xn0"), Res("xn1")]
    st = c.small
    Rst = [Res("st0"), Res("st1")]
    c.RxnT = [Res(f"xnT{i}") for i in range(NT)]
    for i in range(NT):
        b = i % 2
        S.ld("sp", xin[b], x_d[i * 128:(i + 1) * 128, :], writes=[Rxin[b]])
        ss = st[:, 4 * b:4 * b + 1]
        lg = st[:, 4 * b + 1:4 * b + 2]
        rs = st[:, 4 * b + 2:4 * b + 3]
        S.act(junk, xin[b], AF.Square, [Rxin[b]], [Rjunk, Rst[b]], accum_out=ss)
        S.act(lg, ss, AF.Ln, [Rst[b]], [Rst[b]], scale=1.0 / D, bias=EPS)
        S.act(rs, lg, AF.Exp, [Rst[b]], [Rst[b]], scale=-0.5)
        S.stt(xn[b], xin[b], rs, c.gbc[:], ALU.mult, ALU.mult, [Rxin[b], Rst[b], Rg], [Rxn[b]])
        bank = 4 + b
        psT = c.ps[:, bank, :].bitcast(BF16)
        for ch in range(8):
            S.tr(psT[:, ch * 128:(ch + 1) * 128], xn[b][:, ch * 128:(ch + 1) * 128], c.ident,
                 [Rxn[b], c.Rconst], [c.bank[bank]])
        dst = xnT3[:, :, i * 128:(i + 1) * 128]
        src = psT.rearrange("p (c t) -> p c t", c=8)
        S.cp("act" if b == 0 else "dve", dst, src, [c.bank[bank]], [c.RxnT[i]])


def phase_out(c, x_d, y_d, wout_d):
    S = c.S
    wo = c.wbuf[:, :].rearrange("p (c n) -> p c n", c=8)
    Rwo = Res("wo")
    wv = wout_d.rearrange("(c p) n -> p c n", p=128)
    for h in range(2):
        S.ld("pool", wo[:, 4 * h:4 * h + 4, :], wv[:, 4 * h:4 * h + 4, :], writes=[Rwo])
    xin = [c.scr[:, 0:1024], c.scr[:, 1024:2048]]
    Rxin = [Res("cxin0"), Res("cxin1")]
    yo = [c.scr[:, 2048:3072], c.scr[:, 3072:4096]]
    Ryo = [Res("yo0"), Res("yo1")]
    for i in range(NT):
        b = i % 2
        S.ld("sp", xin[b], x_d[i * 128:(i + 1) * 128, :], writes=[Rxin[b]])
        for h in range(2):
            bank = 2 * b + h
            for ch in range(8):
                S.mm(c.ps[:, bank, :], c.ogT[:, ch, i * 128:(i + 1) * 128], wo[:, ch, h * 512:(h + 1) * 512],
                     ch == 0, ch == 7, [Rwo, c.Rog[ch][i // 4]], [c.bank[bank]])
            S.tt("dve", yo[b][:, h * 512:(h + 1) * 512], c.ps[:, bank, :], xin[b][:, h * 512:(h + 1) * 512],
                 ALU.add, [c.bank[bank], Rxin[b]], [Ryo[b]])
        S.ld("sp", y_d[i * 128:(i + 1) * 128, :], yo[b], reads=[Ryo[b]])


def gate_phase(c, w_in, col0):
    S = c.S
    xnT3 = c.xnT[:, :].rearrange("p (c t) -> p c t", c=8)
    wv = w_in.rearrange("(c p) n -> p c n", p=128)
    wg = [c.wbuf[:, 6144 + 1024 * b: 6144 + 1024 * (b + 1)].rearrange("p (c n) -> p c n", c=8) for b in range(2)]
    Rwg = [Res("wg0"), Res("wg1")]
    c.Rog = [[Res(f"og{ch}_{tg}") for tg in range(NG)] for ch in range(8)]
    k = 0
    for ch in range(8):
        b = ch % 2
        S.ld("pool", wg[b], wv[:, :, col0 + ch * 128: col0 + (ch + 1) * 128], writes=[Rwg[b]])
        for tg in range(NG):
            bank = k % 4
            k += 1
            for cc in range(8):
                S.mm(c.ps[:, bank, :], wg[b][:, cc, :], xnT3[:, cc, tg * 512:(tg + 1) * 512], cc == 0, cc == 7,
                     [Rwg[b]] + c.RxnT[4 * tg:4 * tg + 4], [c.bank[bank]])
            S.act(c.ogT[:, ch, tg * 512:(tg + 1) * 512], c.ps[:, bank, :], AF.Silu, [c.bank[bank]], [c.Rog[ch][tg]])


def layer_sb(c, w_in):
    S = c.S
    gate_phase(c, w_in, 3072)
    xnT3 = c.xnT[:, :].rearrange("p (c t) -> p c t", c=8)
    wv = w_in.rearrange("(c p) n -> p c n", p=128)
    qT = c.hbuf[:, 0:S_LEN]
    kT = c.hbuf[:, S_LEN:2 * S_LEN]
    v = c.hbuf[:, 2 * S_LEN:3 * S_LEN].rearrange("p (i f) -> p i f", f=128)
    RqT = [Res(f"qT{g}") for g in range(NG)]
    RkT = [Res(f"kT{g}") for g in range(NG)]
    Rv = [Res(f"v{g}") for g in range(NG)]
    wsl = [c.wbuf[:, 3072 * b:3072 * (b + 1)].rearrange("p (s c n) -> p s c n", s=3, c=8) for b in range(2)]
    Rw = [Res("wsl0"), Res("wsl1")]
    NE, NL, NW, NA = 5, 4, 2, 3
    off = 0
    E, Wb, L, Ab = [], [], [], []
    for i in range(NE):
        E.append(c.scr[:, off:off + 512]); off += 512
    for i in range(NW):
        Wb.append(c.scr[:, off:off + 512]); off += 512
    for i in range(NL):
        L.append(c.scr[:, off:off + 256].bitcast(BF16)); off += 256
    for i in range(NA):
        Ab.append(c.scr[:, off:off + 256].bitcast(BF16)); off += 256
    assert off <= 6144
    RE = [Res(f"E{i}") for i in range(NE)]
    RW = [Res(f"W{i}") for i in range(NW)]
    RL = [Res(f"L{i}") for i in range(NL)]
    RA = [Res(f"A{i}") for i in range(NA)]
    NZ = 4
    ZB = [0, 1, 2, 3]
    AB = [4, 5]
    OB = [6, 7]
    gcount = 0

    def load_w(hp):
        b = hp % 2
        for s in range(3):
            col = s * 1024 + hp * 128
            S.ld("pool", wsl[b][:, s, :, :], wv[:, :, col:col + 128], writes=[Rw[b]])

    load_w(0)
    for hp in range(8):
        b = hp % 2
        if hp + 1 < 8:
            load_w(hp + 1)
        k = 0
        for tg in range(NG):
            for s, (dst, Rd) in enumerate(((qT, RqT), (kT, RkT))):
                bank = k % 4
                k += 1
                for cc in range(8):
                    S.mm(c.ps[:, bank, :], wsl[b][:, s, cc, :], xnT3[:, cc, tg * 512:(tg + 1) * 512], cc == 0, cc == 7,
                         [Rw[b]] + c.RxnT[4 * tg:4 * tg + 4], [c.bank[bank]])
                S.cp("dve" if s == 0 else "act", dst[:, tg * 512:(tg + 1) * 512], c.ps[:, bank, :],
                     [c.bank[bank]], [Rd[tg]])
            bank = k % 4
            k += 1
            for j in range(4):
                i = 4 * tg + j
                for cc in range(8):
                    S.mm(c.ps[:, bank, j * 128:(j + 1) * 128], xnT3[:, cc, i * 128:(i + 1) * 128], wsl[b][:, 2, cc, :],
                         cc == 0, cc == 7, [Rw[b], c.RxnT[i]], [c.bank[bank]])
            S.cp("dve", v[:, 4 * tg:4 * tg + 4, :], c.ps[:, bank, :].rearrange("p (j f) -> p j f", f=128),
                 [c.bank[bank]], [Rv[tg]])
        G = []
        for qg in range(NG):
            nkb = 4 * qg + 4
            for kb in reversed(range(nkb)):
                for hd in (0, 1):
                    G.append((qg, kb, hd))
        n_g = len(G)

        def info(gi):
            qg, kb, hd = G[gi]
            r = kb - 4 * qg
            c0 = r * 128 if r >= 0 else 0
            return qg, kb, hd, r, c0, kb == 4 * qg + 3, kb == 0, gcount + gi

        def st_z(gi):
            qg, kb, hd, r, c0, first, last, gid = info(gi)
            zb = ZB[gid % NZ]
            rows = slice(hd * 64, hd * 64 + 64)
            S.mm(c.ps[:, zb, c0:512], kT[rows, kb * 128:(kb + 1) * 128], qT[rows, qg * 512 + c0:(qg + 1) * 512],
                 True, True, [RkT[kb // 4], RqT[qg]], [c.bank[zb]])

        def st_e(gi):
            qg, kb, hd, r, c0, first, last, gid = info(gi)
            zb = ZB[gid % NZ]
            eb = gid % NE
            S.act(E[eb][:, c0:512], c.ps[:, zb, c0:512], AF.Exp, [c.bank[zb]], [RE[eb]], scale=0.125)
            if r >= 0:
                S.tt("dve", E[eb][:, c0:c0 + 128], E[eb][:, c0:c0 + 128], c.mstrict, ALU.mult,
                     [RE[eb], c.Rconst], [RE[eb]])

        def st_l(gi):
            qg, kb, hd, r, c0, first, last, gid = info(gi)
            eb = gid % NE
            lb = gid % NL
            S.act(L[lb][:, c0:512], E[eb][:, c0:512], AF.Ln, [RE[eb]], [RL[lb]], bias=1.0, scale=1.0)

        def st_cum(gi):
            qg, kb, hd, r, c0, first, last, gid = info(gi)
            lb = gid % NL
            ab = AB[hd]
            S.mm(c.ps[:, ab, c0:512], c.tinc, L[lb][:, c0:512], first, False, [RL[lb], c.Rconst], [c.bank[ab]],
                 skip=True)

        def st_w(gi):
            qg, kb, hd, r, c0, first, last, gid = info(gi)
            ab = AB[hd]
            wb = gid % NW
            eb = gid % NE
            a_i = gid % NA
            S.act(Wb[wb][:, c0:512], c.ps[:, ab, c0:512], AF.Exp, [c.bank[ab]], [RW[wb]])
            S.tt("dve", Ab[a_i][:, c0:512], E[eb][:, c0:512], Wb[wb][:, c0:512], ALU.mult,
                 [RE[eb], RW[wb]], [RA[a_i]])

        def st_av(gi):
            qg, kb, hd, r, c0, first, last, gid = info(gi)
            lb = gid % NL
            ab = AB[hd]
            ob = OB[hd]
            a_i = gid % NA
            rows = slice(hd * 64, hd * 64 + 64)
            if not last:
                S.mm(c.ps[:, ab, c0:512], c.tcar, L[lb][:, c0:512], False, False, [RL[lb], c.Rconst], [c.bank[ab]],
                     skip=True)
            S.mm(c.ps[rows, ob, c0:512], v[:, kb, hd * 64:(hd + 1) * 64], Ab[a_i][:, c0:512], first, last,
                 [RA[a_i], Rv[kb // 4]], [c.bank[ob]], skip=True)
            if last:
                og = c.ogT[rows, hp, qg * 512:(qg + 1) * 512]
                S.tt("dve", og, c.ps[rows, ob, :], og, ALU.mult, [c.bank[ob], c.Rog[hp][qg]], [c.Rog[hp][qg]])

        stages = [(st_z, 0), (st_e, 1), (st_l, 2), (st_cum, 3), (st_w, 4), (st_av, 5)]
        for n in range(n_g + 5):
            for fn, d in reversed(stages):
                gi = n - d
                if 0 <= gi < n_g:
                    fn(gi)
        gcount += n_g


def layer_mla(c, w):
    S = c.S
    w_in = w["w_in"]
    gate_phase(c, w_in, 448)
    S.barrier()
    xnT3 = c.xnT[:, :].rearrange("p (c t) -> p c t", c=8)
    wv = w_in.rearrange("(c p) n -> p c n", p=128)
    sm = c.small
    scr = c.scr
    Rgn = Res("mla_g")
    gqa = c.gbc[:, 0:256]
    gkva = c.gbc[:, 256:384]
    gq192 = c.gbc[:, 384:576]
    gk192 = c.gbc[:, 576:768]
    gkpe = c.gbc[:, 704:768]
    S.ld("sp", gqa, w["q_a_norm"].partition_broadcast(128), writes=[Rgn])
    S.ld("sp", gkva, w["kv_a_norm"].partition_broadcast(128), writes=[Rgn])
    S.ld("sp", gq192, w["q_head_norm"].partition_broadcast(128), writes=[Rgn])
    S.ld("sp", gk192, w["k_head_norm"].partition_broadcast(128), writes=[Rgn])
    S.tt("dve", gq192[:, 0:128], gq192[:, 0:128], gk192[:, 0:128], ALU.mult, [Rgn], [Rgn])
    cosT = scr[:, 0:1024].rearrange("p (i f) -> p i f", f=32)
    sinT = scr[:, 1024:2048].rearrange("p (i f) -> p i f", f=32)
    Rtab = Res("ropetab")
    S.ld("sp", scr[:, 0:2048], c.cf_d[:, CF_COS:CF_COS + 2048], writes=[Rtab])
    kpeT = scr[:, 2048:4096].bitcast(BF16)
    qlnT3 = c.hbuf[:, 0:2 * S_LEN].rearrange("p (a t) -> p a t", a=2)
    kvlnT = c.hbuf[:, 2 * S_LEN:3 * S_LEN]
    sskpe = sm[:, 128:160]
    rstdk = sm[:, 160:192]
    Rqln = [Res(f"qln{g}") for g in range(NG)]
    Rkvln = [Res(f"kvln{g}") for g in range(NG)]
    Rkpe = [Res(f"kpe{g}") for g in range(NG)]
    Rsskpe = [Res(f"sskpe{g}") for g in range(NG)]

    def rope(x, out, i, Rx, Rout, t1, t2, Rt):
        cb = cosT[:, i, :].unsqueeze(1).to_broadcast([128, 2, 32])
        S.tt("dve", t1.rearrange("p (a f) -> p a f", a=2), x.rearrange("p (a f) -> p a f", a=2), cb, ALU.mult,
             [Rx, Rtab], [Rt])
        S.tt("dve", t2[:, 0:32], x[:, 32:64], sinT[:, i, :], ALU.mult, [Rx, Rtab], [Rt])
        S.tt("dve", t2[:, 32:64], x[:, 0:32], sinT[:, i, :], ALU.mult, [Rx, Rtab], [Rt])
        S.tt("dve", out[:, 0:32], t1[:, 0:32], t2[:, 0:32], ALU.subtract, [Rt], [Rout])
        S.tt("dve", out[:, 32:64], t1[:, 32:64], t2[:, 32:64], ALU.add, [Rt], [Rout])

    wlat = c.wbuf[:, 0:3584].rearrange("p (c n) -> p c n", c=8)
    Rwlat = Res("wlat")
    S.ld("pool", wlat[:, 0:4, :], wv[:, 0:4, 0:448], writes=[Rwlat])
    S.ld("pool", wlat[:, 4:8, :], wv[:, 4:8, 0:448], writes=[Rwlat])
    o = 4096
    junk = scr[:, o:o + 448]; o += 448
    lnb = [scr[:, o:o + 192].bitcast(BF16), scr[:, o + 192:o + 384].bitcast(BF16)]; o += 384
    kp = scr[:, o:o + 64]; o += 64
    t1 = scr[:, o:o + 64]; o += 64
    t2 = scr[:, o:o + 64]; o += 64
    kr = [scr[:, o:o + 32].bitcast(BF16), scr[:, o + 32:o + 64].bitcast(BF16)]; o += 64
    assert o <= 6144
    Rjunk, Rkp, Rt = Res("junk"), Res("kp"), Res("ropet")
    Rlnb = [Res("lnb0"), Res("lnb1")]
    Rkr = [Res("kr0"), Res("kr1")]
    Rst = [Res("mst0"), Res("mst1")]
    for i in range(NT):
        tg = i // 4
        pb = i % 2
        bank = 6 + pb
        psL = c.ps[:, bank, 0:448]
        for cc in range(8):
            S.mm(psL, xnT3[:, cc, i * 128:(i + 1) * 128], wlat[:, cc, :], cc == 0, cc == 7,
                 [Rwlat, c.RxnT[i]], [c.bank[bank]])
        sb_ = 192 + 16 * pb
        ss2 = sm[:, sb_:sb_ + 2]
        lg2 = sm[:, sb_ + 2:sb_ + 4]
        rs2 = sm[:, sb_ + 4:sb_ + 6]
        S.act(junk[:, 0:256], c.ps[:, bank, 0:256], AF.Square, [c.bank[bank]], [Rjunk, Rst[pb]], accum_out=ss2[:, 0:1])
        S.act(junk[:, 256:384], c.ps[:, bank, 256:384], AF.Square, [c.bank[bank]], [Rjunk, Rst[pb]], accum_out=ss2[:, 1:2])
        S.act(junk[:, 384:448], c.ps[:, bank, 384:448], AF.Square, [c.bank[bank]], [Rjunk, Rsskpe[tg]],
              accum_out=sskpe[:, i:i + 1])
        S.act(lg2[:, 0:1], ss2[:, 0:1], AF.Ln, [Rst[pb]], [Rst[pb]], scale=1.0 / 256, bias=EPS)
        S.act(lg2[:, 1:2], ss2[:, 1:2], AF.Ln, [Rst[pb]], [Rst[pb]], scale=1.0 / 128, bias=EPS)
        S.act(rs2, lg2, AF.Exp, [Rst[pb]], [Rst[pb]], scale=-0.5)
        S.stt(lnb[pb][:, 0:256], c.ps[:, bank, 0:256], rs2[:, 0:1], gqa, ALU.mult, ALU.mult,
              [c.bank[bank], Rst[pb], Rgn], [Rlnb[pb]])
        S.stt(lnb[pb][:, 256:384], c.ps[:, bank, 256:384], rs2[:, 1:2], gkva, ALU.mult, ALU.mult,
              [c.bank[bank], Rst[pb], Rgn], [Rlnb[pb]])
        S.tt("dve", kp, c.ps[:, bank, 384:448], gkpe, ALU.mult, [c.bank[bank], Rgn], [Rkp])
        rope(kp, kr[pb], i, Rkp, Rkr[pb], t1, t2, Rt)
        tbank = 4 + pb
        psT = c.ps[:, tbank, :].bitcast(BF16)
        for a in range(3):
            S.tr(psT[:, a * 128:(a + 1) * 128], lnb[pb][:, a * 128:(a + 1) * 128], c.ident,
                 [Rlnb[pb], c.Rconst], [c.bank[tbank]])
        S.tr(psT[0:64, 384:512], kr[pb], c.ident, [Rkr[pb], c.Rconst], [c.bank[tbank]])
        S.cp("act", qlnT3[:, :, i * 128:(i + 1) * 128], psT[:, 0:256].rearrange("p (a t) -> p a t", a=2),
             [c.bank[tbank]], [Rqln[tg]])
        S.cp("dve", kvlnT[:, i * 128:(i + 1) * 128], psT[:, 256:384], [c.bank[tbank]], [Rkvln[tg]])
        S.cp("dve", kpeT[0:64, i * 128:(i + 1) * 128], psT[0:64, 384:512], [c.bank[tbank]], [Rkpe[tg]])
    S.barrier()
    wuq = c.wbuf[:, 0:3072].rearrange("p (a n) -> p a n", a=2)
    wukv = c.wbuf[:, 3072:5120]
    Rwu = Res("wu")
    S.ld("pool", wuq, w["w_uq"].rearrange("(a p) n -> p a n", p=128), writes=[Rwu])
    S.ld("pool", wukv, w["w_ukv"], writes=[Rwu])
    X = c.xnT
    qTn = X[:, 0:4096]
    qTp = X[:, 4096:8192]
    kTn = X[:, 8192:12288]
    vh = X[:, 12288:16384].rearrange("p (i f) -> p i f", f=128)
    NP = 3
    Pb = [X[:, 16384 + 512 * j:16384 + 512 * (j + 1)] for j in range(NP)]
    qr = [X[:, 18432 + 256 * j:18432 + 256 * j + 192] for j in range(2)]
    XF = X[:, 20480:24576].bitcast(F32)
    rcb = XF[:, 0:512]
    tbuf = XF[:, 512:1024]
    qpe = XF[:, 1024:1088]
    u1 = XF[:, 1088:1152]
    u2 = XF[:, 1152:1216]
    junk2 = XF[:, 1216:1536]
    RqTn = [Res(f"qTn{g}") for g in range(NG)]
    RqTp = [Res(f"qTp{g}") for g in range(NG)]
    RkTn = [Res(f"kTn{g}") for g in range(NG)]
    Rvh = [Res(f"vh{g}") for g in range(NG)]
    Rrk = [Res(f"rstdk{g}") for g in range(NG)]
    RPb = [Res(f"mPb{j}") for j in range(NP)]
    Rqr = [Res("qr0"), Res("qr1")]
    Rrc, Rtb, Rqpe, Ru, Rj2 = Res("rcb"), Res("tbuf"), Res("qpe"), Res("u"), Res("junk2")
    Rs2 = [Res("hst0"), Res("hst1")]
    gid = 0
    sw = 0
    for h in range(8):
        for tg in range(NG):
            bank = tg % 4
            S.mm(c.ps[:, bank, :], wukv[:, h * 256:h * 256 + 128], kvlnT[:, tg * 512:(tg + 1) * 512], True, True,
                 [Rwu, Rkvln[tg]], [c.bank[bank]])
            S.cp("act", kTn[:, tg * 512:(tg + 1) * 512], c.ps[:, bank, :], [c.bank[bank]], [RkTn[tg]])
        for i in range(NT):
            tg, j = divmod(i, 4)
            pb = i % 2
            bank = 6 + pb
            tok = slice(i * 128, (i + 1) * 128)
            S.mm(c.ps[:, bank, 0:192], qlnT3[:, 0, tok], wuq[:, 0, h * 192:(h + 1) * 192], True, False,
                 [Rwu, Rqln[tg]], [c.bank[bank]])
            S.mm(c.ps[:, bank, 0:192], qlnT3[:, 1, tok], wuq[:, 1, h * 192:(h + 1) * 192], False, True,
                 [Rwu, Rqln[tg]], [c.bank[bank]])
            S.mm(c.ps[:, bank, 192:448], kvlnT[:, tok], wukv[:, h * 256:(h + 1) * 256], True, True,
                 [Rwu, Rkvln[tg]], [c.bank[bank]])
            sb_ = 224 + 16 * pb
            ss2 = sm[:, sb_:sb_ + 2]
            lg2 = sm[:, sb_ + 2:sb_ + 4]
            rsq = sm[:, sb_ + 4:sb_ + 5]
            S.act(junk2[:, 0:192], c.ps[:, bank, 0:192], AF.Square, [c.bank[bank]], [Rj2, Rs2[pb]], accum_out=ss2[:, 0:1])
            S.act(junk2[:, 192:320], c.ps[:, bank, 192:320], AF.Square, [c.bank[bank]], [Rj2, Rs2[pb]],
                  accum_out=ss2[:, 1:2])
            S.tt("dve", ss2[:, 1:2], ss2[:, 1:2], sskpe[:, i:i + 1], ALU.add, [Rs2[pb], Rsskpe[tg]], [Rs2[pb]])
            S.act(lg2, ss2, AF.Ln, [Rs2[pb]], [Rs2[pb]], scale=1.0 / 192, bias=EPS)
            S.act(rsq, lg2[:, 0:1], AF.Exp, [Rs2[pb]], [Rs2[pb]], scale=-0.5)
            S.act(rstdk[:, i:i + 1], lg2[:, 1:2], AF.Exp, [Rs2[pb]], [Rrk[tg]], scale=-0.5, bias=-0.5 * math.log(192.0))
            S.cp("act", vh[:, i, :], c.ps[:, bank, 320:448], [c.bank[bank]], [Rvh[tg]])
            S.stt(qr[pb][:, 0:128], c.ps[:, bank, 0:128], rsq, gq192[:, 0:128], ALU.mult, ALU.mult,
                  [c.bank[bank], Rs2[pb], Rgn], [Rqr[pb]])
            S.stt(qpe, c.ps[:, bank, 128:192], rsq, gq192[:, 128:192], ALU.mult, ALU.mult,
                  [c.bank[bank], Rs2[pb], Rgn], [Rqpe])
            rope(qpe, qr[pb][:, 128:192], i, Rqpe, Rqr[pb], u1, u2, Ru)
            tbank = 4 + tg % 2
            psT = c.ps[:, tbank, :].bitcast(BF16)
            S.tr(psT[:, j * 128:(j + 1) * 128], qr[pb][:, 0:128], c.ident, [Rqr[pb], c.Rconst], [c.bank[tbank]])
            S.tr(psT[0:64, 512 + j * 128:512 + (j + 1) * 128], qr[pb][:, 128:192], c.ident,
                 [Rqr[pb], c.Rconst], [c.bank[tbank]])
            if j == 3:
                S.cp("dve", qTn[:, tg * 512:(tg + 1) * 512], psT[:, 0:512], [c.bank[tbank]], [RqTn[tg]])
                S.cp("dve", qTp[0:64, tg * 512:(tg + 1) * 512], psT[0:64, 512:1024], [c.bank[tbank]], [RqTp[tg]])
        G = []
        for qg in range(NG):
            for kb in reversed(range(4 * qg + 4)):
                G.append((qg, kb))
        n_g = len(G)

        def info(gi):
            qg, kb = G[gi]
            r = kb - 4 * qg
            c0 = r * 128 if r >= 0 else 0
            return qg, kb, r, c0, kb == 4 * qg + 3, kb == 0, gid + gi, sw + qg

        def st_z(gi):
            qg, kb, r, c0, first, last, g_, sw_ = info(gi)
            zb = g_ % 4
            S.mm(c.ps[:, zb, c0:512], kTn[:, kb * 128:(kb + 1) * 128], qTn[:, qg * 512 + c0:(qg + 1) * 512], True, False,
                 [RkTn[kb // 4], RqTn[qg]], [c.bank[zb]])
            S.mm(c.ps[:, zb, c0:512], kpeT[0:64, kb * 128:(kb + 1) * 128], qTp[0:64, qg * 512 + c0:(qg + 1) * 512],
                 False, True, [Rkpe[kb // 4], RqTp[qg]], [c.bank[zb]])

        def st_e(gi):
            qg, kb, r, c0, first, last, g_, sw_ = info(gi)
            zb = g_ % 4
            pi = g_ % NP
            S.act(Pb[pi][:, c0:512], c.ps[:, zb, c0:512], AF.Exp, [c.bank[zb], Rrk[kb // 4]], [RPb[pi]],
                  scale=rstdk[:, kb:kb + 1])
            if r >= 0:
                S.tt("dve", Pb[pi][:, c0:c0 + 128], Pb[pi][:, c0:c0 + 128], c.mincl, ALU.mult,
                     [RPb[pi], c.Rconst], [RPb[pi]])

        def st_av(gi):
            qg, kb, r, c0, first, last, g_, sw_ = info(gi)
            pi = g_ % NP
            ob = 4 + sw_ % 2
            db = 6 + sw_ % 2
            S.mm(c.ps[:, ob, c0:512], vh[:, kb, :], Pb[pi][:, c0:512], first, last, [Rvh[kb // 4], RPb[pi]],
                 [c.bank[ob]], skip=True)
            S.mm(c.ps[:, db, c0:512], c.ones, Pb[pi][:, c0:512], first, last, [c.Rconst, RPb[pi]],
                 [c.bank[db]], skip=True)
            if last:
                S.op("dve", lambda e, o_=rcb, i_=c.ps[:, db, :]: e.reciprocal(out=o_, in_=i_), [c.bank[db]], [Rrc])
                S.tt("dve", tbuf, c.ps[:, ob, :], rcb, ALU.mult, [c.bank[ob], Rrc], [Rtb])
                og = c.ogT[:, h, qg * 512:(qg + 1) * 512]
                S.tt("dve", og, tbuf, og, ALU.mult, [Rtb, c.Rog[h][qg]], [c.Rog[h][qg]])

        stages = [(st_z, 0), (st_e, 1), (st_av, 2)]
        for n in range(n_g + 2):
            for fn, d in reversed(stages):
                gi = n - d
                if 0 <= gi < n_g:
                    fn(gi)
        gid += n_g
        sw += NG


def layer_swa(c, w):
    S = c.S
    w_in = w["w_in"]
    gate_phase(c, w_in, 1536)
    S.barrier()
    xnT3 = c.xnT[:, :].rearrange("p (c t) -> p c t", c=8)
    wv = w_in.rearrange("(c p) n -> p c n", p=128)
    qT3 = c.hbuf[:, 0:2 * S_LEN].rearrange("p (a t) -> p a t", a=2)
    kT2 = c.hbuf[:, 2 * S_LEN:3 * S_LEN]
    scr = c.scr
    off = 0
    vg = scr[:, off:off + 1024].bitcast(BF16).rearrange("p (i f) -> p i f", f=64); off += 1024
    junk = scr[:, off:off + 320]; off += 320
    tmpq = scr[:, off:off + 256]; off += 256
    qn = [scr[:, off:off + 128].bitcast(BF16), scr[:, off + 128:off + 256].bitcast(BF16)]; off += 256
    biasg = scr[:, off:off + 1024]; off += 1024
    Tb = scr[:, off:off + 1024]; off += 1024
    Pb = [scr[:, off:off + 512].bitcast(BF16), scr[:, off + 512:off + 1024].bitcast(BF16)]; off += 1024
    dnb = scr[:, off:off + 256]; off += 256
    rcb = scr[:, off:off + 256]; off += 256
    tb = scr[:, off:off + 256]; off += 256
    gqk4 = scr[:, off:off + 256]; off += 256
    assert off <= 6144
    sm = c.small
    kscale = sm[:, 16:48]
    es16 = sm[:, 48:64]
    Rjunk, Rtmpq, Rbias, RTb, Rdn, Rrc, Rtb, Rgqk, Res16 = (Res(n) for n in
                                                          ("junk", "tmpq", "biasg", "Tb", "dnb", "rcb", "tb", "gqk", "es16"))
    Rqn = [Res("qn0"), Res("qn1")]
    RPb = [Res("Pb0"), Res("Pb1")]
    Rst = [Res("sst0"), Res("sst1")]
    RqT = [Res(f"qT{g}") for g in range(NG)]
    RkT = [Res(f"kT{g}") for g in range(NG)]
    Rvg = [Res(f"vg{g}") for g in range(NG)]
    Rks = [Res(f"ks{g}") for g in range(NG)]
    wtm = [c.wbuf[:, 4096 * b:4096 * b + 3072].rearrange("p (c n) -> p c n", c=8) for b in range(2)]
    wk2 = [c.wbuf[:, 4096 * b + 3072:4096 * (b + 1)].rearrange("p (c n) -> p c n", c=8) for b in range(2)]
    Rw = [Res("swaw0"), Res("swaw1")]
    gk4 = junk[:, 0:256]
    for j in range(4):
        S.ld("sp", gqk4[:, j * 64:(j + 1) * 64], w["q_head_norm"].partition_broadcast(128), writes=[Rgqk])
        S.ld("sp", gk4[:, j * 64:(j + 1) * 64], w["k_head_norm"].partition_broadcast(128), writes=[Rjunk])
    S.tt("dve", gqk4, gqk4, gk4, ALU.mult, [Rjunk, Rgqk], [Rgqk])
    S.ld("sp", es16, w["sinks"].partition_broadcast(128), writes=[Res16])
    S.act(es16, es16, AF.Exp, [Res16], [Res16])

    def load_w(g):
        b = g % 2
        S.ld("pool", wtm[b][:, :, 0:256], wv[:, :, g * 256:(g + 1) * 256], writes=[Rw[b]])
        S.ld("pool", wtm[b][:, :, 256:320], wv[:, :, 1024 + g * 64:1024 + (g + 1) * 64], writes=[Rw[b]])
        S.ld("pool", wtm[b][:, :, 320:384], wv[:, :, 1280 + g * 64:1280 + (g + 1) * 64], writes=[Rw[b]])
        for d in range(2):
            S.ld("pool", wk2[b][:, :, d * 64:(d + 1) * 64], wv[:, :, 1024 + g * 64:1024 + (g + 1) * 64], writes=[Rw[b]])

    load_w(0)
    for g in range(4):
        b = g % 2
        if g + 1 < 4:
            load_w(g + 1)
        for kbi in range(2):
            S.ld("sp", biasg[:, kbi * 512:(kbi + 1) * 512],
                 c.cf_d[:, CF_BIAS + kbi * 2048 + g * 512:CF_BIAS + kbi * 2048 + (g + 1) * 512], writes=[Rbias])
        for tg in range(NG):
            bank = tg % 4
            for cc in range(8):
                S.mm(c.ps[:, bank, :], wk2[b][:, cc, :], xnT3[:, cc, tg * 512:(tg + 1) * 512], cc == 0, cc == 7,
                     [Rw[b]] + c.RxnT[4 * tg:4 * tg + 4], [c.bank[bank]])
            S.cp("act", kT2[:, tg * 512:(tg + 1) * 512], c.ps[:, bank, :], [c.bank[bank]], [RkT[tg]])
        for i in range(NT):
            tg, j = divmod(i, 4)
            pb = i % 2
            bank = 6 + pb
            psTM = c.ps[:, bank, 0:384]
            for cc in range(8):
                S.mm(psTM, xnT3[:, cc, i * 128:(i + 1) * 128], wtm[b][:, cc, :], cc == 0, cc == 7,
                     [Rw[b], c.RxnT[i]], [c.bank[bank]])
            ss5 = sm[:, 64 + 16 * pb:64 + 16 * pb + 5]
            lg5 = sm[:, 72 + 16 * pb:72 + 16 * pb + 5]
            rs4 = sm[:, 96 + 8 * pb:96 + 8 * pb + 4]
            S.act(junk, c.ps[:, bank, 0:320], AF.Square, [c.bank[bank]], [Rjunk])
            S.op("dve", lambda e, o=ss5, i_=junk.rearrange("p (h f) -> p h f", f=64): e.reduce_sum(out=o, in_=i_, axis=AX.X),
                 [Rjunk], [Rst[pb]])
            S.act(lg5, ss5, AF.Ln, [Rst[pb]], [Rst[pb]], scale=1.0 / 64, bias=EPS)
            S.act(rs4, lg5[:, 0:4], AF.Exp, [Rst[pb]], [Rst[pb]], scale=-0.5)
            S.act(kscale[:, i:i + 1], lg5[:, 4:5], AF.Exp, [Rst[pb]], [Rks[tg]], scale=-0.5, bias=math.log(0.125))
            S.tt("dve", tmpq.rearrange("p (h f) -> p h f", f=64),
                 c.ps[:, bank, 0:256].rearrange("p (h f) -> p h f", f=64),
                 rs4.unsqueeze(2).to_broadcast([128, 4, 64]), ALU.mult, [c.bank[bank], Rst[pb]], [Rtmpq])
            S.tt("dve", qn[pb], tmpq, gqk4, ALU.mult, [Rtmpq, Rgqk], [Rqn[pb]])
            S.cp("act", vg[:, i, :], c.ps[:, bank, 320:384], [c.bank[bank]], [Rvg[tg]])
            tbank = 4 + tg % 2
            psT = c.ps[:, tbank, :].bitcast(BF16)
            for a in range(2):
                S.tr(psT[:, a * 512 + j * 128:a * 512 + (j + 1) * 128], qn[pb][:, a * 128:(a + 1) * 128], c.ident,
                     [Rqn[pb], c.Rconst], [c.bank[tbank]])
            if j == 3:
                S.cp("dve", qT3[:, :, tg * 512:(tg + 1) * 512], psT.rearrange("p (a t) -> p a t", a=2),
                     [c.bank[tbank]], [RqT[tg]])
        for qb in range(NT):
            kbs = [(0, qb - 1), (1, qb)] if qb > 0 else [(1, qb)]
            zb0 = 2 * (qb % 2)
            for kbi, kb in kbs:
                for hq in range(4):
                    p = hq % 2
                    rows = slice(p * 64, p * 64 + 64)
                    col = (kbi * 2 + hq // 2) * 128
                    S.mm(c.ps[:, zb0 + p, col:col + 128], kT2[rows, kb * 128:(kb + 1) * 128],
                         qT3[rows, hq // 2, qb * 128:(qb + 1) * 128], True, True,
                         [RkT[kb // 4], RqT[qb // 4]], [c.bank[zb0 + p]])
            for kbi, kb in kbs:
                for p in range(2):
                    src = c.ps[:, zb0 + p, kbi * 256:(kbi + 1) * 256].rearrange("q (a t) -> q a t", a=2)
                    dst = Tb[:, kbi * 512:(kbi + 1) * 512].rearrange("q (a p t) -> q p a t", a=2, p=2)[:, p]
                    bia = biasg[:, kbi * 512:(kbi + 1) * 512].rearrange("q (a p t) -> q p a t", a=2, p=2)[:, p]
                    S.stt(dst, src, kscale[:, kb:kb + 1], bia, ALU.mult, ALU.add,
                          [c.bank[zb0 + p], Rks[kb // 4], Rbias], [RTb])
            pbi = qb % 2
            lo = 0 if qb > 0 else 512
            S.act(Pb[pbi][:, lo:1024], Tb[:, lo:1024], AF.Exp, [RTb], [RPb[pbi]])
            ob = 4 + qb % 2
            for hq in range(4):
                rows = slice((hq % 2) * 64, (hq % 2) * 64 + 64)
                col = (hq // 2) * 128
                for n_, (kbi, kb) in enumerate(kbs):
                    S.mm(c.ps[rows, ob, col:col + 128], vg[:, kb, :], Pb[pbi][:, (kbi * 4 + hq) * 128:(kbi * 4 + hq + 1) * 128],
                         n_ == 0, n_ == len(kbs) - 1, [Rvg[kb // 4], RPb[pbi]], [c.bank[ob]])
                for n_, (kbi, kb) in enumerate(kbs):
                    S.mm(c.ps[rows, ob, 256 + col:256 + col + 128], c.ones[:, 0:64],
                         Pb[pbi][:, (kbi * 4 + hq) * 128:(kbi * 4 + hq + 1) * 128],
                         n_ == 0, n_ == len(kbs) - 1, [c.Rconst, RPb[pbi]], [c.bank[ob]])
            for hq in range(4):
                rows = slice((hq % 2) * 64, (hq % 2) * 64 + 64)
                col = (hq // 2) * 128
                h = 4 * g + hq
                S.ts("dve", dnb[rows, col:col + 128], c.ps[rows, ob, 256 + col:256 + col + 128], es16[rows, h:h + 1],
                     None, ALU.add, None, [c.bank[ob], Res16], [Rdn])
            S.op("dve", lambda e, o=rcb, i_=dnb: e.reciprocal(out=o, in_=i_), [Rdn], [Rrc])
            S.tt("dve", tb, c.ps[:, ob, 0:256], rcb, ALU.mult, [c.bank[ob], Rrc], [Rtb])
            og = c.ogT[:, 2 * g:2 * g + 2, qb * 128:(qb + 1) * 128]
            Rogs = [c.Rog[2 * g][qb // 4], c.Rog[2 * g + 1][qb // 4]]
            S.tt("dve", og, tb.rearrange("p (a t) -> p a t", a=2), og, ALU.mult, [Rtb] + Rogs, Rogs)


LAUNCH_GROUPS = [[0, 1, 2, 3]]
_CONSTS = None


def run_layers(layers, xs, inputs):
    global _CONSTS
    if _CONSTS is None:
        _CONSTS = make_consts()
    cbf, cf = _CONSTS
    nc = build_program(layers)
    names = [n for li in layers for n in LAYER_WEIGHTS[li]]
    in_maps = []
    for b in range(len(xs)):
        m = {"x": np.ascontiguousarray(xs[b], dtype=np.float32), "cbf": cbf, "cf": cf}
        for n in names:
            m[n] = np.ascontiguousarray(inputs[n], dtype=np.float32)
        in_maps.append(m)
    res = run_bass_kernel_spmd(nc, in_maps, core_ids=list(range(len(xs))))
    return [r["y"] for r in res.results]


def kernel(**inputs):
    x = np.asarray(inputs["x"])
    xs = [x[b] for b in range(x.shape[0])]
    for grp in LAUNCH_GROUPS:
        xs = run_layers(grp, xs, inputs)
    return np.stack(xs, axis=0).astype(np.float32)
```

```python
import math
import numpy as np
import ml_dtypes
import concourse.bass as bass
import concourse.mybir as mybir
from concourse.bass_utils import run_bass_kernel_spmd

F32 = mybir.dt.float32
BF16 = mybir.dt.bfloat16
AF = mybir.ActivationFunctionType
ALU = mybir.AluOpType
AX = mybir.AxisListType

S_LEN = 4096
D = 1024
NT = S_LEN // 128
NG = S_LEN // 512
EPS = 1e-6

LAYER_KIND = ["sb", "mla", "swa", "sb"]
LAYER_WEIGHTS = [
    ["l0_norm", "l0_w_in", "l0_w_out"],
    ["l1_norm", "l1_w_in", "l1_q_a_norm", "l1_w_uq", "l1_kv_a_norm", "l1_w_ukv",
     "l1_q_head_norm", "l1_k_head_norm", "l1_w_out"],
    ["l2_norm", "l2_w_in", "l2_q_head_norm", "l2_k_head_norm", "l2_sinks", "l2_w_out"],
    ["l3_norm", "l3_w_in", "l3_w_out"],
]
WSHAPES = {
    "l0_norm": [1024], "l0_w_in": [1024, 4096], "l0_w_out": [1024, 1024],
    "l1_norm": [1024], "l1_w_in": [1024, 1472], "l1_q_a_norm": [256], "l1_w_uq": [256, 1536],
    "l1_kv_a_norm": [128], "l1_w_ukv": [128, 2048], "l1_q_head_norm": [192], "l1_k_head_norm": [192],
    "l1_w_out": [1024, 1024],
    "l2_norm": [1024], "l2_w_in": [1024, 2560], "l2_q_head_norm": [64], "l2_k_head_norm": [64],
    "l2_sinks": [16], "l2_w_out": [1024, 1024],
    "l3_norm": [1024], "l3_w_in": [1024, 4096], "l3_w_out": [1024, 1024],
}


class Res:
    __slots__ = ("name", "w", "r", "excl")

    def __init__(self, name, excl=False):
        self.name = name
        self.w = None
        self.r = {}
        self.excl = excl


class Sched:
    ENGS = ("pe", "act", "dve", "pool", "sp")

    def __init__(self, sems, dma_pools):
        self.sems = sems
        self.cnt = {k: 0 for k in sems}
        self.ops = {e: [] for e in self.ENGS}
        self.seen = {e: {} for e in self.ENGS}
        self.dma_pools = dma_pools
        self.dma_rr = {q: 0 for q in dma_pools}
        self.nwaits = 0

    def _wait(self, e, key, val):
        if val <= 0 or val <= self.seen[e].get(key, 0):
            return
        self.seen[e][key] = val
        sem = self.sems[key]
        self.ops[e].append(lambda eng, sem=sem, val=val: eng.wait_ge(sem, val))
        self.nwaits += 1

    def _sync(self, e, reads, writes, dma):
        rd = [r for r in reads if not r.excl]
        wr = list(writes) + [r for r in reads if r.excl]
        need = []
        for r in rd:
            if r.w is not None:
                need.append((r.w, True))
        for r in wr:
            if r.w is not None:
                need.append((r.w, False))
            for ev in r.r.values():
                need.append((ev, False))
        for (key, val, eng), raw in need:
            if eng == e and not dma:
                if e == "pe":
                    continue
            self._wait(e, key, val)
        return rd, wr

    def _update(self, rd, wr, ev):
        for r in rd:
            r.r[ev[0]] = ev
        for r in wr:
            r.w = ev
            r.r = {}

    def op(self, e, fn, reads=(), writes=()):
        rd, wr = self._sync(e, reads, writes, False)
        self.cnt[e] += 1
        sem = self.sems[e]
        self.ops[e].append(lambda eng, fn=fn, sem=sem: fn(eng).then_inc(sem, 1))
        self._update(rd, wr, (e, self.cnt[e], e))

    def dma(self, q, fn, reads=(), writes=()):
        pool = self.dma_pools[q]
        k = pool[self.dma_rr[q] % len(pool)]
        self.dma_rr[q] += 1
        self._wait(q, k, self.cnt[k])
        rd, wr = self._sync(q, reads, writes, True)
        self.cnt[k] += 16
        sem = self.sems[k]
        self.ops[q].append(lambda eng, fn=fn, sem=sem: fn(eng).then_inc(sem, 16))
        self._update(rd, wr, (k, self.cnt[k], None))

    def barrier(self, engines=None):
        for e in (engines or self.ENGS):
            for k, v in self.cnt.items():
                if k != e:
                    self._wait(e, k, v)


    def mm(self, out, lhsT, rhs, start, stop, reads, writes, skip=False):
        if skip:
            self.op("pe", lambda e: e.matmul(out, lhsT=lhsT, rhs=rhs, start=start, stop=stop, skip_group_check=True),
                    reads, writes)
        else:
            self.op("pe", lambda e: e.matmul(out, lhsT=lhsT, rhs=rhs, start=start, stop=stop), reads, writes)

    def tr(self, out, in_, ident, reads, writes):
        self.op("pe", lambda e: e.transpose(out=out, in_=in_, identity=ident), reads, writes)

    def act(self, out, in_, func, reads, writes, **kw):
        self.op("act", lambda e: e.activation(out=out, in_=in_, func=func, **kw), reads, writes)

    def tt(self, eng, out, in0, in1, op, reads, writes):
        self.op(eng, lambda e: e.tensor_tensor(out=out, in0=in0, in1=in1, op=op), reads, writes)

    def stt(self, out, in0, scalar, in1, op0, op1, reads, writes):
        self.op("dve", lambda e: e.scalar_tensor_tensor(out=out, in0=in0, scalar=scalar, in1=in1, op0=op0, op1=op1),
                reads, writes)

    def ts(self, eng, out, in0, s1, s2, op0, op1, reads, writes):
        if op1 is None:
            self.op(eng, lambda e: e.tensor_scalar(out=out, in0=in0, scalar1=s1, scalar2=None, op0=op0), reads, writes)
        else:
            self.op(eng, lambda e: e.tensor_scalar(out=out, in0=in0, scalar1=s1, scalar2=s2, op0=op0, op1=op1),
                    reads, writes)

    def cp(self, eng, out, in_, reads, writes):
        if eng == "act":
            self.op("act", lambda e: e.copy(out=out, in_=in_), reads, writes)
        else:
            self.op(eng, lambda e: e.tensor_copy(out=out, in_=in_), reads, writes)

    def ld(self, q, out, in_, reads=(), writes=()):
        self.dma(q, lambda e: e.dma_start(out=out, in_=in_), reads, writes)

    def emit(self, block):
        def mk(e):
            def body(eng):
                for f in self.ops[e]:
                    f(eng)
            return body
        block.tensor(mk("pe"))
        block.scalar(mk("act"))
        block.vector(mk("dve"))
        block.gpsimd(mk("pool"))
        block.sync(mk("sp"))


def make_consts():
    j = np.arange(128)[:, None]
    s = np.arange(128)[None, :]
    ident = (j == s).astype(np.float32)
    tinc = -(j >= s).astype(np.float32)
    tcar = -(j < s).astype(np.float32)
    ones = np.ones((128, 128), np.float32)
    cbf = np.concatenate([ident, tinc, tcar, ones], axis=1)
    mstrict = (s > j).astype(np.float32)
    mincl = (s >= j).astype(np.float32)
    half = 32
    inv_freq = (10000.0 ** (-np.arange(half, dtype=np.float32) / half)).astype(np.float32)
    pos = np.arange(S_LEN, dtype=np.float32)
    ang = (pos[:, None] * inv_freq[None, :]).astype(np.float32)
    cos = np.cos(ang).astype(np.float32)
    sin = np.sin(ang).astype(np.float32)
    cosT = cos.reshape(NT, 128, 32).transpose(1, 0, 2).reshape(128, NT * 32)
    sinT = sin.reshape(NT, 128, 32).transpose(1, 0, 2).reshape(128, NT * 32)
    slopes = (2.0 ** (-8.0 * np.arange(1, 17, dtype=np.float32) / 16)).astype(np.float32)
    NEG = -30000.0
    bias = np.zeros((128, 2, 16, 128), np.float32)
    for kbi in range(2):
        rel = (s + 128 - (j + 128 * kbi)).astype(np.float32)
        valid = (rel >= 0) & (rel < 128)
        for h in range(16):
            bias[:, kbi, h, :] = np.where(valid, -slopes[h] * rel, NEG)
    cf = np.concatenate([mstrict, mincl, cosT, sinT, bias.reshape(128, -1)], axis=1).astype(np.float32)
    return cbf.astype(np.float32), cf


CF_MSTRICT = 0
CF_MINCL = 128
CF_COS = 256
CF_SIN = 256 + NT * 32
CF_BIAS = 256 + 2 * NT * 32
CF_TOTAL = CF_BIAS + 2 * 16 * 128


class Ctx:
    pass


def build_program(layers, dbg=None):
    nc = bass.Bass("TRN2", target_bir_lowering=False)
    x_in = nc.dram_tensor("x", [S_LEN, D], F32, kind="ExternalInput").ap()
    y_out = nc.dram_tensor("y", [S_LEN, D], F32, kind="ExternalOutput").ap()
    cbf_d = nc.dram_tensor("cbf", [128, 512], F32, kind="ExternalInput").ap()
    cf_d = nc.dram_tensor("cf", [128, CF_TOTAL], F32, kind="ExternalInput").ap()
    W = {}
    for li in layers:
        for n in LAYER_WEIGHTS[li]:
            W[n] = nc.dram_tensor(n, WSHAPES[n], F32, kind="ExternalInput").ap()
    xs = [x_in]
    for i in range(len(layers) - 1):
        xs.append(nc.dram_tensor(f"xmid{i}", [S_LEN, D], F32, kind="Internal").ap())
    xs.append(y_out)

    from contextlib import ExitStack
    with ExitStack() as st:
        def sb(name, shape, dt):
            return st.enter_context(nc.sbuf_tensor(name, shape, dt))
        c = Ctx()
        c.nc = nc
        c.xnT = sb("xnT", [128, 8 * S_LEN], BF16)
        c.ogT = sb("ogT", [128, 8, S_LEN], BF16)
        c.hbuf = sb("hbuf", [128, 3 * S_LEN], BF16)
        c.wbuf = sb("wbuf", [128, 8192], BF16)
        c.scr = sb("scr", [128, 6144], F32)
        c.gbc = sb("gbc", [128, 1024], F32)
        c.cbf = sb("cbfs", [128, 512], BF16)
        c.cmask = sb("cmask", [128, 256], F32)
        c.small = sb("small", [128, 512], F32)
        c.ps = st.enter_context(nc.psum_tensor("ps", [128, 8, 512], F32))
        sem_names = list(Sched.ENGS) + [f"d{i}" for i in range(20)]
        sems = {k: st.enter_context(nc.semaphore(f"s_{k}")) for k in sem_names}
        S = Sched(sems, {"sp": [f"d{i}" for i in range(0, 8)],
                         "pool": [f"d{i}" for i in range(8, 16)],
                         "act": [f"d{i}" for i in range(16, 20)]})
        c.S = S
        c.W = W
        c.cf_d = cf_d
        c.dbg = dbg
        c.bank = [Res(f"bank{b}", excl=True) for b in range(8)]
        c.ident = c.cbf[:, 0:128]
        c.tinc = c.cbf[:, 128:256]
        c.tcar = c.cbf[:, 256:384]
        c.ones = c.cbf[:, 384:512]
        c.mstrict = c.cmask[:, 0:128]
        c.mincl = c.cmask[:, 128:256]
        c.Rconst = Res("const")
        S.ld("pool", c.cbf[:], cbf_d[:, :], writes=[c.Rconst])
        S.ld("sp", c.cmask[:], cf_d[:, 0:256], writes=[c.Rconst])
        S.barrier()

        block = st.enter_context(nc.Block())
        for idx, li in enumerate(layers):
            kind = LAYER_KIND[li]
            pre = f"l{li}_"
            phase_norm(c, xs[idx], W[pre + "norm"])
            S.barrier()
            if kind == "sb":
                layer_sb(c, W[pre + "w_in"])
            elif kind == "mla":
                layer_mla(c, {k[3:]: v for k, v in W.items() if k.startswith(pre)})
            else:
                layer_swa(c, {k[3:]: v for k, v in W.items() if k.startswith(pre)})
            S.barrier()
            phase_out(c, xs[idx], xs[idx + 1], W[pre + "w_out"])
            S.barrier()
        S.emit(block)
    return nc


def phase_norm(c, x_d, g_d):
    S = c.S
    xnT3 = c.xnT[:, :].rearrange("p (c t) -> p c t", c=8)
    Rg = Res("gbc")
    S.ld("sp", c.gbc[:], g_d.partition_broadcast(128), writes=[Rg])
    xin = [c.scr[:, 0:1024], c.scr[:, 1024:2048]]
    Rxin = [Res("xin0"), Res("xin1")]
    junk = c.scr[:, 2048:3072]
    Rjunk = Res("junk")
    xn = [c.scr[:, 3072:3584].bitcast(BF16), c.scr[:, 3584:4096].bitcast(BF16)]
    Rxn = [Res("xn0"), Res("xn1")]
    st = c.small
    Rst = [Res("st0"), Res("st1")]
    c.RxnT = [Res(f"xnT{i}") for i in range(NT)]
    for i in range(NT):
        b = i % 2
        S.ld("sp", xin[b], x_d[i * 128:(i + 1) * 128, :], writes=[Rxin[b]])
        ss = st[:, 4 * b:4 * b + 1]
        lg = st[:, 4 * b + 1:4 * b + 2]
        rs = st[:, 4 * b + 2:4 * b + 3]
        S.act(junk, xin[b], AF.Square, [Rxin[b]], [Rjunk, Rst[b]], accum_out=ss)
        S.act(lg, ss, AF.Ln, [Rst[b]], [Rst[b]], scale=1.0 / D, bias=EPS)
        S.act(rs, lg, AF.Exp, [Rst[b]], [Rst[b]], scale=-0.5)
        S.stt(xn[b], xin[b], rs, c.gbc[:], ALU.mult, ALU.mult, [Rxin[b], Rst[b], Rg], [Rxn[b]])
        bank = 4 + b
        psT = c.ps[:, bank, :].bitcast(BF16)
        for ch in range(8):
            S.tr(psT[:, ch * 128:(ch + 1) * 128], xn[b][:, ch * 128:(ch + 1) * 128], c.ident,
                 [Rxn[b], c.Rconst], [c.bank[bank]])
        dst = xnT3[:, :, i * 128:(i + 1) * 128]
        src = psT.rearrange("p (c t) -> p c t", c=8)
        S.cp("act" if b == 0 else "dve", dst, src, [c.bank[bank]], [c.RxnT[i]])


def phase_out(c, x_d, y_d, wout_d):
    S = c.S
    wo = c.wbuf[:, :].rearrange("p (c n) -> p c n", c=8)
    Rwo = Res("wo")
    wv = wout_d.rearrange("(c p) n -> p c n", p=128)
    for h in range(2):
        S.ld("pool", wo[:, 4 * h:4 * h + 4, :], wv[:, 4 * h:4 * h + 4, :], writes=[Rwo])
    xin = [c.scr[:, 0:1024], c.scr[:, 1024:2048]]
    Rxin = [Res("cxin0"), Res("cxin1")]
    yo = [c.scr[:, 2048:3072], c.scr[:, 3072:4096]]
    Ryo = [Res("yo0"), Res("yo1")]
    for i in range(NT):
        b = i % 2
        S.ld("sp", xin[b], x_d[i * 128:(i + 1) * 128, :], writes=[Rxin[b]])
        for h in range(2):
            bank = 2 * b + h
            for ch in range(8):
                S.mm(c.ps[:, bank, :], c.ogT[:, ch, i * 128:(i + 1) * 128], wo[:, ch, h * 512:(h + 1) * 512],
                     ch == 0, ch == 7, [Rwo, c.Rog[ch][i // 4]], [c.bank[bank]])
            S.tt("dve", yo[b][:, h * 512:(h + 1) * 512], c.ps[:, bank, :], xin[b][:, h * 512:(h + 1) * 512],
                 ALU.add, [c.bank[bank], Rxin[b]], [Ryo[b]])
        S.ld("sp", y_d[i * 128:(i + 1) * 128, :], yo[b], reads=[Ryo[b]])


def gate_phase(c, w_in, col0):
    S = c.S
    xnT3 = c.xnT[:, :].rearrange("p (c t) -> p c t", c=8)
    wv = w_in.rearrange("(c p) n -> p c n", p=128)
    wg = [c.wbuf[:, 6144 + 1024 * b: 6144 + 1024 * (b + 1)].rearrange("p (c n) -> p c n", c=8) for b in range(2)]
    Rwg = [Res("wg0"), Res("wg1")]
    c.Rog = [[Res(f"og{ch}_{tg}") for tg in range(NG)] for ch in range(8)]
    k = 0
    for ch in range(8):
        b = ch % 2
        S.ld("pool", wg[b], wv[:, :, col0 + ch * 128: col0 + (ch + 1) * 128], writes=[Rwg[b]])
        for tg in range(NG):
            bank = k % 4
            k += 1
            for cc in range(8):
                S.mm(c.ps[:, bank, :], wg[b][:, cc, :], xnT3[:, cc, tg * 512:(tg + 1) * 512], cc == 0, cc == 7,
                     [Rwg[b]] + c.RxnT[4 * tg:4 * tg + 4], [c.bank[bank]])
            S.act(c.ogT[:, ch, tg * 512:(tg + 1) * 512], c.ps[:, bank, :], AF.Silu, [c.bank[bank]], [c.Rog[ch][tg]])


def layer_sb(c, w_in):
    S = c.S
    gate_phase(c, w_in, 3072)
    S.barrier()
    xnT3 = c.xnT[:, :].rearrange("p (c t) -> p c t", c=8)
    wv = w_in.rearrange("(c p) n -> p c n", p=128)
    qT = c.hbuf[:, 0:S_LEN]
    kT = c.hbuf[:, S_LEN:2 * S_LEN]
    v = c.hbuf[:, 2 * S_LEN:3 * S_LEN].rearrange("p (i f) -> p i f", f=128)
    RqT = [Res(f"qT{g}") for g in range(NG)]
    RkT = [Res(f"kT{g}") for g in range(NG)]
    Rv = [Res(f"v{g}") for g in range(NG)]
    wsl = [c.wbuf[:, 3072 * b:3072 * (b + 1)].rearrange("p (s c n) -> p s c n", s=3, c=8) for b in range(2)]
    Rw = [Res("wsl0"), Res("wsl1")]
    NE, NL, NW, NA = 3, 4, 2, 2
    off = 0
    E, L, Wb, Ab = [], [], [], []
    for i in range(NE):
        E.append(c.scr[:, off:off + 1024].rearrange("p (h t) -> p h t", h=2)); off += 1024
    for i in range(NL):
        L.append(c.scr[:, off:off + 512].bitcast(BF16).rearrange("p (h t) -> p h t", h=2)); off += 512
    for i in range(NW):
        Wb.append(c.scr[:, off:off + 512].bitcast(BF16).rearrange("p (h t) -> p h t", h=2)); off += 512
    assert off <= 6144
    for i in range(NA):
        Ab.append(c.wbuf[:, 6144 + 1024 * i:6144 + 1024 * (i + 1)].rearrange("p (h t) -> p h t", h=2))
    RE = [Res(f"E{i}") for i in range(NE)]
    RL = [Res(f"L{i}") for i in range(NL)]
    RW = [Res(f"W{i}") for i in range(NW)]
    RA = [Res(f"A{i}") for i in range(NA)]
    AB = [4, 5]
    OB = [6, 7]
    gcount = 0
    mask2 = c.mstrict.unsqueeze(1).to_broadcast([128, 2, 128])

    def load_w(hp):
        b = hp % 2
        for s in range(3):
            col = s * 1024 + hp * 128
            S.ld("pool", wsl[b][:, s, :, :], wv[:, :, col:col + 128], writes=[Rw[b]])

    load_w(0)
    for hp in range(8):
        b = hp % 2
        if hp + 1 < 8:
            load_w(hp + 1)
        k = 0
        for tg in range(NG):
            for s, (dst, Rd) in enumerate(((qT, RqT), (kT, RkT))):
                bank = k % 4
                k += 1
                for cc in range(8):
                    S.mm(c.ps[:, bank, :], wsl[b][:, s, cc, :], xnT3[:, cc, tg * 512:(tg + 1) * 512], cc == 0, cc == 7,
                         [Rw[b]] + c.RxnT[4 * tg:4 * tg + 4], [c.bank[bank]])
                if s == 0:
                    S.cp("dve", dst[:, tg * 512:(tg + 1) * 512], c.ps[:, bank, :], [c.bank[bank]], [Rd[tg]])
                else:
                    S.cp("act", kT[:, tg * 512:(tg + 1) * 512], c.ps[:, bank, :], [c.bank[bank]], [RkT[tg]])
            bank = k % 4
            k += 1
            for j in range(4):
                i = 4 * tg + j
                for cc in range(8):
                    S.mm(c.ps[:, bank, j * 128:(j + 1) * 128], xnT3[:, cc, i * 128:(i + 1) * 128], wsl[b][:, 2, cc, :],
                         cc == 0, cc == 7, [Rw[b], c.RxnT[i]], [c.bank[bank]])
            S.cp("dve", v[:, 4 * tg:4 * tg + 4, :], c.ps[:, bank, :].rearrange("p (j f) -> p j f", f=128),
                 [c.bank[bank]], [Rv[tg]])
        G = []
        for qg in range(NG):
            for kb in reversed(range(4 * qg + 4)):
                G.append((qg, kb))
        n_g = len(G)

        def info(gi):
            qg, kb = G[gi]
            r = kb - 4 * qg
            c0 = r * 128 if r >= 0 else 0
            return qg, kb, r, c0, kb == 4 * qg + 3, kb == 0, gcount + gi

        def st_z(gi):
            qg, kb, r, c0, first, last, gid = info(gi)
            zp = 2 * (gid % 2)
            for hd in range(2):
                rows = slice(hd * 64, hd * 64 + 64)
                S.mm(c.ps[:, zp + hd, c0:512], kT[rows, kb * 128:(kb + 1) * 128], qT[rows, qg * 512 + c0:(qg + 1) * 512],
                     True, True, [RkT[kb // 4], RqT[qg]], [c.bank[zp + hd]])

        def st_e(gi):
            qg, kb, r, c0, first, last, gid = info(gi)
            zp = 2 * (gid % 2)
            eb = gid % NE
            S.act(E[eb][:, :, c0:512], c.ps[:, zp:zp + 2, c0:512], AF.Exp, [c.bank[zp], c.bank[zp + 1]], [RE[eb]],
                  scale=0.125)
            if r >= 0:
                S.tt("dve", E[eb][:, :, c0:c0 + 128], E[eb][:, :, c0:c0 + 128], mask2, ALU.mult,
                     [RE[eb], c.Rconst], [RE[eb]])

        def st_l(gi):
            qg, kb, r, c0, first, last, gid = info(gi)
            eb = gid % NE
            lb = gid % NL
            S.act(L[lb][:, :, c0:512], E[eb][:, :, c0:512], AF.Ln, [RE[eb]], [RL[lb]], bias=1.0, scale=1.0)

        def st_cum(gi):
            qg, kb, r, c0, first, last, gid = info(gi)
            lb = gid % NL
            for hd in range(2):
                S.mm(c.ps[:, AB[hd], c0:512], c.tinc, L[lb][:, hd, c0:512], first, False, [RL[lb], c.Rconst],
                     [c.bank[AB[hd]]], skip=True)

        def st_w(gi):
            qg, kb, r, c0, first, last, gid = info(gi)
            wb = gid % NW
            eb = gid % NE
            a_i = gid % NA
            S.act(Wb[wb][:, :, c0:512], c.ps[:, 4:6, c0:512], AF.Exp, [c.bank[4], c.bank[5]], [RW[wb]])
            S.tt("dve", Ab[a_i][:, :, c0:512], E[eb][:, :, c0:512], Wb[wb][:, :, c0:512], ALU.mult,
                 [RE[eb], RW[wb]], [RA[a_i]])

        def st_car(gi):
            qg, kb, r, c0, first, last, gid = info(gi)
            lb = gid % NL
            if not last:
                for hd in range(2):
                    S.mm(c.ps[:, AB[hd], c0:512], c.tcar, L[lb][:, hd, c0:512], False, False, [RL[lb], c.Rconst],
                         [c.bank[AB[hd]]], skip=True)

        def st_av(gi):
            qg, kb, r, c0, first, last, gid = info(gi)
            a_i = gid % NA
            for hd in range(2):
                rows = slice(hd * 64, hd * 64 + 64)
                S.mm(c.ps[rows, OB[hd], c0:512], v[:, kb, hd * 64:(hd + 1) * 64], Ab[a_i][:, hd, c0:512], first, last,
                     [RA[a_i], Rv[kb // 4]], [c.bank[OB[hd]]], skip=True)
            if last:
                for hd in range(2):
                    rows = slice(hd * 64, hd * 64 + 64)
                    og = c.ogT[rows, hp, qg * 512:(qg + 1) * 512]
                    S.tt("dve", og, c.ps[rows, OB[hd], :], og, ALU.mult, [c.bank[OB[hd]], c.Rog[hp][qg]],
                         [c.Rog[hp][qg]])

        stages = [(st_car, 4), (st_cum, 3), (st_av, 5), (st_z, 0), (st_w, 3), (st_l, 2), (st_e, 1)]
        for n in range(n_g + 5):
            for fn, d in stages:
                gi = n - d
                if 0 <= gi < n_g:
                    fn(gi)
        gcount += n_g


def layer_mla(c, w):
    S = c.S
    w_in = w["w_in"]
    gate_phase(c, w_in, 448)
    S.barrier()
    xnT3 = c.xnT[:, :].rearrange("p (c t) -> p c t", c=8)
    wv = w_in.rearrange("(c p) n -> p c n", p=128)
    sm = c.small
    scr = c.scr
    Rgn = Res("mla_g")
    gqa = c.gbc[:, 0:256]
    gkva = c.gbc[:, 256:384]
    gq192 = c.gbc[:, 384:576]
    gk192 = c.gbc[:, 576:768]
    gkpe = c.gbc[:, 704:768]
    S.ld("sp", gqa, w["q_a_norm"].partition_broadcast(128), writes=[Rgn])
    S.ld("sp", gkva, w["kv_a_norm"].partition_broadcast(128), writes=[Rgn])
    S.ld("sp", gq192, w["q_head_norm"].partition_broadcast(128), writes=[Rgn])
    S.ld("sp", gk192, w["k_head_norm"].partition_broadcast(128), writes=[Rgn])
    S.tt("dve", gq192[:, 0:128], gq192[:, 0:128], gk192[:, 0:128], ALU.mult, [Rgn], [Rgn])
    cosT = scr[:, 0:1024].rearrange("p (i f) -> p i f", f=32)
    sinT = scr[:, 1024:2048].rearrange("p (i f) -> p i f", f=32)
    Rtab = Res("ropetab")
    S.ld("sp", scr[:, 0:2048], c.cf_d[:, CF_COS:CF_COS + 2048], writes=[Rtab])
    kpeT = scr[:, 2048:4096].bitcast(BF16)
    qlnT3 = c.hbuf[:, 0:2 * S_LEN].rearrange("p (a t) -> p a t", a=2)
    kvlnT = c.hbuf[:, 2 * S_LEN:3 * S_LEN]
    sskpe = sm[:, 128:160]
    rstdk = sm[:, 160:192]
    Rqln = [Res(f"qln{g}") for g in range(NG)]
    Rkvln = [Res(f"kvln{g}") for g in range(NG)]
    Rkpe = [Res(f"kpe{g}") for g in range(NG)]
    Rsskpe = [Res(f"sskpe{g}") for g in range(NG)]

    def rope(x, out, i, Rx, Rout, t1, t2, Rt):
        cb = cosT[:, i, :].unsqueeze(1).to_broadcast([128, 2, 32])
        S.tt("dve", t1.rearrange("p (a f) -> p a f", a=2), x.rearrange("p (a f) -> p a f", a=2), cb, ALU.mult,
             [Rx, Rtab], [Rt])
        S.tt("dve", t2[:, 0:32], x[:, 32:64], sinT[:, i, :], ALU.mult, [Rx, Rtab], [Rt])
        S.tt("dve", t2[:, 32:64], x[:, 0:32], sinT[:, i, :], ALU.mult, [Rx, Rtab], [Rt])
        S.tt("dve", out[:, 0:32], t1[:, 0:32], t2[:, 0:32], ALU.subtract, [Rt], [Rout])
        S.tt("dve", out[:, 32:64], t1[:, 32:64], t2[:, 32:64], ALU.add, [Rt], [Rout])

    wlat = c.wbuf[:, 0:3584].rearrange("p (c n) -> p c n", c=8)
    Rwlat = Res("wlat")
    S.ld("pool", wlat[:, 0:4, :], wv[:, 0:4, 0:448], writes=[Rwlat])
    S.ld("pool", wlat[:, 4:8, :], wv[:, 4:8, 0:448], writes=[Rwlat])
    o = 4096
    junk = scr[:, o:o + 448]; o += 448
    lnb = [scr[:, o:o + 192].bitcast(BF16), scr[:, o + 192:o + 384].bitcast(BF16)]; o += 384
    kp = scr[:, o:o + 64]; o += 64
    t1 = scr[:, o:o + 64]; o += 64
    t2 = scr[:, o:o + 64]; o += 64
    kr = [scr[:, o:o + 32].bitcast(BF16), scr[:, o + 32:o + 64].bitcast(BF16)]; o += 64
    assert o <= 6144
    Rjunk, Rkp, Rt = Res("junk"), Res("kp"), Res("ropet")
    Rlnb = [Res("lnb0"), Res("lnb1")]
    Rkr = [Res("kr0"), Res("kr1")]
    Rst = [Res("mst0"), Res("mst1")]
    for i in range(NT):
        tg = i // 4
        pb = i % 2
        bank = 6 + pb
        psL = c.ps[:, bank, 0:448]
        for cc in range(8):
            S.mm(psL, xnT3[:, cc, i * 128:(i + 1) * 128], wlat[:, cc, :], cc == 0, cc == 7,
                 [Rwlat, c.RxnT[i]], [c.bank[bank]])
        sb_ = 192 + 16 * pb
        ss2 = sm[:, sb_:sb_ + 2]
        lg2 = sm[:, sb_ + 2:sb_ + 4]
        rs2 = sm[:, sb_ + 4:sb_ + 6]
        S.act(junk[:, 0:256], c.ps[:, bank, 0:256], AF.Square, [c.bank[bank]], [Rjunk, Rst[pb]], accum_out=ss2[:, 0:1])
        S.act(junk[:, 256:384], c.ps[:, bank, 256:384], AF.Square, [c.bank[bank]], [Rjunk, Rst[pb]], accum_out=ss2[:, 1:2])
        S.act(junk[:, 384:448], c.ps[:, bank, 384:448], AF.Square, [c.bank[bank]], [Rjunk, Rsskpe[tg]],
              accum_out=sskpe[:, i:i + 1])
        S.act(lg2[:, 0:1], ss2[:, 0:1], AF.Ln, [Rst[pb]], [Rst[pb]], scale=1.0 / 256, bias=EPS)
        S.act(lg2[:, 1:2], ss2[:, 1:2], AF.Ln, [Rst[pb]], [Rst[pb]], scale=1.0 / 128, bias=EPS)
        S.act(rs2, lg2, AF.Exp, [Rst[pb]], [Rst[pb]], scale=-0.5)
        S.stt(lnb[pb][:, 0:256], c.ps[:, bank, 0:256], rs2[:, 0:1], gqa, ALU.mult, ALU.mult,
              [c.bank[bank], Rst[pb], Rgn], [Rlnb[pb]])
        S.stt(lnb[pb][:, 256:384], c.ps[:, bank, 256:384], rs2[:, 1:2], gkva, ALU.mult, ALU.mult,
              [c.bank[bank], Rst[pb], Rgn], [Rlnb[pb]])
        S.tt("dve", kp, c.ps[:, bank, 384:448], gkpe, ALU.mult, [c.bank[bank], Rgn], [Rkp])
        rope(kp, kr[pb], i, Rkp, Rkr[pb], t1, t2, Rt)
        tbank = 4 + pb
        psT = c.ps[:, tbank, :].bitcast(BF16)
        for a in range(3):
            S.tr(psT[:, a * 128:(a + 1) * 128], lnb[pb][:, a * 128:(a + 1) * 128], c.ident,
                 [Rlnb[pb], c.Rconst], [c.bank[tbank]])
        S.tr(psT[0:64, 384:512], kr[pb], c.ident, [Rkr[pb], c.Rconst], [c.bank[tbank]])
        S.cp("act", qlnT3[:, :, i * 128:(i + 1) * 128], psT[:, 0:256].rearrange("p (a t) -> p a t", a=2),
             [c.bank[tbank]], [Rqln[tg]])
        S.cp("dve", kvlnT[:, i * 128:(i + 1) * 128], psT[:, 256:384], [c.bank[tbank]], [Rkvln[tg]])
        S.cp("dve", kpeT[0:64, i * 128:(i + 1) * 128], psT[0:64, 384:512], [c.bank[tbank]], [Rkpe[tg]])
    S.barrier()
    wuq = c.wbuf[:, 0:3072].rearrange("p (a n) -> p a n", a=2)
    wukv = c.wbuf[:, 3072:5120]
    Rwu = Res("wu")
    S.ld("pool", wuq, w["w_uq"].rearrange("(a p) n -> p a n", p=128), writes=[Rwu])
    S.ld("pool", wukv, w["w_ukv"], writes=[Rwu])
    X = c.xnT
    qTn = X[:, 0:4096]
    qTp = X[:, 4096:8192]
    kTn = X[:, 8192:12288]
    vh = X[:, 12288:16384].rearrange("p (i f) -> p i f", f=128)
    NP = 3
    Pb = [X[:, 16384 + 512 * j:16384 + 512 * (j + 1)] for j in range(NP)]
    qr = [X[:, 18432 + 256 * j:18432 + 256 * j + 192] for j in range(2)]
    XF = X[:, 20480:24576].bitcast(F32)
    rcb = XF[:, 0:512]
    tbuf = XF[:, 512:1024]
    qpe = XF[:, 1024:1088]
    u1 = XF[:, 1088:1152]
    u2 = XF[:, 1152:1216]
    junk2 = XF[:, 1216:1536]
    RqTn = [Res(f"qTn{g}") for g in range(NG)]
    RqTp = [Res(f"qTp{g}") for g in range(NG)]
    RkTn = [Res(f"kTn{g}") for g in range(NG)]
    Rvh = [Res(f"vh{g}") for g in range(NG)]
    Rrk = [Res(f"rstdk{g}") for g in range(NG)]
    RPb = [Res(f"mPb{j}") for j in range(NP)]
    Rqr = [Res("qr0"), Res("qr1")]
    Rrc, Rtb, Rqpe, Ru, Rj2 = Res("rcb"), Res("tbuf"), Res("qpe"), Res("u"), Res("junk2")
    Rs2 = [Res("hst0"), Res("hst1")]
    gid = 0
    sw = 0
    for h in range(8):
        for tg in range(NG):
            bank = tg % 4
            S.mm(c.ps[:, bank, :], wukv[:, h * 256:h * 256 + 128], kvlnT[:, tg * 512:(tg + 1) * 512], True, True,
                 [Rwu, Rkvln[tg]], [c.bank[bank]])
            S.cp("act", kTn[:, tg * 512:(tg + 1) * 512], c.ps[:, bank, :], [c.bank[bank]], [RkTn[tg]])
        for i in range(NT):
            tg, j = divmod(i, 4)
            pb = i % 2
            bank = 6 + pb
            tok = slice(i * 128, (i + 1) * 128)
            S.mm(c.ps[:, bank, 0:192], qlnT3[:, 0, tok], wuq[:, 0, h * 192:(h + 1) * 192], True, False,
                 [Rwu, Rqln[tg]], [c.bank[bank]])
            S.mm(c.ps[:, bank, 0:192], qlnT3[:, 1, tok], wuq[:, 1, h * 192:(h + 1) * 192], False, True,
                 [Rwu, Rqln[tg]], [c.bank[bank]])
            S.mm(c.ps[:, bank, 192:448], kvlnT[:, tok], wukv[:, h * 256:(h + 1) * 256], True, True,
                 [Rwu, Rkvln[tg]], [c.bank[bank]])
            sb_ = 224 + 16 * pb
            ss2 = sm[:, sb_:sb_ + 2]
            lg2 = sm[:, sb_ + 2:sb_ + 4]
            rsq = sm[:, sb_ + 4:sb_ + 5]
            S.act(junk2[:, 0:192], c.ps[:, bank, 0:192], AF.Square, [c.bank[bank]], [Rj2, Rs2[pb]], accum_out=ss2[:, 0:1])
            S.act(junk2[:, 192:320], c.ps[:, bank, 192:320], AF.Square, [c.bank[bank]], [Rj2, Rs2[pb]],
                  accum_out=ss2[:, 1:2])
            S.tt("dve", ss2[:, 1:2], ss2[:, 1:2], sskpe[:, i:i + 1], ALU.add, [Rs2[pb], Rsskpe[tg]], [Rs2[pb]])
            S.act(lg2, ss2, AF.Ln, [Rs2[pb]], [Rs2[pb]], scale=1.0 / 192, bias=EPS)
            S.act(rsq, lg2[:, 0:1], AF.Exp, [Rs2[pb]], [Rs2[pb]], scale=-0.5)
            S.act(rstdk[:, i:i + 1], lg2[:, 1:2], AF.Exp, [Rs2[pb]], [Rrk[tg]], scale=-0.5, bias=-0.5 * math.log(192.0))
            S.cp("act", vh[:, i, :], c.ps[:, bank, 320:448], [c.bank[bank]], [Rvh[tg]])
            S.stt(qr[pb][:, 0:128], c.ps[:, bank, 0:128], rsq, gq192[:, 0:128], ALU.mult, ALU.mult,
                  [c.bank[bank], Rs2[pb], Rgn], [Rqr[pb]])
            S.stt(qpe, c.ps[:, bank, 128:192], rsq, gq192[:, 128:192], ALU.mult, ALU.mult,
                  [c.bank[bank], Rs2[pb], Rgn], [Rqpe])
            rope(qpe, qr[pb][:, 128:192], i, Rqpe, Rqr[pb], u1, u2, Ru)
            tbank = 4 + tg % 2
            psT = c.ps[:, tbank, :].bitcast(BF16)
            S.tr(psT[:, j * 128:(j + 1) * 128], qr[pb][:, 0:128], c.ident, [Rqr[pb], c.Rconst], [c.bank[tbank]])
            S.tr(psT[0:64, 512 + j * 128:512 + (j + 1) * 128], qr[pb][:, 128:192], c.ident,
                 [Rqr[pb], c.Rconst], [c.bank[tbank]])
            if j == 3:
                S.cp("dve", qTn[:, tg * 512:(tg + 1) * 512], psT[:, 0:512], [c.bank[tbank]], [RqTn[tg]])
                S.cp("dve", qTp[0:64, tg * 512:(tg + 1) * 512], psT[0:64, 512:1024], [c.bank[tbank]], [RqTp[tg]])
        G = []
        for qg in range(NG):
            for kb in reversed(range(4 * qg + 4)):
                G.append((qg, kb))
        n_g = len(G)

        def info(gi):
            qg, kb = G[gi]
            r = kb - 4 * qg
            c0 = r * 128 if r >= 0 else 0
            return qg, kb, r, c0, kb == 4 * qg + 3, kb == 0, gid + gi, sw + qg

        def st_z(gi):
            qg, kb, r, c0, first, last, g_, sw_ = info(gi)
            zb = g_ % 4
            S.mm(c.ps[:, zb, c0:512], kTn[:, kb * 128:(kb + 1) * 128], qTn[:, qg * 512 + c0:(qg + 1) * 512], True, False,
                 [RkTn[kb // 4], RqTn[qg]], [c.bank[zb]])
            S.mm(c.ps[:, zb, c0:512], kpeT[0:64, kb * 128:(kb + 1) * 128], qTp[0:64, qg * 512 + c0:(qg + 1) * 512],
                 False, True, [Rkpe[kb // 4], RqTp[qg]], [c.bank[zb]])

        def st_e(gi):
            qg, kb, r, c0, first, last, g_, sw_ = info(gi)
            zb = g_ % 4
            pi = g_ % NP
            S.act(Pb[pi][:, c0:512], c.ps[:, zb, c0:512], AF.Exp, [c.bank[zb], Rrk[kb // 4]], [RPb[pi]],
                  scale=rstdk[:, kb:kb + 1])
            if r >= 0:
                S.tt("dve", Pb[pi][:, c0:c0 + 128], Pb[pi][:, c0:c0 + 128], c.mincl, ALU.mult,
                     [RPb[pi], c.Rconst], [RPb[pi]])

        def st_av(gi):
            qg, kb, r, c0, first, last, g_, sw_ = info(gi)
            pi = g_ % NP
            ob = 4 + sw_ % 2
            db = 6 + sw_ % 2
            S.mm(c.ps[:, ob, c0:512], vh[:, kb, :], Pb[pi][:, c0:512], first, last, [Rvh[kb // 4], RPb[pi]],
                 [c.bank[ob]], skip=True)
            S.mm(c.ps[:, db, c0:512], c.ones, Pb[pi][:, c0:512], first, last, [c.Rconst, RPb[pi]],
                 [c.bank[db]], skip=True)
            if last:
                S.op("dve", lambda e, o_=rcb, i_=c.ps[:, db, :]: e.reciprocal(out=o_, in_=i_), [c.bank[db]], [Rrc])
                S.tt("dve", tbuf, c.ps[:, ob, :], rcb, ALU.mult, [c.bank[ob], Rrc], [Rtb])
                og = c.ogT[:, h, qg * 512:(qg + 1) * 512]
                S.tt("dve", og, tbuf, og, ALU.mult, [Rtb, c.Rog[h][qg]], [c.Rog[h][qg]])

        stages = [(st_z, 0), (st_e, 1), (st_av, 2)]
        for n in range(n_g + 2):
            for fn, d in reversed(stages):
                gi = n - d
                if 0 <= gi < n_g:
                    fn(gi)
        gid += n_g
        sw += NG


def layer_swa(c, w):
    S = c.S
    w_in = w["w_in"]
    gate_phase(c, w_in, 1536)
    S.barrier()
    xnT3 = c.xnT[:, :].rearrange("p (c t) -> p c t", c=8)
    wv = w_in.rearrange("(c p) n -> p c n", p=128)
    qT3 = c.hbuf[:, 0:2 * S_LEN].rearrange("p (a t) -> p a t", a=2)
    kT2 = c.hbuf[:, 2 * S_LEN:3 * S_LEN]
    scr = c.scr
    off = 0
    vg = scr[:, off:off + 1024].bitcast(BF16).rearrange("p (i f) -> p i f", f=64); off += 1024
    junk = scr[:, off:off + 320]; off += 320
    tmpq = scr[:, off:off + 256]; off += 256
    qn = [scr[:, off:off + 128].bitcast(BF16), scr[:, off + 128:off + 256].bitcast(BF16)]; off += 256
    biasg = scr[:, off:off + 1024]; off += 1024
    Tb = scr[:, off:off + 1024]; off += 1024
    Pb = [scr[:, off:off + 512].bitcast(BF16), scr[:, off + 512:off + 1024].bitcast(BF16)]; off += 1024
    dnb = scr[:, off:off + 256]; off += 256
    rcb = scr[:, off:off + 256]; off += 256
    tb = scr[:, off:off + 256]; off += 256
    gqk4 = scr[:, off:off + 256]; off += 256
    assert off <= 6144
    sm = c.small
    kscale = sm[:, 16:48]
    es16 = sm[:, 48:64]
    Rjunk, Rtmpq, Rbias, RTb, Rdn, Rrc, Rtb, Rgqk, Res16 = (Res(n) for n in
                                                          ("junk", "tmpq", "biasg", "Tb", "dnb", "rcb", "tb", "gqk", "es16"))
    Rqn = [Res("qn0"), Res("qn1")]
    RPb = [Res("Pb0"), Res("Pb1")]
    Rst = [Res("sst0"), Res("sst1")]
    RqT = [Res(f"qT{g}") for g in range(NG)]
    RkT = [Res(f"kT{g}") for g in range(NG)]
    Rvg = [Res(f"vg{g}") for g in range(NG)]
    Rks = [Res(f"ks{g}") for g in range(NG)]
    wtm = [c.wbuf[:, 4096 * b:4096 * b + 3072].rearrange("p (c n) -> p c n", c=8) for b in range(2)]
    wk2 = [c.wbuf[:, 4096 * b + 3072:4096 * (b + 1)].rearrange("p (c n) -> p c n", c=8) for b in range(2)]
    Rw = [Res("swaw0"), Res("swaw1")]
    gk4 = junk[:, 0:256]
    for j in range(4):
        S.ld("sp", gqk4[:, j * 64:(j + 1) * 64], w["q_head_norm"].partition_broadcast(128), writes=[Rgqk])
        S.ld("sp", gk4[:, j * 64:(j + 1) * 64], w["k_head_norm"].partition_broadcast(128), writes=[Rjunk])
    S.tt("dve", gqk4, gqk4, gk4, ALU.mult, [Rjunk, Rgqk], [Rgqk])
    S.ld("sp", es16, w["sinks"].partition_broadcast(128), writes=[Res16])
    S.act(es16, es16, AF.Exp, [Res16], [Res16])

    def load_w(g):
        b = g % 2
        S.ld("pool", wtm[b][:, :, 0:256], wv[:, :, g * 256:(g + 1) * 256], writes=[Rw[b]])
        S.ld("pool", wtm[b][:, :, 256:320], wv[:, :, 1024 + g * 64:1024 + (g + 1) * 64], writes=[Rw[b]])
        S.ld("pool", wtm[b][:, :, 320:384], wv[:, :, 1280 + g * 64:1280 + (g + 1) * 64], writes=[Rw[b]])
        for d in range(2):
            S.ld("pool", wk2[b][:, :, d * 64:(d + 1) * 64], wv[:, :, 1024 + g * 64:1024 + (g + 1) * 64], writes=[Rw[b]])

    load_w(0)
    for g in range(4):
        b = g % 2
        if g + 1 < 4:
            load_w(g + 1)
        for kbi in range(2):
            S.ld("sp", biasg[:, kbi * 512:(kbi + 1) * 512],
                 c.cf_d[:, CF_BIAS + kbi * 2048 + g * 512:CF_BIAS + kbi * 2048 + (g + 1) * 512], writes=[Rbias])
        for tg in range(NG):
            bank = tg % 4
            for cc in range(8):
                S.mm(c.ps[:, bank, :], wk2[b][:, cc, :], xnT3[:, cc, tg * 512:(tg + 1) * 512], cc == 0, cc == 7,
                     [Rw[b]] + c.RxnT[4 * tg:4 * tg + 4], [c.bank[bank]])
            S.cp("act", kT2[:, tg * 512:(tg + 1) * 512], c.ps[:, bank, :], [c.bank[bank]], [RkT[tg]])
        for i in range(NT):
            tg, j = divmod(i, 4)
            pb = i % 2
            bank = 6 + pb
            psTM = c.ps[:, bank, 0:384]
            for cc in range(8):
                S.mm(psTM, xnT3[:, cc, i * 128:(i + 1) * 128], wtm[b][:, cc, :], cc == 0, cc == 7,
                     [Rw[b], c.RxnT[i]], [c.bank[bank]])
            ss5 = sm[:, 64 + 16 * pb:64 + 16 * pb + 5]
            lg5 = sm[:, 72 + 16 * pb:72 + 16 * pb + 5]
            rs4 = sm[:, 96 + 8 * pb:96 + 8 * pb + 4]
            S.act(junk, c.ps[:, bank, 0:320], AF.Square, [c.bank[bank]], [Rjunk])
            S.op("dve", lambda e, o=ss5, i_=junk.rearrange("p (h f) -> p h f", f=64): e.reduce_sum(out=o, in_=i_, axis=AX.X),
                 [Rjunk], [Rst[pb]])
            S.act(lg5, ss5, AF.Ln, [Rst[pb]], [Rst[pb]], scale=1.0 / 64, bias=EPS)
            S.act(rs4, lg5[:, 0:4], AF.Exp, [Rst[pb]], [Rst[pb]], scale=-0.5)
            S.act(kscale[:, i:i + 1], lg5[:, 4:5], AF.Exp, [Rst[pb]], [Rks[tg]], scale=-0.5, bias=math.log(0.125))
            S.tt("dve", tmpq.rearrange("p (h f) -> p h f", f=64),
                 c.ps[:, bank, 0:256].rearrange("p (h f) -> p h f", f=64),
                 rs4.unsqueeze(2).to_broadcast([128, 4, 64]), ALU.mult, [c.bank[bank], Rst[pb]], [Rtmpq])
            S.tt("dve", qn[pb], tmpq, gqk4, ALU.mult, [Rtmpq, Rgqk], [Rqn[pb]])
            S.cp("act", vg[:, i, :], c.ps[:, bank, 320:384], [c.bank[bank]], [Rvg[tg]])
            tbank = 4 + tg % 2
            psT = c.ps[:, tbank, :].bitcast(BF16)
            for a in range(2):
                S.tr(psT[:, a * 512 + j * 128:a * 512 + (j + 1) * 128], qn[pb][:, a * 128:(a + 1) * 128], c.ident,
                     [Rqn[pb], c.Rconst], [c.bank[tbank]])
            if j == 3:
                S.cp("dve", qT3[:, :, tg * 512:(tg + 1) * 512], psT.rearrange("p (a t) -> p a t", a=2),
                     [c.bank[tbank]], [RqT[tg]])
        for qb in range(NT):
            kbs = [(0, qb - 1), (1, qb)] if qb > 0 else [(1, qb)]
            zb0 = 2 * (qb % 2)
            for kbi, kb in kbs:
                for hq in range(4):
                    p = hq % 2
                    rows = slice(p * 64, p * 64 + 64)
                    col = (kbi * 2 + hq // 2) * 128
                    S.mm(c.ps[:, zb0 + p, col:col + 128], kT2[rows, kb * 128:(kb + 1) * 128],
                         qT3[rows, hq // 2, qb * 128:(qb + 1) * 128], True, True,
                         [RkT[kb // 4], RqT[qb // 4]], [c.bank[zb0 + p]])
            for kbi, kb in kbs:
                for p in range(2):
                    src = c.ps[:, zb0 + p, kbi * 256:(kbi + 1) * 256].rearrange("q (a t) -> q a t", a=2)
                    dst = Tb[:, kbi * 512:(kbi + 1) * 512].rearrange("q (a p t) -> q p a t", a=2, p=2)[:, p]
                    bia = biasg[:, kbi * 512:(kbi + 1) * 512].rearrange("q (a p t) -> q p a t", a=2, p=2)[:, p]
                    S.stt(dst, src, kscale[:, kb:kb + 1], bia, ALU.mult, ALU.add,
                          [c.bank[zb0 + p], Rks[kb // 4], Rbias], [RTb])
            pbi = qb % 2
            lo = 0 if qb > 0 else 512
            S.act(Pb[pbi][:, lo:1024], Tb[:, lo:1024], AF.Exp, [RTb], [RPb[pbi]])
            ob = 4 + qb % 2
            for hq in range(4):
                rows = slice((hq % 2) * 64, (hq % 2) * 64 + 64)
                col = (hq // 2) * 128
                for n_, (kbi, kb) in enumerate(kbs):
                    S.mm(c.ps[rows, ob, col:col + 128], vg[:, kb, :], Pb[pbi][:, (kbi * 4 + hq) * 128:(kbi * 4 + hq + 1) * 128],
                         n_ == 0, n_ == len(kbs) - 1, [Rvg[kb // 4], RPb[pbi]], [c.bank[ob]])
                for n_, (kbi, kb) in enumerate(kbs):
                    S.mm(c.ps[rows, ob, 256 + col:256 + col + 128], c.ones[:, 0:64],
                         Pb[pbi][:, (kbi * 4 + hq) * 128:(kbi * 4 + hq + 1) * 128],
                         n_ == 0, n_ == len(kbs) - 1, [c.Rconst, RPb[pbi]], [c.bank[ob]])
            for hq in range(4):
                rows = slice((hq % 2) * 64, (hq % 2) * 64 + 64)
                col = (hq // 2) * 128
                h = 4 * g + hq
                S.ts("dve", dnb[rows, col:col + 128], c.ps[rows, ob, 256 + col:256 + col + 128], es16[rows, h:h + 1],
                     None, ALU.add, None, [c.bank[ob], Res16], [Rdn])
            S.op("dve", lambda e, o=rcb, i_=dnb: e.reciprocal(out=o, in_=i_), [Rdn], [Rrc])
            S.tt("dve", tb, c.ps[:, ob, 0:256], rcb, ALU.mult, [c.bank[ob], Rrc], [Rtb])
            og = c.ogT[:, 2 * g:2 * g + 2, qb * 128:(qb + 1) * 128]
            Rogs = [c.Rog[2 * g][qb // 4], c.Rog[2 * g + 1][qb // 4]]
            S.tt("dve", og, tb.rearrange("p (a t) -> p a t", a=2), og, ALU.mult, [Rtb] + Rogs, Rogs)


LAUNCH_GROUPS = [[0, 1, 2, 3]]
_CONSTS = None


def run_layers(layers, xs, inputs):
    global _CONSTS
    if _CONSTS is None:
        _CONSTS = make_consts()
    cbf, cf = _CONSTS
    nc = build_program(layers)
    names = [n for li in layers for n in LAYER_WEIGHTS[li]]
    in_maps = []
    for b in range(len(xs)):
        m = {"x": np.ascontiguousarray(xs[b], dtype=np.float32), "cbf": cbf, "cf": cf}
        for n in names:
            m[n] = np.ascontiguousarray(inputs[n], dtype=np.float32)
        in_maps.append(m)
    res = run_bass_kernel_spmd(nc, in_maps, core_ids=list(range(len(xs))))
    return [r["y"] for r in res.results]


def kernel(**inputs):
    x = np.asarray(inputs["x"])
    xs = [x[b] for b in range(x.shape[0])]
    for grp in LAUNCH_GROUPS:
        xs = run_layers(grp, xs, inputs)
    return np.stack(xs, axis=0).astype(np.float32)
```

```python
import math
import numpy as np
import ml_dtypes
import concourse.bass as bass
import concourse.mybir as mybir
from concourse.bass_utils import run_bass_kernel_spmd

F32 = mybir.dt.float32
BF16 = mybir.dt.bfloat16
AF = mybir.ActivationFunctionType
ALU = mybir.AluOpType
AX = mybir.AxisListType

S_LEN = 4096
D = 1024
NT = S_LEN // 128
NG = S_LEN // 512
EPS = 1e-6

LAYER_KIND = ["sb", "mla", "swa", "sb"]
LAYER_WEIGHTS = [
    ["l0_norm", "l0_w_in", "l0_w_out"],
    ["l1_norm", "l1_w_in", "l1_q_a_norm", "l1_w_uq", "l1_kv_a_norm", "l1_w_ukv",
     "l1_q_head_norm", "l1_k_head_norm", "l1_w_out"],
    ["l2_norm", "l2_w_in", "l2_q_head_norm", "l2_k_head_norm", "l2_sinks", "l2_w_out"],
    ["l3_norm", "l3_w_in", "l3_w_out"],
]
WSHAPES = {
    "l0_norm": [1024], "l0_w_in": [1024, 4096], "l0_w_out": [1024, 1024],
    "l1_norm": [1024], "l1_w_in": [1024, 1472], "l1_q_a_norm": [256], "l1_w_uq": [256, 1536],
    "l1_kv_a_norm": [128], "l1_w_ukv": [128, 2048], "l1_q_head_norm": [192], "l1_k_head_norm": [192],
    "l1_w_out": [1024, 1024],
    "l2_norm": [1024], "l2_w_in": [1024, 2560], "l2_q_head_norm": [64], "l2_k_head_norm": [64],
    "l2_sinks": [16], "l2_w_out": [1024, 1024],
    "l3_norm": [1024], "l3_w_in": [1024, 4096], "l3_w_out": [1024, 1024],
}


class Res:
    __slots__ = ("name", "w", "r", "excl")

    def __init__(self, name, excl=False):
        self.name = name
        self.w = None
        self.r = {}
        self.excl = excl


class Sched:
    ENGS = ("pe", "act", "dve", "pool", "sp")

    def __init__(self, sems, dma_pools):
        self.sems = sems
        self.cnt = {k: 0 for k in sems}
        self.ops = {e: [] for e in self.ENGS}
        self.seen = {e: {} for e in self.ENGS}
        self.dma_pools = dma_pools
        self.dma_rr = {q: 0 for q in dma_pools}
        self.nwaits = 0

    def _wait(self, e, key, val):
        if val <= 0 or val <= self.seen[e].get(key, 0):
            return
        self.seen[e][key] = val
        sem = self.sems[key]
        self.ops[e].append(lambda eng, sem=sem, val=val: eng.wait_ge(sem, val))
        self.nwaits += 1

    def _sync(self, e, reads, writes, dma):
        rd = [r for r in reads if not r.excl]
        wr = list(writes) + [r for r in reads if r.excl]
        need = []
        for r in rd:
            if r.w is not None:
                need.append((r.w, True))
        for r in wr:
            if r.w is not None:
                need.append((r.w, False))
            for ev in r.r.values():
                need.append((ev, False))
        for (key, val, eng), raw in need:
            if eng == e and not dma:
                if e == "pe":
                    continue
            self._wait(e, key, val)
        return rd, wr

    def _update(self, rd, wr, ev):
        for r in rd:
            r.r[ev[0]] = ev
        for r in wr:
            r.w = ev
            r.r = {}

    def op(self, e, fn, reads=(), writes=()):
        rd, wr = self._sync(e, reads, writes, False)
        self.cnt[e] += 1
        sem = self.sems[e]
        self.ops[e].append(lambda eng, fn=fn, sem=sem: fn(eng).then_inc(sem, 1))
        self._update(rd, wr, (e, self.cnt[e], e))

    def dma(self, q, fn, reads=(), writes=()):
        pool = self.dma_pools[q]
        k = pool[self.dma_rr[q] % len(pool)]
        self.dma_rr[q] += 1
        self._wait(q, k, self.cnt[k])
        rd, wr = self._sync(q, reads, writes, True)
        self.cnt[k] += 16
        sem = self.sems[k]
        self.ops[q].append(lambda eng, fn=fn, sem=sem: fn(eng).then_inc(sem, 16))
        self._update(rd, wr, (k, self.cnt[k], None))

    def barrier(self, engines=None):
        for e in (engines or self.ENGS):
            for k, v in self.cnt.items():
                if k != e:
                    self._wait(e, k, v)


    def mm(self, out, lhsT, rhs, start, stop, reads, writes, skip=False):
        if skip:
            self.op("pe", lambda e: e.matmul(out, lhsT=lhsT, rhs=rhs, start=start, stop=stop, skip_group_check=True),
                    reads, writes)
        else:
            self.op("pe", lambda e: e.matmul(out, lhsT=lhsT, rhs=rhs, start=start, stop=stop), reads, writes)

    def tr(self, out, in_, ident, reads, writes):
        self.op("pe", lambda e: e.transpose(out=out, in_=in_, identity=ident), reads, writes)

    def act(self, out, in_, func, reads, writes, **kw):
        self.op("act", lambda e: e.activation(out=out, in_=in_, func=func, **kw), reads, writes)

    def tt(self, eng, out, in0, in1, op, reads, writes):
        self.op(eng, lambda e: e.tensor_tensor(out=out, in0=in0, in1=in1, op=op), reads, writes)

    def stt(self, out, in0, scalar, in1, op0, op1, reads, writes):
        self.op("dve", lambda e: e.scalar_tensor_tensor(out=out, in0=in0, scalar=scalar, in1=in1, op0=op0, op1=op1),
                reads, writes)

    def ts(self, eng, out, in0, s1, s2, op0, op1, reads, writes):
        if op1 is None:
            self.op(eng, lambda e: e.tensor_scalar(out=out, in0=in0, scalar1=s1, scalar2=None, op0=op0), reads, writes)
        else:
            self.op(eng, lambda e: e.tensor_scalar(out=out, in0=in0, scalar1=s1, scalar2=s2, op0=op0, op1=op1),
                    reads, writes)

    def cp(self, eng, out, in_, reads, writes):
        if eng == "act":
            self.op("act", lambda e: e.copy(out=out, in_=in_), reads, writes)
        else:
            self.op(eng, lambda e: e.tensor_copy(out=out, in_=in_), reads, writes)

    def ld(self, q, out, in_, reads=(), writes=()):
        self.dma(q, lambda e: e.dma_start(out=out, in_=in_), reads, writes)

    def emit(self, block):
        def mk(e):
            def body(eng):
                for f in self.ops[e]:
                    f(eng)
            return body
        block.tensor(mk("pe"))
        block.scalar(mk("act"))
        block.vector(mk("dve"))
        block.gpsimd(mk("pool"))
        block.sync(mk("sp"))


def make_consts():
    j = np.arange(128)[:, None]
    s = np.arange(128)[None, :]
    ident = (j == s).astype(np.float32)
    tinc = -(j >= s).astype(np.float32)
    tcar = -(j < s).astype(np.float32)
    ones = np.ones((128, 128), np.float32)
    cbf = np.concatenate([ident, tinc, tcar, ones], axis=1)
    mstrict = (s > j).astype(np.float32)
    mincl = (s >= j).astype(np.float32)
    half = 32
    inv_freq = (10000.0 ** (-np.arange(half, dtype=np.float32) / half)).astype(np.float32)
    pos = np.arange(S_LEN, dtype=np.float32)
    ang = (pos[:, None] * inv_freq[None, :]).astype(np.float32)
    cos = np.cos(ang).astype(np.float32)
    sin = np.sin(ang).astype(np.float32)
    cosT = cos.reshape(NT, 128, 32).transpose(1, 0, 2).reshape(128, NT * 32)
    sinT = sin.reshape(NT, 128, 32).transpose(1, 0, 2).reshape(128, NT * 32)
    slopes = (2.0 ** (-8.0 * np.arange(1, 17, dtype=np.float32) / 16)).astype(np.float32)
    NEG = -30000.0
    bias = np.zeros((128, 2, 16, 128), np.float32)
    for kbi in range(2):
        rel = (s + 128 - (j + 128 * kbi)).astype(np.float32)
        valid = (rel >= 0) & (rel < 128)
        for h in range(16):
            bias[:, kbi, h, :] = np.where(valid, -slopes[h] * rel, NEG)
    cf = np.concatenate([mstrict, mincl, cosT, sinT, bias.reshape(128, -1)], axis=1).astype(np.float32)
    return cbf.astype(np.float32), cf


CF_MSTRICT = 0
CF_MINCL = 128
CF_COS = 256
CF_SIN = 256 + NT * 32
CF_BIAS = 256 + 2 * NT * 32
CF_TOTAL = CF_BIAS + 2 * 16 * 128


class Ctx:
    pass


def build_program(layers, dbg=None):
    nc = bass.Bass("TRN2", target_bir_lowering=False)
    x_in = nc.dram_tensor("x", [S_LEN, D], F32, kind="ExternalInput").ap()
    y_out = nc.dram_tensor("y", [S_LEN, D], F32, kind="ExternalOutput").ap()
    cbf_d = nc.dram_tensor("cbf", [128, 512], F32, kind="ExternalInput").ap()
    cf_d = nc.dram_tensor("cf", [128, CF_TOTAL], F32, kind="ExternalInput").ap()
    W = {}
    for li in layers:
        for n in LAYER_WEIGHTS[li]:
            W[n] = nc.dram_tensor(n, WSHAPES[n], F32, kind="ExternalInput").ap()
    xs = [x_in]
    for i in range(len(layers) - 1):
        xs.append(nc.dram_tensor(f"xmid{i}", [S_LEN, D], F32, kind="Internal").ap())
    xs.append(y_out)

    from contextlib import ExitStack
    with ExitStack() as st:
        def sb(name, shape, dt):
            return st.enter_context(nc.sbuf_tensor(name, shape, dt))
        c = Ctx()
        c.nc = nc
        c.xnT = sb("xnT", [128, 8 * S_LEN], BF16)
        c.ogT = sb("ogT", [128, 8, S_LEN], BF16)
        c.hbuf = sb("hbuf", [128, 3 * S_LEN], BF16)
        c.wbuf = sb("wbuf", [128, 8192], BF16)
        c.scr = sb("scr", [128, 6144], F32)
        c.gbc = sb("gbc", [128, 1024], F32)
        c.cbf = sb("cbfs", [128, 512], BF16)
        c.cmask = sb("cmask", [128, 256], F32)
        c.small = sb("small", [128, 512], F32)
        c.ps = st.enter_context(nc.psum_tensor("ps", [128, 8, 512], F32))
        sem_names = list(Sched.ENGS) + [f"d{i}" for i in range(20)]
        sems = {k: st.enter_context(nc.semaphore(f"s_{k}")) for k in sem_names}
        S = Sched(sems, {"sp": [f"d{i}" for i in range(0, 8)],
                         "pool": [f"d{i}" for i in range(8, 16)],
                         "act": [f"d{i}" for i in range(16, 20)]})
        c.S = S
        c.W = W
        c.cf_d = cf_d
        c.dbg = dbg
        c.bank = [Res(f"bank{b}", excl=True) for b in range(8)]
        c.ident = c.cbf[:, 0:128]
        c.tinc = c.cbf[:, 128:256]
        c.tcar = c.cbf[:, 256:384]
        c.ones = c.cbf[:, 384:512]
        c.mstrict = c.cmask[:, 0:128]
        c.mincl = c.cmask[:, 128:256]
        c.Rconst = Res("const")
        S.ld("pool", c.cbf[:], cbf_d[:, :], writes=[c.Rconst])
        S.ld("sp", c.cmask[:], cf_d[:, 0:256], writes=[c.Rconst])
        S.barrier()

        block = st.enter_context(nc.Block())
        for idx, li in enumerate(layers):
            kind = LAYER_KIND[li]
            pre = f"l{li}_"
            if idx == 0:
                phase_norm(c, xs[idx], W[pre + "norm"])
                S.barrier()
            if kind == "sb":
                layer_sb(c, W[pre + "w_in"])
            elif kind == "mla":
                layer_mla(c, {k[3:]: v for k, v in W.items() if k.startswith(pre)})
            else:
                layer_swa(c, {k[3:]: v for k, v in W.items() if k.startswith(pre)})
            S.barrier()
            nxt = W[f"l{layers[idx + 1]}_norm"] if idx + 1 < len(layers) else None
            phase_out(c, xs[idx], xs[idx + 1], W[pre + "w_out"], nxt)
            S.barrier()
        S.emit(block)
    return nc


def norm_stages(c, Rg, src_of, Rsrc_of, tag):
    S = c.S
    xnT3 = c.xnT[:, :].rearrange("p (c t) -> p c t", c=8)
    junk = c.scr[:, 0:1024]
    Rjunk = Res("junk" + tag)
    xn = [c.scr[:, 1024 + 512 * j:1536 + 512 * j].bitcast(BF16) for j in range(2)]
    Rxn = [Res(f"xn{j}" + tag) for j in range(2)]
    NSTAT = 4
    Rst = [Res(f"nst{j}" + tag) for j in range(NSTAT)]
    c.RxnT = [Res(f"xnT{i}" + tag) for i in range(NT)]

    def cols(i):
        k = i % NSTAT
        return c.small[:, 4 * k:4 * k + 1], c.small[:, 4 * k + 1:4 * k + 2], c.small[:, 4 * k + 2:4 * k + 3], k

    def n_stat(i):
        ss, lg, rs, k = cols(i)
        S.act(junk, src_of(i), AF.Square, [Rsrc_of(i)], [Rjunk, Rst[k]], accum_out=ss)
        S.act(lg, ss, AF.Ln, [Rst[k]], [Rst[k]], scale=1.0 / D, bias=EPS)
        S.act(rs, lg, AF.Exp, [Rst[k]], [Rst[k]], scale=-0.5)

    def n_xn(i):
        ss, lg, rs, k = cols(i)
        b = i % 2
        S.stt(xn[b], src_of(i), rs, c.gbc[:], ALU.mult, ALU.mult, [Rsrc_of(i), Rst[k], Rg], [Rxn[b]])

    def n_tr(i):
        b = i % 2
        bank = 4 + b
        psT = c.ps[:, bank, :].bitcast(BF16)
        for ch in range(8):
            S.tr(psT[:, ch * 128:(ch + 1) * 128], xn[b][:, ch * 128:(ch + 1) * 128], c.ident,
                 [Rxn[b], c.Rconst], [c.bank[bank]])
        dst = xnT3[:, :, i * 128:(i + 1) * 128]
        src = psT.rearrange("p (c t) -> p c t", c=8)
        S.cp("act" if b == 0 else "dve", dst, src, [c.bank[bank]], [c.RxnT[i]])

    return n_stat, n_xn, n_tr


def phase_norm(c, x_d, g_d):
    S = c.S
    Rg = Res("gbc")
    S.ld("sp", c.gbc[:], g_d.partition_broadcast(128), writes=[Rg])
    hb = c.hbuf[:, :].bitcast(F32)
    NX = 4
    xin = [hb[:, 1024 * j:1024 * (j + 1)] for j in range(NX)]
    Rxin = [Res(f"xin{j}") for j in range(NX)]
    n_stat, n_xn, n_tr = norm_stages(c, Rg, lambda i: xin[i % NX], lambda i: Rxin[i % NX], "A")

    def n_ld(i):
        S.ld("sp", xin[i % NX], x_d[i * 128:(i + 1) * 128, :], writes=[Rxin[i % NX]])

    stages = [(n_tr, 4), (n_xn, 3), (n_stat, 2), (n_ld, 0)]
    for n in range(NT + 4):
        for fn, d in stages:
            i = n - d
            if 0 <= i < NT:
                fn(i)


def phase_out(c, x_d, y_d, wout_d, next_g=None):
    S = c.S
    wo = c.wbuf[:, :].rearrange("p (c n) -> p c n", c=8)
    Rwo = Res("wo")
    wv = wout_d.rearrange("(c p) n -> p c n", p=128)
    for h in range(2):
        S.ld("pool", wo[:, 4 * h:4 * h + 4, :], wv[:, 4 * h:4 * h + 4, :], writes=[Rwo])
    Rg = Res("gbcC")
    if next_g is not None:
        S.ld("sp", c.gbc[:], next_g.partition_broadcast(128), writes=[Rg])
    hb = c.hbuf[:, :].bitcast(F32)
    NX = 3
    xin = [hb[:, 1024 * j:1024 * (j + 1)] for j in range(NX)]
    yo = [hb[:, 3072 + 1024 * j:3072 + 1024 * (j + 1)] for j in range(NX)]
    Rxin = [Res(f"cxin{j}") for j in range(NX)]
    Ryo = [Res(f"yo{j}") for j in range(NX)]
    Rog = c.Rog

    def c_ld(i):
        S.ld("sp", xin[i % NX], x_d[i * 128:(i + 1) * 128, :], writes=[Rxin[i % NX]])

    def c_mm(i):
        for h in range(2):
            bank = 2 * (i % 2) + h
            for ch in range(8):
                S.mm(c.ps[:, bank, :], c.ogT[:, ch, i * 128:(i + 1) * 128], wo[:, ch, h * 512:(h + 1) * 512],
                     ch == 0, ch == 7, [Rwo, Rog[ch][i // 4]], [c.bank[bank]])

    def c_add(i):
        k = i % NX
        for h in range(2):
            bank = 2 * (i % 2) + h
            S.tt("dve", yo[k][:, h * 512:(h + 1) * 512], c.ps[:, bank, :], xin[k][:, h * 512:(h + 1) * 512],
                 ALU.add, [c.bank[bank], Rxin[k]], [Ryo[k]])
        S.ld("pool", y_d[i * 128:(i + 1) * 128, :], yo[k], reads=[Ryo[k]])

    stages = [(c_add, 3), (c_mm, 2), (c_ld, 0)]
    tail = 3
    if next_g is not None:
        n_stat, n_xn, n_tr = norm_stages(c, Rg, lambda i: yo[i % NX], lambda i: Ryo[i % NX], "C")
        stages = [(n_tr, 6), (n_xn, 5), (n_stat, 4)] + stages
        tail = 6
    for n in range(NT + tail):
        for fn, d in stages:
            i = n - d
            if 0 <= i < NT:
                fn(i)


def gate_phase(c, w_in, col0):
    S = c.S
    xnT3 = c.xnT[:, :].rearrange("p (c t) -> p c t", c=8)
    wv = w_in.rearrange("(c p) n -> p c n", p=128)
    wg = [c.wbuf[:, 6144 + 1024 * b: 6144 + 1024 * (b + 1)].rearrange("p (c n) -> p c n", c=8) for b in range(2)]
    Rwg = [Res("wg0"), Res("wg1")]
    c.Rog = [[Res(f"og{ch}_{tg}") for tg in range(NG)] for ch in range(8)]
    k = 0
    for ch in range(8):
        b = ch % 2
        S.ld("pool", wg[b], wv[:, :, col0 + ch * 128: col0 + (ch + 1) * 128], writes=[Rwg[b]])
        for tg in range(NG):
            bank = k % 4
            k += 1
            for cc in range(8):
                S.mm(c.ps[:, bank, :], wg[b][:, cc, :], xnT3[:, cc, tg * 512:(tg + 1) * 512], cc == 0, cc == 7,
                     [Rwg[b]] + c.RxnT[4 * tg:4 * tg + 4], [c.bank[bank]])
            S.act(c.ogT[:, ch, tg * 512:(tg + 1) * 512], c.ps[:, bank, :], AF.Silu, [c.bank[bank]], [c.Rog[ch][tg]])


def layer_sb(c, w_in):
    S = c.S
    gate_phase(c, w_in, 3072)
    S.barrier()
    xnT3 = c.xnT[:, :].rearrange("p (c t) -> p c t", c=8)
    wv = w_in.rearrange("(c p) n -> p c n", p=128)
    qT = c.hbuf[:, 0:S_LEN]
    kT = c.hbuf[:, S_LEN:2 * S_LEN]
    v = c.hbuf[:, 2 * S_LEN:3 * S_LEN].rearrange("p (i f) -> p i f", f=128)
    RqT = [Res(f"qT{g}") for g in range(NG)]
    RkT = [Res(f"kT{g}") for g in range(NG)]
    Rv = [Res(f"v{g}") for g in range(NG)]
    wsl = [c.wbuf[:, 3072 * b:3072 * (b + 1)].rearrange("p (s c n) -> p s c n", s=3, c=8) for b in range(2)]
    Rw = [Res("wsl0"), Res("wsl1")]
    NE, NL, NW, NA = 3, 4, 2, 2
    off = 0
    E, L, Wb, Ab = [], [], [], []
    for i in range(NE):
        E.append(c.scr[:, off:off + 1024].rearrange("p (h t) -> p h t", h=2)); off += 1024
    for i in range(NL):
        L.append(c.scr[:, off:off + 512].bitcast(BF16).rearrange("p (h t) -> p h t", h=2)); off += 512
    for i in range(NW):
        Wb.append(c.scr[:, off:off + 512].bitcast(BF16).rearrange("p (h t) -> p h t", h=2)); off += 512
    assert off <= 6144
    for i in range(NA):
        Ab.append(c.wbuf[:, 6144 + 1024 * i:6144 + 1024 * (i + 1)].rearrange("p (h t) -> p h t", h=2))
    RE = [Res(f"E{i}") for i in range(NE)]
    RL = [Res(f"L{i}") for i in range(NL)]
    RW = [Res(f"W{i}") for i in range(NW)]
    RA = [Res(f"A{i}") for i in range(NA)]
    AB = [4, 5]
    OB = [6, 7]
    gcount = 0
    mask2 = c.mstrict.unsqueeze(1).to_broadcast([128, 2, 128])

    def load_w(hp):
        b = hp % 2
        for s in range(3):
            col = s * 1024 + hp * 128
            S.ld("pool", wsl[b][:, s, :, :], wv[:, :, col:col + 128], writes=[Rw[b]])

    load_w(0)
    for hp in range(8):
        b = hp % 2
        if hp + 1 < 8:
            load_w(hp + 1)
        k = 0
        for tg in range(NG):
            for s, (dst, Rd) in enumerate(((qT, RqT), (kT, RkT))):
                bank = k % 4
                k += 1
                for cc in range(8):
                    S.mm(c.ps[:, bank, :], wsl[b][:, s, cc, :], xnT3[:, cc, tg * 512:(tg + 1) * 512], cc == 0, cc == 7,
                         [Rw[b]] + c.RxnT[4 * tg:4 * tg + 4], [c.bank[bank]])
                if s == 0:
                    S.cp("dve", dst[:, tg * 512:(tg + 1) * 512], c.ps[:, bank, :], [c.bank[bank]], [Rd[tg]])
                else:
                    S.cp("act", kT[:, tg * 512:(tg + 1) * 512], c.ps[:, bank, :], [c.bank[bank]], [RkT[tg]])
            bank = k % 4
            k += 1
            for j in range(4):
                i = 4 * tg + j
                for cc in range(8):
                    S.mm(c.ps[:, bank, j * 128:(j + 1) * 128], xnT3[:, cc, i * 128:(i + 1) * 128], wsl[b][:, 2, cc, :],
                         cc == 0, cc == 7, [Rw[b], c.RxnT[i]], [c.bank[bank]])
            S.cp("dve", v[:, 4 * tg:4 * tg + 4, :], c.ps[:, bank, :].rearrange("p (j f) -> p j f", f=128),
                 [c.bank[bank]], [Rv[tg]])
        G = []
        for qg in range(NG):
            for kb in reversed(range(4 * qg + 4)):
                G.append((qg, kb))
        n_g = len(G)

        def info(gi):
            qg, kb = G[gi]
            r = kb - 4 * qg
            c0 = r * 128 if r >= 0 else 0
            return qg, kb, r, c0, kb == 4 * qg + 3, kb == 0, gcount + gi

        def st_z(gi):
            qg, kb, r, c0, first, last, gid = info(gi)
            zp = 2 * (gid % 2)
            for hd in range(2):
                rows = slice(hd * 64, hd * 64 + 64)
                S.mm(c.ps[:, zp + hd, c0:512], kT[rows, kb * 128:(kb + 1) * 128], qT[rows, qg * 512 + c0:(qg + 1) * 512],
                     True, True, [RkT[kb // 4], RqT[qg]], [c.bank[zp + hd]])

        def st_e(gi):
            qg, kb, r, c0, first, last, gid = info(gi)
            zp = 2 * (gid % 2)
            eb = gid % NE
            S.act(E[eb][:, :, c0:512], c.ps[:, zp:zp + 2, c0:512], AF.Exp, [c.bank[zp], c.bank[zp + 1]], [RE[eb]],
                  scale=0.125)
            if r >= 0:
                S.tt("dve", E[eb][:, :, c0:c0 + 128], E[eb][:, :, c0:c0 + 128], mask2, ALU.mult,
                     [RE[eb], c.Rconst], [RE[eb]])

        def st_l(gi):
            qg, kb, r, c0, first, last, gid = info(gi)
            eb = gid % NE
            lb = gid % NL
            S.act(L[lb][:, :, c0:512], E[eb][:, :, c0:512], AF.Ln, [RE[eb]], [RL[lb]], bias=1.0, scale=1.0)

        def st_cum(gi):
            qg, kb, r, c0, first, last, gid = info(gi)
            lb = gid % NL
            for hd in range(2):
                S.mm(c.ps[:, AB[hd], c0:512], c.tinc, L[lb][:, hd, c0:512], first, False, [RL[lb], c.Rconst],
                     [c.bank[AB[hd]]], skip=True)

        def st_w(gi):
            qg, kb, r, c0, first, last, gid = info(gi)
            wb = gid % NW
            eb = gid % NE
            a_i = gid % NA
            S.act(Wb[wb][:, :, c0:512], c.ps[:, 4:6, c0:512], AF.Exp, [c.bank[4], c.bank[5]], [RW[wb]])
            S.tt("dve", Ab[a_i][:, :, c0:512], E[eb][:, :, c0:512], Wb[wb][:, :, c0:512], ALU.mult,
                 [RE[eb], RW[wb]], [RA[a_i]])

        def st_car(gi):
            qg, kb, r, c0, first, last, gid = info(gi)
            lb = gid % NL
            if not last:
                for hd in range(2):
                    S.mm(c.ps[:, AB[hd], c0:512], c.tcar, L[lb][:, hd, c0:512], False, False, [RL[lb], c.Rconst],
                         [c.bank[AB[hd]]], skip=True)

        def st_av(gi):
            qg, kb, r, c0, first, last, gid = info(gi)
            a_i = gid % NA
            for hd in range(2):
                rows = slice(hd * 64, hd * 64 + 64)
                S.mm(c.ps[rows, OB[hd], c0:512], v[:, kb, hd * 64:(hd + 1) * 64], Ab[a_i][:, hd, c0:512], first, last,
                     [RA[a_i], Rv[kb // 4]], [c.bank[OB[hd]]], skip=True)
            if last:
                for hd in range(2):
                    rows = slice(hd * 64, hd * 64 + 64)
                    og = c.ogT[rows, hp, qg * 512:(qg + 1) * 512]
                    S.tt("dve", og, c.ps[rows, OB[hd], :], og, ALU.mult, [c.bank[OB[hd]], c.Rog[hp][qg]],
                         [c.Rog[hp][qg]])

        stages = [(st_car, 4), (st_cum, 3), (st_av, 5), (st_z, 0), (st_w, 3), (st_l, 2), (st_e, 1)]
        for n in range(n_g + 5):
            for fn, d in stages:
                gi = n - d
                if 0 <= gi < n_g:
                    fn(gi)
        gcount += n_g


def layer_mla(c, w):
    S = c.S
    w_in = w["w_in"]
    gate_phase(c, w_in, 448)
    S.barrier()
    xnT3 = c.xnT[:, :].rearrange("p (c t) -> p c t", c=8)
    wv = w_in.rearrange("(c p) n -> p c n", p=128)
    sm = c.small
    scr = c.scr
    Rgn = Res("mla_g")
    gqa = c.gbc[:, 0:256]
    gkva = c.gbc[:, 256:384]
    gq192 = c.gbc[:, 384:576]
    gk192 = c.gbc[:, 576:768]
    gkpe = c.gbc[:, 704:768]
    S.ld("sp", gqa, w["q_a_norm"].partition_broadcast(128), writes=[Rgn])
    S.ld("sp", gkva, w["kv_a_norm"].partition_broadcast(128), writes=[Rgn])
    S.ld("sp", gq192, w["q_head_norm"].partition_broadcast(128), writes=[Rgn])
    S.ld("sp", gk192, w["k_head_norm"].partition_broadcast(128), writes=[Rgn])
    S.tt("dve", gq192[:, 0:128], gq192[:, 0:128], gk192[:, 0:128], ALU.mult, [Rgn], [Rgn])
    cosT = scr[:, 0:1024].rearrange("p (i f) -> p i f", f=32)
    sinT = scr[:, 1024:2048].rearrange("p (i f) -> p i f", f=32)
    Rtab = Res("ropetab")
    S.ld("sp", scr[:, 0:2048], c.cf_d[:, CF_COS:CF_COS + 2048], writes=[Rtab])
    kpeT = scr[:, 2048:4096].bitcast(BF16)
    qlnT3 = c.hbuf[:, 0:2 * S_LEN].rearrange("p (a t) -> p a t", a=2)
    kvlnT = c.hbuf[:, 2 * S_LEN:3 * S_LEN]
    sskpe = sm[:, 128:160]
    rstdk = sm[:, 160:192]
    Rqln = [Res(f"qln{g}") for g in range(NG)]
    Rkvln = [Res(f"kvln{g}") for g in range(NG)]
    Rkpe = [Res(f"kpe{g}") for g in range(NG)]
    Rsskpe = [Res(f"sskpe{g}") for g in range(NG)]

    def rope(x, out, i, Rx, Rout, t1, t2, Rt):
        cb = cosT[:, i, :].unsqueeze(1).to_broadcast([128, 2, 32])
        S.tt("dve", t1.rearrange("p (a f) -> p a f", a=2), x.rearrange("p (a f) -> p a f", a=2), cb, ALU.mult,
             [Rx, Rtab], [Rt])
        S.tt("dve", t2[:, 0:32], x[:, 32:64], sinT[:, i, :], ALU.mult, [Rx, Rtab], [Rt])
        S.tt("dve", t2[:, 32:64], x[:, 0:32], sinT[:, i, :], ALU.mult, [Rx, Rtab], [Rt])
        S.tt("dve", out[:, 0:32], t1[:, 0:32], t2[:, 0:32], ALU.subtract, [Rt], [Rout])
        S.tt("dve", out[:, 32:64], t1[:, 32:64], t2[:, 32:64], ALU.add, [Rt], [Rout])

    wlat = c.wbuf[:, 0:3584].rearrange("p (c n) -> p c n", c=8)
    Rwlat = Res("wlat")
    S.ld("pool", wlat[:, 0:4, :], wv[:, 0:4, 0:448], writes=[Rwlat])
    S.ld("pool", wlat[:, 4:8, :], wv[:, 4:8, 0:448], writes=[Rwlat])
    o = 4096
    junk = scr[:, o:o + 448]; o += 448
    lnb = [scr[:, o:o + 192].bitcast(BF16), scr[:, o + 192:o + 384].bitcast(BF16)]; o += 384
    kp = scr[:, o:o + 64]; o += 64
    t1 = scr[:, o:o + 64]; o += 64
    t2 = scr[:, o:o + 64]; o += 64
    kr = [scr[:, o:o + 32].bitcast(BF16), scr[:, o + 32:o + 64].bitcast(BF16)]; o += 64
    assert o <= 6144
    Rjunk, Rkp, Rt = Res("junk"), Res("kp"), Res("ropet")
    Rlnb = [Res("lnb0"), Res("lnb1")]
    Rkr = [Res("kr0"), Res("kr1")]
    Rst = [Res("mst0"), Res("mst1")]
    for i in range(NT):
        tg = i // 4
        pb = i % 2
        bank = 6 + pb
        psL = c.ps[:, bank, 0:448]
        for cc in range(8):
            S.mm(psL, xnT3[:, cc, i * 128:(i + 1) * 128], wlat[:, cc, :], cc == 0, cc == 7,
                 [Rwlat, c.RxnT[i]], [c.bank[bank]])
        sb_ = 192 + 16 * pb
        ss2 = sm[:, sb_:sb_ + 2]
        lg2 = sm[:, sb_ + 2:sb_ + 4]
        rs2 = sm[:, sb_ + 4:sb_ + 6]
        S.act(junk[:, 0:256], c.ps[:, bank, 0:256], AF.Square, [c.bank[bank]], [Rjunk, Rst[pb]], accum_out=ss2[:, 0:1])
        S.act(junk[:, 256:384], c.ps[:, bank, 256:384], AF.Square, [c.bank[bank]], [Rjunk, Rst[pb]], accum_out=ss2[:, 1:2])
        S.act(junk[:, 384:448], c.ps[:, bank, 384:448], AF.Square, [c.bank[bank]], [Rjunk, Rsskpe[tg]],
              accum_out=sskpe[:, i:i + 1])
        S.act(lg2[:, 0:1], ss2[:, 0:1], AF.Ln, [Rst[pb]], [Rst[pb]], scale=1.0 / 256, bias=EPS)
        S.act(lg2[:, 1:2], ss2[:, 1:2], AF.Ln, [Rst[pb]], [Rst[pb]], scale=1.0 / 128, bias=EPS)
        S.act(rs2, lg2, AF.Exp, [Rst[pb]], [Rst[pb]], scale=-0.5)
        S.stt(lnb[pb][:, 0:256], c.ps[:, bank, 0:256], rs2[:, 0:1], gqa, ALU.mult, ALU.mult,
              [c.bank[bank], Rst[pb], Rgn], [Rlnb[pb]])
        S.stt(lnb[pb][:, 256:384], c.ps[:, bank, 256:384], rs2[:, 1:2], gkva, ALU.mult, ALU.mult,
              [c.bank[bank], Rst[pb], Rgn], [Rlnb[pb]])
        S.tt("dve", kp, c.ps[:, bank, 384:448], gkpe, ALU.mult, [c.bank[bank], Rgn], [Rkp])
        rope(kp, kr[pb], i, Rkp, Rkr[pb], t1, t2, Rt)
        tbank = 4 + pb
        psT = c.ps[:, tbank, :].bitcast(BF16)
        for a in range(3):
            S.tr(psT[:, a * 128:(a + 1) * 128], lnb[pb][:, a * 128:(a + 1) * 128], c.ident,
                 [Rlnb[pb], c.Rconst], [c.bank[tbank]])
        S.tr(psT[0:64, 384:512], kr[pb], c.ident, [Rkr[pb], c.Rconst], [c.bank[tbank]])
        S.cp("act", qlnT3[:, :, i * 128:(i + 1) * 128], psT[:, 0:256].rearrange("p (a t) -> p a t", a=2),
             [c.bank[tbank]], [Rqln[tg]])
        S.cp("dve", kvlnT[:, i * 128:(i + 1) * 128], psT[:, 256:384], [c.bank[tbank]], [Rkvln[tg]])
        S.cp("dve", kpeT[0:64, i * 128:(i + 1) * 128], psT[0:64, 384:512], [c.bank[tbank]], [Rkpe[tg]])
    S.barrier()
    wuq = c.wbuf[:, 0:3072].rearrange("p (a n) -> p a n", a=2)
    wukv = c.wbuf[:, 3072:5120]
    Rwu = Res("wu")
    S.ld("pool", wuq, w["w_uq"].rearrange("(a p) n -> p a n", p=128), writes=[Rwu])
    S.ld("pool", wukv, w["w_ukv"], writes=[Rwu])
    X = c.xnT
    qTn = X[:, 0:4096]
    qTp = X[:, 4096:8192]
    kTn = X[:, 8192:12288]
    vh = X[:, 12288:16384].rearrange("p (i f) -> p i f", f=128)
    NP = 3
    Pb = [X[:, 16384 + 512 * j:16384 + 512 * (j + 1)] for j in range(NP)]
    NQR = 3
    qr = [X[:, 18432 + 256 * j:18432 + 256 * j + 192] for j in range(NQR)]
    XF = X[:, 20480:24576].bitcast(F32)
    rcb = XF[:, 0:512]
    tbuf = XF[:, 512:1024]
    qpe = [XF[:, 1024 + 64 * j:1088 + 64 * j] for j in range(2)]
    u1 = [XF[:, 1152 + 64 * j:1216 + 64 * j] for j in range(2)]
    u2 = [XF[:, 1280 + 64 * j:1344 + 64 * j] for j in range(2)]
    junk2 = [XF[:, 1408 + 320 * j:1408 + 320 * (j + 1)] for j in range(2)]
    RqTn = [Res(f"qTn{g}") for g in range(NG)]
    RqTp = [Res(f"qTp{g}") for g in range(NG)]
    RkTn = [Res(f"kTn{g}") for g in range(NG)]
    Rvh = [Res(f"vh{g}") for g in range(NG)]
    Rrk = [Res(f"rstdk{g}") for g in range(NG)]
    RPb = [Res(f"mPb{j}") for j in range(NP)]
    Rqr = [Res(f"qr{j}") for j in range(NQR)]
    Rrc, Rtb = Res("rcb"), Res("tbuf")
    Rqpe = [Res("qpe0"), Res("qpe1")]
    Ru = [Res("u0"), Res("u1")]
    Rj2 = [Res("junk20"), Res("junk21")]
    NST = 4
    Rs2 = [Res(f"hst{j}") for j in range(NST)]
    QB = [0, 1, 2, 3, 6, 7]
    gid = 0
    sw = 0
    tcount = 0
    for h in range(8):
        for tg in range(NG):
            bank = 4 + tg % 2
            S.mm(c.ps[:, bank, :], wukv[:, h * 256:h * 256 + 128], kvlnT[:, tg * 512:(tg + 1) * 512], True, True,
                 [Rwu, Rkvln[tg]], [c.bank[bank]])
            S.cp("act", kTn[:, tg * 512:(tg + 1) * 512], c.ps[:, bank, :], [c.bank[bank]], [RkTn[tg]])

        def tinfo(i):
            t = tcount + i
            tg, j = divmod(i, 4)
            sb_ = 224 + 8 * (t % NST)
            return (t, tg, j, QB[t % len(QB)], slice(i * 128, (i + 1) * 128), sm[:, sb_:sb_ + 2], sm[:, sb_ + 2:sb_ + 4],
                    sm[:, sb_ + 4:sb_ + 5], t % NST, t % NQR, t % 2)

        def t_mm(i):
            t, tg, j, bank, tok, ss2, lg2, rsq, si, qi, pi = tinfo(i)
            S.mm(c.ps[:, bank, 0:192], qlnT3[:, 0, tok], wuq[:, 0, h * 192:(h + 1) * 192], True, False,
                 [Rwu, Rqln[tg]], [c.bank[bank]])
            S.mm(c.ps[:, bank, 0:192], qlnT3[:, 1, tok], wuq[:, 1, h * 192:(h + 1) * 192], False, True,
                 [Rwu, Rqln[tg]], [c.bank[bank]])
            S.mm(c.ps[:, bank, 192:448], kvlnT[:, tok], wukv[:, h * 256:(h + 1) * 256], True, True,
                 [Rwu, Rkvln[tg]], [c.bank[bank]])

        def t_sq(i):
            t, tg, j, bank, tok, ss2, lg2, rsq, si, qi, pi = tinfo(i)
            S.act(junk2[pi][:, 0:192], c.ps[:, bank, 0:192], AF.Square, [c.bank[bank]], [Rj2[pi], Rs2[si]],
                  accum_out=ss2[:, 0:1])
            S.act(junk2[pi][:, 192:320], c.ps[:, bank, 192:320], AF.Square, [c.bank[bank]], [Rj2[pi], Rs2[si]],
                  accum_out=ss2[:, 1:2])
            S.cp("act", vh[:, i, :], c.ps[:, bank, 320:448], [c.bank[bank]], [Rvh[tg]])

        def t_add(i):
            t, tg, j, bank, tok, ss2, lg2, rsq, si, qi, pi = tinfo(i)
            S.tt("dve", ss2[:, 1:2], ss2[:, 1:2], sskpe[:, i:i + 1], ALU.add, [Rs2[si], Rsskpe[tg]], [Rs2[si]])

        def t_rs(i):
            t, tg, j, bank, tok, ss2, lg2, rsq, si, qi, pi = tinfo(i)
            S.act(lg2, ss2, AF.Ln, [Rs2[si]], [Rs2[si]], scale=1.0 / 192, bias=EPS)
            S.act(rsq, lg2[:, 0:1], AF.Exp, [Rs2[si]], [Rs2[si]], scale=-0.5)
            S.act(rstdk[:, i:i + 1], lg2[:, 1:2], AF.Exp, [Rs2[si]], [Rrk[tg]], scale=-0.5, bias=-0.5 * math.log(192.0))

        def t_qn(i):
            t, tg, j, bank, tok, ss2, lg2, rsq, si, qi, pi = tinfo(i)
            S.stt(qr[qi][:, 0:128], c.ps[:, bank, 0:128], rsq, gq192[:, 0:128], ALU.mult, ALU.mult,
                  [c.bank[bank], Rs2[si], Rgn], [Rqr[qi]])
            S.stt(qpe[pi], c.ps[:, bank, 128:192], rsq, gq192[:, 128:192], ALU.mult, ALU.mult,
                  [c.bank[bank], Rs2[si], Rgn], [Rqpe[pi]])
            rope(qpe[pi], qr[qi][:, 128:192], i, Rqpe[pi], Rqr[qi], u1[pi], u2[pi], Ru[pi])

        def t_tr(i):
            t, tg, j, bank, tok, ss2, lg2, rsq, si, qi, pi = tinfo(i)
            tbank = 4 + tg % 2
            psT = c.ps[:, tbank, :].bitcast(BF16)
            S.tr(psT[:, j * 128:(j + 1) * 128], qr[qi][:, 0:128], c.ident, [Rqr[qi], c.Rconst], [c.bank[tbank]])
            S.tr(psT[0:64, 512 + j * 128:512 + (j + 1) * 128], qr[qi][:, 128:192], c.ident,
                 [Rqr[qi], c.Rconst], [c.bank[tbank]])
            if j == 3:
                S.cp("dve", qTn[:, tg * 512:(tg + 1) * 512], psT[:, 0:512], [c.bank[tbank]], [RqTn[tg]])
                S.cp("dve", qTp[0:64, tg * 512:(tg + 1) * 512], psT[0:64, 512:1024], [c.bank[tbank]], [RqTp[tg]])

        tstages = [(t_tr, 5), (t_qn, 4), (t_rs, 3), (t_add, 2), (t_sq, 1), (t_mm, 0)]
        for n in range(NT + 5):
            for fn, d in tstages:
                i = n - d
                if 0 <= i < NT:
                    fn(i)
        tcount += NT
        G = []
        for qg in range(NG):
            for kb in reversed(range(4 * qg + 4)):
                G.append((qg, kb))
        n_g = len(G)

        def info(gi):
            qg, kb = G[gi]
            r = kb - 4 * qg
            c0 = r * 128 if r >= 0 else 0
            return qg, kb, r, c0, kb == 4 * qg + 3, kb == 0, gid + gi, sw + qg

        def st_z(gi):
            qg, kb, r, c0, first, last, g_, sw_ = info(gi)
            zb = g_ % 4
            S.mm(c.ps[:, zb, c0:512], kTn[:, kb * 128:(kb + 1) * 128], qTn[:, qg * 512 + c0:(qg + 1) * 512], True, False,
                 [RkTn[kb // 4], RqTn[qg]], [c.bank[zb]])
            S.mm(c.ps[:, zb, c0:512], kpeT[0:64, kb * 128:(kb + 1) * 128], qTp[0:64, qg * 512 + c0:(qg + 1) * 512],
                 False, True, [Rkpe[kb // 4], RqTp[qg]], [c.bank[zb]])

        def st_e(gi):
            qg, kb, r, c0, first, last, g_, sw_ = info(gi)
            zb = g_ % 4
            pi = g_ % NP
            S.act(Pb[pi][:, c0:512], c.ps[:, zb, c0:512], AF.Exp, [c.bank[zb], Rrk[kb // 4]], [RPb[pi]],
                  scale=rstdk[:, kb:kb + 1])
            if r >= 0:
                S.tt("dve", Pb[pi][:, c0:c0 + 128], Pb[pi][:, c0:c0 + 128], c.mincl, ALU.mult,
                     [RPb[pi], c.Rconst], [RPb[pi]])

        def st_av(gi):
            qg, kb, r, c0, first, last, g_, sw_ = info(gi)
            pi = g_ % NP
            ob = 4 + sw_ % 2
            db = 6 + sw_ % 2
            S.mm(c.ps[:, ob, c0:512], vh[:, kb, :], Pb[pi][:, c0:512], first, last, [Rvh[kb // 4], RPb[pi]],
                 [c.bank[ob]], skip=True)
            S.mm(c.ps[:, db, c0:512], c.ones, Pb[pi][:, c0:512], first, last, [c.Rconst, RPb[pi]],
                 [c.bank[db]], skip=True)
            if last:
                S.op("dve", lambda e, o_=rcb, i_=c.ps[:, db, :]: e.reciprocal(out=o_, in_=i_), [c.bank[db]], [Rrc])
                S.tt("dve", tbuf, c.ps[:, ob, :], rcb, ALU.mult, [c.bank[ob], Rrc], [Rtb])
                og = c.ogT[:, h, qg * 512:(qg + 1) * 512]
                S.tt("dve", og, tbuf, og, ALU.mult, [Rtb, c.Rog[h][qg]], [c.Rog[h][qg]])

        stages = [(st_z, 0), (st_e, 1), (st_av, 2)]
        for n in range(n_g + 2):
            for fn, d in reversed(stages):
                gi = n - d
                if 0 <= gi < n_g:
                    fn(gi)
        gid += n_g
        sw += NG


def layer_swa(c, w):
    S = c.S
    w_in = w["w_in"]
    gate_phase(c, w_in, 1536)
    S.barrier()
    xnT3 = c.xnT[:, :].rearrange("p (c t) -> p c t", c=8)
    wv = w_in.rearrange("(c p) n -> p c n", p=128)
    qT3 = c.hbuf[:, 0:2 * S_LEN].rearrange("p (a t) -> p a t", a=2)
    kT2 = c.hbuf[:, 2 * S_LEN:3 * S_LEN]
    scr = c.scr
    off = 0
    vg = scr[:, off:off + 1024].bitcast(BF16).rearrange("p (i f) -> p i f", f=64); off += 1024
    junk = [scr[:, off + 320 * j:off + 320 * (j + 1)] for j in range(2)]; off += 640
    tmpq = [scr[:, off + 256 * j:off + 256 * (j + 1)] for j in range(2)]; off += 512
    NQN = 3
    qn = [scr[:, off + 128 * j:off + 128 * (j + 1)].bitcast(BF16) for j in range(NQN)]; off += 128 * NQN
    biasg = scr[:, off:off + 1024]; off += 1024
    Tb = scr[:, off:off + 1024]; off += 1024
    Pb = [scr[:, off:off + 512].bitcast(BF16), scr[:, off + 512:off + 1024].bitcast(BF16)]; off += 1024
    dnb = scr[:, off:off + 256]; off += 256
    gqk4 = scr[:, off:off + 256]; off += 256
    assert off <= 6144, off
    sm = c.small
    kscale = sm[:, 16:48]
    es16 = sm[:, 48:64]
    Rbias, RTb, Rdn, Rgqk, Res16 = (Res(n) for n in ("biasg", "Tb", "dnb", "gqk", "es16"))
    Rjunk = [Res("junk0"), Res("junk1")]
    Rtmpq = [Res("tmpq0"), Res("tmpq1")]
    Rqn = [Res(f"qn{j}") for j in range(NQN)]
    RPb = [Res("Pb0"), Res("Pb1")]
    NST = 4
    Rst = [Res(f"sst{j}") for j in range(NST)]
    RqT = [Res(f"qT{g}") for g in range(NG)]
    RkT = [Res(f"kT{g}") for g in range(NG)]
    Rvg = [Res(f"vg{g}") for g in range(NG)]
    Rks = [Res(f"ks{g}") for g in range(NG)]
    wtm = [c.wbuf[:, 4096 * b:4096 * b + 3072].rearrange("p (c n) -> p c n", c=8) for b in range(2)]
    wk2 = [c.wbuf[:, 4096 * b + 3072:4096 * (b + 1)].rearrange("p (c n) -> p c n", c=8) for b in range(2)]
    Rw = [Res("swaw0"), Res("swaw1")]
    gk4 = junk[0][:, 0:256]
    for j in range(4):
        S.ld("sp", gqk4[:, j * 64:(j + 1) * 64], w["q_head_norm"].partition_broadcast(128), writes=[Rgqk])
        S.ld("sp", gk4[:, j * 64:(j + 1) * 64], w["k_head_norm"].partition_broadcast(128), writes=[Rjunk[0]])
    S.tt("dve", gqk4, gqk4, gk4, ALU.mult, [Rjunk[0], Rgqk], [Rgqk])
    S.ld("sp", es16, w["sinks"].partition_broadcast(128), writes=[Res16])
    S.act(es16, es16, AF.Exp, [Res16], [Res16])

    def load_w(g):
        b = g % 2
        S.ld("pool", wtm[b][:, :, 0:256], wv[:, :, g * 256:(g + 1) * 256], writes=[Rw[b]])
        S.ld("pool", wtm[b][:, :, 256:320], wv[:, :, 1024 + g * 64:1024 + (g + 1) * 64], writes=[Rw[b]])
        S.ld("pool", wtm[b][:, :, 320:384], wv[:, :, 1280 + g * 64:1280 + (g + 1) * 64], writes=[Rw[b]])
        for d in range(2):
            S.ld("pool", wk2[b][:, :, d * 64:(d + 1) * 64], wv[:, :, 1024 + g * 64:1024 + (g + 1) * 64], writes=[Rw[b]])

    load_w(0)
    TB = [0, 1, 2, 3, 6, 7]
    tcount = 0
    acount = 0
    for g in range(4):
        b = g % 2
        if g + 1 < 4:
            load_w(g + 1)
        for kbi in range(2):
            S.ld("sp", biasg[:, kbi * 512:(kbi + 1) * 512],
                 c.cf_d[:, CF_BIAS + kbi * 2048 + g * 512:CF_BIAS + kbi * 2048 + (g + 1) * 512], writes=[Rbias])
        for tg in range(NG):
            bank = 4 + tg % 2
            for cc in range(8):
                S.mm(c.ps[:, bank, :], wk2[b][:, cc, :], xnT3[:, cc, tg * 512:(tg + 1) * 512], cc == 0, cc == 7,
                     [Rw[b]] + c.RxnT[4 * tg:4 * tg + 4], [c.bank[bank]])
            S.cp("act", kT2[:, tg * 512:(tg + 1) * 512], c.ps[:, bank, :], [c.bank[bank]], [RkT[tg]])

        def tinfo(i):
            t = tcount + i
            tg, j = divmod(i, 4)
            sb_ = 64 + 16 * (t % NST)
            return (t, tg, j, TB[t % len(TB)], sm[:, sb_:sb_ + 5], sm[:, sb_ + 5:sb_ + 10], sm[:, sb_ + 10:sb_ + 14],
                    t % NST, t % 2, t % NQN)

        def t_mm(i):
            t, tg, j, bank, ss5, lg5, rs4, si, pi, qi = tinfo(i)
            for cc in range(8):
                S.mm(c.ps[:, bank, 0:384], xnT3[:, cc, i * 128:(i + 1) * 128], wtm[b][:, cc, :], cc == 0, cc == 7,
                     [Rw[b], c.RxnT[i]], [c.bank[bank]])

        def t_sq(i):
            t, tg, j, bank, ss5, lg5, rs4, si, pi, qi = tinfo(i)
            S.act(junk[pi], c.ps[:, bank, 0:320], AF.Square, [c.bank[bank]], [Rjunk[pi]])
            S.cp("act", vg[:, i, :], c.ps[:, bank, 320:384], [c.bank[bank]], [Rvg[tg]])

        def t_red(i):
            t, tg, j, bank, ss5, lg5, rs4, si, pi, qi = tinfo(i)
            S.op("dve", lambda e, o=ss5, i_=junk[pi].rearrange("p (h f) -> p h f", f=64): e.reduce_sum(out=o, in_=i_, axis=AX.X),
                 [Rjunk[pi]], [Rst[si]])

        def t_rs(i):
            t, tg, j, bank, ss5, lg5, rs4, si, pi, qi = tinfo(i)
            S.act(lg5, ss5, AF.Ln, [Rst[si]], [Rst[si]], scale=1.0 / 64, bias=EPS)
            S.act(rs4, lg5[:, 0:4], AF.Exp, [Rst[si]], [Rst[si]], scale=-0.5)
            S.act(kscale[:, i:i + 1], lg5[:, 4:5], AF.Exp, [Rst[si]], [Rks[tg]], scale=-0.5, bias=math.log(0.125))

        def t_qn(i):
            t, tg, j, bank, ss5, lg5, rs4, si, pi, qi = tinfo(i)
            S.tt("dve", tmpq[pi].rearrange("p (h f) -> p h f", f=64),
                 c.ps[:, bank, 0:256].rearrange("p (h f) -> p h f", f=64),
                 rs4.unsqueeze(2).to_broadcast([128, 4, 64]), ALU.mult, [c.bank[bank], Rst[si]], [Rtmpq[pi]])
            S.tt("dve", qn[qi], tmpq[pi], gqk4, ALU.mult, [Rtmpq[pi], Rgqk], [Rqn[qi]])

        def t_tr(i):
            t, tg, j, bank, ss5, lg5, rs4, si, pi, qi = tinfo(i)
            tbank = 4 + tg % 2
            psT = c.ps[:, tbank, :].bitcast(BF16)
            for a in range(2):
                S.tr(psT[:, a * 512 + j * 128:a * 512 + (j + 1) * 128], qn[qi][:, a * 128:(a + 1) * 128], c.ident,
                     [Rqn[qi], c.Rconst], [c.bank[tbank]])
            if j == 3:
                S.cp("dve", qT3[:, :, tg * 512:(tg + 1) * 512], psT.rearrange("p (a t) -> p a t", a=2),
                     [c.bank[tbank]], [RqT[tg]])

        tstages = [(t_tr, 5), (t_qn, 4), (t_rs, 3), (t_red, 2), (t_sq, 1), (t_mm, 0)]
        for n in range(NT + 5):
            for fn, d in tstages:
                i = n - d
                if 0 <= i < NT:
                    fn(i)
        tcount += NT

        def ainfo(qb):
            a_ = acount + qb
            kbs = [(0, qb - 1), (1, qb)] if qb > 0 else [(1, qb)]
            return a_, kbs, 2 * (a_ % 2), a_ % 2, 4 + a_ % 2

        def a_qk(qb):
            a_, kbs, zb0, pbi, ob = ainfo(qb)
            for kbi, kb in kbs:
                for hq in range(4):
                    p = hq % 2
                    rows = slice(p * 64, p * 64 + 64)
                    col = (kbi * 2 + hq // 2) * 128
                    S.mm(c.ps[:, zb0 + p, col:col + 128], kT2[rows, kb * 128:(kb + 1) * 128],
                         qT3[rows, hq // 2, qb * 128:(qb + 1) * 128], True, True,
                         [RkT[kb // 4], RqT[qb // 4]], [c.bank[zb0 + p]])

        def a_bias(qb):
            a_, kbs, zb0, pbi, ob = ainfo(qb)
            for kbi, kb in kbs:
                for p in range(2):
                    src = c.ps[:, zb0 + p, kbi * 256:(kbi + 1) * 256].rearrange("q (a t) -> q a t", a=2)
                    dst = Tb[:, kbi * 512:(kbi + 1) * 512].rearrange("q (a p t) -> q p a t", a=2, p=2)[:, p]
                    bia = biasg[:, kbi * 512:(kbi + 1) * 512].rearrange("q (a p t) -> q p a t", a=2, p=2)[:, p]
                    S.stt(dst, src, kscale[:, kb:kb + 1], bia, ALU.mult, ALU.add,
                          [c.bank[zb0 + p], Rks[kb // 4], Rbias], [RTb])

        def a_exp(qb):
            a_, kbs, zb0, pbi, ob = ainfo(qb)
            lo = 0 if qb > 0 else 512
            S.act(Pb[pbi][:, lo:1024], Tb[:, lo:1024], AF.Exp, [RTb], [RPb[pbi]])

        def a_av(qb):
            a_, kbs, zb0, pbi, ob = ainfo(qb)
            for hq in range(4):
                rows = slice((hq % 2) * 64, (hq % 2) * 64 + 64)
                col = (hq // 2) * 128
                for n_, (kbi, kb) in enumerate(kbs):
                    S.mm(c.ps[rows, ob, col:col + 128], vg[:, kb, :], Pb[pbi][:, (kbi * 4 + hq) * 128:(kbi * 4 + hq + 1) * 128],
                         n_ == 0, n_ == len(kbs) - 1, [Rvg[kb // 4], RPb[pbi]], [c.bank[ob]])
                for n_, (kbi, kb) in enumerate(kbs):
                    S.mm(c.ps[rows, ob, 256 + col:256 + col + 128], c.ones[:, 0:64],
                         Pb[pbi][:, (kbi * 4 + hq) * 128:(kbi * 4 + hq + 1) * 128],
                         n_ == 0, n_ == len(kbs) - 1, [c.Rconst, RPb[pbi]], [c.bank[ob]])

        def a_out(qb):
            a_, kbs, zb0, pbi, ob = ainfo(qb)
            for hq in range(4):
                rows = slice((hq % 2) * 64, (hq % 2) * 64 + 64)
                col = (hq // 2) * 128
                h = 4 * g + hq
                S.ts("dve", dnb[rows, col:col + 128], c.ps[rows, ob, 256 + col:256 + col + 128], es16[rows, h:h + 1],
                     None, ALU.add, None, [c.bank[ob], Res16], [Rdn])
            S.act(dnb, dnb, AF.Ln, [Rdn], [Rdn])
            S.act(dnb, dnb, AF.Exp, [Rdn], [Rdn], scale=-1.0)
            S.tt("dve", dnb, c.ps[:, ob, 0:256], dnb, ALU.mult, [c.bank[ob], Rdn], [Rdn])
            og = c.ogT[:, 2 * g:2 * g + 2, qb * 128:(qb + 1) * 128]
            Rogs = [c.Rog[2 * g][qb // 4], c.Rog[2 * g + 1][qb // 4]]
            S.tt("dve", og, dnb.rearrange("p (a t) -> p a t", a=2), og, ALU.mult, [Rdn] + Rogs, Rogs)

        astages = [(a_out, 4), (a_av, 3), (a_exp, 2), (a_bias, 1), (a_qk, 0)]
        for n in range(NT + 4):
            for fn, d in astages:
                qb = n - d
                if 0 <= qb < NT:
                    fn(qb)
        acount += NT


LAUNCH_GROUPS = [[0, 1, 2, 3]]
_CONSTS = None


def run_layers(layers, xs, inputs):
    global _CONSTS
    if _CONSTS is None:
        _CONSTS = make_consts()
    cbf, cf = _CONSTS
    nc = build_program(layers)
    names = [n for li in layers for n in LAYER_WEIGHTS[li]]
    in_maps = []
    for b in range(len(xs)):
        m = {"x": np.ascontiguousarray(xs[b], dtype=np.float32), "cbf": cbf, "cf": cf}
        for n in names:
            m[n] = np.ascontiguousarray(inputs[n], dtype=np.float32)
        in_maps.append(m)
    res = run_bass_kernel_spmd(nc, in_maps, core_ids=list(range(len(xs))))
    return [r["y"] for r in res.results]


def kernel(**inputs):
    x = np.asarray(inputs["x"])
    xs = [x[b] for b in range(x.shape[0])]
    for grp in LAUNCH_GROUPS:
        xs = run_layers(grp, xs, inputs)
    return np.stack(xs, axis=0).astype(np.float32)
```

```python
import math
import numpy as np
import ml_dtypes
import concourse.bass as bass
import concourse.mybir as mybir
from concourse.bass_utils import run_bass_kernel_spmd

F32 = mybir.dt.float32
BF16 = mybir.dt.bfloat16
AF = mybir.ActivationFunctionType
ALU = mybir.AluOpType
AX = mybir.AxisListType

S_LEN = 4096
D = 1024
NT = S_LEN // 128
NG = S_LEN // 512
EPS = 1e-6

LAYER_KIND = ["sb", "mla", "swa", "sb"]
LAYER_WEIGHTS = [
    ["l0_norm", "l0_w_in", "l0_w_out"],
    ["l1_norm", "l1_w_in", "l1_q_a_norm", "l1_w_uq", "l1_kv_a_norm", "l1_w_ukv",
     "l1_q_head_norm", "l1_k_head_norm", "l1_w_out"],
    ["l2_norm", "l2_w_in", "l2_q_head_norm", "l2_k_head_norm", "l2_sinks", "l2_w_out"],
    ["l3_norm", "l3_w_in", "l3_w_out"],
]
WSHAPES = {
    "l0_norm": [1024], "l0_w_in": [1024, 4096], "l0_w_out": [1024, 1024],
    "l1_norm": [1024], "l1_w_in": [1024, 1472], "l1_q_a_norm": [256], "l1_w_uq": [256, 1536],
    "l1_kv_a_norm": [128], "l1_w_ukv": [128, 2048], "l1_q_head_norm": [192], "l1_k_head_norm": [192],
    "l1_w_out": [1024, 1024],
    "l2_norm": [1024], "l2_w_in": [1024, 2560], "l2_q_head_norm": [64], "l2_k_head_norm": [64],
    "l2_sinks": [16], "l2_w_out": [1024, 1024],
    "l3_norm": [1024], "l3_w_in": [1024, 4096], "l3_w_out": [1024, 1024],
}


class Res:
    __slots__ = ("name", "w", "r", "excl")

    def __init__(self, name, excl=False):
        self.name = name
        self.w = None
        self.r = {}
        self.excl = excl


class Sched:
    ENGS = ("pe", "act", "dve", "pool", "sp")

    def __init__(self, sems, dma_pools):
        self.sems = sems
        self.cnt = {k: 0 for k in sems}
        self.ops = {e: [] for e in self.ENGS}
        self.seen = {e: {} for e in self.ENGS}
        self.dma_pools = dma_pools
        self.dma_rr = {q: 0 for q in dma_pools}
        self.nwaits = 0

    def _wait(self, e, key, val):
        if val <= 0 or val <= self.seen[e].get(key, 0):
            return
        self.seen[e][key] = val
        sem = self.sems[key]
        self.ops[e].append(lambda eng, sem=sem, val=val: eng.wait_ge(sem, val))
        self.nwaits += 1

    def _sync(self, e, reads, writes, dma):
        rd = [r for r in reads if not r.excl]
        wr = list(writes) + [r for r in reads if r.excl]
        need = []
        for r in rd:
            if r.w is not None:
                need.append((r.w, True))
        for r in wr:
            if r.w is not None:
                need.append((r.w, False))
            for ev in r.r.values():
                need.append((ev, False))
        for (key, val, eng), raw in need:
            if eng == e and not dma:
                if e == "pe":
                    continue
            self._wait(e, key, val)
        return rd, wr

    def _update(self, rd, wr, ev):
        for r in rd:
            r.r[ev[0]] = ev
        for r in wr:
            r.w = ev
            r.r = {}

    def op(self, e, fn, reads=(), writes=()):
        rd, wr = self._sync(e, reads, writes, False)
        self.cnt[e] += 1
        sem = self.sems[e]
        self.ops[e].append(lambda eng, fn=fn, sem=sem: fn(eng).then_inc(sem, 1))
        self._update(rd, wr, (e, self.cnt[e], e))

    def dma(self, q, fn, reads=(), writes=()):
        pool = self.dma_pools[q]
        k = pool[self.dma_rr[q] % len(pool)]
        self.dma_rr[q] += 1
        self._wait(q, k, self.cnt[k])
        rd, wr = self._sync(q, reads, writes, True)
        self.cnt[k] += 16
        sem = self.sems[k]
        self.ops[q].append(lambda eng, fn=fn, sem=sem: fn(eng).then_inc(sem, 16))
        self._update(rd, wr, (k, self.cnt[k], None))

    def barrier(self, engines=None):
        for e in (engines or self.ENGS):
            for k, v in self.cnt.items():
                if k != e:
                    self._wait(e, k, v)


    def mm(self, out, lhsT, rhs, start, stop, reads, writes, skip=False):
        if skip:
            self.op("pe", lambda e: e.matmul(out, lhsT=lhsT, rhs=rhs, start=start, stop=stop, skip_group_check=True),
                    reads, writes)
        else:
            self.op("pe", lambda e: e.matmul(out, lhsT=lhsT, rhs=rhs, start=start, stop=stop), reads, writes)

    def tr(self, out, in_, ident, reads, writes):
        self.op("pe", lambda e: e.transpose(out=out, in_=in_, identity=ident), reads, writes)

    def act(self, out, in_, func, reads, writes, **kw):
        self.op("act", lambda e: e.activation(out=out, in_=in_, func=func, **kw), reads, writes)

    def tt(self, eng, out, in0, in1, op, reads, writes):
        self.op(eng, lambda e: e.tensor_tensor(out=out, in0=in0, in1=in1, op=op), reads, writes)

    def stt(self, out, in0, scalar, in1, op0, op1, reads, writes):
        self.op("dve", lambda e: e.scalar_tensor_tensor(out=out, in0=in0, scalar=scalar, in1=in1, op0=op0, op1=op1),
                reads, writes)

    def ts(self, eng, out, in0, s1, s2, op0, op1, reads, writes):
        if op1 is None:
            self.op(eng, lambda e: e.tensor_scalar(out=out, in0=in0, scalar1=s1, scalar2=None, op0=op0), reads, writes)
        else:
            self.op(eng, lambda e: e.tensor_scalar(out=out, in0=in0, scalar1=s1, scalar2=s2, op0=op0, op1=op1),
                    reads, writes)

    def cp(self, eng, out, in_, reads, writes):
        if eng == "act":
            self.op("act", lambda e: e.copy(out=out, in_=in_), reads, writes)
        else:
            self.op(eng, lambda e: e.tensor_copy(out=out, in_=in_), reads, writes)

    def ld(self, q, out, in_, reads=(), writes=()):
        self.dma(q, lambda e: e.dma_start(out=out, in_=in_), reads, writes)

    def emit(self, block):
        def mk(e):
            def body(eng):
                for f in self.ops[e]:
                    f(eng)
            return body
        block.tensor(mk("pe"))
        block.scalar(mk("act"))
        block.vector(mk("dve"))
        block.gpsimd(mk("pool"))
        block.sync(mk("sp"))


def make_consts():
    j = np.arange(128)[:, None]
    s = np.arange(128)[None, :]
    ident = (j == s).astype(np.float32)
    tinc = -(j >= s).astype(np.float32)
    tcar = -(j < s).astype(np.float32)
    ones = np.ones((128, 128), np.float32)
    cbf = np.concatenate([ident, tinc, tcar, ones], axis=1)
    mstrict = (s > j).astype(np.float32)
    mincl = (s >= j).astype(np.float32)
    half = 32
    inv_freq = (10000.0 ** (-np.arange(half, dtype=np.float32) / half)).astype(np.float32)
    pos = np.arange(S_LEN, dtype=np.float32)
    ang = (pos[:, None] * inv_freq[None, :]).astype(np.float32)
    cos = np.cos(ang).astype(np.float32)
    sin = np.sin(ang).astype(np.float32)
    cosT = cos.reshape(NT, 128, 32).transpose(1, 0, 2).reshape(128, NT * 32)
    sinT = sin.reshape(NT, 128, 32).transpose(1, 0, 2).reshape(128, NT * 32)
    slopes = (2.0 ** (-8.0 * np.arange(1, 17, dtype=np.float32) / 16)).astype(np.float32)
    NEG = -30000.0
    bias = np.zeros((128, 2, 16, 128), np.float32)
    for kbi in range(2):
        rel = (s + 128 - (j + 128 * kbi)).astype(np.float32)
        valid = (rel >= 0) & (rel < 128)
        for h in range(16):
            bias[:, kbi, h, :] = np.where(valid, -slopes[h] * rel, NEG)
    cf = np.concatenate([mstrict, mincl, cosT, sinT, bias.reshape(128, -1)], axis=1).astype(np.float32)
    return cbf.astype(np.float32), cf


CF_MSTRICT = 0
CF_MINCL = 128
CF_COS = 256
CF_SIN = 256 + NT * 32
CF_BIAS = 256 + 2 * NT * 32
CF_TOTAL = CF_BIAS + 2 * 16 * 128


class Ctx:
    pass


def build_program(layers, dbg=None):
    nc = bass.Bass("TRN2", target_bir_lowering=False)
    x_in = nc.dram_tensor("x", [S_LEN, D], F32, kind="ExternalInput").ap()
    y_out = nc.dram_tensor("y", [S_LEN, D], F32, kind="ExternalOutput").ap()
    cbf_d = nc.dram_tensor("cbf", [128, 512], F32, kind="ExternalInput").ap()
    cf_d = nc.dram_tensor("cf", [128, CF_TOTAL], F32, kind="ExternalInput").ap()
    W = {}
    for li in layers:
        for n in LAYER_WEIGHTS[li]:
            W[n] = nc.dram_tensor(n, WSHAPES[n], F32, kind="ExternalInput").ap()
    xs = [x_in]
    for i in range(len(layers) - 1):
        xs.append(nc.dram_tensor(f"xmid{i}", [S_LEN, D], F32, kind="Internal").ap())
    xs.append(y_out)

    from contextlib import ExitStack
    with ExitStack() as st:
        def sb(name, shape, dt):
            return st.enter_context(nc.sbuf_tensor(name, shape, dt))
        c = Ctx()
        c.nc = nc
        c.xnT = sb("xnT", [128, 8 * S_LEN], BF16)
        c.ogT = sb("ogT", [128, 8, S_LEN], BF16)
        c.hbuf = sb("hbuf", [128, 3 * S_LEN], BF16)
        c.wbuf = sb("wbuf", [128, 8192], BF16)
        c.scr = sb("scr", [128, 6144], F32)
        c.gbc = sb("gbc", [128, 1024], F32)
        c.cbf = sb("cbfs", [128, 512], BF16)
        c.cmask = sb("cmask", [128, 256], F32)
        c.small = sb("small", [128, 512], F32)
        c.ps = st.enter_context(nc.psum_tensor("ps", [128, 8, 512], F32))
        sem_names = list(Sched.ENGS) + [f"d{i}" for i in range(20)]
        sems = {k: st.enter_context(nc.semaphore(f"s_{k}")) for k in sem_names}
        S = Sched(sems, {"sp": [f"d{i}" for i in range(0, 8)],
                         "pool": [f"d{i}" for i in range(8, 16)],
                         "act": [f"d{i}" for i in range(16, 20)]})
        c.S = S
        c.W = W
        c.cf_d = cf_d
        c.dbg = dbg
        c.bank = [Res(f"bank{b}", excl=True) for b in range(8)]
        c.ident = c.cbf[:, 0:128]
        c.tinc = c.cbf[:, 128:256]
        c.tcar = c.cbf[:, 256:384]
        c.ones = c.cbf[:, 384:512]
        c.mstrict = c.cmask[:, 0:128]
        c.mincl = c.cmask[:, 128:256]
        c.Rconst = Res("const")
        S.ld("pool", c.cbf[:], cbf_d[:, :], writes=[c.Rconst])
        S.ld("sp", c.cmask[:], cf_d[:, 0:256], writes=[c.Rconst])
        S.barrier()

        block = st.enter_context(nc.Block())
        for idx, li in enumerate(layers):
            kind = LAYER_KIND[li]
            pre = f"l{li}_"
            if idx == 0:
                phase_norm(c, xs[idx], W[pre + "norm"])
                S.barrier()
            if kind == "sb":
                layer_sb(c, W[pre + "w_in"])
            elif kind == "mla":
                layer_mla(c, {k[3:]: v for k, v in W.items() if k.startswith(pre)})
            else:
                layer_swa(c, {k[3:]: v for k, v in W.items() if k.startswith(pre)})
            S.barrier()
            nxt = W[f"l{layers[idx + 1]}_norm"] if idx + 1 < len(layers) else None
            phase_out(c, xs[idx], xs[idx + 1], W[pre + "w_out"], nxt)
            S.barrier()
        S.emit(block)
    return nc


def norm_stages(c, Rg, src_of, Rsrc_of, tag):
    S = c.S
    xnT3 = c.xnT[:, :].rearrange("p (c t) -> p c t", c=8)
    junk = c.scr[:, 0:1024]
    Rjunk = Res("junk" + tag)
    xn = [c.scr[:, 1024 + 512 * j:1536 + 512 * j].bitcast(BF16) for j in range(2)]
    Rxn = [Res(f"xn{j}" + tag) for j in range(2)]
    NSTAT = 4
    Rst = [Res(f"nst{j}" + tag) for j in range(NSTAT)]
    c.RxnT = [Res(f"xnT{i}" + tag) for i in range(NT)]

    def cols(i):
        k = i % NSTAT
        return c.small[:, 4 * k:4 * k + 1], c.small[:, 4 * k + 1:4 * k + 2], c.small[:, 4 * k + 2:4 * k + 3], k

    def n_stat(i):
        ss, lg, rs, k = cols(i)
        S.act(junk, src_of(i), AF.Square, [Rsrc_of(i)], [Rjunk, Rst[k]], accum_out=ss)
        S.act(lg, ss, AF.Ln, [Rst[k]], [Rst[k]], scale=1.0 / D, bias=EPS)
        S.act(rs, lg, AF.Exp, [Rst[k]], [Rst[k]], scale=-0.5)

    def n_xn(i):
        ss, lg, rs, k = cols(i)
        b = i % 2
        S.stt(xn[b], src_of(i), rs, c.gbc[:], ALU.mult, ALU.mult, [Rsrc_of(i), Rst[k], Rg], [Rxn[b]])

    def n_tr(i):
        b = i % 2
        bank = 4 + b
        psT = c.ps[:, bank, :].bitcast(BF16)
        for ch in range(8):
            S.tr(psT[:, ch * 128:(ch + 1) * 128], xn[b][:, ch * 128:(ch + 1) * 128], c.ident,
                 [Rxn[b], c.Rconst], [c.bank[bank]])
        dst = xnT3[:, :, i * 128:(i + 1) * 128]
        src = psT.rearrange("p (c t) -> p c t", c=8)
        S.cp("act" if b == 0 else "dve", dst, src, [c.bank[bank]], [c.RxnT[i]])

    return n_stat, n_xn, n_tr


def phase_norm(c, x_d, g_d):
    S = c.S
    Rg = Res("gbc")
    S.ld("sp", c.gbc[:], g_d.partition_broadcast(128), writes=[Rg])
    hb = c.hbuf[:, :].bitcast(F32)
    NX = 4
    xin = [hb[:, 1024 * j:1024 * (j + 1)] for j in range(NX)]
    Rxin = [Res(f"xin{j}") for j in range(NX)]
    n_stat, n_xn, n_tr = norm_stages(c, Rg, lambda i: xin[i % NX], lambda i: Rxin[i % NX], "A")

    def n_ld(i):
        S.ld("sp", xin[i % NX], x_d[i * 128:(i + 1) * 128, :], writes=[Rxin[i % NX]])

    stages = [(n_tr, 4), (n_xn, 3), (n_stat, 2), (n_ld, 0)]
    for n in range(NT + 4):
        for fn, d in stages:
            i = n - d
            if 0 <= i < NT:
                fn(i)


def phase_out(c, x_d, y_d, wout_d, next_g=None):
    S = c.S
    wo = c.wbuf[:, :].rearrange("p (c n) -> p c n", c=8)
    Rwo = Res("wo")
    wv = wout_d.rearrange("(c p) n -> p c n", p=128)
    for h in range(2):
        S.ld("pool", wo[:, 4 * h:4 * h + 4, :], wv[:, 4 * h:4 * h + 4, :], writes=[Rwo])
    Rg = Res("gbcC")
    if next_g is not None:
        S.ld("sp", c.gbc[:], next_g.partition_broadcast(128), writes=[Rg])
    hb = c.hbuf[:, :].bitcast(F32)
    NX = 3
    xin = [hb[:, 1024 * j:1024 * (j + 1)] for j in range(NX)]
    yo = [hb[:, 3072 + 1024 * j:3072 + 1024 * (j + 1)] for j in range(NX)]
    Rxin = [Res(f"cxin{j}") for j in range(NX)]
    Ryo = [Res(f"yo{j}") for j in range(NX)]
    Rog = c.Rog

    def c_ld(i):
        S.ld("sp", xin[i % NX], x_d[i * 128:(i + 1) * 128, :], writes=[Rxin[i % NX]])

    def c_mm(i):
        for h in range(2):
            bank = 2 * (i % 2) + h
            for ch in range(8):
                S.mm(c.ps[:, bank, :], c.ogT[:, ch, i * 128:(i + 1) * 128], wo[:, ch, h * 512:(h + 1) * 512],
                     ch == 0, ch == 7, [Rwo, Rog[ch][i // 4]], [c.bank[bank]])

    def c_add(i):
        k = i % NX
        for h in range(2):
            bank = 2 * (i % 2) + h
            S.tt("dve", yo[k][:, h * 512:(h + 1) * 512], c.ps[:, bank, :], xin[k][:, h * 512:(h + 1) * 512],
                 ALU.add, [c.bank[bank], Rxin[k]], [Ryo[k]])
        S.ld("pool", y_d[i * 128:(i + 1) * 128, :], yo[k], reads=[Ryo[k]])

    stages = [(c_add, 3), (c_mm, 2), (c_ld, 0)]
    tail = 3
    if next_g is not None:
        n_stat, n_xn, n_tr = norm_stages(c, Rg, lambda i: yo[i % NX], lambda i: Ryo[i % NX], "C")
        stages = [(n_tr, 6), (n_xn, 5), (n_stat, 4)] + stages
        tail = 6
    for n in range(NT + tail):
        for fn, d in stages:
            i = n - d
            if 0 <= i < NT:
                fn(i)


def gate_phase(c, w_in, col0):
    S = c.S
    xnT3 = c.xnT[:, :].rearrange("p (c t) -> p c t", c=8)
    wv = w_in.rearrange("(c p) n -> p c n", p=128)
    wg = [c.wbuf[:, 6144 + 1024 * b: 6144 + 1024 * (b + 1)].rearrange("p (c n) -> p c n", c=8) for b in range(2)]
    Rwg = [Res("wg0"), Res("wg1")]
    c.Rog = [[Res(f"og{ch}_{tg}") for tg in range(NG)] for ch in range(8)]
    k = 0
    for ch in range(8):
        b = ch % 2
        S.ld("pool", wg[b], wv[:, :, col0 + ch * 128: col0 + (ch + 1) * 128], writes=[Rwg[b]])
        for tg in range(NG):
            bank = k % 4
            k += 1
            for cc in range(8):
                S.mm(c.ps[:, bank, :], wg[b][:, cc, :], xnT3[:, cc, tg * 512:(tg + 1) * 512], cc == 0, cc == 7,
                     [Rwg[b]] + c.RxnT[4 * tg:4 * tg + 4], [c.bank[bank]])
            S.act(c.ogT[:, ch, tg * 512:(tg + 1) * 512], c.ps[:, bank, :], AF.Silu, [c.bank[bank]], [c.Rog[ch][tg]])


def layer_sb(c, w_in):
    S = c.S
    gate_phase(c, w_in, 3072)
    S.barrier()
    xnT3 = c.xnT[:, :].rearrange("p (c t) -> p c t", c=8)
    wv = w_in.rearrange("(c p) n -> p c n", p=128)
    qT = c.hbuf[:, 0:S_LEN]
    kT = c.hbuf[:, S_LEN:2 * S_LEN]
    v = c.hbuf[:, 2 * S_LEN:3 * S_LEN].rearrange("p (i f) -> p i f", f=128)
    RqT = [Res(f"qT{g}") for g in range(NG)]
    RkT = [Res(f"kT{g}") for g in range(NG)]
    Rv = [Res(f"v{g}") for g in range(NG)]
    wsl = [c.wbuf[:, 3072 * b:3072 * (b + 1)].rearrange("p (s c n) -> p s c n", s=3, c=8) for b in range(2)]
    Rw = [Res("wsl0"), Res("wsl1")]
    NE, NL, NW, NA = 3, 4, 2, 2
    off = 0
    E, L, Wb, Ab = [], [], [], []
    for i in range(NE):
        E.append(c.scr[:, off:off + 1024].rearrange("p (h t) -> p h t", h=2)); off += 1024
    for i in range(NL):
        L.append(c.scr[:, off:off + 512].bitcast(BF16).rearrange("p (h t) -> p h t", h=2)); off += 512
    for i in range(NW):
        Wb.append(c.scr[:, off:off + 512].bitcast(BF16).rearrange("p (h t) -> p h t", h=2)); off += 512
    assert off <= 6144
    for i in range(NA):
        Ab.append(c.wbuf[:, 6144 + 1024 * i:6144 + 1024 * (i + 1)].rearrange("p (h t) -> p h t", h=2))
    RE = [Res(f"E{i}") for i in range(NE)]
    RL = [Res(f"L{i}") for i in range(NL)]
    RW = [Res(f"W{i}") for i in range(NW)]
    RA = [Res(f"A{i}") for i in range(NA)]
    AB = [4, 5]
    OBK = 6
    gcount = 0
    mask2 = c.mstrict.unsqueeze(1).to_broadcast([128, 2, 128])

    def load_w(hp):
        b = hp % 2
        for s in range(3):
            col = s * 1024 + hp * 128
            S.ld("pool", wsl[b][:, s, :, :], wv[:, :, col:col + 128], writes=[Rw[b]])

    def proj_groups(hp, banks):
        b = hp % 2
        k = 0
        for tg in reversed(range(NG)):
            for s_, (dst, Rd) in enumerate(((qT, RqT), (kT, RkT))):
                bank = banks[k % len(banks)]
                k += 1

                def g_qk(s_=s_, dst=dst, Rd=Rd, bank=bank, tg=tg):
                    for cc in range(8):
                        S.mm(c.ps[:, bank, :], wsl[b][:, s_, cc, :], xnT3[:, cc, tg * 512:(tg + 1) * 512], cc == 0, cc == 7,
                             [Rw[b]] + c.RxnT[4 * tg:4 * tg + 4], [c.bank[bank]])
                    S.cp("dve", dst[:, tg * 512:(tg + 1) * 512], c.ps[:, bank, :], [c.bank[bank]], [Rd[tg]])
                yield tg, g_qk
            bank = banks[k % len(banks)]
            k += 1
            for j in range(4):
                def g_v(j=j, bank=bank, tg=tg):
                    i = 4 * tg + j
                    for cc in range(8):
                        S.mm(c.ps[:, bank, j * 128:(j + 1) * 128], xnT3[:, cc, i * 128:(i + 1) * 128], wsl[b][:, 2, cc, :],
                             cc == 0, cc == 7, [Rw[b], c.RxnT[i]], [c.bank[bank]])
                    if j == 3:
                        S.cp("dve", v[:, 4 * tg:4 * tg + 4, :], c.ps[:, bank, :].rearrange("p (j f) -> p j f", f=128),
                             [c.bank[bank]], [Rv[tg]])
                yield tg, g_v

    load_w(0)
    load_w(1)
    for _, g_ in proj_groups(0, [0, 1, 2, 3]):
        g_()
    for hp in range(8):
        if 1 <= hp and hp + 1 < 8:
            load_w(hp + 1)
        pending = list(proj_groups(hp + 1, [7])) if hp + 1 < 8 else []
        G = []
        sweep_end = {}
        for qg in reversed(range(NG)):
            for kb in reversed(range(4 * qg + 4)):
                G.append((qg, kb))
            sweep_end[qg] = len(G) - 1
        n_g = len(G)

        def info(gi):
            qg, kb = G[gi]
            r = kb - 4 * qg
            c0 = r * 128 if r >= 0 else 0
            return qg, kb, r, c0, kb == 4 * qg + 3, kb == 0, gcount + gi

        def st_z(gi):
            qg, kb, r, c0, first, last, gid = info(gi)
            zp = 2 * (gid % 2)
            for hd in range(2):
                rows = slice(hd * 64, hd * 64 + 64)
                S.mm(c.ps[:, zp + hd, c0:512], kT[rows, kb * 128:(kb + 1) * 128], qT[rows, qg * 512 + c0:(qg + 1) * 512],
                     True, True, [RkT[kb // 4], RqT[qg]], [c.bank[zp + hd]])

        def st_e(gi):
            qg, kb, r, c0, first, last, gid = info(gi)
            zp = 2 * (gid % 2)
            eb = gid % NE
            S.act(E[eb][:, :, c0:512], c.ps[:, zp:zp + 2, c0:512], AF.Exp, [c.bank[zp], c.bank[zp + 1]], [RE[eb]],
                  scale=0.125)
            if r >= 0:
                S.tt("dve", E[eb][:, :, c0:c0 + 128], E[eb][:, :, c0:c0 + 128], mask2, ALU.mult,
                     [RE[eb], c.Rconst], [RE[eb]])

        def st_l(gi):
            qg, kb, r, c0, first, last, gid = info(gi)
            eb = gid % NE
            lb = gid % NL
            S.act(L[lb][:, :, c0:512], E[eb][:, :, c0:512], AF.Ln, [RE[eb]], [RL[lb]], bias=1.0, scale=1.0)

        def st_cum(gi):
            qg, kb, r, c0, first, last, gid = info(gi)
            lb = gid % NL
            for hd in range(2):
                S.mm(c.ps[:, AB[hd], c0:512], c.tinc, L[lb][:, hd, c0:512], first, False, [RL[lb], c.Rconst],
                     [c.bank[AB[hd]]], skip=True)

        def st_w(gi):
            qg, kb, r, c0, first, last, gid = info(gi)
            wb = gid % NW
            eb = gid % NE
            a_i = gid % NA
            S.act(Wb[wb][:, :, c0:512], c.ps[:, 4:6, c0:512], AF.Exp, [c.bank[4], c.bank[5]], [RW[wb]])
            S.tt("dve", Ab[a_i][:, :, c0:512], E[eb][:, :, c0:512], Wb[wb][:, :, c0:512], ALU.mult,
                 [RE[eb], RW[wb]], [RA[a_i]])

        def st_car(gi):
            qg, kb, r, c0, first, last, gid = info(gi)
            lb = gid % NL
            if not last:
                for hd in range(2):
                    S.mm(c.ps[:, AB[hd], c0:512], c.tcar, L[lb][:, hd, c0:512], False, False, [RL[lb], c.Rconst],
                         [c.bank[AB[hd]]], skip=True)

        def st_av(gi):
            qg, kb, r, c0, first, last, gid = info(gi)
            a_i = gid % NA
            for hd in range(2):
                rows = slice(hd * 64, hd * 64 + 64)
                S.mm(c.ps[rows, OBK, c0:512], v[:, kb, hd * 64:(hd + 1) * 64], Ab[a_i][:, hd, c0:512], first, last,
                     [RA[a_i], Rv[kb // 4]], [c.bank[OBK]], skip=True)
            if last:
                og = c.ogT[:, hp, qg * 512:(qg + 1) * 512]
                S.tt("dve", og, c.ps[:, OBK, :], og, ALU.mult, [c.bank[OBK], c.Rog[hp][qg]], [c.Rog[hp][qg]])

        stages = [(st_car, 4), (st_cum, 3), (st_av, 5), (st_z, 0), (st_w, 3), (st_l, 2), (st_e, 1)]
        for n in range(n_g + 5):
            for fn, d in stages:
                gi = n - d
                if 0 <= gi < n_g:
                    fn(gi)
            if pending and n >= sweep_end[pending[0][0]] + 6:
                pending.pop(0)[1]()
        for _, g_ in pending:
            g_()
        gcount += n_g


def layer_mla(c, w):
    S = c.S
    w_in = w["w_in"]
    gate_phase(c, w_in, 448)
    S.barrier()
    xnT3 = c.xnT[:, :].rearrange("p (c t) -> p c t", c=8)
    wv = w_in.rearrange("(c p) n -> p c n", p=128)
    sm = c.small
    scr = c.scr
    Rgn = Res("mla_g")
    gqa = c.gbc[:, 0:256]
    gkva = c.gbc[:, 256:384]
    gq192 = c.gbc[:, 384:576]
    gk192 = c.gbc[:, 576:768]
    gkpe = c.gbc[:, 704:768]
    S.ld("sp", gqa, w["q_a_norm"].partition_broadcast(128), writes=[Rgn])
    S.ld("sp", gkva, w["kv_a_norm"].partition_broadcast(128), writes=[Rgn])
    S.ld("sp", gq192, w["q_head_norm"].partition_broadcast(128), writes=[Rgn])
    S.ld("sp", gk192, w["k_head_norm"].partition_broadcast(128), writes=[Rgn])
    S.tt("dve", gq192[:, 0:128], gq192[:, 0:128], gk192[:, 0:128], ALU.mult, [Rgn], [Rgn])
    cosT = scr[:, 0:1024].rearrange("p (i f) -> p i f", f=32)
    sinT = scr[:, 1024:2048].rearrange("p (i f) -> p i f", f=32)
    Rtab = Res("ropetab")
    S.ld("sp", scr[:, 0:2048], c.cf_d[:, CF_COS:CF_COS + 2048], writes=[Rtab])
    kpeT = scr[:, 2048:4096].bitcast(BF16)
    qlnT3 = c.hbuf[:, 0:2 * S_LEN].rearrange("p (a t) -> p a t", a=2)
    kvlnT = c.hbuf[:, 2 * S_LEN:3 * S_LEN]
    sskpe = sm[:, 128:160]
    rstdk = sm[:, 160:192]
    Rqln = [Res(f"qln{g}") for g in range(NG)]
    Rkvln = [Res(f"kvln{g}") for g in range(NG)]
    Rkpe = [Res(f"kpe{g}") for g in range(NG)]
    Rsskpe = [Res(f"sskpe{g}") for g in range(NG)]

    def rope(x, out, i, Rx, Rout, t1, t2, Rt):
        cb = cosT[:, i, :].unsqueeze(1).to_broadcast([128, 2, 32])
        S.tt("dve", t1.rearrange("p (a f) -> p a f", a=2), x.rearrange("p (a f) -> p a f", a=2), cb, ALU.mult,
             [Rx, Rtab], [Rt])
        S.tt("dve", t2[:, 0:32], x[:, 32:64], sinT[:, i, :], ALU.mult, [Rx, Rtab], [Rt])
        S.tt("dve", t2[:, 32:64], x[:, 0:32], sinT[:, i, :], ALU.mult, [Rx, Rtab], [Rt])
        S.tt("dve", out[:, 0:32], t1[:, 0:32], t2[:, 0:32], ALU.subtract, [Rt], [Rout])
        S.tt("dve", out[:, 32:64], t1[:, 32:64], t2[:, 32:64], ALU.add, [Rt], [Rout])

    wlat = c.wbuf[:, 0:3584].rearrange("p (c n) -> p c n", c=8)
    Rwlat = Res("wlat")
    S.ld("pool", wlat[:, 0:4, :], wv[:, 0:4, 0:448], writes=[Rwlat])
    S.ld("pool", wlat[:, 4:8, :], wv[:, 4:8, 0:448], writes=[Rwlat])
    o = 4096
    junk = scr[:, o:o + 448]; o += 448
    lnb = [scr[:, o:o + 192].bitcast(BF16), scr[:, o + 192:o + 384].bitcast(BF16)]; o += 384
    kp = scr[:, o:o + 64]; o += 64
    t1 = scr[:, o:o + 64]; o += 64
    t2 = scr[:, o:o + 64]; o += 64
    kr = [scr[:, o:o + 32].bitcast(BF16), scr[:, o + 32:o + 64].bitcast(BF16)]; o += 64
    assert o <= 6144
    Rjunk, Rkp, Rt = Res("junk"), Res("kp"), Res("ropet")
    Rlnb = [Res("lnb0"), Res("lnb1")]
    Rkr = [Res("kr0"), Res("kr1")]
    Rst = [Res("mst0"), Res("mst1")]
    for i in range(NT):
        tg = i // 4
        pb = i % 2
        bank = 6 + pb
        psL = c.ps[:, bank, 0:448]
        for cc in range(8):
            S.mm(psL, xnT3[:, cc, i * 128:(i + 1) * 128], wlat[:, cc, :], cc == 0, cc == 7,
                 [Rwlat, c.RxnT[i]], [c.bank[bank]])
        sb_ = 192 + 16 * pb
        ss2 = sm[:, sb_:sb_ + 2]
        lg2 = sm[:, sb_ + 2:sb_ + 4]
        rs2 = sm[:, sb_ + 4:sb_ + 6]
        S.act(junk[:, 0:256], c.ps[:, bank, 0:256], AF.Square, [c.bank[bank]], [Rjunk, Rst[pb]], accum_out=ss2[:, 0:1])
        S.act(junk[:, 256:384], c.ps[:, bank, 256:384], AF.Square, [c.bank[bank]], [Rjunk, Rst[pb]], accum_out=ss2[:, 1:2])
        S.act(junk[:, 384:448], c.ps[:, bank, 384:448], AF.Square, [c.bank[bank]], [Rjunk, Rsskpe[tg]],
              accum_out=sskpe[:, i:i + 1])
        S.act(lg2[:, 0:1], ss2[:, 0:1], AF.Ln, [Rst[pb]], [Rst[pb]], scale=1.0 / 256, bias=EPS)
        S.act(lg2[:, 1:2], ss2[:, 1:2], AF.Ln, [Rst[pb]], [Rst[pb]], scale=1.0 / 128, bias=EPS)
        S.act(rs2, lg2, AF.Exp, [Rst[pb]], [Rst[pb]], scale=-0.5)
        S.stt(lnb[pb][:, 0:256], c.ps[:, bank, 0:256], rs2[:, 0:1], gqa, ALU.mult, ALU.mult,
              [c.bank[bank], Rst[pb], Rgn], [Rlnb[pb]])
        S.stt(lnb[pb][:, 256:384], c.ps[:, bank, 256:384], rs2[:, 1:2], gkva, ALU.mult, ALU.mult,
              [c.bank[bank], Rst[pb], Rgn], [Rlnb[pb]])
        S.tt("dve", kp, c.ps[:, bank, 384:448], gkpe, ALU.mult, [c.bank[bank], Rgn], [Rkp])
        rope(kp, kr[pb], i, Rkp, Rkr[pb], t1, t2, Rt)
        tbank = 4 + pb
        psT = c.ps[:, tbank, :].bitcast(BF16)
        for a in range(3):
            S.tr(psT[:, a * 128:(a + 1) * 128], lnb[pb][:, a * 128:(a + 1) * 128], c.ident,
                 [Rlnb[pb], c.Rconst], [c.bank[tbank]])
        S.tr(psT[0:64, 384:512], kr[pb], c.ident, [Rkr[pb], c.Rconst], [c.bank[tbank]])
        S.cp("act", qlnT3[:, :, i * 128:(i + 1) * 128], psT[:, 0:256].rearrange("p (a t) -> p a t", a=2),
             [c.bank[tbank]], [Rqln[tg]])
        S.cp("dve", kvlnT[:, i * 128:(i + 1) * 128], psT[:, 256:384], [c.bank[tbank]], [Rkvln[tg]])
        S.cp("dve", kpeT[0:64, i * 128:(i + 1) * 128], psT[0:64, 384:512], [c.bank[tbank]], [Rkpe[tg]])
    S.barrier()
    wuq = c.wbuf[:, 0:3072].rearrange("p (a n) -> p a n", a=2)
    wukv = c.wbuf[:, 3072:5120]
    Rwu = Res("wu")
    S.ld("pool", wuq, w["w_uq"].rearrange("(a p) n -> p a n", p=128), writes=[Rwu])
    S.ld("pool", wukv, w["w_ukv"], writes=[Rwu])
    X = c.xnT
    qTn = X[:, 0:4096]
    qTp = X[:, 4096:8192]
    kTn = X[:, 8192:12288]
    vh = X[:, 12288:16384].rearrange("p (i f) -> p i f", f=128)
    NP = 3
    Pb = [X[:, 16384 + 512 * j:16384 + 512 * (j + 1)] for j in range(NP)]
    NQR = 3
    qr = [X[:, 18432 + 256 * j:18432 + 256 * j + 192] for j in range(NQR)]
    XF = X[:, 20480:24576].bitcast(F32)
    rcb = XF[:, 0:512]
    tbuf = XF[:, 512:1024]
    qpe = [XF[:, 1024 + 64 * j:1088 + 64 * j] for j in range(2)]
    u1 = [XF[:, 1152 + 64 * j:1216 + 64 * j] for j in range(2)]
    u2 = [XF[:, 1280 + 64 * j:1344 + 64 * j] for j in range(2)]
    junk2 = [XF[:, 1408 + 320 * j:1408 + 320 * (j + 1)] for j in range(2)]
    RqTn = [Res(f"qTn{g}") for g in range(NG)]
    RqTp = [Res(f"qTp{g}") for g in range(NG)]
    RkTn = [Res(f"kTn{g}") for g in range(NG)]
    Rvh = [Res(f"vh{g}") for g in range(NG)]
    Rrk = [Res(f"rstdk{g}") for g in range(NG)]
    RPb = [Res(f"mPb{j}") for j in range(NP)]
    Rqr = [Res(f"qr{j}") for j in range(NQR)]
    Rrc, Rtb = Res("rcb"), Res("tbuf")
    Rqpe = [Res("qpe0"), Res("qpe1")]
    Ru = [Res("u0"), Res("u1")]
    Rj2 = [Res("junk20"), Res("junk21")]
    NST = 4
    Rs2 = [Res(f"hst{j}") for j in range(NST)]
    QB = [0, 1, 2, 3, 6, 7]
    gid = 0
    sw = 0
    tcount = 0
    for h in range(8):
        for tg in range(NG):
            bank = 4 + tg % 2
            S.mm(c.ps[:, bank, :], wukv[:, h * 256:h * 256 + 128], kvlnT[:, tg * 512:(tg + 1) * 512], True, True,
                 [Rwu, Rkvln[tg]], [c.bank[bank]])
            S.cp("act", kTn[:, tg * 512:(tg + 1) * 512], c.ps[:, bank, :], [c.bank[bank]], [RkTn[tg]])

        def tinfo(i):
            t = tcount + i
            tg, j = divmod(i, 4)
            sb_ = 224 + 8 * (t % NST)
            return (t, tg, j, QB[t % len(QB)], slice(i * 128, (i + 1) * 128), sm[:, sb_:sb_ + 2], sm[:, sb_ + 2:sb_ + 4],
                    sm[:, sb_ + 4:sb_ + 5], t % NST, t % NQR, t % 2)

        def t_mm(i):
            t, tg, j, bank, tok, ss2, lg2, rsq, si, qi, pi = tinfo(i)
            S.mm(c.ps[:, bank, 0:192], qlnT3[:, 0, tok], wuq[:, 0, h * 192:(h + 1) * 192], True, False,
                 [Rwu, Rqln[tg]], [c.bank[bank]])
            S.mm(c.ps[:, bank, 0:192], qlnT3[:, 1, tok], wuq[:, 1, h * 192:(h + 1) * 192], False, True,
                 [Rwu, Rqln[tg]], [c.bank[bank]])
            S.mm(c.ps[:, bank, 192:448], kvlnT[:, tok], wukv[:, h * 256:(h + 1) * 256], True, True,
                 [Rwu, Rkvln[tg]], [c.bank[bank]])

        def t_sq(i):
            t, tg, j, bank, tok, ss2, lg2, rsq, si, qi, pi = tinfo(i)
            S.act(junk2[pi][:, 0:192], c.ps[:, bank, 0:192], AF.Square, [c.bank[bank]], [Rj2[pi], Rs2[si]],
                  accum_out=ss2[:, 0:1])
            S.act(junk2[pi][:, 192:320], c.ps[:, bank, 192:320], AF.Square, [c.bank[bank]], [Rj2[pi], Rs2[si]],
                  accum_out=ss2[:, 1:2])
            S.cp("act", vh[:, i, :], c.ps[:, bank, 320:448], [c.bank[bank]], [Rvh[tg]])

        def t_add(i):
            t, tg, j, bank, tok, ss2, lg2, rsq, si, qi, pi = tinfo(i)
            S.tt("dve", ss2[:, 1:2], ss2[:, 1:2], sskpe[:, i:i + 1], ALU.add, [Rs2[si], Rsskpe[tg]], [Rs2[si]])

        def t_rs(i):
            t, tg, j, bank, tok, ss2, lg2, rsq, si, qi, pi = tinfo(i)
            S.act(lg2, ss2, AF.Ln, [Rs2[si]], [Rs2[si]], scale=1.0 / 192, bias=EPS)
            S.act(rsq, lg2[:, 0:1], AF.Exp, [Rs2[si]], [Rs2[si]], scale=-0.5)
            S.act(rstdk[:, i:i + 1], lg2[:, 1:2], AF.Exp, [Rs2[si]], [Rrk[tg]], scale=-0.5, bias=-0.5 * math.log(192.0))

        def t_qn(i):
            t, tg, j, bank, tok, ss2, lg2, rsq, si, qi, pi = tinfo(i)
            S.stt(qr[qi][:, 0:128], c.ps[:, bank, 0:128], rsq, gq192[:, 0:128], ALU.mult, ALU.mult,
                  [c.bank[bank], Rs2[si], Rgn], [Rqr[qi]])
            S.stt(qpe[pi], c.ps[:, bank, 128:192], rsq, gq192[:, 128:192], ALU.mult, ALU.mult,
                  [c.bank[bank], Rs2[si], Rgn], [Rqpe[pi]])
            rope(qpe[pi], qr[qi][:, 128:192], i, Rqpe[pi], Rqr[qi], u1[pi], u2[pi], Ru[pi])

        def t_tr(i):
            t, tg, j, bank, tok, ss2, lg2, rsq, si, qi, pi = tinfo(i)
            tbank = 4 + tg % 2
            psT = c.ps[:, tbank, :].bitcast(BF16)
            S.tr(psT[:, j * 128:(j + 1) * 128], qr[qi][:, 0:128], c.ident, [Rqr[qi], c.Rconst], [c.bank[tbank]])
            S.tr(psT[0:64, 512 + j * 128:512 + (j + 1) * 128], qr[qi][:, 128:192], c.ident,
                 [Rqr[qi], c.Rconst], [c.bank[tbank]])
            if j == 3:
                S.cp("dve", qTn[:, tg * 512:(tg + 1) * 512], psT[:, 0:512], [c.bank[tbank]], [RqTn[tg]])
                S.cp("dve", qTp[0:64, tg * 512:(tg + 1) * 512], psT[0:64, 512:1024], [c.bank[tbank]], [RqTp[tg]])

        tstages = [(t_tr, 5), (t_qn, 4), (t_rs, 3), (t_add, 2), (t_sq, 1), (t_mm, 0)]
        for n in range(NT + 5):
            for fn, d in tstages:
                i = n - d
                if 0 <= i < NT:
                    fn(i)
        tcount += NT
        G = []
        for qg in range(NG):
            for kb in reversed(range(4 * qg + 4)):
                G.append((qg, kb))
        n_g = len(G)

        def info(gi):
            qg, kb = G[gi]
            r = kb - 4 * qg
            c0 = r * 128 if r >= 0 else 0
            return qg, kb, r, c0, kb == 4 * qg + 3, kb == 0, gid + gi, sw + qg

        def st_z(gi):
            qg, kb, r, c0, first, last, g_, sw_ = info(gi)
            zb = g_ % 4
            S.mm(c.ps[:, zb, c0:512], kTn[:, kb * 128:(kb + 1) * 128], qTn[:, qg * 512 + c0:(qg + 1) * 512], True, False,
                 [RkTn[kb // 4], RqTn[qg]], [c.bank[zb]])
            S.mm(c.ps[:, zb, c0:512], kpeT[0:64, kb * 128:(kb + 1) * 128], qTp[0:64, qg * 512 + c0:(qg + 1) * 512],
                 False, True, [Rkpe[kb // 4], RqTp[qg]], [c.bank[zb]])

        def st_e(gi):
            qg, kb, r, c0, first, last, g_, sw_ = info(gi)
            zb = g_ % 4
            pi = g_ % NP
            S.act(Pb[pi][:, c0:512], c.ps[:, zb, c0:512], AF.Exp, [c.bank[zb], Rrk[kb // 4]], [RPb[pi]],
                  scale=rstdk[:, kb:kb + 1])
            if r >= 0:
                S.tt("dve", Pb[pi][:, c0:c0 + 128], Pb[pi][:, c0:c0 + 128], c.mincl, ALU.mult,
                     [RPb[pi], c.Rconst], [RPb[pi]])

        def st_av(gi):
            qg, kb, r, c0, first, last, g_, sw_ = info(gi)
            pi = g_ % NP
            ob = 4 + sw_ % 2
            db = 6 + sw_ % 2
            S.mm(c.ps[:, ob, c0:512], vh[:, kb, :], Pb[pi][:, c0:512], first, last, [Rvh[kb // 4], RPb[pi]],
                 [c.bank[ob]], skip=True)
            S.mm(c.ps[:, db, c0:512], c.ones, Pb[pi][:, c0:512], first, last, [c.Rconst, RPb[pi]],
                 [c.bank[db]], skip=True)
            if last:
                S.op("dve", lambda e, o_=rcb, i_=c.ps[:, db, :]: e.reciprocal(out=o_, in_=i_), [c.bank[db]], [Rrc])
                S.tt("dve", tbuf, c.ps[:, ob, :], rcb, ALU.mult, [c.bank[ob], Rrc], [Rtb])
                og = c.ogT[:, h, qg * 512:(qg + 1) * 512]
                S.tt("dve", og, tbuf, og, ALU.mult, [Rtb, c.Rog[h][qg]], [c.Rog[h][qg]])

        stages = [(st_z, 0), (st_e, 1), (st_av, 2)]
        for n in range(n_g + 2):
            for fn, d in reversed(stages):
                gi = n - d
                if 0 <= gi < n_g:
                    fn(gi)
        gid += n_g
        sw += NG


def layer_swa(c, w):
    S = c.S
    w_in = w["w_in"]
    gate_phase(c, w_in, 1536)
    S.barrier()
    xnT3 = c.xnT[:, :].rearrange("p (c t) -> p c t", c=8)
    wv = w_in.rearrange("(c p) n -> p c n", p=128)
    qT3 = c.hbuf[:, 0:2 * S_LEN].rearrange("p (a t) -> p a t", a=2)
    kT2 = c.hbuf[:, 2 * S_LEN:3 * S_LEN]
    scr = c.scr
    off = 0
    vg = scr[:, off:off + 1024].bitcast(BF16).rearrange("p (i f) -> p i f", f=64); off += 1024
    junk = [scr[:, off + 320 * j:off + 320 * (j + 1)] for j in range(2)]; off += 640
    tmpq = [scr[:, off + 256 * j:off + 256 * (j + 1)] for j in range(2)]; off += 512
    NQN = 3
    qn = [scr[:, off + 128 * j:off + 128 * (j + 1)].bitcast(BF16) for j in range(NQN)]; off += 128 * NQN
    biasg = scr[:, off:off + 1024]; off += 1024
    Tb = scr[:, off:off + 1024]; off += 1024
    Pb = [scr[:, off:off + 512].bitcast(BF16), scr[:, off + 512:off + 1024].bitcast(BF16)]; off += 1024
    dnb = scr[:, off:off + 256]; off += 256
    gqk4 = scr[:, off:off + 256]; off += 256
    assert off <= 6144, off
    sm = c.small
    kscale = sm[:, 16:48]
    es16 = sm[:, 48:64]
    Rbias, RTb, Rdn, Rgqk, Res16 = (Res(n) for n in ("biasg", "Tb", "dnb", "gqk", "es16"))
    Rjunk = [Res("junk0"), Res("junk1")]
    Rtmpq = [Res("tmpq0"), Res("tmpq1")]
    Rqn = [Res(f"qn{j}") for j in range(NQN)]
    RPb = [Res("Pb0"), Res("Pb1")]
    NST = 4
    Rst = [Res(f"sst{j}") for j in range(NST)]
    RqT = [Res(f"qT{g}") for g in range(NG)]
    RkT = [Res(f"kT{g}") for g in range(NG)]
    Rvg = [Res(f"vg{g}") for g in range(NG)]
    Rks = [Res(f"ks{g}") for g in range(NG)]
    wtm = [c.wbuf[:, 4096 * b:4096 * b + 3072].rearrange("p (c n) -> p c n", c=8) for b in range(2)]
    wk2 = [c.wbuf[:, 4096 * b + 3072:4096 * (b + 1)].rearrange("p (c n) -> p c n", c=8) for b in range(2)]
    Rw = [Res("swaw0"), Res("swaw1")]
    gk4 = junk[0][:, 0:256]
    for j in range(4):
        S.ld("sp", gqk4[:, j * 64:(j + 1) * 64], w["q_head_norm"].partition_broadcast(128), writes=[Rgqk])
        S.ld("sp", gk4[:, j * 64:(j + 1) * 64], w["k_head_norm"].partition_broadcast(128), writes=[Rjunk[0]])
    S.tt("dve", gqk4, gqk4, gk4, ALU.mult, [Rjunk[0], Rgqk], [Rgqk])
    S.ld("sp", es16, w["sinks"].partition_broadcast(128), writes=[Res16])
    S.act(es16, es16, AF.Exp, [Res16], [Res16])

    def load_w(g):
        b = g % 2
        S.ld("pool", wtm[b][:, :, 0:256], wv[:, :, g * 256:(g + 1) * 256], writes=[Rw[b]])
        S.ld("pool", wtm[b][:, :, 256:320], wv[:, :, 1024 + g * 64:1024 + (g + 1) * 64], writes=[Rw[b]])
        S.ld("pool", wtm[b][:, :, 320:384], wv[:, :, 1280 + g * 64:1280 + (g + 1) * 64], writes=[Rw[b]])
        for d in range(2):
            S.ld("pool", wk2[b][:, :, d * 64:(d + 1) * 64], wv[:, :, 1024 + g * 64:1024 + (g + 1) * 64], writes=[Rw[b]])

    load_w(0)
    TB = [0, 1, 2, 3, 6, 7]
    tcount = 0
    acount = 0
    for g in range(4):
        b = g % 2
        if g + 1 < 4:
            load_w(g + 1)
        for kbi in range(2):
            S.ld("sp", biasg[:, kbi * 512:(kbi + 1) * 512],
                 c.cf_d[:, CF_BIAS + kbi * 2048 + g * 512:CF_BIAS + kbi * 2048 + (g + 1) * 512], writes=[Rbias])
        for tg in range(NG):
            bank = 4 + tg % 2
            for cc in range(8):
                S.mm(c.ps[:, bank, :], wk2[b][:, cc, :], xnT3[:, cc, tg * 512:(tg + 1) * 512], cc == 0, cc == 7,
                     [Rw[b]] + c.RxnT[4 * tg:4 * tg + 4], [c.bank[bank]])
            S.cp("act", kT2[:, tg * 512:(tg + 1) * 512], c.ps[:, bank, :], [c.bank[bank]], [RkT[tg]])

        def tinfo(i):
            t = tcount + i
            tg, j = divmod(i, 4)
            sb_ = 64 + 16 * (t % NST)
            return (t, tg, j, TB[t % len(TB)], sm[:, sb_:sb_ + 5], sm[:, sb_ + 5:sb_ + 10], sm[:, sb_ + 10:sb_ + 14],
                    t % NST, t % 2, t % NQN)

        def t_mm(i):
            t, tg, j, bank, ss5, lg5, rs4, si, pi, qi = tinfo(i)
            for cc in range(8):
                S.mm(c.ps[:, bank, 0:384], xnT3[:, cc, i * 128:(i + 1) * 128], wtm[b][:, cc, :], cc == 0, cc == 7,
                     [Rw[b], c.RxnT[i]], [c.bank[bank]])

        def t_sq(i):
            t, tg, j, bank, ss5, lg5, rs4, si, pi, qi = tinfo(i)
            S.act(junk[pi], c.ps[:, bank, 0:320], AF.Square, [c.bank[bank]], [Rjunk[pi]])
            S.cp("act", vg[:, i, :], c.ps[:, bank, 320:384], [c.bank[bank]], [Rvg[tg]])

        def t_red(i):
            t, tg, j, bank, ss5, lg5, rs4, si, pi, qi = tinfo(i)
            S.op("dve", lambda e, o=ss5, i_=junk[pi].rearrange("p (h f) -> p h f", f=64): e.reduce_sum(out=o, in_=i_, axis=AX.X),
                 [Rjunk[pi]], [Rst[si]])

        def t_rs(i):
            t, tg, j, bank, ss5, lg5, rs4, si, pi, qi = tinfo(i)
            S.act(lg5, ss5, AF.Ln, [Rst[si]], [Rst[si]], scale=1.0 / 64, bias=EPS)
            S.act(rs4, lg5[:, 0:4], AF.Exp, [Rst[si]], [Rst[si]], scale=-0.5)
            S.act(kscale[:, i:i + 1], lg5[:, 4:5], AF.Exp, [Rst[si]], [Rks[tg]], scale=-0.5, bias=math.log(0.125))

        def t_qn(i):
            t, tg, j, bank, ss5, lg5, rs4, si, pi, qi = tinfo(i)
            S.tt("dve", tmpq[pi].rearrange("p (h f) -> p h f", f=64),
                 c.ps[:, bank, 0:256].rearrange("p (h f) -> p h f", f=64),
                 rs4.unsqueeze(2).to_broadcast([128, 4, 64]), ALU.mult, [c.bank[bank], Rst[si]], [Rtmpq[pi]])
            S.tt("dve", qn[qi], tmpq[pi], gqk4, ALU.mult, [Rtmpq[pi], Rgqk], [Rqn[qi]])

        def t_tr(i):
            t, tg, j, bank, ss5, lg5, rs4, si, pi, qi = tinfo(i)
            tbank = 4 + tg % 2
            psT = c.ps[:, tbank, :].bitcast(BF16)
            for a in range(2):
                S.tr(psT[:, a * 512 + j * 128:a * 512 + (j + 1) * 128], qn[qi][:, a * 128:(a + 1) * 128], c.ident,
                     [Rqn[qi], c.Rconst], [c.bank[tbank]])
            if j == 3:
                S.cp("dve", qT3[:, :, tg * 512:(tg + 1) * 512], psT.rearrange("p (a t) -> p a t", a=2),
                     [c.bank[tbank]], [RqT[tg]])

        tstages = [(t_tr, 5), (t_qn, 4), (t_rs, 3), (t_red, 2), (t_sq, 1), (t_mm, 0)]
        for n in range(NT + 5):
            for fn, d in tstages:
                i = n - d
                if 0 <= i < NT:
                    fn(i)
        tcount += NT

        def ainfo(qb):
            a_ = acount + qb
            kbs = [(0, qb - 1), (1, qb)] if qb > 0 else [(1, qb)]
            return a_, kbs, 2 * (a_ % 2), a_ % 2, 4 + a_ % 2

        def a_qk(qb):
            a_, kbs, zb0, pbi, ob = ainfo(qb)
            for kbi, kb in kbs:
                for hq in range(4):
                    p = hq % 2
                    rows = slice(p * 64, p * 64 + 64)
                    col = (kbi * 2 + hq // 2) * 128
                    S.mm(c.ps[:, zb0 + p, col:col + 128], kT2[rows, kb * 128:(kb + 1) * 128],
                         qT3[rows, hq // 2, qb * 128:(qb + 1) * 128], True, True,
                         [RkT[kb // 4], RqT[qb // 4]], [c.bank[zb0 + p]])

        def a_bias(qb):
            a_, kbs, zb0, pbi, ob = ainfo(qb)
            for kbi, kb in kbs:
                for p in range(2):
                    src = c.ps[:, zb0 + p, kbi * 256:(kbi + 1) * 256].rearrange("q (a t) -> q a t", a=2)
                    dst = Tb[:, kbi * 512:(kbi + 1) * 512].rearrange("q (a p t) -> q p a t", a=2, p=2)[:, p]
                    bia = biasg[:, kbi * 512:(kbi + 1) * 512].rearrange("q (a p t) -> q p a t", a=2, p=2)[:, p]
                    S.stt(dst, src, kscale[:, kb:kb + 1], bia, ALU.mult, ALU.add,
                          [c.bank[zb0 + p], Rks[kb // 4], Rbias], [RTb])

        def a_exp(qb):
            a_, kbs, zb0, pbi, ob = ainfo(qb)
            lo = 0 if qb > 0 else 512
            S.act(Pb[pbi][:, lo:1024], Tb[:, lo:1024], AF.Exp, [RTb], [RPb[pbi]])

        def a_av(qb):
            a_, kbs, zb0, pbi, ob = ainfo(qb)
            for hq in range(4):
                rows = slice((hq % 2) * 64, (hq % 2) * 64 + 64)
                col = (hq // 2) * 128
                for n_, (kbi, kb) in enumerate(kbs):
                    S.mm(c.ps[rows, ob, col:col + 128], vg[:, kb, :], Pb[pbi][:, (kbi * 4 + hq) * 128:(kbi * 4 + hq + 1) * 128],
                         n_ == 0, n_ == len(kbs) - 1, [Rvg[kb // 4], RPb[pbi]], [c.bank[ob]])
                for n_, (kbi, kb) in enumerate(kbs):
                    S.mm(c.ps[rows, ob, 256 + col:256 + col + 128], c.ones[:, 0:64],
                         Pb[pbi][:, (kbi * 4 + hq) * 128:(kbi * 4 + hq + 1) * 128],
                         n_ == 0, n_ == len(kbs) - 1, [c.Rconst, RPb[pbi]], [c.bank[ob]])

        def a_out(qb):
            a_, kbs, zb0, pbi, ob = ainfo(qb)
            for hq in range(4):
                rows = slice((hq % 2) * 64, (hq % 2) * 64 + 64)
                col = (hq // 2) * 128
                h = 4 * g + hq
                S.ts("dve", dnb[rows, col:col + 128], c.ps[rows, ob, 256 + col:256 + col + 128], es16[rows, h:h + 1],
                     None, ALU.add, None, [c.bank[ob], Res16], [Rdn])
            S.act(dnb, dnb, AF.Ln, [Rdn], [Rdn])
            S.act(dnb, dnb, AF.Exp, [Rdn], [Rdn], scale=-1.0)
            S.tt("dve", dnb, c.ps[:, ob, 0:256], dnb, ALU.mult, [c.bank[ob], Rdn], [Rdn])
            og = c.ogT[:, 2 * g:2 * g + 2, qb * 128:(qb + 1) * 128]
            Rogs = [c.Rog[2 * g][qb // 4], c.Rog[2 * g + 1][qb // 4]]
            S.tt("dve", og, dnb.rearrange("p (a t) -> p a t", a=2), og, ALU.mult, [Rdn] + Rogs, Rogs)

        astages = [(a_out, 4), (a_av, 3), (a_exp, 2), (a_bias, 1), (a_qk, 0)]
        for n in range(NT + 4):
            for fn, d in astages:
                qb = n - d
                if 0 <= qb < NT:
                    fn(qb)
        acount += NT


LAUNCH_GROUPS = [[0, 1, 2, 3]]
_CONSTS = None


def run_layers(layers, xs, inputs):
    global _CONSTS
    if _CONSTS is None:
        _CONSTS = make_consts()
    cbf, cf = _CONSTS
    nc = build_program(layers)
    names = [n for li in layers for n in LAYER_WEIGHTS[li]]
    in_maps = []
    for b in range(len(xs)):
        m = {"x": np.ascontiguousarray(xs[b], dtype=np.float32), "cbf": cbf, "cf": cf}
        for n in names:
            m[n] = np.ascontiguousarray(inputs[n], dtype=np.float32)
        in_maps.append(m)
    res = run_bass_kernel_spmd(nc, in_maps, core_ids=list(range(len(xs))))
    return [r["y"] for r in res.results]


def kernel(**inputs):
    x = np.asarray(inputs["x"])
    xs = [x[b] for b in range(x.shape[0])]
    for grp in LAUNCH_GROUPS:
        xs = run_layers(grp, xs, inputs)
    return np.stack(xs, axis=0).astype(np.float32)
```

```python
import math
import numpy as np
import ml_dtypes
import concourse.bass as bass
import concourse.mybir as mybir
from concourse.bass_utils import run_bass_kernel_spmd

F32 = mybir.dt.float32
BF16 = mybir.dt.bfloat16
AF = mybir.ActivationFunctionType
ALU = mybir.AluOpType
AX = mybir.AxisListType

S_LEN = 4096
D = 1024
NT = S_LEN // 128
NG = S_LEN // 512
EPS = 1e-6

LAYER_KIND = ["sb", "mla", "swa", "sb"]
LAYER_WEIGHTS = [
    ["l0_norm", "l0_w_in", "l0_w_out"],
    ["l1_norm", "l1_w_in", "l1_q_a_norm", "l1_w_uq", "l1_kv_a_norm", "l1_w_ukv",
     "l1_q_head_norm", "l1_k_head_norm", "l1_w_out"],
    ["l2_norm", "l2_w_in", "l2_q_head_norm", "l2_k_head_norm", "l2_sinks", "l2_w_out"],
    ["l3_norm", "l3_w_in", "l3_w_out"],
]
WSHAPES = {
    "l0_norm": [1024], "l0_w_in": [1024, 4096], "l0_w_out": [1024, 1024],
    "l1_norm": [1024], "l1_w_in": [1024, 1472], "l1_q_a_norm": [256], "l1_w_uq": [256, 1536],
    "l1_kv_a_norm": [128], "l1_w_ukv": [128, 2048], "l1_q_head_norm": [192], "l1_k_head_norm": [192],
    "l1_w_out": [1024, 1024],
    "l2_norm": [1024], "l2_w_in": [1024, 2560], "l2_q_head_norm": [64], "l2_k_head_norm": [64],
    "l2_sinks": [16], "l2_w_out": [1024, 1024],
    "l3_norm": [1024], "l3_w_in": [1024, 4096], "l3_w_out": [1024, 1024],
}


class Res:
    __slots__ = ("name", "w", "r", "excl")

    def __init__(self, name, excl=False):
        self.name = name
        self.w = None
        self.r = {}
        self.excl = excl


class Sched:
    ENGS = ("pe", "act", "dve", "pool", "sp")

    def __init__(self, sems, dma_pools):
        self.sems = sems
        self.cnt = {k: 0 for k in sems}
        self.ops = {e: [] for e in self.ENGS}
        self.seen = {e: {} for e in self.ENGS}
        self.dma_pools = dma_pools
        self.dma_rr = {q: 0 for q in dma_pools}
        self.nwaits = 0

    def _wait(self, e, key, val):
        if val <= 0 or val <= self.seen[e].get(key, 0):
            return
        self.seen[e][key] = val
        sem = self.sems[key]
        self.ops[e].append(lambda eng, sem=sem, val=val: eng.wait_ge(sem, val))
        self.nwaits += 1

    def _sync(self, e, reads, writes, dma):
        rd = [r for r in reads if not r.excl]
        wr = list(writes) + [r for r in reads if r.excl]
        need = []
        for r in rd:
            if r.w is not None:
                need.append((r.w, True))
        for r in wr:
            if r.w is not None:
                need.append((r.w, False))
            for ev in r.r.values():
                need.append((ev, False))
        for (key, val, eng), raw in need:
            if eng == e and not dma:
                if e == "pe":
                    continue
            self._wait(e, key, val)
        return rd, wr

    def _update(self, rd, wr, ev):
        for r in rd:
            r.r[ev[0]] = ev
        for r in wr:
            r.w = ev
            r.r = {}

    def op(self, e, fn, reads=(), writes=()):
        rd, wr = self._sync(e, reads, writes, False)
        self.cnt[e] += 1
        sem = self.sems[e]
        self.ops[e].append(lambda eng, fn=fn, sem=sem: fn(eng).then_inc(sem, 1))
        self._update(rd, wr, (e, self.cnt[e], e))

    def dma(self, q, fn, reads=(), writes=()):
        pool = self.dma_pools[q]
        k = pool[self.dma_rr[q] % len(pool)]
        self.dma_rr[q] += 1
        self._wait(q, k, self.cnt[k])
        rd, wr = self._sync(q, reads, writes, True)
        self.cnt[k] += 16
        sem = self.sems[k]
        self.ops[q].append(lambda eng, fn=fn, sem=sem: fn(eng).then_inc(sem, 16))
        self._update(rd, wr, (k, self.cnt[k], None))

    def barrier(self, engines=None):
        for e in (engines or self.ENGS):
            for k, v in self.cnt.items():
                if k != e:
                    self._wait(e, k, v)


    def mm(self, out, lhsT, rhs, start, stop, reads, writes, skip=False):
        if skip:
            self.op("pe", lambda e: e.matmul(out, lhsT=lhsT, rhs=rhs, start=start, stop=stop, skip_group_check=True),
                    reads, writes)
        else:
            self.op("pe", lambda e: e.matmul(out, lhsT=lhsT, rhs=rhs, start=start, stop=stop), reads, writes)

    def tr(self, out, in_, ident, reads, writes):
        self.op("pe", lambda e: e.transpose(out=out, in_=in_, identity=ident), reads, writes)

    def act(self, out, in_, func, reads, writes, **kw):
        self.op("act", lambda e: e.activation(out=out, in_=in_, func=func, **kw), reads, writes)

    def tt(self, eng, out, in0, in1, op, reads, writes):
        self.op(eng, lambda e: e.tensor_tensor(out=out, in0=in0, in1=in1, op=op), reads, writes)

    def stt(self, out, in0, scalar, in1, op0, op1, reads, writes):
        self.op("dve", lambda e: e.scalar_tensor_tensor(out=out, in0=in0, scalar=scalar, in1=in1, op0=op0, op1=op1),
                reads, writes)

    def ts(self, eng, out, in0, s1, s2, op0, op1, reads, writes):
        if op1 is None:
            self.op(eng, lambda e: e.tensor_scalar(out=out, in0=in0, scalar1=s1, scalar2=None, op0=op0), reads, writes)
        else:
            self.op(eng, lambda e: e.tensor_scalar(out=out, in0=in0, scalar1=s1, scalar2=s2, op0=op0, op1=op1),
                    reads, writes)

    def cp(self, eng, out, in_, reads, writes):
        if eng == "act":
            self.op("act", lambda e: e.copy(out=out, in_=in_), reads, writes)
        else:
            self.op(eng, lambda e: e.tensor_copy(out=out, in_=in_), reads, writes)

    def ld(self, q, out, in_, reads=(), writes=()):
        self.dma(q, lambda e: e.dma_start(out=out, in_=in_), reads, writes)

    def emit(self, block):
        def mk(e):
            def body(eng):
                for f in self.ops[e]:
                    f(eng)
            return body
        block.tensor(mk("pe"))
        block.scalar(mk("act"))
        block.vector(mk("dve"))
        block.gpsimd(mk("pool"))
        block.sync(mk("sp"))


def make_consts():
    j = np.arange(128)[:, None]
    s = np.arange(128)[None, :]
    ident = (j == s).astype(np.float32)
    tinc = -(j >= s).astype(np.float32)
    tcar = -(j < s).astype(np.float32)
    ones = np.ones((128, 128), np.float32)
    cbf = np.concatenate([ident, tinc, tcar, ones], axis=1)
    mstrict = (s > j).astype(np.float32)
    mincl = (s >= j).astype(np.float32)
    half = 32
    inv_freq = (10000.0 ** (-np.arange(half, dtype=np.float32) / half)).astype(np.float32)
    pos = np.arange(S_LEN, dtype=np.float32)
    ang = (pos[:, None] * inv_freq[None, :]).astype(np.float32)
    cos = np.cos(ang).astype(np.float32)
    sin = np.sin(ang).astype(np.float32)
    cosT = cos.reshape(NT, 128, 32).transpose(1, 0, 2).reshape(128, NT * 32)
    sinT = sin.reshape(NT, 128, 32).transpose(1, 0, 2).reshape(128, NT * 32)
    slopes = (2.0 ** (-8.0 * np.arange(1, 17, dtype=np.float32) / 16)).astype(np.float32)
    NEG = -30000.0
    bias = np.zeros((128, 2, 16, 128), np.float32)
    for kbi in range(2):
        rel = (s + 128 - (j + 128 * kbi)).astype(np.float32)
        valid = (rel >= 0) & (rel < 128)
        for h in range(16):
            bias[:, kbi, h, :] = np.where(valid, -slopes[h] * rel, NEG)
    cf = np.concatenate([mstrict, mincl, cosT, sinT, bias.reshape(128, -1)], axis=1).astype(np.float32)
    return cbf.astype(np.float32), cf


CF_MSTRICT = 0
CF_MINCL = 128
CF_COS = 256
CF_SIN = 256 + NT * 32
CF_BIAS = 256 + 2 * NT * 32
CF_TOTAL = CF_BIAS + 2 * 16 * 128


class Ctx:
    pass


def build_program(layers, dbg=None):
    nc = bass.Bass("TRN2", target_bir_lowering=False)
    x_in = nc.dram_tensor("x", [S_LEN, D], F32, kind="ExternalInput").ap()
    y_out = nc.dram_tensor("y", [S_LEN, D], F32, kind="ExternalOutput").ap()
    cbf_d = nc.dram_tensor("cbf", [128, 512], F32, kind="ExternalInput").ap()
    cf_d = nc.dram_tensor("cf", [128, CF_TOTAL], F32, kind="ExternalInput").ap()
    W = {}
    for li in layers:
        for n in LAYER_WEIGHTS[li]:
            W[n] = nc.dram_tensor(n, WSHAPES[n], F32, kind="ExternalInput").ap()
    xs = [x_in]
    for i in range(len(layers) - 1):
        xs.append(nc.dram_tensor(f"xmid{i}", [S_LEN, D], F32, kind="Internal").ap())
    xs.append(y_out)

    from contextlib import ExitStack
    with ExitStack() as st:
        def sb(name, shape, dt):
            return st.enter_context(nc.sbuf_tensor(name, shape, dt))
        c = Ctx()
        c.nc = nc
        c.xnT = sb("xnT", [128, 8 * S_LEN], BF16)
        c.ogT = sb("ogT", [128, 8, S_LEN], BF16)
        c.hbuf = sb("hbuf", [128, 3 * S_LEN], BF16)
        c.wbuf = sb("wbuf", [128, 8192], BF16)
        c.scr = sb("scr", [128, 6144], F32)
        c.gbc = sb("gbc", [128, 1024], F32)
        c.cbf = sb("cbfs", [128, 512], BF16)
        c.cmask = sb("cmask", [128, 256], F32)
        c.small = sb("small", [128, 512], F32)
        c.ps = st.enter_context(nc.psum_tensor("ps", [128, 8, 512], F32))
        sem_names = list(Sched.ENGS) + [f"d{i}" for i in range(20)]
        sems = {k: st.enter_context(nc.semaphore(f"s_{k}")) for k in sem_names}
        S = Sched(sems, {"sp": [f"d{i}" for i in range(0, 8)],
                         "pool": [f"d{i}" for i in range(8, 16)],
                         "act": [f"d{i}" for i in range(16, 20)]})
        c.S = S
        c.W = W
        c.cf_d = cf_d
        c.dbg = dbg
        c.bank = [Res(f"bank{b}", excl=True) for b in range(8)]
        c.ident = c.cbf[:, 0:128]
        c.tinc = c.cbf[:, 128:256]
        c.tcar = c.cbf[:, 256:384]
        c.ones = c.cbf[:, 384:512]
        c.mstrict = c.cmask[:, 0:128]
        c.mincl = c.cmask[:, 128:256]
        c.Rconst = Res("const")
        S.ld("pool", c.cbf[:], cbf_d[:, :], writes=[c.Rconst])
        S.ld("sp", c.cmask[:], cf_d[:, 0:256], writes=[c.Rconst])
        S.barrier()

        block = st.enter_context(nc.Block())
        for idx, li in enumerate(layers):
            kind = LAYER_KIND[li]
            pre = f"l{li}_"
            if idx == 0:
                phase_norm(c, xs[idx], W[pre + "norm"])
                S.barrier()
            if kind == "sb":
                layer_sb(c, W[pre + "w_in"])
            elif kind == "mla":
                layer_mla(c, {k[3:]: v for k, v in W.items() if k.startswith(pre)})
            else:
                layer_swa(c, {k[3:]: v for k, v in W.items() if k.startswith(pre)})
            S.barrier()
            nxt = W[f"l{layers[idx + 1]}_norm"] if idx + 1 < len(layers) else None
            phase_out(c, xs[idx], xs[idx + 1], W[pre + "w_out"], nxt)
            S.barrier()
        S.emit(block)
    return nc


def norm_stages(c, Rg, src_of, Rsrc_of, tag):
    S = c.S
    xnT3 = c.xnT[:, :].rearrange("p (c t) -> p c t", c=8)
    junk = c.scr[:, 0:1024]
    Rjunk = Res("junk" + tag)
    xn = [c.scr[:, 1024 + 512 * j:1536 + 512 * j].bitcast(BF16) for j in range(2)]
    Rxn = [Res(f"xn{j}" + tag) for j in range(2)]
    NSTAT = 4
    Rst = [Res(f"nst{j}" + tag) for j in range(NSTAT)]
    c.RxnT = [Res(f"xnT{i}" + tag) for i in range(NT)]

    def cols(i):
        k = i % NSTAT
        return c.small[:, 4 * k:4 * k + 1], c.small[:, 4 * k + 1:4 * k + 2], c.small[:, 4 * k + 2:4 * k + 3], k

    def n_stat(i):
        ss, lg, rs, k = cols(i)
        S.act(junk, src_of(i), AF.Square, [Rsrc_of(i)], [Rjunk, Rst[k]], accum_out=ss)
        S.act(lg, ss, AF.Ln, [Rst[k]], [Rst[k]], scale=1.0 / D, bias=EPS)
        S.act(rs, lg, AF.Exp, [Rst[k]], [Rst[k]], scale=-0.5)

    def n_xn(i):
        ss, lg, rs, k = cols(i)
        b = i % 2
        S.stt(xn[b], src_of(i), rs, c.gbc[:], ALU.mult, ALU.mult, [Rsrc_of(i), Rst[k], Rg], [Rxn[b]])

    def n_tr(i):
        b = i % 2
        bank = 4 + b
        psT = c.ps[:, bank, :].bitcast(BF16)
        for ch in range(8):
            S.tr(psT[:, ch * 128:(ch + 1) * 128], xn[b][:, ch * 128:(ch + 1) * 128], c.ident,
                 [Rxn[b], c.Rconst], [c.bank[bank]])
        dst = xnT3[:, :, i * 128:(i + 1) * 128]
        src = psT.rearrange("p (c t) -> p c t", c=8)
        S.cp("act" if b == 0 else "dve", dst, src, [c.bank[bank]], [c.RxnT[i]])

    return n_stat, n_xn, n_tr


def phase_norm(c, x_d, g_d):
    S = c.S
    Rg = Res("gbc")
    S.ld("sp", c.gbc[:], g_d.partition_broadcast(128), writes=[Rg])
    hb = c.hbuf[:, :].bitcast(F32)
    NX = 4
    xin = [hb[:, 1024 * j:1024 * (j + 1)] for j in range(NX)]
    Rxin = [Res(f"xin{j}") for j in range(NX)]
    n_stat, n_xn, n_tr = norm_stages(c, Rg, lambda i: xin[i % NX], lambda i: Rxin[i % NX], "A")

    def n_ld(i):
        S.ld("sp", xin[i % NX], x_d[i * 128:(i + 1) * 128, :], writes=[Rxin[i % NX]])

    stages = [(n_tr, 4), (n_xn, 3), (n_stat, 2), (n_ld, 0)]
    for n in range(NT + 4):
        for fn, d in stages:
            i = n - d
            if 0 <= i < NT:
                fn(i)


def phase_out(c, x_d, y_d, wout_d, next_g=None):
    S = c.S
    wo = c.wbuf[:, :].rearrange("p (c n) -> p c n", c=8)
    Rwo = Res("wo")
    wv = wout_d.rearrange("(c p) n -> p c n", p=128)
    for h in range(2):
        S.ld("pool", wo[:, 4 * h:4 * h + 4, :], wv[:, 4 * h:4 * h + 4, :], writes=[Rwo])
    Rg = Res("gbcC")
    if next_g is not None:
        S.ld("sp", c.gbc[:], next_g.partition_broadcast(128), writes=[Rg])
    hb = c.hbuf[:, :].bitcast(F32)
    NX = 3
    xin = [hb[:, 1024 * j:1024 * (j + 1)] for j in range(NX)]
    yo = [hb[:, 3072 + 1024 * j:3072 + 1024 * (j + 1)] for j in range(NX)]
    Rxin = [Res(f"cxin{j}") for j in range(NX)]
    Ryo = [Res(f"yo{j}") for j in range(NX)]
    Rog = c.Rog

    def c_ld(i):
        S.ld("sp", xin[i % NX], x_d[i * 128:(i + 1) * 128, :], writes=[Rxin[i % NX]])

    def c_mm(i):
        for h in range(2):
            bank = 2 * (i % 2) + h
            for ch in range(8):
                S.mm(c.ps[:, bank, :], c.ogT[:, ch, i * 128:(i + 1) * 128], wo[:, ch, h * 512:(h + 1) * 512],
                     ch == 0, ch == 7, [Rwo, Rog[ch][i // 4]], [c.bank[bank]])

    def c_add(i):
        k = i % NX
        for h in range(2):
            bank = 2 * (i % 2) + h
            S.tt("dve", yo[k][:, h * 512:(h + 1) * 512], c.ps[:, bank, :], xin[k][:, h * 512:(h + 1) * 512],
                 ALU.add, [c.bank[bank], Rxin[k]], [Ryo[k]])
        S.ld("pool", y_d[i * 128:(i + 1) * 128, :], yo[k], reads=[Ryo[k]])

    stages = [(c_add, 3), (c_mm, 2), (c_ld, 0)]
    tail = 3
    if next_g is not None:
        n_stat, n_xn, n_tr = norm_stages(c, Rg, lambda i: yo[i % NX], lambda i: Ryo[i % NX], "C")
        stages = [(n_tr, 6), (n_xn, 5), (n_stat, 4)] + stages
        tail = 6
    for n in range(NT + tail):
        for fn, d in stages:
            i = n - d
            if 0 <= i < NT:
                fn(i)


def gate_phase(c, w_in, col0):
    S = c.S
    xnT3 = c.xnT[:, :].rearrange("p (c t) -> p c t", c=8)
    wv = w_in.rearrange("(c p) n -> p c n", p=128)
    wg = [c.wbuf[:, 6144 + 1024 * b: 6144 + 1024 * (b + 1)].rearrange("p (c n) -> p c n", c=8) for b in range(2)]
    Rwg = [Res("wg0"), Res("wg1")]
    c.Rog = [[Res(f"og{ch}_{tg}") for tg in range(NG)] for ch in range(8)]
    k = 0
    for ch in range(8):
        b = ch % 2
        S.ld("pool", wg[b], wv[:, :, col0 + ch * 128: col0 + (ch + 1) * 128], writes=[Rwg[b]])
        for tg in range(NG):
            bank = k % 4
            k += 1
            for cc in range(8):
                S.mm(c.ps[:, bank, :], wg[b][:, cc, :], xnT3[:, cc, tg * 512:(tg + 1) * 512], cc == 0, cc == 7,
                     [Rwg[b]] + c.RxnT[4 * tg:4 * tg + 4], [c.bank[bank]])
            S.act(c.ogT[:, ch, tg * 512:(tg + 1) * 512], c.ps[:, bank, :], AF.Silu, [c.bank[bank]], [c.Rog[ch][tg]])


def layer_sb(c, w_in):
    S = c.S
    gate_phase(c, w_in, 3072)
    S.barrier()
    xnT3 = c.xnT[:, :].rearrange("p (c t) -> p c t", c=8)
    wv = w_in.rearrange("(c p) n -> p c n", p=128)
    qT = c.hbuf[:, 0:S_LEN]
    kT = c.hbuf[:, S_LEN:2 * S_LEN]
    v = c.hbuf[:, 2 * S_LEN:3 * S_LEN].rearrange("p (i f) -> p i f", f=128)
    RqT = [Res(f"qT{g}") for g in range(NG)]
    RkT = [Res(f"kT{g}") for g in range(NG)]
    Rv = [Res(f"v{g}") for g in range(NG)]
    wsl = [c.wbuf[:, 3072 * b:3072 * (b + 1)].rearrange("p (s c n) -> p s c n", s=3, c=8) for b in range(2)]
    Rw = [Res("wsl0"), Res("wsl1")]
    NE, NL, NW, NA = 3, 4, 2, 2
    off = 0
    E, L, Wb, Ab = [], [], [], []
    for i in range(NE):
        E.append(c.scr[:, off:off + 1024].rearrange("p (h t) -> p h t", h=2)); off += 1024
    for i in range(NL):
        L.append(c.scr[:, off:off + 512].bitcast(BF16).rearrange("p (h t) -> p h t", h=2)); off += 512
    for i in range(NW):
        Wb.append(c.scr[:, off:off + 512].bitcast(BF16).rearrange("p (h t) -> p h t", h=2)); off += 512
    assert off <= 6144
    for i in range(NA):
        Ab.append(c.wbuf[:, 6144 + 1024 * i:6144 + 1024 * (i + 1)].rearrange("p (h t) -> p h t", h=2))
    RE = [Res(f"E{i}") for i in range(NE)]
    RL = [Res(f"L{i}") for i in range(NL)]
    RW = [Res(f"W{i}") for i in range(NW)]
    RA = [Res(f"A{i}") for i in range(NA)]
    AB = [4, 5]
    OBK = 6
    gcount = 0
    mask2 = c.mstrict.unsqueeze(1).to_broadcast([128, 2, 128])

    def load_w(hp):
        b = hp % 2
        for s in range(3):
            col = s * 1024 + hp * 128
            S.ld("pool", wsl[b][:, s, :, :], wv[:, :, col:col + 128], writes=[Rw[b]])

    def proj_groups(hp, banks):
        b = hp % 2
        k = 0
        for tg in reversed(range(NG)):
            for s_, (dst, Rd) in enumerate(((qT, RqT), (kT, RkT))):
                bank = banks[k % len(banks)]
                k += 1

                def g_qk(s_=s_, dst=dst, Rd=Rd, bank=bank, tg=tg):
                    for cc in range(8):
                        S.mm(c.ps[:, bank, :], wsl[b][:, s_, cc, :], xnT3[:, cc, tg * 512:(tg + 1) * 512], cc == 0, cc == 7,
                             [Rw[b]] + c.RxnT[4 * tg:4 * tg + 4], [c.bank[bank]])
                    S.cp("dve", dst[:, tg * 512:(tg + 1) * 512], c.ps[:, bank, :], [c.bank[bank]], [Rd[tg]])
                yield tg, g_qk
            bank = banks[k % len(banks)]
            k += 1
            for j in range(4):
                def g_v(j=j, bank=bank, tg=tg):
                    i = 4 * tg + j
                    for cc in range(8):
                        S.mm(c.ps[:, bank, j * 128:(j + 1) * 128], xnT3[:, cc, i * 128:(i + 1) * 128], wsl[b][:, 2, cc, :],
                             cc == 0, cc == 7, [Rw[b], c.RxnT[i]], [c.bank[bank]])
                    if j == 3:
                        S.cp("dve", v[:, 4 * tg:4 * tg + 4, :], c.ps[:, bank, :].rearrange("p (j f) -> p j f", f=128),
                             [c.bank[bank]], [Rv[tg]])
                yield tg, g_v

    load_w(0)
    load_w(1)
    for _, g_ in proj_groups(0, [0, 1, 2, 3]):
        g_()
    for hp in range(8):
        if 1 <= hp and hp + 1 < 8:
            load_w(hp + 1)
        pending = list(proj_groups(hp + 1, [7])) if hp + 1 < 8 else []
        G = []
        sweep_end = {}
        for qg in reversed(range(NG)):
            for kb in reversed(range(4 * qg + 4)):
                G.append((qg, kb))
            sweep_end[qg] = len(G) - 1
        n_g = len(G)

        def info(gi):
            qg, kb = G[gi]
            r = kb - 4 * qg
            c0 = r * 128 if r >= 0 else 0
            return qg, kb, r, c0, kb == 4 * qg + 3, kb == 0, gcount + gi

        def st_z(gi):
            qg, kb, r, c0, first, last, gid = info(gi)
            zp = 2 * (gid % 2)
            for hd in range(2):
                rows = slice(hd * 64, hd * 64 + 64)
                S.mm(c.ps[:, zp + hd, c0:512], kT[rows, kb * 128:(kb + 1) * 128], qT[rows, qg * 512 + c0:(qg + 1) * 512],
                     True, True, [RkT[kb // 4], RqT[qg]], [c.bank[zp + hd]])

        def st_e(gi):
            qg, kb, r, c0, first, last, gid = info(gi)
            zp = 2 * (gid % 2)
            eb = gid % NE
            S.act(E[eb][:, :, c0:512], c.ps[:, zp:zp + 2, c0:512], AF.Exp, [c.bank[zp], c.bank[zp + 1]], [RE[eb]],
                  scale=0.125)
            if r >= 0:
                S.tt("dve", E[eb][:, :, c0:c0 + 128], E[eb][:, :, c0:c0 + 128], mask2, ALU.mult,
                     [RE[eb], c.Rconst], [RE[eb]])

        def st_l(gi):
            qg, kb, r, c0, first, last, gid = info(gi)
            eb = gid % NE
            lb = gid % NL
            S.act(L[lb][:, :, c0:512], E[eb][:, :, c0:512], AF.Ln, [RE[eb]], [RL[lb]], bias=1.0, scale=1.0)

        def st_cum(gi):
            qg, kb, r, c0, first, last, gid = info(gi)
            lb = gid % NL
            for hd in range(2):
                S.mm(c.ps[:, AB[hd], c0:512], c.tinc, L[lb][:, hd, c0:512], first, False, [RL[lb], c.Rconst],
                     [c.bank[AB[hd]]], skip=True)

        def st_w(gi):
            qg, kb, r, c0, first, last, gid = info(gi)
            wb = gid % NW
            eb = gid % NE
            a_i = gid % NA
            S.act(Wb[wb][:, :, c0:512], c.ps[:, 4:6, c0:512], AF.Exp, [c.bank[4], c.bank[5]], [RW[wb]])
            S.tt("dve", Ab[a_i][:, :, c0:512], E[eb][:, :, c0:512], Wb[wb][:, :, c0:512], ALU.mult,
                 [RE[eb], RW[wb]], [RA[a_i]])

        def st_car(gi):
            qg, kb, r, c0, first, last, gid = info(gi)
            lb = gid % NL
            if not last:
                for hd in range(2):
                    S.mm(c.ps[:, AB[hd], c0:512], c.tcar, L[lb][:, hd, c0:512], False, False, [RL[lb], c.Rconst],
                         [c.bank[AB[hd]]], skip=True)

        def st_av(gi):
            qg, kb, r, c0, first, last, gid = info(gi)
            a_i = gid % NA
            for hd in range(2):
                rows = slice(hd * 64, hd * 64 + 64)
                S.mm(c.ps[rows, OBK, c0:512], v[:, kb, hd * 64:(hd + 1) * 64], Ab[a_i][:, hd, c0:512], first, last,
                     [RA[a_i], Rv[kb // 4]], [c.bank[OBK]], skip=True)
            if last:
                og = c.ogT[:, hp, qg * 512:(qg + 1) * 512]
                S.tt("dve", og, c.ps[:, OBK, :], og, ALU.mult, [c.bank[OBK], c.Rog[hp][qg]], [c.Rog[hp][qg]])

        stages = [(st_car, 4), (st_cum, 3), (st_av, 5), (st_z, 0), (st_w, 3), (st_l, 2), (st_e, 1)]
        for n in range(n_g + 5):
            for fn, d in stages:
                gi = n - d
                if 0 <= gi < n_g:
                    fn(gi)
            if pending and n >= sweep_end[pending[0][0]] + 6:
                pending.pop(0)[1]()
        for _, g_ in pending:
            g_()
        gcount += n_g


def layer_mla(c, w):
    S = c.S
    w_in = w["w_in"]
    gate_phase(c, w_in, 448)
    S.barrier()
    xnT3 = c.xnT[:, :].rearrange("p (c t) -> p c t", c=8)
    wv = w_in.rearrange("(c p) n -> p c n", p=128)
    sm = c.small
    scr = c.scr
    Rgn = Res("mla_g")
    gqa = c.gbc[:, 0:256]
    gkva = c.gbc[:, 256:384]
    gq192 = c.gbc[:, 384:576]
    gk192 = c.gbc[:, 576:768]
    gkpe = c.gbc[:, 704:768]
    S.ld("sp", gqa, w["q_a_norm"].partition_broadcast(128), writes=[Rgn])
    S.ld("sp", gkva, w["kv_a_norm"].partition_broadcast(128), writes=[Rgn])
    S.ld("sp", gq192, w["q_head_norm"].partition_broadcast(128), writes=[Rgn])
    S.ld("sp", gk192, w["k_head_norm"].partition_broadcast(128), writes=[Rgn])
    S.tt("dve", gq192[:, 0:128], gq192[:, 0:128], gk192[:, 0:128], ALU.mult, [Rgn], [Rgn])
    cosT = scr[:, 0:1024].rearrange("p (i f) -> p i f", f=32)
    sinT = scr[:, 1024:2048].rearrange("p (i f) -> p i f", f=32)
    Rtab = Res("ropetab")
    S.ld("sp", scr[:, 0:2048], c.cf_d[:, CF_COS:CF_COS + 2048], writes=[Rtab])
    kpeT = scr[:, 2048:4096].bitcast(BF16)
    qlnT3 = c.hbuf[:, 0:2 * S_LEN].rearrange("p (a t) -> p a t", a=2)
    kvlnT = c.hbuf[:, 2 * S_LEN:3 * S_LEN]
    sskpe = sm[:, 128:160]
    rstdk = sm[:, 160:192]
    Rqln = [Res(f"qln{g}") for g in range(NG)]
    Rkvln = [Res(f"kvln{g}") for g in range(NG)]
    Rkpe = [Res(f"kpe{g}") for g in range(NG)]
    Rsskpe = [Res(f"sskpe{g}") for g in range(NG)]

    def rope(x, out, i, Rx, Rout, t1, t2, Rt):
        cb = cosT[:, i, :].unsqueeze(1).to_broadcast([128, 2, 32])
        S.tt("pool", t1.rearrange("p (a f) -> p a f", a=2), x.rearrange("p (a f) -> p a f", a=2), cb, ALU.mult,
             [Rx, Rtab], [Rt])
        S.tt("pool", t2[:, 0:32], x[:, 32:64], sinT[:, i, :], ALU.mult, [Rx, Rtab], [Rt])
        S.tt("pool", t2[:, 32:64], x[:, 0:32], sinT[:, i, :], ALU.mult, [Rx, Rtab], [Rt])
        S.tt("pool", out[:, 0:32], t1[:, 0:32], t2[:, 0:32], ALU.subtract, [Rt], [Rout])
        S.tt("pool", out[:, 32:64], t1[:, 32:64], t2[:, 32:64], ALU.add, [Rt], [Rout])

    wlat = c.wbuf[:, 0:3584].rearrange("p (c n) -> p c n", c=8)
    Rwlat = Res("wlat")
    S.ld("pool", wlat[:, 0:4, :], wv[:, 0:4, 0:448], writes=[Rwlat])
    S.ld("pool", wlat[:, 4:8, :], wv[:, 4:8, 0:448], writes=[Rwlat])
    o = 4096
    junk = [scr[:, o + 448 * j:o + 448 * (j + 1)] for j in range(2)]; o += 896
    NLN = 3
    lnb = [scr[:, o + 192 * j:o + 192 * (j + 1)].bitcast(BF16) for j in range(NLN)]; o += 192 * NLN
    kp = [scr[:, o + 64 * j:o + 64 * (j + 1)] for j in range(2)]; o += 128
    t1 = [scr[:, o + 64 * j:o + 64 * (j + 1)] for j in range(2)]; o += 128
    t2 = [scr[:, o + 64 * j:o + 64 * (j + 1)] for j in range(2)]; o += 128
    kr = [scr[:, o + 32 * j:o + 32 * (j + 1)].bitcast(BF16) for j in range(2)]; o += 64
    assert o <= 6144, o
    Rjunk = [Res("junk0"), Res("junk1")]
    Rkp = [Res("kp0"), Res("kp1")]
    Rt = [Res("ropet0"), Res("ropet1")]
    Rlnb = [Res(f"lnb{j}") for j in range(NLN)]
    Rkr = [Res("kr0"), Res("kr1")]
    NSB = 4
    Rst = [Res(f"mst{j}") for j in range(NSB)]
    LB = [0, 1, 2, 3, 6, 7]

    def binfo(i):
        k = i % NSB
        sb_ = 192 + 8 * k
        return i // 4, LB[i % len(LB)], sm[:, sb_:sb_ + 2], sm[:, sb_ + 2:sb_ + 4], sm[:, sb_ + 4:sb_ + 6], k, i % 2, i % NLN

    def b_mm(i):
        tg, bank, ss2, lg2, rs2, k, pb, li = binfo(i)
        for cc in range(8):
            S.mm(c.ps[:, bank, 0:448], xnT3[:, cc, i * 128:(i + 1) * 128], wlat[:, cc, :], cc == 0, cc == 7,
                 [Rwlat, c.RxnT[i]], [c.bank[bank]])

    def b_sq(i):
        tg, bank, ss2, lg2, rs2, k, pb, li = binfo(i)
        S.act(junk[pb][:, 0:256], c.ps[:, bank, 0:256], AF.Square, [c.bank[bank]], [Rjunk[pb], Rst[k]], accum_out=ss2[:, 0:1])
        S.act(junk[pb][:, 256:384], c.ps[:, bank, 256:384], AF.Square, [c.bank[bank]], [Rjunk[pb], Rst[k]],
              accum_out=ss2[:, 1:2])
        S.act(junk[pb][:, 384:448], c.ps[:, bank, 384:448], AF.Square, [c.bank[bank]], [Rjunk[pb], Rsskpe[tg]],
              accum_out=sskpe[:, i:i + 1])

    def b_rs(i):
        tg, bank, ss2, lg2, rs2, k, pb, li = binfo(i)
        S.act(lg2[:, 0:1], ss2[:, 0:1], AF.Ln, [Rst[k]], [Rst[k]], scale=1.0 / 256, bias=EPS)
        S.act(lg2[:, 1:2], ss2[:, 1:2], AF.Ln, [Rst[k]], [Rst[k]], scale=1.0 / 128, bias=EPS)
        S.act(rs2, lg2, AF.Exp, [Rst[k]], [Rst[k]], scale=-0.5)

    def b_ln(i):
        tg, bank, ss2, lg2, rs2, k, pb, li = binfo(i)
        S.stt(lnb[li][:, 0:256], c.ps[:, bank, 0:256], rs2[:, 0:1], gqa, ALU.mult, ALU.mult,
              [c.bank[bank], Rst[k], Rgn], [Rlnb[li]])
        S.stt(lnb[li][:, 256:384], c.ps[:, bank, 256:384], rs2[:, 1:2], gkva, ALU.mult, ALU.mult,
              [c.bank[bank], Rst[k], Rgn], [Rlnb[li]])
        S.tt("dve", kp[pb], c.ps[:, bank, 384:448], gkpe, ALU.mult, [c.bank[bank], Rgn], [Rkp[pb]])

    def b_rope(i):
        tg, bank, ss2, lg2, rs2, k, pb, li = binfo(i)
        rope(kp[pb], kr[pb], i, Rkp[pb], Rkr[pb], t1[pb], t2[pb], Rt[pb])

    def b_tr(i):
        tg, bank, ss2, lg2, rs2, k, pb, li = binfo(i)
        tbank = 4 + pb
        psT = c.ps[:, tbank, :].bitcast(BF16)
        for a in range(3):
            S.tr(psT[:, a * 128:(a + 1) * 128], lnb[li][:, a * 128:(a + 1) * 128], c.ident,
                 [Rlnb[li], c.Rconst], [c.bank[tbank]])
        S.tr(psT[0:64, 384:512], kr[pb], c.ident, [Rkr[pb], c.Rconst], [c.bank[tbank]])
        S.cp("act", qlnT3[:, :, i * 128:(i + 1) * 128], psT[:, 0:256].rearrange("p (a t) -> p a t", a=2),
             [c.bank[tbank]], [Rqln[tg]])
        S.cp("dve", kvlnT[:, i * 128:(i + 1) * 128], psT[:, 256:384], [c.bank[tbank]], [Rkvln[tg]])
        S.cp("dve", kpeT[0:64, i * 128:(i + 1) * 128], psT[0:64, 384:512], [c.bank[tbank]], [Rkpe[tg]])

    bstages = [(b_tr, 5), (b_rope, 4), (b_ln, 3), (b_rs, 2), (b_sq, 1), (b_mm, 0)]
    for n in range(NT + 5):
        for fn, d in bstages:
            i = n - d
            if 0 <= i < NT:
                fn(i)
    S.barrier()
    wuq = c.wbuf[:, 0:3072].rearrange("p (a n) -> p a n", a=2)
    wukv = c.wbuf[:, 3072:5120]
    Rwu = Res("wu")
    S.ld("pool", wuq, w["w_uq"].rearrange("(a p) n -> p a n", p=128), writes=[Rwu])
    S.ld("pool", wukv, w["w_ukv"], writes=[Rwu])
    X = c.xnT
    qTn = X[:, 0:4096]
    qTp = X[:, 4096:8192]
    kTn = X[:, 8192:12288]
    vh = X[:, 12288:16384].rearrange("p (i f) -> p i f", f=128)
    NP = 3
    Pb = [X[:, 16384 + 512 * j:16384 + 512 * (j + 1)] for j in range(NP)]
    NQR = 3
    qr = [X[:, 18432 + 256 * j:18432 + 256 * j + 192] for j in range(NQR)]
    XF = X[:, 20480:24576].bitcast(F32)
    rcb = XF[:, 0:512]
    tbuf = XF[:, 512:1024]
    qpe = [XF[:, 1024 + 64 * j:1088 + 64 * j] for j in range(2)]
    u1 = [XF[:, 1152 + 64 * j:1216 + 64 * j] for j in range(2)]
    u2 = [XF[:, 1280 + 64 * j:1344 + 64 * j] for j in range(2)]
    junk2 = [XF[:, 1408 + 320 * j:1408 + 320 * (j + 1)] for j in range(2)]
    Rsum = [X[:, 24576 + 1024 * j:24576 + 1024 * (j + 1)].bitcast(F32) for j in range(2)]
    ones32 = X[:, 26624:26880].bitcast(F32)
    RRs = [Res("Rsum0"), Res("Rsum1")]
    Rones32 = Res("ones32")
    S.op("pool", lambda e, o=ones32: e.memset(o, 1.0), [], [Rones32])
    RqTn = [Res(f"qTn{g}") for g in range(NG)]
    RqTp = [Res(f"qTp{g}") for g in range(NG)]
    RkTn = [Res(f"kTn{g}") for g in range(NG)]
    Rvh = [Res(f"vh{g}") for g in range(NG)]
    Rrk = [Res(f"rstdk{g}") for g in range(NG)]
    RPb = [Res(f"mPb{j}") for j in range(NP)]
    Rqr = [Res(f"qr{j}") for j in range(NQR)]
    Rrc, Rtb = Res("rcb"), Res("tbuf")
    Rqpe = [Res("qpe0"), Res("qpe1")]
    Ru = [Res("u0"), Res("u1")]
    Rj2 = [Res("junk20"), Res("junk21")]
    NST = 4
    Rs2 = [Res(f"hst{j}") for j in range(NST)]
    QB = [0, 1, 2, 3, 6, 7]
    gid = 0
    sw = 0
    tcount = 0
    for h in range(8):
        for tg in range(NG):
            bank = 4 + tg % 2
            S.mm(c.ps[:, bank, :], wukv[:, h * 256:h * 256 + 128], kvlnT[:, tg * 512:(tg + 1) * 512], True, True,
                 [Rwu, Rkvln[tg]], [c.bank[bank]])
            S.cp("act", kTn[:, tg * 512:(tg + 1) * 512], c.ps[:, bank, :], [c.bank[bank]], [RkTn[tg]])

        def tinfo(i):
            t = tcount + i
            tg, j = divmod(i, 4)
            sb_ = 224 + 8 * (t % NST)
            return (t, tg, j, QB[t % len(QB)], slice(i * 128, (i + 1) * 128), sm[:, sb_:sb_ + 2], sm[:, sb_ + 2:sb_ + 4],
                    sm[:, sb_ + 4:sb_ + 5], t % NST, t % NQR, t % 2)

        def t_mm(i):
            t, tg, j, bank, tok, ss2, lg2, rsq, si, qi, pi = tinfo(i)
            S.mm(c.ps[:, bank, 0:192], qlnT3[:, 0, tok], wuq[:, 0, h * 192:(h + 1) * 192], True, False,
                 [Rwu, Rqln[tg]], [c.bank[bank]])
            S.mm(c.ps[:, bank, 0:192], qlnT3[:, 1, tok], wuq[:, 1, h * 192:(h + 1) * 192], False, True,
                 [Rwu, Rqln[tg]], [c.bank[bank]])
            S.mm(c.ps[:, bank, 192:448], kvlnT[:, tok], wukv[:, h * 256:(h + 1) * 256], True, True,
                 [Rwu, Rkvln[tg]], [c.bank[bank]])

        def t_sq(i):
            t, tg, j, bank, tok, ss2, lg2, rsq, si, qi, pi = tinfo(i)
            S.act(junk2[pi][:, 0:192], c.ps[:, bank, 0:192], AF.Square, [c.bank[bank]], [Rj2[pi], Rs2[si]],
                  accum_out=ss2[:, 0:1])
            S.act(junk2[pi][:, 192:320], c.ps[:, bank, 192:320], AF.Square, [c.bank[bank]], [Rj2[pi], Rs2[si]],
                  accum_out=ss2[:, 1:2])
            S.cp("act", vh[:, i, :], c.ps[:, bank, 320:448], [c.bank[bank]], [Rvh[tg]])

        def t_add(i):
            t, tg, j, bank, tok, ss2, lg2, rsq, si, qi, pi = tinfo(i)
            S.tt("dve", ss2[:, 1:2], ss2[:, 1:2], sskpe[:, i:i + 1], ALU.add, [Rs2[si], Rsskpe[tg]], [Rs2[si]])

        def t_rs(i):
            t, tg, j, bank, tok, ss2, lg2, rsq, si, qi, pi = tinfo(i)
            S.act(lg2, ss2, AF.Ln, [Rs2[si]], [Rs2[si]], scale=1.0 / 192, bias=EPS)
            S.act(rsq, lg2[:, 0:1], AF.Exp, [Rs2[si]], [Rs2[si]], scale=-0.5)
            S.act(rstdk[:, i:i + 1], lg2[:, 1:2], AF.Exp, [Rs2[si]], [Rrk[tg]], scale=-0.5, bias=-0.5 * math.log(192.0))

        def t_qn(i):
            t, tg, j, bank, tok, ss2, lg2, rsq, si, qi, pi = tinfo(i)
            S.stt(qr[qi][:, 0:128], c.ps[:, bank, 0:128], rsq, gq192[:, 0:128], ALU.mult, ALU.mult,
                  [c.bank[bank], Rs2[si], Rgn], [Rqr[qi]])
            S.stt(qpe[pi], c.ps[:, bank, 128:192], rsq, gq192[:, 128:192], ALU.mult, ALU.mult,
                  [c.bank[bank], Rs2[si], Rgn], [Rqpe[pi]])
            rope(qpe[pi], qr[qi][:, 128:192], i, Rqpe[pi], Rqr[qi], u1[pi], u2[pi], Ru[pi])

        def t_tr(i):
            t, tg, j, bank, tok, ss2, lg2, rsq, si, qi, pi = tinfo(i)
            tbank = 4 + tg % 2
            psT = c.ps[:, tbank, :].bitcast(BF16)
            S.tr(psT[:, j * 128:(j + 1) * 128], qr[qi][:, 0:128], c.ident, [Rqr[qi], c.Rconst], [c.bank[tbank]])
            S.tr(psT[0:64, 512 + j * 128:512 + (j + 1) * 128], qr[qi][:, 128:192], c.ident,
                 [Rqr[qi], c.Rconst], [c.bank[tbank]])
            if j == 3:
                S.cp("dve", qTn[:, tg * 512:(tg + 1) * 512], psT[:, 0:512], [c.bank[tbank]], [RqTn[tg]])
                S.cp("dve", qTp[0:64, tg * 512:(tg + 1) * 512], psT[0:64, 512:1024], [c.bank[tbank]], [RqTp[tg]])

        tstages = [(t_tr, 5), (t_qn, 4), (t_rs, 3), (t_add, 2), (t_sq, 1), (t_mm, 0)]
        for n in range(NT + 5):
            for fn, d in tstages:
                i = n - d
                if 0 <= i < NT:
                    fn(i)
        tcount += NT
        G = []
        for qg in range(NG):
            for kb in reversed(range(4 * qg + 4)):
                G.append((qg, kb))
        n_g = len(G)

        def info(gi):
            qg, kb = G[gi]
            r = kb - 4 * qg
            c0 = r * 128 if r >= 0 else 0
            return qg, kb, r, c0, kb == 4 * qg + 3, kb == 0, gid + gi, sw + qg

        def st_z(gi):
            qg, kb, r, c0, first, last, g_, sw_ = info(gi)
            zb = g_ % 4
            S.mm(c.ps[:, zb, c0:512], kTn[:, kb * 128:(kb + 1) * 128], qTn[:, qg * 512 + c0:(qg + 1) * 512], True, False,
                 [RkTn[kb // 4], RqTn[qg]], [c.bank[zb]])
            S.mm(c.ps[:, zb, c0:512], kpeT[0:64, kb * 128:(kb + 1) * 128], qTp[0:64, qg * 512 + c0:(qg + 1) * 512],
                 False, True, [Rkpe[kb // 4], RqTp[qg]], [c.bank[zb]])

        def st_e(gi):
            qg, kb, r, c0, first, last, g_, sw_ = info(gi)
            zb = g_ % 4
            pi = g_ % NP
            S.act(Pb[pi][:, c0:512], c.ps[:, zb, c0:512], AF.Exp, [c.bank[zb], Rrk[kb // 4]], [RPb[pi]],
                  scale=rstdk[:, kb:kb + 1])
            if r >= 0:
                S.tt("dve", Pb[pi][:, c0:c0 + 128], Pb[pi][:, c0:c0 + 128], c.mincl, ALU.mult,
                     [RPb[pi], c.Rconst], [RPb[pi]])

        def st_av(gi):
            qg, kb, r, c0, first, last, g_, sw_ = info(gi)
            pi = g_ % NP
            ob = 4 + sw_ % 2
            db = 6 + sw_ % 2
            ri = sw_ % 2
            S.mm(c.ps[:, ob, c0:512], vh[:, kb, :], Pb[pi][:, c0:512], first, last, [Rvh[kb // 4], RPb[pi]],
                 [c.bank[ob]], skip=True)
            S.mm(c.ps[:, db, c0:512], c.ones, Pb[pi][:, c0:512], first, last, [c.Rconst, RPb[pi]],
                 [c.bank[db]], skip=True)
            if last:
                S.act(rcb, c.ps[:, db, :], AF.Ln, [c.bank[db]], [Rrc])
                S.act(rcb, rcb, AF.Exp, [Rrc], [Rrc], scale=-1.0)
                S.tt("dve", tbuf, c.ps[:, ob, :], rcb, ALU.mult, [c.bank[ob], Rrc], [Rtb])
                og = c.ogT[:, h, qg * 512:(qg + 1) * 512]
                S.tt("dve", og, tbuf, og, ALU.mult, [Rtb, c.Rog[h][qg]], [c.Rog[h][qg]])

        stages = [(st_z, 0), (st_e, 1), (st_av, 2)]
        for n in range(n_g + 2):
            for fn, d in reversed(stages):
                gi = n - d
                if 0 <= gi < n_g:
                    fn(gi)
        gid += n_g
        sw += NG


def layer_swa(c, w):
    S = c.S
    w_in = w["w_in"]
    gate_phase(c, w_in, 1536)
    S.barrier()
    xnT3 = c.xnT[:, :].rearrange("p (c t) -> p c t", c=8)
    wv = w_in.rearrange("(c p) n -> p c n", p=128)
    qT3 = c.hbuf[:, 0:2 * S_LEN].rearrange("p (a t) -> p a t", a=2)
    kT2 = c.hbuf[:, 2 * S_LEN:3 * S_LEN]
    scr = c.scr
    off = 0
    vg = scr[:, off:off + 1024].bitcast(BF16).rearrange("p (i f) -> p i f", f=64); off += 1024
    junk = [scr[:, off + 320 * j:off + 320 * (j + 1)] for j in range(2)]; off += 640
    tmpq = [scr[:, off + 256 * j:off + 256 * (j + 1)] for j in range(2)]; off += 512
    NQN = 3
    qn = [scr[:, off + 128 * j:off + 128 * (j + 1)].bitcast(BF16) for j in range(NQN)]; off += 128 * NQN
    biasg = scr[:, off:off + 1024]; off += 1024
    Tb = scr[:, off:off + 1024]; off += 1024
    Pb = [scr[:, off:off + 512].bitcast(BF16), scr[:, off + 512:off + 1024].bitcast(BF16)]; off += 1024
    dnb = scr[:, off:off + 256]; off += 256
    gqk4 = scr[:, off:off + 256]; off += 256
    assert off <= 6144, off
    sm = c.small
    kscale = sm[:, 16:48]
    es16 = sm[:, 48:64]
    Rbias, RTb, Rdn, Rgqk, Res16 = (Res(n) for n in ("biasg", "Tb", "dnb", "gqk", "es16"))
    Rjunk = [Res("junk0"), Res("junk1")]
    Rtmpq = [Res("tmpq0"), Res("tmpq1")]
    Rqn = [Res(f"qn{j}") for j in range(NQN)]
    RPb = [Res("Pb0"), Res("Pb1")]
    NST = 4
    Rst = [Res(f"sst{j}") for j in range(NST)]
    RqT = [Res(f"qT{g}") for g in range(NG)]
    RkT = [Res(f"kT{g}") for g in range(NG)]
    Rvg = [Res(f"vg{g}") for g in range(NG)]
    Rks = [Res(f"ks{g}") for g in range(NG)]
    wtm = [c.wbuf[:, 4096 * b:4096 * b + 3072].rearrange("p (c n) -> p c n", c=8) for b in range(2)]
    wk2 = [c.wbuf[:, 4096 * b + 3072:4096 * (b + 1)].rearrange("p (c n) -> p c n", c=8) for b in range(2)]
    Rw = [Res("swaw0"), Res("swaw1")]
    gk4 = junk[0][:, 0:256]
    for j in range(4):
        S.ld("sp", gqk4[:, j * 64:(j + 1) * 64], w["q_head_norm"].partition_broadcast(128), writes=[Rgqk])
        S.ld("sp", gk4[:, j * 64:(j + 1) * 64], w["k_head_norm"].partition_broadcast(128), writes=[Rjunk[0]])
    S.tt("dve", gqk4, gqk4, gk4, ALU.mult, [Rjunk[0], Rgqk], [Rgqk])
    S.ld("sp", es16, w["sinks"].partition_broadcast(128), writes=[Res16])
    S.act(es16, es16, AF.Exp, [Res16], [Res16])

    def load_w(g):
        b = g % 2
        S.ld("pool", wtm[b][:, :, 0:256], wv[:, :, g * 256:(g + 1) * 256], writes=[Rw[b]])
        S.ld("pool", wtm[b][:, :, 256:320], wv[:, :, 1024 + g * 64:1024 + (g + 1) * 64], writes=[Rw[b]])
        S.ld("pool", wtm[b][:, :, 320:384], wv[:, :, 1280 + g * 64:1280 + (g + 1) * 64], writes=[Rw[b]])
        for d in range(2):
            S.ld("pool", wk2[b][:, :, d * 64:(d + 1) * 64], wv[:, :, 1024 + g * 64:1024 + (g + 1) * 64], writes=[Rw[b]])

    load_w(0)
    TB = [0, 1, 2, 3, 6, 7]
    tcount = 0
    acount = 0
    for g in range(4):
        b = g % 2
        if g + 1 < 4:
            load_w(g + 1)
        for kbi in range(2):
            S.ld("sp", biasg[:, kbi * 512:(kbi + 1) * 512],
                 c.cf_d[:, CF_BIAS + kbi * 2048 + g * 512:CF_BIAS + kbi * 2048 + (g + 1) * 512], writes=[Rbias])
        for tg in range(NG):
            bank = 4 + tg % 2
            for cc in range(8):
                S.mm(c.ps[:, bank, :], wk2[b][:, cc, :], xnT3[:, cc, tg * 512:(tg + 1) * 512], cc == 0, cc == 7,
                     [Rw[b]] + c.RxnT[4 * tg:4 * tg + 4], [c.bank[bank]])
            S.cp("act", kT2[:, tg * 512:(tg + 1) * 512], c.ps[:, bank, :], [c.bank[bank]], [RkT[tg]])

        def tinfo(i):
            t = tcount + i
            tg, j = divmod(i, 4)
            sb_ = 64 + 16 * (t % NST)
            return (t, tg, j, TB[t % len(TB)], sm[:, sb_:sb_ + 5], sm[:, sb_ + 5:sb_ + 10], sm[:, sb_ + 10:sb_ + 14],
                    t % NST, t % 2, t % NQN)

        def t_mm(i):
            t, tg, j, bank, ss5, lg5, rs4, si, pi, qi = tinfo(i)
            for cc in range(8):
                S.mm(c.ps[:, bank, 0:384], xnT3[:, cc, i * 128:(i + 1) * 128], wtm[b][:, cc, :], cc == 0, cc == 7,
                     [Rw[b], c.RxnT[i]], [c.bank[bank]])

        def t_sq(i):
            t, tg, j, bank, ss5, lg5, rs4, si, pi, qi = tinfo(i)
            S.act(junk[pi], c.ps[:, bank, 0:320], AF.Square, [c.bank[bank]], [Rjunk[pi]])
            S.cp("act", vg[:, i, :], c.ps[:, bank, 320:384], [c.bank[bank]], [Rvg[tg]])

        def t_red(i):
            t, tg, j, bank, ss5, lg5, rs4, si, pi, qi = tinfo(i)
            S.op("dve", lambda e, o=ss5, i_=junk[pi].rearrange("p (h f) -> p h f", f=64): e.reduce_sum(out=o, in_=i_, axis=AX.X),
                 [Rjunk[pi]], [Rst[si]])

        def t_rs(i):
            t, tg, j, bank, ss5, lg5, rs4, si, pi, qi = tinfo(i)
            S.act(lg5, ss5, AF.Ln, [Rst[si]], [Rst[si]], scale=1.0 / 64, bias=EPS)
            S.act(rs4, lg5[:, 0:4], AF.Exp, [Rst[si]], [Rst[si]], scale=-0.5)
            S.act(kscale[:, i:i + 1], lg5[:, 4:5], AF.Exp, [Rst[si]], [Rks[tg]], scale=-0.5, bias=math.log(0.125))

        def t_qn(i):
            t, tg, j, bank, ss5, lg5, rs4, si, pi, qi = tinfo(i)
            S.tt("dve", tmpq[pi].rearrange("p (h f) -> p h f", f=64),
                 c.ps[:, bank, 0:256].rearrange("p (h f) -> p h f", f=64),
                 rs4.unsqueeze(2).to_broadcast([128, 4, 64]), ALU.mult, [c.bank[bank], Rst[si]], [Rtmpq[pi]])
            S.tt("pool", qn[qi], tmpq[pi], gqk4, ALU.mult, [Rtmpq[pi], Rgqk], [Rqn[qi]])

        def t_tr(i):
            t, tg, j, bank, ss5, lg5, rs4, si, pi, qi = tinfo(i)
            tbank = 4 + tg % 2
            psT = c.ps[:, tbank, :].bitcast(BF16)
            for a in range(2):
                S.tr(psT[:, a * 512 + j * 128:a * 512 + (j + 1) * 128], qn[qi][:, a * 128:(a + 1) * 128], c.ident,
                     [Rqn[qi], c.Rconst], [c.bank[tbank]])
            if j == 3:
                S.cp("dve", qT3[:, :, tg * 512:(tg + 1) * 512], psT.rearrange("p (a t) -> p a t", a=2),
                     [c.bank[tbank]], [RqT[tg]])

        tstages = [(t_tr, 5), (t_qn, 4), (t_rs, 3), (t_red, 2), (t_sq, 1), (t_mm, 0)]
        for n in range(NT + 5):
            for fn, d in tstages:
                i = n - d
                if 0 <= i < NT:
                    fn(i)
        tcount += NT

        def ainfo(qb):
            a_ = acount + qb
            kbs = [(0, qb - 1), (1, qb)] if qb > 0 else [(1, qb)]
            return a_, kbs, 2 * (a_ % 2), a_ % 2, 4 + a_ % 2

        def a_qk(qb):
            a_, kbs, zb0, pbi, ob = ainfo(qb)
            for kbi, kb in kbs:
                for hq in range(4):
                    p = hq % 2
                    rows = slice(p * 64, p * 64 + 64)
                    col = (kbi * 2 + hq // 2) * 128
                    S.mm(c.ps[:, zb0 + p, col:col + 128], kT2[rows, kb * 128:(kb + 1) * 128],
                         qT3[rows, hq // 2, qb * 128:(qb + 1) * 128], True, True,
                         [RkT[kb // 4], RqT[qb // 4]], [c.bank[zb0 + p]])

        def a_bias(qb):
            a_, kbs, zb0, pbi, ob = ainfo(qb)
            for kbi, kb in kbs:
                for p in range(2):
                    src = c.ps[:, zb0 + p, kbi * 256:(kbi + 1) * 256].rearrange("q (a t) -> q a t", a=2)
                    dst = Tb[:, kbi * 512:(kbi + 1) * 512].rearrange("q (a p t) -> q p a t", a=2, p=2)[:, p]
                    bia = biasg[:, kbi * 512:(kbi + 1) * 512].rearrange("q (a p t) -> q p a t", a=2, p=2)[:, p]
                    S.stt(dst, src, kscale[:, kb:kb + 1], bia, ALU.mult, ALU.add,
                          [c.bank[zb0 + p], Rks[kb // 4], Rbias], [RTb])

        def a_exp(qb):
            a_, kbs, zb0, pbi, ob = ainfo(qb)
            lo = 0 if qb > 0 else 512
            S.act(Pb[pbi][:, lo:1024], Tb[:, lo:1024], AF.Exp, [RTb], [RPb[pbi]])

        def a_av(qb):
            a_, kbs, zb0, pbi, ob = ainfo(qb)
            for hq in range(4):
                rows = slice((hq % 2) * 64, (hq % 2) * 64 + 64)
                col = (hq // 2) * 128
                for n_, (kbi, kb) in enumerate(kbs):
                    S.mm(c.ps[rows, ob, col:col + 128], vg[:, kb, :], Pb[pbi][:, (kbi * 4 + hq) * 128:(kbi * 4 + hq + 1) * 128],
                         n_ == 0, n_ == len(kbs) - 1, [Rvg[kb // 4], RPb[pbi]], [c.bank[ob]])
                for n_, (kbi, kb) in enumerate(kbs):
                    S.mm(c.ps[rows, ob, 256 + col:256 + col + 128], c.ones[:, 0:64],
                         Pb[pbi][:, (kbi * 4 + hq) * 128:(kbi * 4 + hq + 1) * 128],
                         n_ == 0, n_ == len(kbs) - 1, [c.Rconst, RPb[pbi]], [c.bank[ob]])

        def a_out(qb):
            a_, kbs, zb0, pbi, ob = ainfo(qb)
            for hq in range(4):
                rows = slice((hq % 2) * 64, (hq % 2) * 64 + 64)
                col = (hq // 2) * 128
                h = 4 * g + hq
                S.ts("dve", dnb[rows, col:col + 128], c.ps[rows, ob, 256 + col:256 + col + 128], es16[rows, h:h + 1],
                     None, ALU.add, None, [c.bank[ob], Res16], [Rdn])
            S.act(dnb, dnb, AF.Ln, [Rdn], [Rdn])
            S.act(dnb, dnb, AF.Exp, [Rdn], [Rdn], scale=-1.0)
            S.tt("dve", dnb, c.ps[:, ob, 0:256], dnb, ALU.mult, [c.bank[ob], Rdn], [Rdn])
            og = c.ogT[:, 2 * g:2 * g + 2, qb * 128:(qb + 1) * 128]
            Rogs = [c.Rog[2 * g][qb // 4], c.Rog[2 * g + 1][qb // 4]]
            S.tt("pool", og, dnb.rearrange("p (a t) -> p a t", a=2), og, ALU.mult, [Rdn] + Rogs, Rogs)

        astages = [(a_out, 4), (a_av, 3), (a_exp, 2), (a_bias, 1), (a_qk, 0)]
        for n in range(NT + 4):
            for fn, d in astages:
                qb = n - d
                if 0 <= qb < NT:
                    fn(qb)
        acount += NT


LAUNCH_GROUPS = [[0, 1, 2, 3]]
_CONSTS = None


def run_layers(layers, xs, inputs):
    global _CONSTS
    if _CONSTS is None:
        _CONSTS = make_consts()
    cbf, cf = _CONSTS
    nc = build_program(layers)
    names = [n for li in layers for n in LAYER_WEIGHTS[li]]
    in_maps = []
    for b in range(len(xs)):
        m = {"x": np.ascontiguousarray(xs[b], dtype=np.float32), "cbf": cbf, "cf": cf}
        for n in names:
            m[n] = np.ascontiguousarray(inputs[n], dtype=np.float32)
        in_maps.append(m)
    res = run_bass_kernel_spmd(nc, in_maps, core_ids=list(range(len(xs))))
    return [r["y"] for r in res.results]


def kernel(**inputs):
    x = np.asarray(inputs["x"])
    xs = [x[b] for b in range(x.shape[0])]
    for grp in LAUNCH_GROUPS:
        xs = run_layers(grp, xs, inputs)
    return np.stack(xs, axis=0).astype(np.float32)
```

```python
import math
import numpy as np
import ml_dtypes
import concourse.bass as bass
import concourse.mybir as mybir
from concourse.bass_utils import run_bass_kernel_spmd

F32 = mybir.dt.float32
BF16 = mybir.dt.bfloat16
AF = mybir.ActivationFunctionType
ALU = mybir.AluOpType
AX = mybir.AxisListType

S_LEN = 4096
D = 1024
NT = S_LEN // 128
NG = S_LEN // 512
EPS = 1e-6

LAYER_KIND = ["sb", "mla", "swa", "sb"]
LAYER_WEIGHTS = [
    ["l0_norm", "l0_w_in", "l0_w_out"],
    ["l1_norm", "l1_w_in", "l1_q_a_norm", "l1_w_uq", "l1_kv_a_norm", "l1_w_ukv",
     "l1_q_head_norm", "l1_k_head_norm", "l1_w_out"],
    ["l2_norm", "l2_w_in", "l2_q_head_norm", "l2_k_head_norm", "l2_sinks", "l2_w_out"],
    ["l3_norm", "l3_w_in", "l3_w_out"],
]
WSHAPES = {
    "l0_norm": [1024], "l0_w_in": [1024, 4096], "l0_w_out": [1024, 1024],
    "l1_norm": [1024], "l1_w_in": [1024, 1472], "l1_q_a_norm": [256], "l1_w_uq": [256, 1536],
    "l1_kv_a_norm": [128], "l1_w_ukv": [128, 2048], "l1_q_head_norm": [192], "l1_k_head_norm": [192],
    "l1_w_out": [1024, 1024],
    "l2_norm": [1024], "l2_w_in": [1024, 2560], "l2_q_head_norm": [64], "l2_k_head_norm": [64],
    "l2_sinks": [16], "l2_w_out": [1024, 1024],
    "l3_norm": [1024], "l3_w_in": [1024, 4096], "l3_w_out": [1024, 1024],
}


class Res:
    __slots__ = ("name", "w", "r", "excl")

    def __init__(self, name, excl=False):
        self.name = name
        self.w = None
        self.r = {}
        self.excl = excl


class Sched:
    ENGS = ("pe", "act", "dve", "pool", "sp")

    def __init__(self, sems, dma_pools):
        self.sems = sems
        self.cnt = {k: 0 for k in sems}
        self.ops = {e: [] for e in self.ENGS}
        self.seen = {e: {} for e in self.ENGS}
        self.dma_pools = dma_pools
        self.dma_rr = {q: 0 for q in dma_pools}
        self.nwaits = 0

    def _wait(self, e, key, val):
        if val <= 0 or val <= self.seen[e].get(key, 0):
            return
        self.seen[e][key] = val
        sem = self.sems[key]
        self.ops[e].append(lambda eng, sem=sem, val=val: eng.wait_ge(sem, val))
        self.nwaits += 1

    def _sync(self, e, reads, writes, dma):
        rd = [r for r in reads if not r.excl]
        wr = list(writes) + [r for r in reads if r.excl]
        need = []
        for r in rd:
            if r.w is not None:
                need.append((r.w, True))
        for r in wr:
            if r.w is not None:
                need.append((r.w, False))
            for ev in r.r.values():
                need.append((ev, False))
        for (key, val, eng), raw in need:
            if eng == e and not dma:
                if e == "pe":
                    continue
            self._wait(e, key, val)
        return rd, wr

    def _update(self, rd, wr, ev):
        for r in rd:
            r.r[ev[0]] = ev
        for r in wr:
            r.w = ev
            r.r = {}

    def op(self, e, fn, reads=(), writes=()):
        rd, wr = self._sync(e, reads, writes, False)
        self.cnt[e] += 1
        sem = self.sems[e]
        self.ops[e].append(lambda eng, fn=fn, sem=sem: fn(eng).then_inc(sem, 1))
        self._update(rd, wr, (e, self.cnt[e], e))

    def dma(self, q, fn, reads=(), writes=()):
        pool = self.dma_pools[q]
        k = pool[self.dma_rr[q] % len(pool)]
        self.dma_rr[q] += 1
        self._wait(q, k, self.cnt[k])
        rd, wr = self._sync(q, reads, writes, True)
        self.cnt[k] += 16
        sem = self.sems[k]
        self.ops[q].append(lambda eng, fn=fn, sem=sem: fn(eng).then_inc(sem, 16))
        self._update(rd, wr, (k, self.cnt[k], None))

    def barrier(self, engines=None):
        for e in (engines or self.ENGS):
            for k, v in self.cnt.items():
                if k != e:
                    self._wait(e, k, v)


    def mm(self, out, lhsT, rhs, start, stop, reads, writes, skip=False):
        if skip:
            self.op("pe", lambda e: e.matmul(out, lhsT=lhsT, rhs=rhs, start=start, stop=stop, skip_group_check=True),
                    reads, writes)
        else:
            self.op("pe", lambda e: e.matmul(out, lhsT=lhsT, rhs=rhs, start=start, stop=stop), reads, writes)

    def tr(self, out, in_, ident, reads, writes):
        self.op("pe", lambda e: e.transpose(out=out, in_=in_, identity=ident), reads, writes)

    def act(self, out, in_, func, reads, writes, **kw):
        self.op("act", lambda e: e.activation(out=out, in_=in_, func=func, **kw), reads, writes)

    def tt(self, eng, out, in0, in1, op, reads, writes):
        self.op(eng, lambda e: e.tensor_tensor(out=out, in0=in0, in1=in1, op=op), reads, writes)

    def stt(self, out, in0, scalar, in1, op0, op1, reads, writes):
        self.op("dve", lambda e: e.scalar_tensor_tensor(out=out, in0=in0, scalar=scalar, in1=in1, op0=op0, op1=op1),
                reads, writes)

    def ts(self, eng, out, in0, s1, s2, op0, op1, reads, writes):
        if op1 is None:
            self.op(eng, lambda e: e.tensor_scalar(out=out, in0=in0, scalar1=s1, scalar2=None, op0=op0), reads, writes)
        else:
            self.op(eng, lambda e: e.tensor_scalar(out=out, in0=in0, scalar1=s1, scalar2=s2, op0=op0, op1=op1),
                    reads, writes)

    def cp(self, eng, out, in_, reads, writes):
        if eng == "act":
            self.op("act", lambda e: e.copy(out=out, in_=in_), reads, writes)
        else:
            self.op(eng, lambda e: e.tensor_copy(out=out, in_=in_), reads, writes)

    def ld(self, q, out, in_, reads=(), writes=()):
        self.dma(q, lambda e: e.dma_start(out=out, in_=in_), reads, writes)

    def emit(self, block):
        def mk(e):
            def body(eng):
                for f in self.ops[e]:
                    f(eng)
            return body
        block.tensor(mk("pe"))
        block.scalar(mk("act"))
        block.vector(mk("dve"))
        block.gpsimd(mk("pool"))
        block.sync(mk("sp"))


def make_consts():
    j = np.arange(128)[:, None]
    s = np.arange(128)[None, :]
    ident = (j == s).astype(np.float32)
    tinc = -(j >= s).astype(np.float32)
    tcar = -(j < s).astype(np.float32)
    ones = np.ones((128, 128), np.float32)
    cbf = np.concatenate([ident, tinc, tcar, ones], axis=1)
    mstrict = (s > j).astype(np.float32)
    mincl = (s >= j).astype(np.float32)
    half = 32
    inv_freq = (10000.0 ** (-np.arange(half, dtype=np.float32) / half)).astype(np.float32)
    pos = np.arange(S_LEN, dtype=np.float32)
    ang = (pos[:, None] * inv_freq[None, :]).astype(np.float32)
    cos = np.cos(ang).astype(np.float32)
    sin = np.sin(ang).astype(np.float32)
    cosT = cos.reshape(NT, 128, 32).transpose(1, 0, 2).reshape(128, NT * 32)
    sinT = sin.reshape(NT, 128, 32).transpose(1, 0, 2).reshape(128, NT * 32)
    slopes = (2.0 ** (-8.0 * np.arange(1, 17, dtype=np.float32) / 16)).astype(np.float32)
    NEG = -30000.0
    bias = np.zeros((128, 2, 16, 128), np.float32)
    for kbi in range(2):
        rel = (s + 128 - (j + 128 * kbi)).astype(np.float32)
        valid = (rel >= 0) & (rel < 128)
        for h in range(16):
            bias[:, kbi, h, :] = np.where(valid, -slopes[h] * rel, NEG)
    cf = np.concatenate([mstrict, mincl, cosT, sinT, bias.reshape(128, -1)], axis=1).astype(np.float32)
    return cbf.astype(np.float32), cf


CF_MSTRICT = 0
CF_MINCL = 128
CF_COS = 256
CF_SIN = 256 + NT * 32
CF_BIAS = 256 + 2 * NT * 32
CF_TOTAL = CF_BIAS + 2 * 16 * 128


class Ctx:
    pass


def build_program(layers, dbg=None):
    nc = bass.Bass("TRN2", target_bir_lowering=False)
    x_in = nc.dram_tensor("x", [S_LEN, D], F32, kind="ExternalInput").ap()
    y_out = nc.dram_tensor("y", [S_LEN, D], F32, kind="ExternalOutput").ap()
    cbf_d = nc.dram_tensor("cbf", [128, 512], F32, kind="ExternalInput").ap()
    cf_d = nc.dram_tensor("cf", [128, CF_TOTAL], F32, kind="ExternalInput").ap()
    W = {}
    for li in layers:
        for n in LAYER_WEIGHTS[li]:
            W[n] = nc.dram_tensor(n, WSHAPES[n], F32, kind="ExternalInput").ap()
    xs = [x_in]
    for i in range(len(layers) - 1):
        xs.append(nc.dram_tensor(f"xmid{i}", [S_LEN, D], F32, kind="Internal").ap())
    xs.append(y_out)

    from contextlib import ExitStack
    with ExitStack() as st:
        def sb(name, shape, dt):
            return st.enter_context(nc.sbuf_tensor(name, shape, dt))
        c = Ctx()
        c.nc = nc
        c.xnT = sb("xnT", [128, 8 * S_LEN], BF16)
        c.ogT = sb("ogT", [128, 8, S_LEN], BF16)
        c.hbuf = sb("hbuf", [128, 3 * S_LEN], BF16)
        c.wbuf = sb("wbuf", [128, 8192], BF16)
        c.scr = sb("scr", [128, 6144], F32)
        c.gbc = sb("gbc", [128, 1024], F32)
        c.cbf = sb("cbfs", [128, 512], BF16)
        c.cmask = sb("cmask", [128, 256], F32)
        c.small = sb("small", [128, 512], F32)
        c.ps = st.enter_context(nc.psum_tensor("ps", [128, 8, 512], F32))
        sem_names = list(Sched.ENGS) + [f"d{i}" for i in range(20)]
        sems = {k: st.enter_context(nc.semaphore(f"s_{k}")) for k in sem_names}
        S = Sched(sems, {"sp": [f"d{i}" for i in range(0, 8)],
                         "pool": [f"d{i}" for i in range(8, 16)],
                         "act": [f"d{i}" for i in range(16, 20)]})
        c.S = S
        c.W = W
        c.cf_d = cf_d
        c.dbg = dbg
        c.bank = [Res(f"bank{b}", excl=True) for b in range(8)]
        c.ident = c.cbf[:, 0:128]
        c.tinc = c.cbf[:, 128:256]
        c.tcar = c.cbf[:, 256:384]
        c.ones = c.cbf[:, 384:512]
        c.mstrict = c.cmask[:, 0:128]
        c.mincl = c.cmask[:, 128:256]
        c.Rconst = Res("const")
        S.ld("pool", c.cbf[:], cbf_d[:, :], writes=[c.Rconst])
        S.ld("sp", c.cmask[:], cf_d[:, 0:256], writes=[c.Rconst])
        S.barrier()

        block = st.enter_context(nc.Block())
        for idx, li in enumerate(layers):
            kind = LAYER_KIND[li]
            pre = f"l{li}_"
            if idx == 0:
                phase_norm(c, xs[idx], W[pre + "norm"])
                S.barrier()
            if kind == "sb":
                layer_sb(c, W[pre + "w_in"])
            elif kind == "mla":
                layer_mla(c, {k[3:]: v for k, v in W.items() if k.startswith(pre)})
            else:
                layer_swa(c, {k[3:]: v for k, v in W.items() if k.startswith(pre)})
            S.barrier()
            nxt = W[f"l{layers[idx + 1]}_norm"] if idx + 1 < len(layers) else None
            phase_out(c, xs[idx], xs[idx + 1], W[pre + "w_out"], nxt)
            S.barrier()
        S.emit(block)
    return nc


def norm_stages(c, Rg, src_of, Rsrc_of, tag):
    S = c.S
    xnT3 = c.xnT[:, :].rearrange("p (c t) -> p c t", c=8)
    junk = c.scr[:, 0:1024]
    Rjunk = Res("junk" + tag)
    xn = [c.scr[:, 1024 + 512 * j:1536 + 512 * j].bitcast(BF16) for j in range(2)]
    Rxn = [Res(f"xn{j}" + tag) for j in range(2)]
    NSTAT = 4
    Rst = [Res(f"nst{j}" + tag) for j in range(NSTAT)]
    c.RxnT = [Res(f"xnT{i}" + tag) for i in range(NT)]

    def cols(i):
        k = i % NSTAT
        return c.small[:, 4 * k:4 * k + 1], c.small[:, 4 * k + 1:4 * k + 2], c.small[:, 4 * k + 2:4 * k + 3], k

    def n_stat(i):
        ss, lg, rs, k = cols(i)
        S.act(junk, src_of(i), AF.Square, [Rsrc_of(i)], [Rjunk, Rst[k]], accum_out=ss)
        S.act(lg, ss, AF.Ln, [Rst[k]], [Rst[k]], scale=1.0 / D, bias=EPS)
        S.act(rs, lg, AF.Exp, [Rst[k]], [Rst[k]], scale=-0.5)

    def n_xn(i):
        ss, lg, rs, k = cols(i)
        b = i % 2
        S.stt(xn[b], src_of(i), rs, c.gbc[:], ALU.mult, ALU.mult, [Rsrc_of(i), Rst[k], Rg], [Rxn[b]])

    def n_tr(i):
        b = i % 2
        bank = 4 + b
        psT = c.ps[:, bank, :].bitcast(BF16)
        for ch in range(8):
            S.tr(psT[:, ch * 128:(ch + 1) * 128], xn[b][:, ch * 128:(ch + 1) * 128], c.ident,
                 [Rxn[b], c.Rconst], [c.bank[bank]])
        dst = xnT3[:, :, i * 128:(i + 1) * 128]
        src = psT.rearrange("p (c t) -> p c t", c=8)
        S.cp("act" if b == 0 else "dve", dst, src, [c.bank[bank]], [c.RxnT[i]])

    return n_stat, n_xn, n_tr


def phase_norm(c, x_d, g_d):
    S = c.S
    Rg = Res("gbc")
    S.ld("sp", c.gbc[:], g_d.partition_broadcast(128), writes=[Rg])
    hb = c.hbuf[:, :].bitcast(F32)
    NX = 4
    xin = [hb[:, 1024 * j:1024 * (j + 1)] for j in range(NX)]
    Rxin = [Res(f"xin{j}") for j in range(NX)]
    n_stat, n_xn, n_tr = norm_stages(c, Rg, lambda i: xin[i % NX], lambda i: Rxin[i % NX], "A")

    def n_ld(i):
        S.ld("sp", xin[i % NX], x_d[i * 128:(i + 1) * 128, :], writes=[Rxin[i % NX]])

    stages = [(n_tr, 4), (n_xn, 3), (n_stat, 2), (n_ld, 0)]
    for n in range(NT + 4):
        for fn, d in stages:
            i = n - d
            if 0 <= i < NT:
                fn(i)


def phase_out(c, x_d, y_d, wout_d, next_g=None):
    S = c.S
    wo = c.wbuf[:, :].rearrange("p (c n) -> p c n", c=8)
    Rwo = Res("wo")
    wv = wout_d.rearrange("(c p) n -> p c n", p=128)
    for h in range(2):
        S.ld("pool", wo[:, 4 * h:4 * h + 4, :], wv[:, 4 * h:4 * h + 4, :], writes=[Rwo])
    Rg = Res("gbcC")
    if next_g is not None:
        S.ld("sp", c.gbc[:], next_g.partition_broadcast(128), writes=[Rg])
    hb = c.hbuf[:, :].bitcast(F32)
    NX = 3
    xin = [hb[:, 1024 * j:1024 * (j + 1)] for j in range(NX)]
    yo = [hb[:, 3072 + 1024 * j:3072 + 1024 * (j + 1)] for j in range(NX)]
    Rxin = [Res(f"cxin{j}") for j in range(NX)]
    Ryo = [Res(f"yo{j}") for j in range(NX)]
    Rog = c.Rog

    def c_ld(i):
        S.ld("sp", xin[i % NX], x_d[i * 128:(i + 1) * 128, :], writes=[Rxin[i % NX]])

    def c_mm(i):
        for h in range(2):
            bank = 2 * (i % 2) + h
            for ch in range(8):
                S.mm(c.ps[:, bank, :], c.ogT[:, ch, i * 128:(i + 1) * 128], wo[:, ch, h * 512:(h + 1) * 512],
                     ch == 0, ch == 7, [Rwo, Rog[ch][i // 4]], [c.bank[bank]])

    def c_add(i):
        k = i % NX
        for h in range(2):
            bank = 2 * (i % 2) + h
            S.tt("dve", yo[k][:, h * 512:(h + 1) * 512], c.ps[:, bank, :], xin[k][:, h * 512:(h + 1) * 512],
                 ALU.add, [c.bank[bank], Rxin[k]], [Ryo[k]])
        S.ld("pool", y_d[i * 128:(i + 1) * 128, :], yo[k], reads=[Ryo[k]])

    stages = [(c_add, 3), (c_mm, 2), (c_ld, 0)]
    tail = 3
    if next_g is not None:
        n_stat, n_xn, n_tr = norm_stages(c, Rg, lambda i: yo[i % NX], lambda i: Ryo[i % NX], "C")
        stages = [(n_tr, 6), (n_xn, 5), (n_stat, 4)] + stages
        tail = 6
    for n in range(NT + tail):
        for fn, d in stages:
            i = n - d
            if 0 <= i < NT:
                fn(i)


def gate_phase(c, w_in, col0):
    S = c.S
    xnT3 = c.xnT[:, :].rearrange("p (c t) -> p c t", c=8)
    wv = w_in.rearrange("(c p) n -> p c n", p=128)
    wg = [c.wbuf[:, 6144 + 1024 * b: 6144 + 1024 * (b + 1)].rearrange("p (c n) -> p c n", c=8) for b in range(2)]
    Rwg = [Res("wg0"), Res("wg1")]
    c.Rog = [[Res(f"og{ch}_{tg}") for tg in range(NG)] for ch in range(8)]
    k = 0
    for ch in range(8):
        b = ch % 2
        S.ld("pool", wg[b], wv[:, :, col0 + ch * 128: col0 + (ch + 1) * 128], writes=[Rwg[b]])
        for tg in range(NG):
            bank = k % 4
            k += 1
            for cc in range(8):
                S.mm(c.ps[:, bank, :], wg[b][:, cc, :], xnT3[:, cc, tg * 512:(tg + 1) * 512], cc == 0, cc == 7,
                     [Rwg[b]] + c.RxnT[4 * tg:4 * tg + 4], [c.bank[bank]])
            S.act(c.ogT[:, ch, tg * 512:(tg + 1) * 512], c.ps[:, bank, :], AF.Silu, [c.bank[bank]], [c.Rog[ch][tg]])


def layer_sb(c, w_in):
    S = c.S
    gate_phase(c, w_in, 3072)
    S.barrier()
    xnT3 = c.xnT[:, :].rearrange("p (c t) -> p c t", c=8)
    wv = w_in.rearrange("(c p) n -> p c n", p=128)
    qT = c.hbuf[:, 0:S_LEN]
    kT = c.hbuf[:, S_LEN:2 * S_LEN]
    v = c.hbuf[:, 2 * S_LEN:3 * S_LEN].rearrange("p (i f) -> p i f", f=128)
    RqT = [Res(f"qT{g}") for g in range(NG)]
    RkT = [Res(f"kT{g}") for g in range(NG)]
    Rv = [Res(f"v{g}") for g in range(NG)]
    wsl = [c.wbuf[:, 3072 * b:3072 * (b + 1)].rearrange("p (s c n) -> p s c n", s=3, c=8) for b in range(2)]
    Rw = [Res("wsl0"), Res("wsl1")]
    NE, NL, NW, NA = 3, 4, 2, 2
    off = 0
    E, L, Wb, Ab = [], [], [], []
    for i in range(NE):
        E.append(c.scr[:, off:off + 1024].rearrange("p (h t) -> p h t", h=2)); off += 1024
    for i in range(NL):
        L.append(c.scr[:, off:off + 512].bitcast(BF16).rearrange("p (h t) -> p h t", h=2)); off += 512
    for i in range(NW):
        Wb.append(c.scr[:, off:off + 512].bitcast(BF16).rearrange("p (h t) -> p h t", h=2)); off += 512
    assert off <= 6144
    for i in range(NA):
        Ab.append(c.wbuf[:, 6144 + 1024 * i:6144 + 1024 * (i + 1)].rearrange("p (h t) -> p h t", h=2))
    RE = [Res(f"E{i}") for i in range(NE)]
    RL = [Res(f"L{i}") for i in range(NL)]
    RW = [Res(f"W{i}") for i in range(NW)]
    RA = [Res(f"A{i}") for i in range(NA)]
    AB = [4, 5]
    OBK = 6
    gcount = 0
    mask2 = c.mstrict.unsqueeze(1).to_broadcast([128, 2, 128])

    def load_w(hp):
        b = hp % 2
        for s in range(3):
            col = s * 1024 + hp * 128
            S.ld("pool", wsl[b][:, s, :, :], wv[:, :, col:col + 128], writes=[Rw[b]])

    def proj_groups(hp, banks):
        b = hp % 2
        k = 0
        for tg in reversed(range(NG)):
            for s_, (dst, Rd) in enumerate(((qT, RqT), (kT, RkT))):
                bank = banks[k % len(banks)]
                k += 1

                def g_qk(s_=s_, dst=dst, Rd=Rd, bank=bank, tg=tg):
                    for cc in range(8):
                        S.mm(c.ps[:, bank, :], wsl[b][:, s_, cc, :], xnT3[:, cc, tg * 512:(tg + 1) * 512], cc == 0, cc == 7,
                             [Rw[b]] + c.RxnT[4 * tg:4 * tg + 4], [c.bank[bank]])
                    S.cp("dve", dst[:, tg * 512:(tg + 1) * 512], c.ps[:, bank, :], [c.bank[bank]], [Rd[tg]])
                yield tg, g_qk
            bank = banks[k % len(banks)]
            k += 1
            for j in range(4):
                def g_v(j=j, bank=bank, tg=tg):
                    i = 4 * tg + j
                    for cc in range(8):
                        S.mm(c.ps[:, bank, j * 128:(j + 1) * 128], xnT3[:, cc, i * 128:(i + 1) * 128], wsl[b][:, 2, cc, :],
                             cc == 0, cc == 7, [Rw[b], c.RxnT[i]], [c.bank[bank]])
                    if j == 3:
                        S.cp("dve", v[:, 4 * tg:4 * tg + 4, :], c.ps[:, bank, :].rearrange("p (j f) -> p j f", f=128),
                             [c.bank[bank]], [Rv[tg]])
                yield tg, g_v

    load_w(0)
    load_w(1)
    for _, g_ in proj_groups(0, [0, 1, 2, 3]):
        g_()
    for hp in range(8):
        if 1 <= hp and hp + 1 < 8:
            load_w(hp + 1)
        pending = list(proj_groups(hp + 1, [7])) if hp + 1 < 8 else []
        G = []
        sweep_end = {}
        for qg in reversed(range(NG)):
            for kb in reversed(range(4 * qg + 4)):
                G.append((qg, kb))
            sweep_end[qg] = len(G) - 1
        n_g = len(G)

        def info(gi):
            qg, kb = G[gi]
            r = kb - 4 * qg
            c0 = r * 128 if r >= 0 else 0
            return qg, kb, r, c0, kb == 4 * qg + 3, kb == 0, gcount + gi

        def st_z(gi):
            qg, kb, r, c0, first, last, gid = info(gi)
            zp = 2 * (gid % 2)
            for hd in range(2):
                rows = slice(hd * 64, hd * 64 + 64)
                S.mm(c.ps[:, zp + hd, c0:512], kT[rows, kb * 128:(kb + 1) * 128], qT[rows, qg * 512 + c0:(qg + 1) * 512],
                     True, True, [RkT[kb // 4], RqT[qg]], [c.bank[zp + hd]])

        def st_e(gi):
            qg, kb, r, c0, first, last, gid = info(gi)
            zp = 2 * (gid % 2)
            eb = gid % NE
            S.act(E[eb][:, :, c0:512], c.ps[:, zp:zp + 2, c0:512], AF.Exp, [c.bank[zp], c.bank[zp + 1]], [RE[eb]],
                  scale=0.125)
            if r >= 0:
                S.tt("dve", E[eb][:, :, c0:c0 + 128], E[eb][:, :, c0:c0 + 128], mask2, ALU.mult,
                     [RE[eb], c.Rconst], [RE[eb]])

        def st_l(gi):
            qg, kb, r, c0, first, last, gid = info(gi)
            eb = gid % NE
            lb = gid % NL
            S.act(L[lb][:, :, c0:512], E[eb][:, :, c0:512], AF.Ln, [RE[eb]], [RL[lb]], bias=1.0, scale=1.0)

        def st_cum(gi):
            qg, kb, r, c0, first, last, gid = info(gi)
            lb = gid % NL
            for hd in range(2):
                S.mm(c.ps[:, AB[hd], c0:512], c.tinc, L[lb][:, hd, c0:512], first, False, [RL[lb], c.Rconst],
                     [c.bank[AB[hd]]], skip=True)

        def st_w(gi):
            qg, kb, r, c0, first, last, gid = info(gi)
            wb = gid % NW
            eb = gid % NE
            a_i = gid % NA
            S.act(Wb[wb][:, :, c0:512], c.ps[:, 4:6, c0:512], AF.Exp, [c.bank[4], c.bank[5]], [RW[wb]])
            S.tt("dve", Ab[a_i][:, :, c0:512], E[eb][:, :, c0:512], Wb[wb][:, :, c0:512], ALU.mult,
                 [RE[eb], RW[wb]], [RA[a_i]])

        def st_car(gi):
            qg, kb, r, c0, first, last, gid = info(gi)
            lb = gid % NL
            if not last:
                for hd in range(2):
                    S.mm(c.ps[:, AB[hd], c0:512], c.tcar, L[lb][:, hd, c0:512], False, False, [RL[lb], c.Rconst],
                         [c.bank[AB[hd]]], skip=True)

        def st_av(gi):
            qg, kb, r, c0, first, last, gid = info(gi)
            a_i = gid % NA
            for hd in range(2):
                rows = slice(hd * 64, hd * 64 + 64)
                S.mm(c.ps[rows, OBK, c0:512], v[:, kb, hd * 64:(hd + 1) * 64], Ab[a_i][:, hd, c0:512], first, last,
                     [RA[a_i], Rv[kb // 4]], [c.bank[OBK]], skip=True)
            if last:
                og = c.ogT[:, hp, qg * 512:(qg + 1) * 512]
                S.tt("dve", og, c.ps[:, OBK, :], og, ALU.mult, [c.bank[OBK], c.Rog[hp][qg]], [c.Rog[hp][qg]])

        stages = [(st_car, 4), (st_cum, 3), (st_av, 5), (st_z, 0), (st_w, 3), (st_l, 2), (st_e, 1)]
        for n in range(n_g + 5):
            for fn, d in stages:
                gi = n - d
                if 0 <= gi < n_g:
                    fn(gi)
            if pending and n >= sweep_end[pending[0][0]] + 6:
                pending.pop(0)[1]()
        for _, g_ in pending:
            g_()
        gcount += n_g


def layer_mla(c, w):
    S = c.S
    w_in = w["w_in"]
    gate_phase(c, w_in, 448)
    S.barrier()
    xnT3 = c.xnT[:, :].rearrange("p (c t) -> p c t", c=8)
    wv = w_in.rearrange("(c p) n -> p c n", p=128)
    sm = c.small
    scr = c.scr
    Rgn = Res("mla_g")
    gqa = c.gbc[:, 0:256]
    gkva = c.gbc[:, 256:384]
    gq192 = c.gbc[:, 384:576]
    gk192 = c.gbc[:, 576:768]
    gkpe = c.gbc[:, 704:768]
    S.ld("sp", gqa, w["q_a_norm"].partition_broadcast(128), writes=[Rgn])
    S.ld("sp", gkva, w["kv_a_norm"].partition_broadcast(128), writes=[Rgn])
    S.ld("sp", gq192, w["q_head_norm"].partition_broadcast(128), writes=[Rgn])
    S.ld("sp", gk192, w["k_head_norm"].partition_broadcast(128), writes=[Rgn])
    S.tt("dve", gq192[:, 0:128], gq192[:, 0:128], gk192[:, 0:128], ALU.mult, [Rgn], [Rgn])
    cosT = scr[:, 0:1024].rearrange("p (i f) -> p i f", f=32)
    sinT = scr[:, 1024:2048].rearrange("p (i f) -> p i f", f=32)
    Rtab = Res("ropetab")
    S.ld("sp", scr[:, 0:2048], c.cf_d[:, CF_COS:CF_COS + 2048], writes=[Rtab])
    kpeT = scr[:, 2048:4096].bitcast(BF16)
    qlnT3 = c.hbuf[:, 0:2 * S_LEN].rearrange("p (a t) -> p a t", a=2)
    kvlnT = c.hbuf[:, 2 * S_LEN:3 * S_LEN]
    sskpe = sm[:, 128:160]
    rstdk = sm[:, 160:192]
    Rqln = [Res(f"qln{g}") for g in range(NG)]
    Rkvln = [Res(f"kvln{g}") for g in range(NG)]
    Rkpe = [Res(f"kpe{g}") for g in range(NG)]
    Rsskpe = [Res(f"sskpe{g}") for g in range(NG)]

    def rope(x, out, i, Rx, Rout, t1, t2, Rt):
        cb = cosT[:, i, :].unsqueeze(1).to_broadcast([128, 2, 32])
        S.tt("dve", t1.rearrange("p (a f) -> p a f", a=2), x.rearrange("p (a f) -> p a f", a=2), cb, ALU.mult,
             [Rx, Rtab], [Rt])
        S.tt("dve", t2[:, 0:32], x[:, 32:64], sinT[:, i, :], ALU.mult, [Rx, Rtab], [Rt])
        S.tt("dve", t2[:, 32:64], x[:, 0:32], sinT[:, i, :], ALU.mult, [Rx, Rtab], [Rt])
        S.tt("dve", out[:, 0:32], t1[:, 0:32], t2[:, 0:32], ALU.subtract, [Rt], [Rout])
        S.tt("dve", out[:, 32:64], t1[:, 32:64], t2[:, 32:64], ALU.add, [Rt], [Rout])

    wlat = c.wbuf[:, 0:3584].rearrange("p (c n) -> p c n", c=8)
    Rwlat = Res("wlat")
    S.ld("pool", wlat[:, 0:4, :], wv[:, 0:4, 0:448], writes=[Rwlat])
    S.ld("pool", wlat[:, 4:8, :], wv[:, 4:8, 0:448], writes=[Rwlat])
    o = 4096
    junk = [scr[:, o + 448 * j:o + 448 * (j + 1)] for j in range(2)]; o += 896
    NLN = 3
    lnb = [scr[:, o + 192 * j:o + 192 * (j + 1)].bitcast(BF16) for j in range(NLN)]; o += 192 * NLN
    kp = [scr[:, o + 64 * j:o + 64 * (j + 1)] for j in range(2)]; o += 128
    t1 = [scr[:, o + 64 * j:o + 64 * (j + 1)] for j in range(2)]; o += 128
    t2 = [scr[:, o + 64 * j:o + 64 * (j + 1)] for j in range(2)]; o += 128
    kr = [scr[:, o + 32 * j:o + 32 * (j + 1)].bitcast(BF16) for j in range(2)]; o += 64
    assert o <= 6144, o
    Rjunk = [Res("junk0"), Res("junk1")]
    Rkp = [Res("kp0"), Res("kp1")]
    Rt = [Res("ropet0"), Res("ropet1")]
    Rlnb = [Res(f"lnb{j}") for j in range(NLN)]
    Rkr = [Res("kr0"), Res("kr1")]
    NSB = 4
    Rst = [Res(f"mst{j}") for j in range(NSB)]
    LB = [0, 1, 2, 3, 6, 7]

    def binfo(i):
        k = i % NSB
        sb_ = 192 + 8 * k
        return i // 4, LB[i % len(LB)], sm[:, sb_:sb_ + 2], sm[:, sb_ + 2:sb_ + 4], sm[:, sb_ + 4:sb_ + 6], k, i % 2, i % NLN

    def b_mm(i):
        tg, bank, ss2, lg2, rs2, k, pb, li = binfo(i)
        for cc in range(8):
            S.mm(c.ps[:, bank, 0:448], xnT3[:, cc, i * 128:(i + 1) * 128], wlat[:, cc, :], cc == 0, cc == 7,
                 [Rwlat, c.RxnT[i]], [c.bank[bank]])

    def b_sq(i):
        tg, bank, ss2, lg2, rs2, k, pb, li = binfo(i)
        S.act(junk[pb][:, 0:256], c.ps[:, bank, 0:256], AF.Square, [c.bank[bank]], [Rjunk[pb], Rst[k]], accum_out=ss2[:, 0:1])
        S.act(junk[pb][:, 256:384], c.ps[:, bank, 256:384], AF.Square, [c.bank[bank]], [Rjunk[pb], Rst[k]],
              accum_out=ss2[:, 1:2])
        S.act(junk[pb][:, 384:448], c.ps[:, bank, 384:448], AF.Square, [c.bank[bank]], [Rjunk[pb], Rsskpe[tg]],
              accum_out=sskpe[:, i:i + 1])

    def b_rs(i):
        tg, bank, ss2, lg2, rs2, k, pb, li = binfo(i)
        S.act(lg2[:, 0:1], ss2[:, 0:1], AF.Ln, [Rst[k]], [Rst[k]], scale=1.0 / 256, bias=EPS)
        S.act(lg2[:, 1:2], ss2[:, 1:2], AF.Ln, [Rst[k]], [Rst[k]], scale=1.0 / 128, bias=EPS)
        S.act(rs2, lg2, AF.Exp, [Rst[k]], [Rst[k]], scale=-0.5)

    def b_ln(i):
        tg, bank, ss2, lg2, rs2, k, pb, li = binfo(i)
        S.stt(lnb[li][:, 0:256], c.ps[:, bank, 0:256], rs2[:, 0:1], gqa, ALU.mult, ALU.mult,
              [c.bank[bank], Rst[k], Rgn], [Rlnb[li]])
        S.stt(lnb[li][:, 256:384], c.ps[:, bank, 256:384], rs2[:, 1:2], gkva, ALU.mult, ALU.mult,
              [c.bank[bank], Rst[k], Rgn], [Rlnb[li]])
        S.tt("dve", kp[pb], c.ps[:, bank, 384:448], gkpe, ALU.mult, [c.bank[bank], Rgn], [Rkp[pb]])

    def b_rope(i):
        tg, bank, ss2, lg2, rs2, k, pb, li = binfo(i)
        rope(kp[pb], kr[pb], i, Rkp[pb], Rkr[pb], t1[pb], t2[pb], Rt[pb])

    def b_tr(i):
        tg, bank, ss2, lg2, rs2, k, pb, li = binfo(i)
        tbank = 4 + pb
        psT = c.ps[:, tbank, :].bitcast(BF16)
        for a in range(3):
            S.tr(psT[:, a * 128:(a + 1) * 128], lnb[li][:, a * 128:(a + 1) * 128], c.ident,
                 [Rlnb[li], c.Rconst], [c.bank[tbank]])
        S.tr(psT[0:64, 384:512], kr[pb], c.ident, [Rkr[pb], c.Rconst], [c.bank[tbank]])
        S.cp("act", qlnT3[:, :, i * 128:(i + 1) * 128], psT[:, 0:256].rearrange("p (a t) -> p a t", a=2),
             [c.bank[tbank]], [Rqln[tg]])
        S.cp("dve", kvlnT[:, i * 128:(i + 1) * 128], psT[:, 256:384], [c.bank[tbank]], [Rkvln[tg]])
        S.cp("dve", kpeT[0:64, i * 128:(i + 1) * 128], psT[0:64, 384:512], [c.bank[tbank]], [Rkpe[tg]])

    bstages = [(b_tr, 5), (b_rope, 4), (b_ln, 3), (b_rs, 2), (b_sq, 1), (b_mm, 0)]
    for n in range(NT + 5):
        for fn, d in bstages:
            i = n - d
            if 0 <= i < NT:
                fn(i)
    S.barrier()
    wuq = c.wbuf[:, 0:3072].rearrange("p (a n) -> p a n", a=2)
    wukv = c.wbuf[:, 3072:5120]
    Rwu = Res("wu")
    S.ld("pool", wuq, w["w_uq"].rearrange("(a p) n -> p a n", p=128), writes=[Rwu])
    S.ld("pool", wukv, w["w_ukv"], writes=[Rwu])
    X = c.xnT
    qTn = X[:, 0:4096]
    qTp = X[:, 4096:8192]
    kTn = X[:, 8192:12288]
    vh = X[:, 12288:16384].rearrange("p (i f) -> p i f", f=128)
    NP = 3
    Pb = [X[:, 16384 + 512 * j:16384 + 512 * (j + 1)] for j in range(NP)]
    NQR = 3
    qr = [X[:, 18432 + 256 * j:18432 + 256 * j + 192] for j in range(NQR)]
    XF = X[:, 20480:24576].bitcast(F32)
    rcb = XF[:, 0:512]
    tbuf = XF[:, 512:1024]
    qpe = [XF[:, 1024 + 64 * j:1088 + 64 * j] for j in range(2)]
    u1 = [XF[:, 1152 + 64 * j:1216 + 64 * j] for j in range(2)]
    u2 = [XF[:, 1280 + 64 * j:1344 + 64 * j] for j in range(2)]
    junk2 = [XF[:, 1408 + 320 * j:1408 + 320 * (j + 1)] for j in range(2)]
    Rsum = [X[:, 24576 + 1024 * j:24576 + 1024 * (j + 1)].bitcast(F32) for j in range(2)]
    ones32 = X[:, 26624:26880].bitcast(F32)
    RRs = [Res("Rsum0"), Res("Rsum1")]
    Rones32 = Res("ones32")
    S.op("pool", lambda e, o=ones32: e.memset(o, 1.0), [], [Rones32])
    RqTn = [Res(f"qTn{g}") for g in range(NG)]
    RqTp = [Res(f"qTp{g}") for g in range(NG)]
    RkTn = [Res(f"kTn{g}") for g in range(NG)]
    Rvh = [Res(f"vh{g}") for g in range(NG)]
    Rrk = [Res(f"rstdk{g}") for g in range(NG)]
    RPb = [Res(f"mPb{j}") for j in range(NP)]
    Rqr = [Res(f"qr{j}") for j in range(NQR)]
    Rrc, Rtb = Res("rcb"), Res("tbuf")
    Rqpe = [Res("qpe0"), Res("qpe1")]
    Ru = [Res("u0"), Res("u1")]
    Rj2 = [Res("junk20"), Res("junk21")]
    NST = 4
    Rs2 = [Res(f"hst{j}") for j in range(NST)]
    QB = [0, 1, 2, 3, 6, 7]
    gid = 0
    sw = 0
    tcount = 0
    for h in range(8):
        for tg in range(NG):
            bank = 4 + tg % 2
            S.mm(c.ps[:, bank, :], wukv[:, h * 256:h * 256 + 128], kvlnT[:, tg * 512:(tg + 1) * 512], True, True,
                 [Rwu, Rkvln[tg]], [c.bank[bank]])
            S.cp("act", kTn[:, tg * 512:(tg + 1) * 512], c.ps[:, bank, :], [c.bank[bank]], [RkTn[tg]])

        def tinfo(i):
            t = tcount + i
            tg, j = divmod(i, 4)
            sb_ = 224 + 8 * (t % NST)
            return (t, tg, j, QB[t % len(QB)], slice(i * 128, (i + 1) * 128), sm[:, sb_:sb_ + 2], sm[:, sb_ + 2:sb_ + 4],
                    sm[:, sb_ + 4:sb_ + 5], t % NST, t % NQR, t % 2)

        def t_mm(i):
            t, tg, j, bank, tok, ss2, lg2, rsq, si, qi, pi = tinfo(i)
            S.mm(c.ps[:, bank, 0:192], qlnT3[:, 0, tok], wuq[:, 0, h * 192:(h + 1) * 192], True, False,
                 [Rwu, Rqln[tg]], [c.bank[bank]])
            S.mm(c.ps[:, bank, 0:192], qlnT3[:, 1, tok], wuq[:, 1, h * 192:(h + 1) * 192], False, True,
                 [Rwu, Rqln[tg]], [c.bank[bank]])
            S.mm(c.ps[:, bank, 192:448], kvlnT[:, tok], wukv[:, h * 256:(h + 1) * 256], True, True,
                 [Rwu, Rkvln[tg]], [c.bank[bank]])

        def t_sq(i):
            t, tg, j, bank, tok, ss2, lg2, rsq, si, qi, pi = tinfo(i)
            S.act(junk2[pi][:, 0:192], c.ps[:, bank, 0:192], AF.Square, [c.bank[bank]], [Rj2[pi], Rs2[si]],
                  accum_out=ss2[:, 0:1])
            S.act(junk2[pi][:, 192:320], c.ps[:, bank, 192:320], AF.Square, [c.bank[bank]], [Rj2[pi], Rs2[si]],
                  accum_out=ss2[:, 1:2])
            S.cp("act", vh[:, i, :], c.ps[:, bank, 320:448], [c.bank[bank]], [Rvh[tg]])

        def t_add(i):
            t, tg, j, bank, tok, ss2, lg2, rsq, si, qi, pi = tinfo(i)
            S.tt("dve", ss2[:, 1:2], ss2[:, 1:2], sskpe[:, i:i + 1], ALU.add, [Rs2[si], Rsskpe[tg]], [Rs2[si]])

        def t_rs(i):
            t, tg, j, bank, tok, ss2, lg2, rsq, si, qi, pi = tinfo(i)
            S.act(lg2, ss2, AF.Ln, [Rs2[si]], [Rs2[si]], scale=1.0 / 192, bias=EPS)
            S.act(rsq, lg2[:, 0:1], AF.Exp, [Rs2[si]], [Rs2[si]], scale=-0.5)
            S.act(rstdk[:, i:i + 1], lg2[:, 1:2], AF.Exp, [Rs2[si]], [Rrk[tg]], scale=-0.5, bias=-0.5 * math.log(192.0))

        def t_qn(i):
            t, tg, j, bank, tok, ss2, lg2, rsq, si, qi, pi = tinfo(i)
            S.stt(qr[qi][:, 0:128], c.ps[:, bank, 0:128], rsq, gq192[:, 0:128], ALU.mult, ALU.mult,
                  [c.bank[bank], Rs2[si], Rgn], [Rqr[qi]])
            S.stt(qpe[pi], c.ps[:, bank, 128:192], rsq, gq192[:, 128:192], ALU.mult, ALU.mult,
                  [c.bank[bank], Rs2[si], Rgn], [Rqpe[pi]])
            rope(qpe[pi], qr[qi][:, 128:192], i, Rqpe[pi], Rqr[qi], u1[pi], u2[pi], Ru[pi])

        def t_tr(i):
            t, tg, j, bank, tok, ss2, lg2, rsq, si, qi, pi = tinfo(i)
            tbank = 4 + tg % 2
            psT = c.ps[:, tbank, :].bitcast(BF16)
            S.tr(psT[:, j * 128:(j + 1) * 128], qr[qi][:, 0:128], c.ident, [Rqr[qi], c.Rconst], [c.bank[tbank]])
            S.tr(psT[0:64, 512 + j * 128:512 + (j + 1) * 128], qr[qi][:, 128:192], c.ident,
                 [Rqr[qi], c.Rconst], [c.bank[tbank]])
            if j == 3:
                S.cp("dve", qTn[:, tg * 512:(tg + 1) * 512], psT[:, 0:512], [c.bank[tbank]], [RqTn[tg]])
                S.cp("dve", qTp[0:64, tg * 512:(tg + 1) * 512], psT[0:64, 512:1024], [c.bank[tbank]], [RqTp[tg]])

        tstages = [(t_tr, 5), (t_qn, 4), (t_rs, 3), (t_add, 2), (t_sq, 1), (t_mm, 0)]
        for n in range(NT + 5):
            for fn, d in tstages:
                i = n - d
                if 0 <= i < NT:
                    fn(i)
        tcount += NT
        G = []
        for qg in range(NG):
            for kb in reversed(range(4 * qg + 4)):
                G.append((qg, kb))
        n_g = len(G)

        def info(gi):
            qg, kb = G[gi]
            r = kb - 4 * qg
            c0 = r * 128 if r >= 0 else 0
            return qg, kb, r, c0, kb == 4 * qg + 3, kb == 0, gid + gi, sw + qg

        def st_z(gi):
            qg, kb, r, c0, first, last, g_, sw_ = info(gi)
            zb = g_ % 4
            S.mm(c.ps[:, zb, c0:512], kTn[:, kb * 128:(kb + 1) * 128], qTn[:, qg * 512 + c0:(qg + 1) * 512], True, False,
                 [RkTn[kb // 4], RqTn[qg]], [c.bank[zb]])
            S.mm(c.ps[:, zb, c0:512], kpeT[0:64, kb * 128:(kb + 1) * 128], qTp[0:64, qg * 512 + c0:(qg + 1) * 512],
                 False, True, [Rkpe[kb // 4], RqTp[qg]], [c.bank[zb]])

        def st_e(gi):
            qg, kb, r, c0, first, last, g_, sw_ = info(gi)
            zb = g_ % 4
            pi = g_ % NP
            S.act(Pb[pi][:, c0:512], c.ps[:, zb, c0:512], AF.Exp, [c.bank[zb], Rrk[kb // 4]], [RPb[pi]],
                  scale=rstdk[:, kb:kb + 1])
            if r >= 0:
                S.tt("dve", Pb[pi][:, c0:c0 + 128], Pb[pi][:, c0:c0 + 128], c.mincl, ALU.mult,
                     [RPb[pi], c.Rconst], [RPb[pi]])

        def st_av(gi):
            qg, kb, r, c0, first, last, g_, sw_ = info(gi)
            pi = g_ % NP
            ob = 4 + sw_ % 2
            db = 6 + sw_ % 2
            ri = sw_ % 2
            S.mm(c.ps[:, ob, c0:512], vh[:, kb, :], Pb[pi][:, c0:512], first, last, [Rvh[kb // 4], RPb[pi]],
                 [c.bank[ob]], skip=True)
            S.mm(c.ps[:, db, c0:512], c.ones, Pb[pi][:, c0:512], first, last, [c.Rconst, RPb[pi]],
                 [c.bank[db]], skip=True)
            if last:
                S.act(rcb, c.ps[:, db, :], AF.Ln, [c.bank[db]], [Rrc])
                S.act(rcb, rcb, AF.Exp, [Rrc], [Rrc], scale=-1.0)
                S.tt("dve", tbuf, c.ps[:, ob, :], rcb, ALU.mult, [c.bank[ob], Rrc], [Rtb])
                og = c.ogT[:, h, qg * 512:(qg + 1) * 512]
                S.tt("dve", og, tbuf, og, ALU.mult, [Rtb, c.Rog[h][qg]], [c.Rog[h][qg]])

        stages = [(st_z, 0), (st_e, 1), (st_av, 2)]
        for n in range(n_g + 2):
            for fn, d in reversed(stages):
                gi = n - d
                if 0 <= gi < n_g:
                    fn(gi)
        gid += n_g
        sw += NG


def layer_swa(c, w):
    S = c.S
    w_in = w["w_in"]
    gate_phase(c, w_in, 1536)
    S.barrier()
    xnT3 = c.xnT[:, :].rearrange("p (c t) -> p c t", c=8)
    wv = w_in.rearrange("(c p) n -> p c n", p=128)
    qT3 = c.hbuf[:, 0:2 * S_LEN].rearrange("p (a t) -> p a t", a=2)
    kT2 = c.hbuf[:, 2 * S_LEN:3 * S_LEN]
    scr = c.scr
    off = 0
    vg = scr[:, off:off + 1024].bitcast(BF16).rearrange("p (i f) -> p i f", f=64); off += 1024
    junk = [scr[:, off + 320 * j:off + 320 * (j + 1)] for j in range(2)]; off += 640
    tmpq = [scr[:, off + 256 * j:off + 256 * (j + 1)] for j in range(2)]; off += 512
    NQN = 3
    qn = [scr[:, off + 128 * j:off + 128 * (j + 1)].bitcast(BF16) for j in range(NQN)]; off += 128 * NQN
    biasg = scr[:, off:off + 1024]; off += 1024
    Tb = scr[:, off:off + 1024]; off += 1024
    Pb = [scr[:, off:off + 512].bitcast(BF16), scr[:, off + 512:off + 1024].bitcast(BF16)]; off += 1024
    dnb = scr[:, off:off + 256]; off += 256
    gqk4 = scr[:, off:off + 256]; off += 256
    assert off <= 6144, off
    sm = c.small
    kscale = sm[:, 16:48]
    es16 = sm[:, 48:64]
    Rbias, RTb, Rdn, Rgqk, Res16 = (Res(n) for n in ("biasg", "Tb", "dnb", "gqk", "es16"))
    Rjunk = [Res("junk0"), Res("junk1")]
    Rtmpq = [Res("tmpq0"), Res("tmpq1")]
    Rqn = [Res(f"qn{j}") for j in range(NQN)]
    RPb = [Res("Pb0"), Res("Pb1")]
    NST = 4
    Rst = [Res(f"sst{j}") for j in range(NST)]
    RqT = [Res(f"qT{g}") for g in range(NG)]
    RkT = [Res(f"kT{g}") for g in range(NG)]
    Rvg = [Res(f"vg{g}") for g in range(NG)]
    Rks = [Res(f"ks{g}") for g in range(NG)]
    wtm = [c.wbuf[:, 4096 * b:4096 * b + 3072].rearrange("p (c n) -> p c n", c=8) for b in range(2)]
    wk2 = [c.wbuf[:, 4096 * b + 3072:4096 * (b + 1)].rearrange("p (c n) -> p c n", c=8) for b in range(2)]
    Rw = [Res("swaw0"), Res("swaw1")]
    gk4 = junk[0][:, 0:256]
    for j in range(4):
        S.ld("sp", gqk4[:, j * 64:(j + 1) * 64], w["q_head_norm"].partition_broadcast(128), writes=[Rgqk])
        S.ld("sp", gk4[:, j * 64:(j + 1) * 64], w["k_head_norm"].partition_broadcast(128), writes=[Rjunk[0]])
    S.tt("dve", gqk4, gqk4, gk4, ALU.mult, [Rjunk[0], Rgqk], [Rgqk])
    S.ld("sp", es16, w["sinks"].partition_broadcast(128), writes=[Res16])
    S.act(es16, es16, AF.Exp, [Res16], [Res16])

    def load_w(g):
        b = g % 2
        S.ld("pool", wtm[b][:, :, 0:256], wv[:, :, g * 256:(g + 1) * 256], writes=[Rw[b]])
        S.ld("pool", wtm[b][:, :, 256:320], wv[:, :, 1024 + g * 64:1024 + (g + 1) * 64], writes=[Rw[b]])
        S.ld("pool", wtm[b][:, :, 320:384], wv[:, :, 1280 + g * 64:1280 + (g + 1) * 64], writes=[Rw[b]])
        for d in range(2):
            S.ld("pool", wk2[b][:, :, d * 64:(d + 1) * 64], wv[:, :, 1024 + g * 64:1024 + (g + 1) * 64], writes=[Rw[b]])

    load_w(0)
    TB = [0, 1, 2, 3, 6, 7]
    tcount = 0
    acount = 0
    for g in range(4):
        b = g % 2
        if g + 1 < 4:
            load_w(g + 1)
        for kbi in range(2):
            S.ld("sp", biasg[:, kbi * 512:(kbi + 1) * 512],
                 c.cf_d[:, CF_BIAS + kbi * 2048 + g * 512:CF_BIAS + kbi * 2048 + (g + 1) * 512], writes=[Rbias])
        for tg in range(NG):
            bank = 4 + tg % 2
            for cc in range(8):
                S.mm(c.ps[:, bank, :], wk2[b][:, cc, :], xnT3[:, cc, tg * 512:(tg + 1) * 512], cc == 0, cc == 7,
                     [Rw[b]] + c.RxnT[4 * tg:4 * tg + 4], [c.bank[bank]])
            S.cp("act", kT2[:, tg * 512:(tg + 1) * 512], c.ps[:, bank, :], [c.bank[bank]], [RkT[tg]])

        def tinfo(i):
            t = tcount + i
            tg, j = divmod(i, 4)
            sb_ = 64 + 16 * (t % NST)
            return (t, tg, j, TB[t % len(TB)], sm[:, sb_:sb_ + 5], sm[:, sb_ + 5:sb_ + 10], sm[:, sb_ + 10:sb_ + 14],
                    t % NST, t % 2, t % NQN)

        def t_mm(i):
            t, tg, j, bank, ss5, lg5, rs4, si, pi, qi = tinfo(i)
            for cc in range(8):
                S.mm(c.ps[:, bank, 0:384], xnT3[:, cc, i * 128:(i + 1) * 128], wtm[b][:, cc, :], cc == 0, cc == 7,
                     [Rw[b], c.RxnT[i]], [c.bank[bank]])

        def t_sq(i):
            t, tg, j, bank, ss5, lg5, rs4, si, pi, qi = tinfo(i)
            S.act(junk[pi], c.ps[:, bank, 0:320], AF.Square, [c.bank[bank]], [Rjunk[pi]])
            S.cp("act", vg[:, i, :], c.ps[:, bank, 320:384], [c.bank[bank]], [Rvg[tg]])

        def t_red(i):
            t, tg, j, bank, ss5, lg5, rs4, si, pi, qi = tinfo(i)
            S.op("dve", lambda e, o=ss5, i_=junk[pi].rearrange("p (h f) -> p h f", f=64): e.reduce_sum(out=o, in_=i_, axis=AX.X),
                 [Rjunk[pi]], [Rst[si]])

        def t_rs(i):
            t, tg, j, bank, ss5, lg5, rs4, si, pi, qi = tinfo(i)
            S.act(lg5, ss5, AF.Ln, [Rst[si]], [Rst[si]], scale=1.0 / 64, bias=EPS)
            S.act(rs4, lg5[:, 0:4], AF.Exp, [Rst[si]], [Rst[si]], scale=-0.5)
            S.act(kscale[:, i:i + 1], lg5[:, 4:5], AF.Exp, [Rst[si]], [Rks[tg]], scale=-0.5, bias=math.log(0.125))

        def t_qn(i):
            t, tg, j, bank, ss5, lg5, rs4, si, pi, qi = tinfo(i)
            S.tt("dve", tmpq[pi].rearrange("p (h f) -> p h f", f=64),
                 c.ps[:, bank, 0:256].rearrange("p (h f) -> p h f", f=64),
                 rs4.unsqueeze(2).to_broadcast([128, 4, 64]), ALU.mult, [c.bank[bank], Rst[si]], [Rtmpq[pi]])
            S.tt("dve", qn[qi], tmpq[pi], gqk4, ALU.mult, [Rtmpq[pi], Rgqk], [Rqn[qi]])

        def t_tr(i):
            t, tg, j, bank, ss5, lg5, rs4, si, pi, qi = tinfo(i)
            tbank = 4 + tg % 2
            psT = c.ps[:, tbank, :].bitcast(BF16)
            for a in range(2):
                S.tr(psT[:, a * 512 + j * 128:a * 512 + (j + 1) * 128], qn[qi][:, a * 128:(a + 1) * 128], c.ident,
                     [Rqn[qi], c.Rconst], [c.bank[tbank]])
            if j == 3:
                S.cp("dve", qT3[:, :, tg * 512:(tg + 1) * 512], psT.rearrange("p (a t) -> p a t", a=2),
                     [c.bank[tbank]], [RqT[tg]])

        tstages = [(t_tr, 5), (t_qn, 4), (t_rs, 3), (t_red, 2), (t_sq, 1), (t_mm, 0)]
        for n in range(NT + 5):
            for fn, d in tstages:
                i = n - d
                if 0 <= i < NT:
                    fn(i)
        tcount += NT

        def ainfo(qb):
            a_ = acount + qb
            kbs = [(0, qb - 1), (1, qb)] if qb > 0 else [(1, qb)]
            return a_, kbs, 2 * (a_ % 2), a_ % 2, 4 + a_ % 2

        def a_qk(qb):
            a_, kbs, zb0, pbi, ob = ainfo(qb)
            for kbi, kb in kbs:
                for hq in range(4):
                    p = hq % 2
                    rows = slice(p * 64, p * 64 + 64)
                    col = (kbi * 2 + hq // 2) * 128
                    S.mm(c.ps[:, zb0 + p, col:col + 128], kT2[rows, kb * 128:(kb + 1) * 128],
                         qT3[rows, hq // 2, qb * 128:(qb + 1) * 128], True, True,
                         [RkT[kb // 4], RqT[qb // 4]], [c.bank[zb0 + p]])

        def a_bias(qb):
            a_, kbs, zb0, pbi, ob = ainfo(qb)
            for kbi, kb in kbs:
                for p in range(2):
                    src = c.ps[:, zb0 + p, kbi * 256:(kbi + 1) * 256].rearrange("q (a t) -> q a t", a=2)
                    dst = Tb[:, kbi * 512:(kbi + 1) * 512].rearrange("q (a p t) -> q p a t", a=2, p=2)[:, p]
                    bia = biasg[:, kbi * 512:(kbi + 1) * 512].rearrange("q (a p t) -> q p a t", a=2, p=2)[:, p]
                    S.stt(dst, src, kscale[:, kb:kb + 1], bia, ALU.mult, ALU.add,
                          [c.bank[zb0 + p], Rks[kb // 4], Rbias], [RTb])

        def a_exp(qb):
            a_, kbs, zb0, pbi, ob = ainfo(qb)
            lo = 0 if qb > 0 else 512
            S.act(Pb[pbi][:, lo:1024], Tb[:, lo:1024], AF.Exp, [RTb], [RPb[pbi]])

        def a_av(qb):
            a_, kbs, zb0, pbi, ob = ainfo(qb)
            for hq in range(4):
                rows = slice((hq % 2) * 64, (hq % 2) * 64 + 64)
                col = (hq // 2) * 128
                for n_, (kbi, kb) in enumerate(kbs):
                    S.mm(c.ps[rows, ob, col:col + 128], vg[:, kb, :], Pb[pbi][:, (kbi * 4 + hq) * 128:(kbi * 4 + hq + 1) * 128],
                         n_ == 0, n_ == len(kbs) - 1, [Rvg[kb // 4], RPb[pbi]], [c.bank[ob]])
                for n_, (kbi, kb) in enumerate(kbs):
                    S.mm(c.ps[rows, ob, 256 + col:256 + col + 128], c.ones[:, 0:64],
                         Pb[pbi][:, (kbi * 4 + hq) * 128:(kbi * 4 + hq + 1) * 128],
                         n_ == 0, n_ == len(kbs) - 1, [c.Rconst, RPb[pbi]], [c.bank[ob]])

        def a_out(qb):
            a_, kbs, zb0, pbi, ob = ainfo(qb)
            for hq in range(4):
                rows = slice((hq % 2) * 64, (hq % 2) * 64 + 64)
                col = (hq // 2) * 128
                h = 4 * g + hq
                S.ts("dve", dnb[rows, col:col + 128], c.ps[rows, ob, 256 + col:256 + col + 128], es16[rows, h:h + 1],
                     None, ALU.add, None, [c.bank[ob], Res16], [Rdn])
            S.act(dnb, dnb, AF.Ln, [Rdn], [Rdn])
            S.act(dnb, dnb, AF.Exp, [Rdn], [Rdn], scale=-1.0)
            S.tt("dve", dnb, c.ps[:, ob, 0:256], dnb, ALU.mult, [c.bank[ob], Rdn], [Rdn])
            og = c.ogT[:, 2 * g:2 * g + 2, qb * 128:(qb + 1) * 128]
            Rogs = [c.Rog[2 * g][qb // 4], c.Rog[2 * g + 1][qb // 4]]
            S.tt("dve", og, dnb.rearrange("p (a t) -> p a t", a=2), og, ALU.mult, [Rdn] + Rogs, Rogs)

        astages = [(a_out, 4), (a_av, 3), (a_exp, 2), (a_bias, 1), (a_qk, 0)]
        for n in range(NT + 4):
            for fn, d in astages:
                qb = n - d
                if 0 <= qb < NT:
                    fn(qb)
        acount += NT


LAUNCH_GROUPS = [[0, 1, 2, 3]]
_CONSTS = None


def run_layers(layers, xs, inputs):
    global _CONSTS
    if _CONSTS is None:
        _CONSTS = make_consts()
    cbf, cf = _CONSTS
    nc = build_program(layers)
    names = [n for li in layers for n in LAYER_WEIGHTS[li]]
    in_maps = []
    for b in range(len(xs)):
        m = {"x": np.ascontiguousarray(xs[b], dtype=np.float32), "cbf": cbf, "cf": cf}
        for n in names:
            m[n] = np.ascontiguousarray(inputs[n], dtype=np.float32)
        in_maps.append(m)
    res = run_bass_kernel_spmd(nc, in_maps, core_ids=list(range(len(xs))))
    return [r["y"] for r in res.results]


def kernel(**inputs):
    x = np.asarray(inputs["x"])
    xs = [x[b] for b in range(x.shape[0])]
    for grp in LAUNCH_GROUPS:
        xs = run_layers(grp, xs, inputs)
    return np.stack(xs, axis=0).astype(np.float32)
```

```python
import math
import numpy as np
import ml_dtypes
import concourse.bass as bass
import concourse.mybir as mybir
from concourse.bass_utils import run_bass_kernel_spmd

F32 = mybir.dt.float32
BF16 = mybir.dt.bfloat16
AF = mybir.ActivationFunctionType
ALU = mybir.AluOpType
AX = mybir.AxisListType

S_LEN = 4096
D = 1024
NT = S_LEN // 128
NG = S_LEN // 512
EPS = 1e-6

LAYER_KIND = ["sb", "mla", "swa", "sb"]
LAYER_WEIGHTS = [
    ["l0_norm", "l0_w_in", "l0_w_out"],
    ["l1_norm", "l1_w_in", "l1_q_a_norm", "l1_w_uq", "l1_kv_a_norm", "l1_w_ukv",
     "l1_q_head_norm", "l1_k_head_norm", "l1_w_out"],
    ["l2_norm", "l2_w_in", "l2_q_head_norm", "l2_k_head_norm", "l2_sinks", "l2_w_out"],
    ["l3_norm", "l3_w_in", "l3_w_out"],
]
WSHAPES = {
    "l0_norm": [1024], "l0_w_in": [1024, 4096], "l0_w_out": [1024, 1024],
    "l1_norm": [1024], "l1_w_in": [1024, 1472], "l1_q_a_norm": [256], "l1_w_uq": [256, 1536],
    "l1_kv_a_norm": [128], "l1_w_ukv": [128, 2048], "l1_q_head_norm": [192], "l1_k_head_norm": [192],
    "l1_w_out": [1024, 1024],
    "l2_norm": [1024], "l2_w_in": [1024, 2560], "l2_q_head_norm": [64], "l2_k_head_norm": [64],
    "l2_sinks": [16], "l2_w_out": [1024, 1024],
    "l3_norm": [1024], "l3_w_in": [1024, 4096], "l3_w_out": [1024, 1024],
}


class Res:
    __slots__ = ("name", "w", "r", "excl")

    def __init__(self, name, excl=False):
        self.name = name
        self.w = None
        self.r = {}
        self.excl = excl


class Sched:
    ENGS = ("pe", "act", "dve", "pool", "sp")

    def __init__(self, sems, dma_pools):
        self.sems = sems
        self.cnt = {k: 0 for k in sems}
        self.ops = {e: [] for e in self.ENGS}
        self.seen = {e: {} for e in self.ENGS}
        self.dma_pools = dma_pools
        self.dma_rr = {q: 0 for q in dma_pools}
        self.nwaits = 0

    def _wait(self, e, key, val):
        if val <= 0 or val <= self.seen[e].get(key, 0):
            return
        self.seen[e][key] = val
        sem = self.sems[key]
        self.ops[e].append(lambda eng, sem=sem, val=val: eng.wait_ge(sem, val))
        self.nwaits += 1

    def _sync(self, e, reads, writes, dma):
        rd = [r for r in reads if not r.excl]
        wr = list(writes) + [r for r in reads if r.excl]
        need = []
        for r in rd:
            if r.w is not None:
                need.append((r.w, True))
        for r in wr:
            if r.w is not None:
                need.append((r.w, False))
            for ev in r.r.values():
                need.append((ev, False))
        for (key, val, eng), raw in need:
            if eng == e and not dma:
                if e == "pe":
                    continue
            self._wait(e, key, val)
        return rd, wr

    def _update(self, rd, wr, ev):
        for r in rd:
            r.r[ev[0]] = ev
        for r in wr:
            r.w = ev
            r.r = {}

    def op(self, e, fn, reads=(), writes=()):
        rd, wr = self._sync(e, reads, writes, False)
        self.cnt[e] += 1
        sem = self.sems[e]
        self.ops[e].append(lambda eng, fn=fn, sem=sem: fn(eng).then_inc(sem, 1))
        self._update(rd, wr, (e, self.cnt[e], e))

    def dma(self, q, fn, reads=(), writes=()):
        pool = self.dma_pools[q]
        k = pool[self.dma_rr[q] % len(pool)]
        self.dma_rr[q] += 1
        self._wait(q, k, self.cnt[k])
        rd, wr = self._sync(q, reads, writes, True)
        self.cnt[k] += 16
        sem = self.sems[k]
        self.ops[q].append(lambda eng, fn=fn, sem=sem: fn(eng).then_inc(sem, 16))
        self._update(rd, wr, (k, self.cnt[k], None))

    def barrier(self, engines=None):
        for e in (engines or self.ENGS):
            for k, v in self.cnt.items():
                if k != e:
                    self._wait(e, k, v)


    def mm(self, out, lhsT, rhs, start, stop, reads, writes, skip=False):
        if skip:
            self.op("pe", lambda e: e.matmul(out, lhsT=lhsT, rhs=rhs, start=start, stop=stop, skip_group_check=True),
                    reads, writes)
        else:
            self.op("pe", lambda e: e.matmul(out, lhsT=lhsT, rhs=rhs, start=start, stop=stop), reads, writes)

    def tr(self, out, in_, ident, reads, writes):
        self.op("pe", lambda e: e.transpose(out=out, in_=in_, identity=ident), reads, writes)

    def act(self, out, in_, func, reads, writes, **kw):
        self.op("act", lambda e: e.activation(out=out, in_=in_, func=func, **kw), reads, writes)

    def tt(self, eng, out, in0, in1, op, reads, writes):
        self.op(eng, lambda e: e.tensor_tensor(out=out, in0=in0, in1=in1, op=op), reads, writes)

    def stt(self, out, in0, scalar, in1, op0, op1, reads, writes):
        self.op("dve", lambda e: e.scalar_tensor_tensor(out=out, in0=in0, scalar=scalar, in1=in1, op0=op0, op1=op1),
                reads, writes)

    def ts(self, eng, out, in0, s1, s2, op0, op1, reads, writes):
        if op1 is None:
            self.op(eng, lambda e: e.tensor_scalar(out=out, in0=in0, scalar1=s1, scalar2=None, op0=op0), reads, writes)
        else:
            self.op(eng, lambda e: e.tensor_scalar(out=out, in0=in0, scalar1=s1, scalar2=s2, op0=op0, op1=op1),
                    reads, writes)

    def cp(self, eng, out, in_, reads, writes):
        if eng == "act":
            self.op("act", lambda e: e.copy(out=out, in_=in_), reads, writes)
        else:
            self.op(eng, lambda e: e.tensor_copy(out=out, in_=in_), reads, writes)

    def ld(self, q, out, in_, reads=(), writes=()):
        self.dma(q, lambda e: e.dma_start(out=out, in_=in_), reads, writes)

    def emit(self, block):
        def mk(e):
            def body(eng):
                for f in self.ops[e]:
                    f(eng)
            return body
        block.tensor(mk("pe"))
        block.scalar(mk("act"))
        block.vector(mk("dve"))
        block.gpsimd(mk("pool"))
        block.sync(mk("sp"))


def make_consts():
    j = np.arange(128)[:, None]
    s = np.arange(128)[None, :]
    ident = (j == s).astype(np.float32)
    tinc = -(j >= s).astype(np.float32)
    tcar = -(j < s).astype(np.float32)
    ones = np.ones((128, 128), np.float32)
    cbf = np.concatenate([ident, tinc, tcar, ones], axis=1)
    mstrict = (s > j).astype(np.float32)
    mincl = (s >= j).astype(np.float32)
    half = 32
    inv_freq = (10000.0 ** (-np.arange(half, dtype=np.float32) / half)).astype(np.float32)
    pos = np.arange(S_LEN, dtype=np.float32)
    ang = (pos[:, None] * inv_freq[None, :]).astype(np.float32)
    cos = np.cos(ang).astype(np.float32)
    sin = np.sin(ang).astype(np.float32)
    cosT = cos.reshape(NT, 128, 32).transpose(1, 0, 2).reshape(128, NT * 32)
    sinT = sin.reshape(NT, 128, 32).transpose(1, 0, 2).reshape(128, NT * 32)
    slopes = (2.0 ** (-8.0 * np.arange(1, 17, dtype=np.float32) / 16)).astype(np.float32)
    NEG = -30000.0
    bias = np.zeros((128, 2, 16, 128), np.float32)
    for kbi in range(2):
        rel = (s + 128 - (j + 128 * kbi)).astype(np.float32)
        valid = (rel >= 0) & (rel < 128)
        for h in range(16):
            bias[:, kbi, h, :] = np.where(valid, -slopes[h] * rel, NEG)
    cf = np.concatenate([mstrict, mincl, cosT, sinT, bias.reshape(128, -1)], axis=1).astype(np.float32)
    return cbf.astype(np.float32), cf


CF_MSTRICT = 0
CF_MINCL = 128
CF_COS = 256
CF_SIN = 256 + NT * 32
CF_BIAS = 256 + 2 * NT * 32
CF_TOTAL = CF_BIAS + 2 * 16 * 128


class Ctx:
    pass


def build_program(layers, dbg=None):
    nc = bass.Bass("TRN2", target_bir_lowering=False)
    x_in = nc.dram_tensor("x", [S_LEN, D], F32, kind="ExternalInput").ap()
    y_out = nc.dram_tensor("y", [S_LEN, D], F32, kind="ExternalOutput").ap()
    cbf_d = nc.dram_tensor("cbf", [128, 512], F32, kind="ExternalInput").ap()
    cf_d = nc.dram_tensor("cf", [128, CF_TOTAL], F32, kind="ExternalInput").ap()
    W = {}
    for li in layers:
        for n in LAYER_WEIGHTS[li]:
            W[n] = nc.dram_tensor(n, WSHAPES[n], F32, kind="ExternalInput").ap()
    xs = [x_in]
    for i in range(len(layers) - 1):
        xs.append(nc.dram_tensor(f"xmid{i}", [S_LEN, D], F32, kind="Internal").ap())
    xs.append(y_out)

    from contextlib import ExitStack
    with ExitStack() as st:
        def sb(name, shape, dt):
            return st.enter_context(nc.sbuf_tensor(name, shape, dt))
        c = Ctx()
        c.nc = nc
        c.xnT = sb("xnT", [128, 8 * S_LEN], BF16)
        c.ogT = sb("ogT", [128, 8, S_LEN], BF16)
        c.hbuf = sb("hbuf", [128, 3 * S_LEN], BF16)
        c.wbuf = sb("wbuf", [128, 8192], BF16)
        c.scr = sb("scr", [128, 6144], F32)
        c.gbc = sb("gbc", [128, 1024], F32)
        c.cbf = sb("cbfs", [128, 512], BF16)
        c.cmask = sb("cmask", [128, 256], F32)
        c.small = sb("small", [128, 512], F32)
        c.ps = st.enter_context(nc.psum_tensor("ps", [128, 8, 512], F32))
        sem_names = list(Sched.ENGS) + [f"d{i}" for i in range(20)]
        sems = {k: st.enter_context(nc.semaphore(f"s_{k}")) for k in sem_names}
        S = Sched(sems, {"sp": [f"d{i}" for i in range(0, 8)],
                         "pool": [f"d{i}" for i in range(8, 16)],
                         "act": [f"d{i}" for i in range(16, 20)]})
        c.S = S
        c.W = W
        c.cf_d = cf_d
        c.dbg = dbg
        c.bank = [Res(f"bank{b}", excl=True) for b in range(8)]
        c.ident = c.cbf[:, 0:128]
        c.tinc = c.cbf[:, 128:256]
        c.tcar = c.cbf[:, 256:384]
        c.ones = c.cbf[:, 384:512]
        c.mstrict = c.cmask[:, 0:128]
        c.mincl = c.cmask[:, 128:256]
        c.Rconst = Res("const")
        S.ld("pool", c.cbf[:], cbf_d[:, :], writes=[c.Rconst])
        S.ld("sp", c.cmask[:], cf_d[:, 0:256], writes=[c.Rconst])
        S.barrier()

        block = st.enter_context(nc.Block())
        for idx, li in enumerate(layers):
            kind = LAYER_KIND[li]
            pre = f"l{li}_"
            if idx == 0:
                phase_norm(c, xs[idx], W[pre + "norm"])
                S.barrier()
            if kind == "sb":
                layer_sb(c, W[pre + "w_in"])
            elif kind == "mla":
                layer_mla(c, {k[3:]: v for k, v in W.items() if k.startswith(pre)})
            else:
                layer_swa(c, {k[3:]: v for k, v in W.items() if k.startswith(pre)})
            S.barrier()
            nxt = W[f"l{layers[idx + 1]}_norm"] if idx + 1 < len(layers) else None
            phase_out(c, xs[idx], xs[idx + 1], W[pre + "w_out"], nxt)
            S.barrier()
        S.emit(block)
    return nc


def norm_stages(c, Rg, src_of, Rsrc_of, tag):
    S = c.S
    xnT3 = c.xnT[:, :].rearrange("p (c t) -> p c t", c=8)
    junk = c.scr[:, 0:1024]
    Rjunk = Res("junk" + tag)
    xn = [c.scr[:, 1024 + 512 * j:1536 + 512 * j].bitcast(BF16) for j in range(2)]
    Rxn = [Res(f"xn{j}" + tag) for j in range(2)]
    NSTAT = 4
    Rst = [Res(f"nst{j}" + tag) for j in range(NSTAT)]
    c.RxnT = [Res(f"xnT{i}" + tag) for i in range(NT)]

    def cols(i):
        k = i % NSTAT
        return c.small[:, 4 * k:4 * k + 1], c.small[:, 4 * k + 1:4 * k + 2], c.small[:, 4 * k + 2:4 * k + 3], k

    def n_stat(i):
        ss, lg, rs, k = cols(i)
        S.act(junk, src_of(i), AF.Square, [Rsrc_of(i)], [Rjunk, Rst[k]], accum_out=ss)
        S.act(lg, ss, AF.Ln, [Rst[k]], [Rst[k]], scale=1.0 / D, bias=EPS)
        S.act(rs, lg, AF.Exp, [Rst[k]], [Rst[k]], scale=-0.5)

    def n_xn(i):
        ss, lg, rs, k = cols(i)
        b = i % 2
        S.stt(xn[b], src_of(i), rs, c.gbc[:], ALU.mult, ALU.mult, [Rsrc_of(i), Rst[k], Rg], [Rxn[b]])

    def n_tr(i):
        b = i % 2
        bank = 4 + b
        psT = c.ps[:, bank, :].bitcast(BF16)
        for ch in range(8):
            S.tr(psT[:, ch * 128:(ch + 1) * 128], xn[b][:, ch * 128:(ch + 1) * 128], c.ident,
                 [Rxn[b], c.Rconst], [c.bank[bank]])
        dst = xnT3[:, :, i * 128:(i + 1) * 128]
        src = psT.rearrange("p (c t) -> p c t", c=8)
        S.cp("act" if b == 0 else "dve", dst, src, [c.bank[bank]], [c.RxnT[i]])

    return n_stat, n_xn, n_tr


def phase_norm(c, x_d, g_d):
    S = c.S
    Rg = Res("gbc")
    S.ld("sp", c.gbc[:], g_d.partition_broadcast(128), writes=[Rg])
    hb = c.hbuf[:, :].bitcast(F32)
    NX = 4
    xin = [hb[:, 1024 * j:1024 * (j + 1)] for j in range(NX)]
    Rxin = [Res(f"xin{j}") for j in range(NX)]
    n_stat, n_xn, n_tr = norm_stages(c, Rg, lambda i: xin[i % NX], lambda i: Rxin[i % NX], "A")

    def n_ld(i):
        S.ld("sp", xin[i % NX], x_d[i * 128:(i + 1) * 128, :], writes=[Rxin[i % NX]])

    stages = [(n_tr, 4), (n_xn, 3), (n_stat, 2), (n_ld, 0)]
    for n in range(NT + 4):
        for fn, d in stages:
            i = n - d
            if 0 <= i < NT:
                fn(i)


def phase_out(c, x_d, y_d, wout_d, next_g=None):
    S = c.S
    wo = c.wbuf[:, :].rearrange("p (c n) -> p c n", c=8)
    Rwo = Res("wo")
    wv = wout_d.rearrange("(c p) n -> p c n", p=128)
    for h in range(2):
        S.ld("pool", wo[:, 4 * h:4 * h + 4, :], wv[:, 4 * h:4 * h + 4, :], writes=[Rwo])
    Rg = Res("gbcC")
    if next_g is not None:
        S.ld("sp", c.gbc[:], next_g.partition_broadcast(128), writes=[Rg])
    hb = c.hbuf[:, :].bitcast(F32)
    NX = 3
    xin = [hb[:, 1024 * j:1024 * (j + 1)] for j in range(NX)]
    yo = [hb[:, 3072 + 1024 * j:3072 + 1024 * (j + 1)] for j in range(NX)]
    Rxin = [Res(f"cxin{j}") for j in range(NX)]
    Ryo = [Res(f"yo{j}") for j in range(NX)]
    Rog = c.Rog

    def c_ld(i):
        S.ld("sp", xin[i % NX], x_d[i * 128:(i + 1) * 128, :], writes=[Rxin[i % NX]])

    def c_mm(i):
        for h in range(2):
            bank = 2 * (i % 2) + h
            for ch in range(8):
                S.mm(c.ps[:, bank, :], c.ogT[:, ch, i * 128:(i + 1) * 128], wo[:, ch, h * 512:(h + 1) * 512],
                     ch == 0, ch == 7, [Rwo, Rog[ch][i // 4]], [c.bank[bank]])

    def c_add(i):
        k = i % NX
        for h in range(2):
            bank = 2 * (i % 2) + h
            S.tt("dve", yo[k][:, h * 512:(h + 1) * 512], c.ps[:, bank, :], xin[k][:, h * 512:(h + 1) * 512],
                 ALU.add, [c.bank[bank], Rxin[k]], [Ryo[k]])
        S.ld("pool", y_d[i * 128:(i + 1) * 128, :], yo[k], reads=[Ryo[k]])

    stages = [(c_add, 3), (c_mm, 2), (c_ld, 0)]
    tail = 3
    if next_g is not None:
        n_stat, n_xn, n_tr = norm_stages(c, Rg, lambda i: yo[i % NX], lambda i: Ryo[i % NX], "C")
        stages = [(n_tr, 6), (n_xn, 5), (n_stat, 4)] + stages
        tail = 6
    for n in range(NT + tail):
        for fn, d in stages:
            i = n - d
            if 0 <= i < NT:
                fn(i)


def gate_phase(c, w_in, col0):
    S = c.S
    xnT3 = c.xnT[:, :].rearrange("p (c t) -> p c t", c=8)
    wv = w_in.rearrange("(c p) n -> p c n", p=128)
    wg = [c.wbuf[:, 6144 + 1024 * b: 6144 + 1024 * (b + 1)].rearrange("p (c n) -> p c n", c=8) for b in range(2)]
    Rwg = [Res("wg0"), Res("wg1")]
    c.Rog = [[Res(f"og{ch}_{tg}") for tg in range(NG)] for ch in range(8)]
    k = 0
    for ch in range(8):
        b = ch % 2
        S.ld("pool", wg[b], wv[:, :, col0 + ch * 128: col0 + (ch + 1) * 128], writes=[Rwg[b]])
        for tg in range(NG):
            bank = k % 4
            k += 1
            for cc in range(8):
                S.mm(c.ps[:, bank, :], wg[b][:, cc, :], xnT3[:, cc, tg * 512:(tg + 1) * 512], cc == 0, cc == 7,
                     [Rwg[b]] + c.RxnT[4 * tg:4 * tg + 4], [c.bank[bank]])
            S.act(c.ogT[:, ch, tg * 512:(tg + 1) * 512], c.ps[:, bank, :], AF.Silu, [c.bank[bank]], [c.Rog[ch][tg]])


def layer_sb(c, w_in):
    S = c.S
    gate_phase(c, w_in, 3072)
    S.barrier()
    xnT3 = c.xnT[:, :].rearrange("p (c t) -> p c t", c=8)
    wv = w_in.rearrange("(c p) n -> p c n", p=128)
    qT = c.hbuf[:, 0:S_LEN]
    kT = c.hbuf[:, S_LEN:2 * S_LEN]
    v = c.hbuf[:, 2 * S_LEN:3 * S_LEN].rearrange("p (i f) -> p i f", f=128)
    RqT = [Res(f"qT{g}") for g in range(NG)]
    RkT = [Res(f"kT{g}") for g in range(NG)]
    Rv = [Res(f"v{g}") for g in range(NG)]
    wsl = [c.wbuf[:, 3072 * b:3072 * (b + 1)].rearrange("p (s c n) -> p s c n", s=3, c=8) for b in range(2)]
    Rw = [Res("wsl0"), Res("wsl1")]
    NE, NL, NW, NA = 3, 4, 2, 2
    off = 0
    E, L, Wb, Ab = [], [], [], []
    for i in range(NE):
        E.append(c.scr[:, off:off + 1024].rearrange("p (h t) -> p h t", h=2)); off += 1024
    for i in range(NL):
        L.append(c.scr[:, off:off + 512].bitcast(BF16).rearrange("p (h t) -> p h t", h=2)); off += 512
    for i in range(NW):
        Wb.append(c.scr[:, off:off + 512].bitcast(BF16).rearrange("p (h t) -> p h t", h=2)); off += 512
    assert off <= 6144
    for i in range(NA):
        Ab.append(c.wbuf[:, 6144 + 1024 * i:6144 + 1024 * (i + 1)].rearrange("p (h t) -> p h t", h=2))
    RE = [Res(f"E{i}") for i in range(NE)]
    RL = [Res(f"L{i}") for i in range(NL)]
    RW = [Res(f"W{i}") for i in range(NW)]
    RA = [Res(f"A{i}") for i in range(NA)]
    AB = [4, 5]
    OBK = 6
    gcount = 0
    mask2 = c.mstrict.unsqueeze(1).to_broadcast([128, 2, 128])

    def load_w(hp):
        b = hp % 2
        for s in range(3):
            col = s * 1024 + hp * 128
            S.ld("pool", wsl[b][:, s, :, :], wv[:, :, col:col + 128], writes=[Rw[b]])

    def proj_groups(hp, banks):
        b = hp % 2
        k = 0
        for tg in reversed(range(NG)):
            for s_, (dst, Rd) in enumerate(((qT, RqT), (kT, RkT))):
                bank = banks[k % len(banks)]
                k += 1

                def g_qk(s_=s_, dst=dst, Rd=Rd, bank=bank, tg=tg):
                    for cc in range(8):
                        S.mm(c.ps[:, bank, :], wsl[b][:, s_, cc, :], xnT3[:, cc, tg * 512:(tg + 1) * 512], cc == 0, cc == 7,
                             [Rw[b]] + c.RxnT[4 * tg:4 * tg + 4], [c.bank[bank]])
                    S.cp("dve", dst[:, tg * 512:(tg + 1) * 512], c.ps[:, bank, :], [c.bank[bank]], [Rd[tg]])
                yield tg, g_qk
            bank = banks[k % len(banks)]
            k += 1
            for j in range(4):
                def g_v(j=j, bank=bank, tg=tg):
                    i = 4 * tg + j
                    for cc in range(8):
                        S.mm(c.ps[:, bank, j * 128:(j + 1) * 128], xnT3[:, cc, i * 128:(i + 1) * 128], wsl[b][:, 2, cc, :],
                             cc == 0, cc == 7, [Rw[b], c.RxnT[i]], [c.bank[bank]])
                    if j == 3:
                        S.cp("dve", v[:, 4 * tg:4 * tg + 4, :], c.ps[:, bank, :].rearrange("p (j f) -> p j f", f=128),
                             [c.bank[bank]], [Rv[tg]])
                yield tg, g_v

    load_w(0)
    load_w(1)
    for _, g_ in proj_groups(0, [0, 1, 2, 3]):
        g_()
    for hp in range(8):
        if 1 <= hp and hp + 1 < 8:
            load_w(hp + 1)
        pending = list(proj_groups(hp + 1, [7])) if hp + 1 < 8 else []
        G = []
        sweep_end = {}
        for qg in reversed(range(NG)):
            for kb in reversed(range(4 * qg + 4)):
                G.append((qg, kb))
            sweep_end[qg] = len(G) - 1
        n_g = len(G)

        def info(gi):
            qg, kb = G[gi]
            r = kb - 4 * qg
            c0 = r * 128 if r >= 0 else 0
            return qg, kb, r, c0, kb == 4 * qg + 3, kb == 0, gcount + gi

        def st_z(gi):
            qg, kb, r, c0, first, last, gid = info(gi)
            zp = 2 * (gid % 2)
            for hd in range(2):
                rows = slice(hd * 64, hd * 64 + 64)
                S.mm(c.ps[:, zp + hd, c0:512], kT[rows, kb * 128:(kb + 1) * 128], qT[rows, qg * 512 + c0:(qg + 1) * 512],
                     True, True, [RkT[kb // 4], RqT[qg]], [c.bank[zp + hd]])

        def st_e(gi):
            qg, kb, r, c0, first, last, gid = info(gi)
            zp = 2 * (gid % 2)
            eb = gid % NE
            S.act(E[eb][:, :, c0:512], c.ps[:, zp:zp + 2, c0:512], AF.Exp, [c.bank[zp], c.bank[zp + 1]], [RE[eb]],
                  scale=0.125)
            if r >= 0:
                S.tt("dve", E[eb][:, :, c0:c0 + 128], E[eb][:, :, c0:c0 + 128], mask2, ALU.mult,
                     [RE[eb], c.Rconst], [RE[eb]])

        def st_l(gi):
            qg, kb, r, c0, first, last, gid = info(gi)
            eb = gid % NE
            lb = gid % NL
            S.act(L[lb][:, :, c0:512], E[eb][:, :, c0:512], AF.Ln, [RE[eb]], [RL[lb]], bias=1.0, scale=1.0)

        def st_cum(gi):
            qg, kb, r, c0, first, last, gid = info(gi)
            lb = gid % NL
            for hd in range(2):
                S.mm(c.ps[:, AB[hd], c0:512], c.tinc, L[lb][:, hd, c0:512], first, False, [RL[lb], c.Rconst],
                     [c.bank[AB[hd]]], skip=True)

        def st_w(gi):
            qg, kb, r, c0, first, last, gid = info(gi)
            wb = gid % NW
            eb = gid % NE
            a_i = gid % NA
            S.act(Wb[wb][:, :, c0:512], c.ps[:, 4:6, c0:512], AF.Exp, [c.bank[4], c.bank[5]], [RW[wb]])
            S.tt("dve", Ab[a_i][:, :, c0:512], E[eb][:, :, c0:512], Wb[wb][:, :, c0:512], ALU.mult,
                 [RE[eb], RW[wb]], [RA[a_i]])

        def st_car(gi):
            qg, kb, r, c0, first, last, gid = info(gi)
            lb = gid % NL
            if not last:
                for hd in range(2):
                    S.mm(c.ps[:, AB[hd], c0:512], c.tcar, L[lb][:, hd, c0:512], False, False, [RL[lb], c.Rconst],
                         [c.bank[AB[hd]]], skip=True)

        def st_av(gi):
            qg, kb, r, c0, first, last, gid = info(gi)
            a_i = gid % NA
            for hd in range(2):
                rows = slice(hd * 64, hd * 64 + 64)
                S.mm(c.ps[rows, OBK, c0:512], v[:, kb, hd * 64:(hd + 1) * 64], Ab[a_i][:, hd, c0:512], first, last,
                     [RA[a_i], Rv[kb // 4]], [c.bank[OBK]], skip=True)
            if last:
                og = c.ogT[:, hp, qg * 512:(qg + 1) * 512]
                S.tt("dve", og, c.ps[:, OBK, :], og, ALU.mult, [c.bank[OBK], c.Rog[hp][qg]], [c.Rog[hp][qg]])

        stages = [(st_car, 4), (st_cum, 3), (st_av, 5), (st_z, 0), (st_w, 3), (st_l, 2), (st_e, 1)]
        for n in range(n_g + 5):
            for fn, d in stages:
                gi = n - d
                if 0 <= gi < n_g:
                    fn(gi)
            if pending and n >= sweep_end[pending[0][0]] + 6:
                pending.pop(0)[1]()
        for _, g_ in pending:
            g_()
        gcount += n_g


def layer_mla(c, w):
    S = c.S
    w_in = w["w_in"]
    gate_phase(c, w_in, 448)
    S.barrier()
    xnT3 = c.xnT[:, :].rearrange("p (c t) -> p c t", c=8)
    wv = w_in.rearrange("(c p) n -> p c n", p=128)
    sm = c.small
    scr = c.scr
    Rgn = Res("mla_g")
    gqa = c.gbc[:, 0:256]
    gkva = c.gbc[:, 256:384]
    gq192 = c.gbc[:, 384:576]
    gk192 = c.gbc[:, 576:768]
    gkpe = c.gbc[:, 704:768]
    S.ld("sp", gqa, w["q_a_norm"].partition_broadcast(128), writes=[Rgn])
    S.ld("sp", gkva, w["kv_a_norm"].partition_broadcast(128), writes=[Rgn])
    S.ld("sp", gq192, w["q_head_norm"].partition_broadcast(128), writes=[Rgn])
    S.ld("sp", gk192, w["k_head_norm"].partition_broadcast(128), writes=[Rgn])
    S.tt("dve", gq192[:, 0:128], gq192[:, 0:128], gk192[:, 0:128], ALU.mult, [Rgn], [Rgn])
    cosT = scr[:, 0:1024].rearrange("p (i f) -> p i f", f=32)
    sinT = scr[:, 1024:2048].rearrange("p (i f) -> p i f", f=32)
    Rtab = Res("ropetab")
    S.ld("sp", scr[:, 0:2048], c.cf_d[:, CF_COS:CF_COS + 2048], writes=[Rtab])
    kpeT = scr[:, 2048:4096].bitcast(BF16)
    Rkpe_hi = Res("kpe_hi")
    S.op("dve", lambda e, o=scr[64:128, 2048:4096]: e.memset(o, 0.0), [], [Rkpe_hi])
    qlnT3 = c.hbuf[:, 0:2 * S_LEN].rearrange("p (a t) -> p a t", a=2)
    kvlnT = c.hbuf[:, 2 * S_LEN:3 * S_LEN]
    sskpe = sm[:, 128:160]
    rstdk = sm[:, 160:192]
    Rqln = [Res(f"qln{g}") for g in range(NG)]
    Rkvln = [Res(f"kvln{g}") for g in range(NG)]
    Rkpe = [Res(f"kpe{g}") for g in range(NG)]
    Rsskpe = [Res(f"sskpe{g}") for g in range(NG)]

    def rope(x, out, i, Rx, Rout, t1, t2, Rt):
        cb = cosT[:, i, :].unsqueeze(1).to_broadcast([128, 2, 32])
        S.tt("dve", t1.rearrange("p (a f) -> p a f", a=2), x.rearrange("p (a f) -> p a f", a=2), cb, ALU.mult,
             [Rx, Rtab], [Rt])
        S.tt("dve", t2[:, 0:32], x[:, 32:64], sinT[:, i, :], ALU.mult, [Rx, Rtab], [Rt])
        S.tt("dve", t2[:, 32:64], x[:, 0:32], sinT[:, i, :], ALU.mult, [Rx, Rtab], [Rt])
        S.tt("dve", out[:, 0:32], t1[:, 0:32], t2[:, 0:32], ALU.subtract, [Rt], [Rout])
        S.tt("dve", out[:, 32:64], t1[:, 32:64], t2[:, 32:64], ALU.add, [Rt], [Rout])

    wlat = c.wbuf[:, 0:3584].rearrange("p (c n) -> p c n", c=8)
    Rwlat = Res("wlat")
    S.ld("pool", wlat[:, 0:4, :], wv[:, 0:4, 0:448], writes=[Rwlat])
    S.ld("pool", wlat[:, 4:8, :], wv[:, 4:8, 0:448], writes=[Rwlat])
    o = 4096
    junk = [scr[:, o + 448 * j:o + 448 * (j + 1)] for j in range(2)]; o += 896
    NLN = 3
    lnb = [scr[:, o + 192 * j:o + 192 * (j + 1)].bitcast(BF16) for j in range(NLN)]; o += 192 * NLN
    kp = [scr[:, o + 64 * j:o + 64 * (j + 1)] for j in range(2)]; o += 128
    t1 = [scr[:, o + 64 * j:o + 64 * (j + 1)] for j in range(2)]; o += 128
    t2 = [scr[:, o + 64 * j:o + 64 * (j + 1)] for j in range(2)]; o += 128
    kr = [scr[:, o + 32 * j:o + 32 * (j + 1)].bitcast(BF16) for j in range(2)]; o += 64
    assert o <= 6144, o
    Rjunk = [Res("junk0"), Res("junk1")]
    Rkp = [Res("kp0"), Res("kp1")]
    Rt = [Res("ropet0"), Res("ropet1")]
    Rlnb = [Res(f"lnb{j}") for j in range(NLN)]
    Rkr = [Res("kr0"), Res("kr1")]
    NSB = 4
    Rst = [Res(f"mst{j}") for j in range(NSB)]
    LB = [0, 1, 2, 3, 6, 7]

    def binfo(i):
        k = i % NSB
        sb_ = 192 + 8 * k
        return i // 4, LB[i % len(LB)], sm[:, sb_:sb_ + 2], sm[:, sb_ + 2:sb_ + 4], sm[:, sb_ + 4:sb_ + 6], k, i % 2, i % NLN

    def b_mm(i):
        tg, bank, ss2, lg2, rs2, k, pb, li = binfo(i)
        for cc in range(8):
            S.mm(c.ps[:, bank, 0:448], xnT3[:, cc, i * 128:(i + 1) * 128], wlat[:, cc, :], cc == 0, cc == 7,
                 [Rwlat, c.RxnT[i]], [c.bank[bank]])

    def b_sq(i):
        tg, bank, ss2, lg2, rs2, k, pb, li = binfo(i)
        S.act(junk[pb][:, 0:256], c.ps[:, bank, 0:256], AF.Square, [c.bank[bank]], [Rjunk[pb], Rst[k]], accum_out=ss2[:, 0:1])
        S.act(junk[pb][:, 256:384], c.ps[:, bank, 256:384], AF.Square, [c.bank[bank]], [Rjunk[pb], Rst[k]],
              accum_out=ss2[:, 1:2])
        S.act(junk[pb][:, 384:448], c.ps[:, bank, 384:448], AF.Square, [c.bank[bank]], [Rjunk[pb], Rsskpe[tg]],
              accum_out=sskpe[:, i:i + 1])

    def b_rs(i):
        tg, bank, ss2, lg2, rs2, k, pb, li = binfo(i)
        S.act(lg2[:, 0:1], ss2[:, 0:1], AF.Ln, [Rst[k]], [Rst[k]], scale=1.0 / 256, bias=EPS)
        S.act(lg2[:, 1:2], ss2[:, 1:2], AF.Ln, [Rst[k]], [Rst[k]], scale=1.0 / 128, bias=EPS)
        S.act(rs2, lg2, AF.Exp, [Rst[k]], [Rst[k]], scale=-0.5)

    def b_ln(i):
        tg, bank, ss2, lg2, rs2, k, pb, li = binfo(i)
        S.stt(lnb[li][:, 0:256], c.ps[:, bank, 0:256], rs2[:, 0:1], gqa, ALU.mult, ALU.mult,
              [c.bank[bank], Rst[k], Rgn], [Rlnb[li]])
        S.stt(lnb[li][:, 256:384], c.ps[:, bank, 256:384], rs2[:, 1:2], gkva, ALU.mult, ALU.mult,
              [c.bank[bank], Rst[k], Rgn], [Rlnb[li]])
        S.tt("dve", kp[pb], c.ps[:, bank, 384:448], gkpe, ALU.mult, [c.bank[bank], Rgn], [Rkp[pb]])

    def b_rope(i):
        tg, bank, ss2, lg2, rs2, k, pb, li = binfo(i)
        rope(kp[pb], kr[pb], i, Rkp[pb], Rkr[pb], t1[pb], t2[pb], Rt[pb])

    def b_tr(i):
        tg, bank, ss2, lg2, rs2, k, pb, li = binfo(i)
        tbank = 4 + pb
        psT = c.ps[:, tbank, :].bitcast(BF16)
        for a in range(3):
            S.tr(psT[:, a * 128:(a + 1) * 128], lnb[li][:, a * 128:(a + 1) * 128], c.ident,
                 [Rlnb[li], c.Rconst], [c.bank[tbank]])
        S.tr(psT[0:64, 384:512], kr[pb], c.ident, [Rkr[pb], c.Rconst], [c.bank[tbank]])
        S.cp("act", qlnT3[:, :, i * 128:(i + 1) * 128], psT[:, 0:256].rearrange("p (a t) -> p a t", a=2),
             [c.bank[tbank]], [Rqln[tg]])
        S.cp("dve", kvlnT[:, i * 128:(i + 1) * 128], psT[:, 256:384], [c.bank[tbank]], [Rkvln[tg]])
        S.cp("dve", kpeT[0:64, i * 128:(i + 1) * 128], psT[0:64, 384:512], [c.bank[tbank]], [Rkpe[tg]])

    bstages = [(b_tr, 5), (b_rope, 4), (b_ln, 3), (b_rs, 2), (b_sq, 1), (b_mm, 0)]
    for n in range(NT + 5):
        for fn, d in bstages:
            i = n - d
            if 0 <= i < NT:
                fn(i)
    S.barrier()
    wuq = c.wbuf[:, 0:3072].rearrange("p (a n) -> p a n", a=2)
    wukv = c.wbuf[:, 3072:5120]
    Rwu = Res("wu")
    S.ld("pool", wuq, w["w_uq"].rearrange("(a p) n -> p a n", p=128), writes=[Rwu])
    S.ld("pool", wukv, w["w_ukv"], writes=[Rwu])
    X = c.xnT
    qTn = X[:, 0:4096]
    qTp = X[:, 4096:8192]
    RqTp_hi = Res("qTp_hi")
    S.op("dve", lambda e, o=X[64:128, 4096:8192]: e.memset(o, 0.0), [], [RqTp_hi])
    kTn = X[:, 8192:12288]
    vh = X[:, 12288:16384].rearrange("p (i f) -> p i f", f=128)
    NP = 3
    Pb = [X[:, 16384 + 512 * j:16384 + 512 * (j + 1)] for j in range(NP)]
    NQR = 3
    qr = [X[:, 18432 + 256 * j:18432 + 256 * j + 192] for j in range(NQR)]
    XF = X[:, 20480:24576].bitcast(F32)
    rcb = XF[:, 0:512]
    tbuf = XF[:, 512:1024]
    qpe = [XF[:, 1024 + 64 * j:1088 + 64 * j] for j in range(2)]
    u1 = [XF[:, 1152 + 64 * j:1216 + 64 * j] for j in range(2)]
    u2 = [XF[:, 1280 + 64 * j:1344 + 64 * j] for j in range(2)]
    junk2 = [XF[:, 1408 + 320 * j:1408 + 320 * (j + 1)] for j in range(2)]
    Rsum = [X[:, 24576 + 1024 * j:24576 + 1024 * (j + 1)].bitcast(F32) for j in range(2)]
    ones32 = X[:, 26624:26880].bitcast(F32)
    RRs = [Res("Rsum0"), Res("Rsum1")]
    Rones32 = Res("ones32")
    S.op("pool", lambda e, o=ones32: e.memset(o, 1.0), [], [Rones32])
    RqTn = [Res(f"qTn{g}") for g in range(NG)]
    RqTp = [Res(f"qTp{g}") for g in range(NG)]
    RkTn = [Res(f"kTn{g}") for g in range(NG)]
    Rvh = [Res(f"vh{g}") for g in range(NG)]
    Rrk = [Res(f"rstdk{g}") for g in range(NG)]
    RPb = [Res(f"mPb{j}") for j in range(NP)]
    Rqr = [Res(f"qr{j}") for j in range(NQR)]
    Rrc, Rtb = Res("rcb"), Res("tbuf")
    Rqpe = [Res("qpe0"), Res("qpe1")]
    Ru = [Res("u0"), Res("u1")]
    Rj2 = [Res("junk20"), Res("junk21")]
    NST = 4
    Rs2 = [Res(f"hst{j}") for j in range(NST)]
    QB = [0, 1, 2, 3, 6, 7]
    gid = 0
    sw = 0
    tcount = 0
    for h in range(8):
        for tg in range(NG):
            bank = 4 + tg % 2
            S.mm(c.ps[:, bank, :], wukv[:, h * 256:h * 256 + 128], kvlnT[:, tg * 512:(tg + 1) * 512], True, True,
                 [Rwu, Rkvln[tg]], [c.bank[bank]])
            S.cp("act", kTn[:, tg * 512:(tg + 1) * 512], c.ps[:, bank, :], [c.bank[bank]], [RkTn[tg]])

        def tinfo(i):
            t = tcount + i
            tg, j = divmod(i, 4)
            sb_ = 224 + 8 * (t % NST)
            return (t, tg, j, QB[t % len(QB)], slice(i * 128, (i + 1) * 128), sm[:, sb_:sb_ + 2], sm[:, sb_ + 2:sb_ + 4],
                    sm[:, sb_ + 4:sb_ + 5], t % NST, t % NQR, t % 2)

        def t_mm(i):
            t, tg, j, bank, tok, ss2, lg2, rsq, si, qi, pi = tinfo(i)
            S.mm(c.ps[:, bank, 0:192], qlnT3[:, 0, tok], wuq[:, 0, h * 192:(h + 1) * 192], True, False,
                 [Rwu, Rqln[tg]], [c.bank[bank]])
            S.mm(c.ps[:, bank, 0:192], qlnT3[:, 1, tok], wuq[:, 1, h * 192:(h + 1) * 192], False, True,
                 [Rwu, Rqln[tg]], [c.bank[bank]])
            S.mm(c.ps[:, bank, 192:448], kvlnT[:, tok], wukv[:, h * 256:(h + 1) * 256], True, True,
                 [Rwu, Rkvln[tg]], [c.bank[bank]])

        def t_sq(i):
            t, tg, j, bank, tok, ss2, lg2, rsq, si, qi, pi = tinfo(i)
            S.act(junk2[pi][:, 0:192], c.ps[:, bank, 0:192], AF.Square, [c.bank[bank]], [Rj2[pi], Rs2[si]],
                  accum_out=ss2[:, 0:1])
            S.act(junk2[pi][:, 192:320], c.ps[:, bank, 192:320], AF.Square, [c.bank[bank]], [Rj2[pi], Rs2[si]],
                  accum_out=ss2[:, 1:2])
            S.cp("act", vh[:, i, :], c.ps[:, bank, 320:448], [c.bank[bank]], [Rvh[tg]])

        def t_add(i):
            t, tg, j, bank, tok, ss2, lg2, rsq, si, qi, pi = tinfo(i)
            S.tt("dve", ss2[:, 1:2], ss2[:, 1:2], sskpe[:, i:i + 1], ALU.add, [Rs2[si], Rsskpe[tg]], [Rs2[si]])

        def t_rs(i):
            t, tg, j, bank, tok, ss2, lg2, rsq, si, qi, pi = tinfo(i)
            S.act(lg2, ss2, AF.Ln, [Rs2[si]], [Rs2[si]], scale=1.0 / 192, bias=EPS)
            S.act(rsq, lg2[:, 0:1], AF.Exp, [Rs2[si]], [Rs2[si]], scale=-0.5)
            S.act(rstdk[:, i:i + 1], lg2[:, 1:2], AF.Exp, [Rs2[si]], [Rrk[tg]], scale=-0.5, bias=-0.5 * math.log(192.0))

        def t_qn(i):
            t, tg, j, bank, tok, ss2, lg2, rsq, si, qi, pi = tinfo(i)
            S.stt(qr[qi][:, 0:128], c.ps[:, bank, 0:128], rsq, gq192[:, 0:128], ALU.mult, ALU.mult,
                  [c.bank[bank], Rs2[si], Rgn], [Rqr[qi]])
            S.stt(qpe[pi], c.ps[:, bank, 128:192], rsq, gq192[:, 128:192], ALU.mult, ALU.mult,
                  [c.bank[bank], Rs2[si], Rgn], [Rqpe[pi]])
            rope(qpe[pi], qr[qi][:, 128:192], i, Rqpe[pi], Rqr[qi], u1[pi], u2[pi], Ru[pi])

        def t_tr(i):
            t, tg, j, bank, tok, ss2, lg2, rsq, si, qi, pi = tinfo(i)
            tbank = 4 + tg % 2
            psT = c.ps[:, tbank, :].bitcast(BF16)
            S.tr(psT[:, j * 128:(j + 1) * 128], qr[qi][:, 0:128], c.ident, [Rqr[qi], c.Rconst], [c.bank[tbank]])
            S.tr(psT[0:64, 512 + j * 128:512 + (j + 1) * 128], qr[qi][:, 128:192], c.ident,
                 [Rqr[qi], c.Rconst], [c.bank[tbank]])
            if j == 3:
                S.cp("dve", qTn[:, tg * 512:(tg + 1) * 512], psT[:, 0:512], [c.bank[tbank]], [RqTn[tg]])
                S.cp("dve", qTp[0:64, tg * 512:(tg + 1) * 512], psT[0:64, 512:1024], [c.bank[tbank]], [RqTp[tg]])

        tstages = [(t_tr, 5), (t_qn, 4), (t_rs, 3), (t_add, 2), (t_sq, 1), (t_mm, 0)]
        for n in range(NT + 5):
            for fn, d in tstages:
                i = n - d
                if 0 <= i < NT:
                    fn(i)
        tcount += NT
        G = []
        for qg in range(NG):
            for kb in reversed(range(4 * qg + 4)):
                G.append((qg, kb))
        n_g = len(G)

        def info(gi):
            qg, kb = G[gi]
            r = kb - 4 * qg
            c0 = r * 128 if r >= 0 else 0
            return qg, kb, r, c0, kb == 4 * qg + 3, kb == 0, gid + gi, sw + qg

        def st_z(gi):
            qg, kb, r, c0, first, last, g_, sw_ = info(gi)
            zb = g_ % 4
            S.mm(c.ps[:, zb, c0:512], kTn[:, kb * 128:(kb + 1) * 128], qTn[:, qg * 512 + c0:(qg + 1) * 512], True, False,
                 [RkTn[kb // 4], RqTn[qg]], [c.bank[zb]])
            S.mm(c.ps[:, zb, c0:512], kpeT[:, kb * 128:(kb + 1) * 128], qTp[:, qg * 512 + c0:(qg + 1) * 512],
                 False, True, [Rkpe[kb // 4], RqTp[qg], Rkpe_hi, RqTp_hi], [c.bank[zb]])

        def st_e(gi):
            qg, kb, r, c0, first, last, g_, sw_ = info(gi)
            zb = g_ % 4
            pi = g_ % NP
            S.act(Pb[pi][:, c0:512], c.ps[:, zb, c0:512], AF.Exp, [c.bank[zb], Rrk[kb // 4]], [RPb[pi]],
                  scale=rstdk[:, kb:kb + 1])
            if r >= 0:
                S.tt("dve", Pb[pi][:, c0:c0 + 128], Pb[pi][:, c0:c0 + 128], c.mincl, ALU.mult,
                     [RPb[pi], c.Rconst], [RPb[pi]])

        def st_av(gi):
            qg, kb, r, c0, first, last, g_, sw_ = info(gi)
            pi = g_ % NP
            ob = 4 + sw_ % 2
            db = 6 + sw_ % 2
            ri = sw_ % 2
            S.mm(c.ps[:, ob, c0:512], vh[:, kb, :], Pb[pi][:, c0:512], first, last, [Rvh[kb // 4], RPb[pi]],
                 [c.bank[ob]], skip=True)
            S.mm(c.ps[:, db, c0:512], c.ones, Pb[pi][:, c0:512], first, last, [c.Rconst, RPb[pi]],
                 [c.bank[db]], skip=True)
            if last:
                S.act(rcb, c.ps[:, db, :], AF.Ln, [c.bank[db]], [Rrc])
                S.act(rcb, rcb, AF.Exp, [Rrc], [Rrc], scale=-1.0)
                S.tt("dve", tbuf, c.ps[:, ob, :], rcb, ALU.mult, [c.bank[ob], Rrc], [Rtb])
                og = c.ogT[:, h, qg * 512:(qg + 1) * 512]
                S.tt("dve", og, tbuf, og, ALU.mult, [Rtb, c.Rog[h][qg]], [c.Rog[h][qg]])

        stages = [(st_z, 0), (st_e, 1), (st_av, 2)]
        for n in range(n_g + 2):
            for fn, d in reversed(stages):
                gi = n - d
                if 0 <= gi < n_g:
                    fn(gi)
        gid += n_g
        sw += NG


def layer_swa(c, w):
    S = c.S
    w_in = w["w_in"]
    gate_phase(c, w_in, 1536)
    S.barrier()
    xnT3 = c.xnT[:, :].rearrange("p (c t) -> p c t", c=8)
    wv = w_in.rearrange("(c p) n -> p c n", p=128)
    qT3 = c.hbuf[:, 0:2 * S_LEN].rearrange("p (a t) -> p a t", a=2)
    kT2 = c.hbuf[:, 2 * S_LEN:3 * S_LEN]
    scr = c.scr
    off = 0
    vg = scr[:, off:off + 1024].bitcast(BF16).rearrange("p (i f) -> p i f", f=64); off += 1024
    junk = [scr[:, off + 320 * j:off + 320 * (j + 1)] for j in range(2)]; off += 640
    tmpq = [scr[:, off + 256 * j:off + 256 * (j + 1)] for j in range(2)]; off += 512
    NQN = 3
    qn = [scr[:, off + 128 * j:off + 128 * (j + 1)].bitcast(BF16) for j in range(NQN)]; off += 128 * NQN
    biasg = scr[:, off:off + 1024]; off += 1024
    Tb = scr[:, off:off + 1024]; off += 1024
    Pb = [scr[:, off:off + 512].bitcast(BF16), scr[:, off + 512:off + 1024].bitcast(BF16)]; off += 1024
    dnb = scr[:, off:off + 256]; off += 256
    gqk4 = scr[:, off:off + 256]; off += 256
    assert off <= 6144, off
    sm = c.small
    kscale = sm[:, 16:48]
    es16 = sm[:, 48:64]
    Rbias, RTb, Rdn, Rgqk, Res16 = (Res(n) for n in ("biasg", "Tb", "dnb", "gqk", "es16"))
    Rjunk = [Res("junk0"), Res("junk1")]
    Rtmpq = [Res("tmpq0"), Res("tmpq1")]
    Rqn = [Res(f"qn{j}") for j in range(NQN)]
    RPb = [Res("Pb0"), Res("Pb1")]
    NST = 4
    Rst = [Res(f"sst{j}") for j in range(NST)]
    RqT = [Res(f"qT{g}") for g in range(NG)]
    RkT = [Res(f"kT{g}") for g in range(NG)]
    Rvg = [Res(f"vg{g}") for g in range(NG)]
    Rks = [Res(f"ks{g}") for g in range(NG)]
    wtm = [c.wbuf[:, 4096 * b:4096 * b + 3072].rearrange("p (c n) -> p c n", c=8) for b in range(2)]
    wk2 = [c.wbuf[:, 4096 * b + 3072:4096 * (b + 1)].rearrange("p (c n) -> p c n", c=8) for b in range(2)]
    Rw = [Res("swaw0"), Res("swaw1")]
    gk4 = junk[0][:, 0:256]
    for j in range(4):
        S.ld("sp", gqk4[:, j * 64:(j + 1) * 64], w["q_head_norm"].partition_broadcast(128), writes=[Rgqk])
        S.ld("sp", gk4[:, j * 64:(j + 1) * 64], w["k_head_norm"].partition_broadcast(128), writes=[Rjunk[0]])
    S.tt("dve", gqk4, gqk4, gk4, ALU.mult, [Rjunk[0], Rgqk], [Rgqk])
    S.ld("sp", es16, w["sinks"].partition_broadcast(128), writes=[Res16])
    S.act(es16, es16, AF.Exp, [Res16], [Res16])

    def load_w(g):
        b = g % 2
        S.ld("pool", wtm[b][:, :, 0:256], wv[:, :, g * 256:(g + 1) * 256], writes=[Rw[b]])
        S.ld("pool", wtm[b][:, :, 256:320], wv[:, :, 1024 + g * 64:1024 + (g + 1) * 64], writes=[Rw[b]])
        S.ld("pool", wtm[b][:, :, 320:384], wv[:, :, 1280 + g * 64:1280 + (g + 1) * 64], writes=[Rw[b]])
        for d in range(2):
            S.ld("pool", wk2[b][:, :, d * 64:(d + 1) * 64], wv[:, :, 1024 + g * 64:1024 + (g + 1) * 64], writes=[Rw[b]])

    load_w(0)
    TB = [0, 1, 2, 3, 6, 7]
    tcount = 0
    acount = 0
    for g in range(4):
        b = g % 2
        if g + 1 < 4:
            load_w(g + 1)
        for kbi in range(2):
            S.ld("sp", biasg[:, kbi * 512:(kbi + 1) * 512],
                 c.cf_d[:, CF_BIAS + kbi * 2048 + g * 512:CF_BIAS + kbi * 2048 + (g + 1) * 512], writes=[Rbias])
        for tg in range(NG):
            bank = 4 + tg % 2
            for cc in range(8):
                S.mm(c.ps[:, bank, :], wk2[b][:, cc, :], xnT3[:, cc, tg * 512:(tg + 1) * 512], cc == 0, cc == 7,
                     [Rw[b]] + c.RxnT[4 * tg:4 * tg + 4], [c.bank[bank]])
            S.cp("act", kT2[:, tg * 512:(tg + 1) * 512], c.ps[:, bank, :], [c.bank[bank]], [RkT[tg]])

        def tinfo(i):
            t = tcount + i
            tg, j = divmod(i, 4)
            sb_ = 64 + 16 * (t % NST)
            return (t, tg, j, TB[t % len(TB)], sm[:, sb_:sb_ + 5], sm[:, sb_ + 5:sb_ + 10], sm[:, sb_ + 10:sb_ + 14],
                    t % NST, t % 2, t % NQN)

        def t_mm(i):
            t, tg, j, bank, ss5, lg5, rs4, si, pi, qi = tinfo(i)
            for cc in range(8):
                S.mm(c.ps[:, bank, 0:384], xnT3[:, cc, i * 128:(i + 1) * 128], wtm[b][:, cc, :], cc == 0, cc == 7,
                     [Rw[b], c.RxnT[i]], [c.bank[bank]])

        def t_sq(i):
            t, tg, j, bank, ss5, lg5, rs4, si, pi, qi = tinfo(i)
            S.act(junk[pi], c.ps[:, bank, 0:320], AF.Square, [c.bank[bank]], [Rjunk[pi]])
            S.cp("act", vg[:, i, :], c.ps[:, bank, 320:384], [c.bank[bank]], [Rvg[tg]])

        def t_red(i):
            t, tg, j, bank, ss5, lg5, rs4, si, pi, qi = tinfo(i)
            S.op("dve", lambda e, o=ss5, i_=junk[pi].rearrange("p (h f) -> p h f", f=64): e.reduce_sum(out=o, in_=i_, axis=AX.X),
                 [Rjunk[pi]], [Rst[si]])

        def t_rs(i):
            t, tg, j, bank, ss5, lg5, rs4, si, pi, qi = tinfo(i)
            S.act(lg5, ss5, AF.Ln, [Rst[si]], [Rst[si]], scale=1.0 / 64, bias=EPS)
            S.act(rs4, lg5[:, 0:4], AF.Exp, [Rst[si]], [Rst[si]], scale=-0.5)
            S.act(kscale[:, i:i + 1], lg5[:, 4:5], AF.Exp, [Rst[si]], [Rks[tg]], scale=-0.5, bias=math.log(0.125))

        def t_qn(i):
            t, tg, j, bank, ss5, lg5, rs4, si, pi, qi = tinfo(i)
            S.tt("dve", tmpq[pi].rearrange("p (h f) -> p h f", f=64),
                 c.ps[:, bank, 0:256].rearrange("p (h f) -> p h f", f=64),
                 rs4.unsqueeze(2).to_broadcast([128, 4, 64]), ALU.mult, [c.bank[bank], Rst[si]], [Rtmpq[pi]])
            S.tt("dve", qn[qi], tmpq[pi], gqk4, ALU.mult, [Rtmpq[pi], Rgqk], [Rqn[qi]])

        def t_tr(i):
            t, tg, j, bank, ss5, lg5, rs4, si, pi, qi = tinfo(i)
            tbank = 4 + tg % 2
            psT = c.ps[:, tbank, :].bitcast(BF16)
            for a in range(2):
                S.tr(psT[:, a * 512 + j * 128:a * 512 + (j + 1) * 128], qn[qi][:, a * 128:(a + 1) * 128], c.ident,
                     [Rqn[qi], c.Rconst], [c.bank[tbank]])
            if j == 3:
                S.cp("dve", qT3[:, :, tg * 512:(tg + 1) * 512], psT.rearrange("p (a t) -> p a t", a=2),
                     [c.bank[tbank]], [RqT[tg]])

        tstages = [(t_tr, 5), (t_qn, 4), (t_rs, 3), (t_red, 2), (t_sq, 1), (t_mm, 0)]
        for n in range(NT + 5):
            for fn, d in tstages:
                i = n - d
                if 0 <= i < NT:
                    fn(i)
        tcount += NT

        def ainfo(qb):
            a_ = acount + qb
            kbs = [(0, qb - 1), (1, qb)] if qb > 0 else [(1, qb)]
            return a_, kbs, 2 * (a_ % 2), a_ % 2, 4 + a_ % 2

        def a_qk(qb):
            a_, kbs, zb0, pbi, ob = ainfo(qb)
            for kbi, kb in kbs:
                for hq in range(4):
                    p = hq % 2
                    rows = slice(p * 64, p * 64 + 64)
                    col = (kbi * 2 + hq // 2) * 128
                    S.mm(c.ps[:, zb0 + p, col:col + 128], kT2[rows, kb * 128:(kb + 1) * 128],
                         qT3[rows, hq // 2, qb * 128:(qb + 1) * 128], True, True,
                         [RkT[kb // 4], RqT[qb // 4]], [c.bank[zb0 + p]])

        def a_bias(qb):
            a_, kbs, zb0, pbi, ob = ainfo(qb)
            for kbi, kb in kbs:
                for p in range(2):
                    src = c.ps[:, zb0 + p, kbi * 256:(kbi + 1) * 256].rearrange("q (a t) -> q a t", a=2)
                    dst = Tb[:, kbi * 512:(kbi + 1) * 512].rearrange("q (a p t) -> q p a t", a=2, p=2)[:, p]
                    bia = biasg[:, kbi * 512:(kbi + 1) * 512].rearrange("q (a p t) -> q p a t", a=2, p=2)[:, p]
                    S.stt(dst, src, kscale[:, kb:kb + 1], bia, ALU.mult, ALU.add,
                          [c.bank[zb0 + p], Rks[kb // 4], Rbias], [RTb])

        def a_exp(qb):
            a_, kbs, zb0, pbi, ob = ainfo(qb)
            lo = 0 if qb > 0 else 512
            S.act(Pb[pbi][:, lo:1024], Tb[:, lo:1024], AF.Exp, [RTb], [RPb[pbi]])

        def a_av(qb):
            a_, kbs, zb0, pbi, ob = ainfo(qb)
            for hq in range(4):
                rows = slice((hq % 2) * 64, (hq % 2) * 64 + 64)
                col = (hq // 2) * 128
                for n_, (kbi, kb) in enumerate(kbs):
                    S.mm(c.ps[rows, ob, col:col + 128], vg[:, kb, :], Pb[pbi][:, (kbi * 4 + hq) * 128:(kbi * 4 + hq + 1) * 128],
                         n_ == 0, n_ == len(kbs) - 1, [Rvg[kb // 4], RPb[pbi]], [c.bank[ob]])
                for n_, (kbi, kb) in enumerate(kbs):
                    S.mm(c.ps[rows, ob, 256 + col:256 + col + 128], c.ones[:, 0:64],
                         Pb[pbi][:, (kbi * 4 + hq) * 128:(kbi * 4 + hq + 1) * 128],
                         n_ == 0, n_ == len(kbs) - 1, [c.Rconst, RPb[pbi]], [c.bank[ob]])

        def a_out(qb):
            a_, kbs, zb0, pbi, ob = ainfo(qb)
            for hq in range(4):
                rows = slice((hq % 2) * 64, (hq % 2) * 64 + 64)
                col = (hq // 2) * 128
                h = 4 * g + hq
                S.ts("dve", dnb[rows, col:col + 128], c.ps[rows, ob, 256 + col:256 + col + 128], es16[rows, h:h + 1],
                     None, ALU.add, None, [c.bank[ob], Res16], [Rdn])
            S.act(dnb, dnb, AF.Ln, [Rdn], [Rdn])
            S.act(dnb, dnb, AF.Exp, [Rdn], [Rdn], scale=-1.0)
            S.tt("dve", dnb, c.ps[:, ob, 0:256], dnb, ALU.mult, [c.bank[ob], Rdn], [Rdn])
            og = c.ogT[:, 2 * g:2 * g + 2, qb * 128:(qb + 1) * 128]
            Rogs = [c.Rog[2 * g][qb // 4], c.Rog[2 * g + 1][qb // 4]]
            S.tt("dve", og, dnb.rearrange("p (a t) -> p a t", a=2), og, ALU.mult, [Rdn] + Rogs, Rogs)

        astages = [(a_out, 4), (a_av, 3), (a_exp, 2), (a_bias, 1), (a_qk, 0)]
        for n in range(NT + 4):
            for fn, d in astages:
                qb = n - d
                if 0 <= qb < NT:
                    fn(qb)
        acount += NT


LAUNCH_GROUPS = [[0, 1, 2, 3]]
_CONSTS = None


def run_layers(layers, xs, inputs):
    global _CONSTS
    if _CONSTS is None:
        _CONSTS = make_consts()
    cbf, cf = _CONSTS
    nc = build_program(layers)
    names = [n for li in layers for n in LAYER_WEIGHTS[li]]
    in_maps = []
    for b in range(len(xs)):
        m = {"x": np.ascontiguousarray(xs[b], dtype=np.float32), "cbf": cbf, "cf": cf}
        for n in names:
            m[n] = np.ascontiguousarray(inputs[n], dtype=np.float32)
        in_maps.append(m)
    res = run_bass_kernel_spmd(nc, in_maps, core_ids=list(range(len(xs))))
    return [r["y"] for r in res.results]


def kernel(**inputs):
    x = np.asarray(inputs["x"])
    xs = [x[b] for b in range(x.shape[0])]
    for grp in LAUNCH_GROUPS:
        xs = run_layers(grp, xs, inputs)
    return np.stack(xs, axis=0).astype(np.float32)
```

```python
import math
import numpy as np
import ml_dtypes
import concourse.bass as bass
import concourse.mybir as mybir
from concourse.bass_utils import run_bass_kernel_spmd

F32 = mybir.dt.float32
BF16 = mybir.dt.bfloat16
AF = mybir.ActivationFunctionType
ALU = mybir.AluOpType
AX = mybir.AxisListType

S_LEN = 4096
D = 1024
NT = S_LEN // 128
NG = S_LEN // 512
EPS = 1e-6

LAYER_KIND = ["sb", "mla", "swa", "sb"]
LAYER_WEIGHTS = [
    ["l0_norm", "l0_w_in", "l0_w_out"],
    ["l1_norm", "l1_w_in", "l1_q_a_norm", "l1_w_uq", "l1_kv_a_norm", "l1_w_ukv",
     "l1_q_head_norm", "l1_k_head_norm", "l1_w_out"],
    ["l2_norm", "l2_w_in", "l2_q_head_norm", "l2_k_head_norm", "l2_sinks", "l2_w_out"],
    ["l3_norm", "l3_w_in", "l3_w_out"],
]
WSHAPES = {
    "l0_norm": [1024], "l0_w_in": [1024, 4096], "l0_w_out": [1024, 1024],
    "l1_norm": [1024], "l1_w_in": [1024, 1472], "l1_q_a_norm": [256], "l1_w_uq": [256, 1536],
    "l1_kv_a_norm": [128], "l1_w_ukv": [128, 2048], "l1_q_head_norm": [192], "l1_k_head_norm": [192],
    "l1_w_out": [1024, 1024],
    "l2_norm": [1024], "l2_w_in": [1024, 2560], "l2_q_head_norm": [64], "l2_k_head_norm": [64],
    "l2_sinks": [16], "l2_w_out": [1024, 1024],
    "l3_norm": [1024], "l3_w_in": [1024, 4096], "l3_w_out": [1024, 1024],
}


class Res:
    __slots__ = ("name", "w", "r", "excl")

    def __init__(self, name, excl=False):
        self.name = name
        self.w = None
        self.r = {}
        self.excl = excl


class Sched:
    ENGS = ("pe", "act", "dve", "pool", "sp")

    def __init__(self, sems, dma_pools):
        self.sems = sems
        self.cnt = {k: 0 for k in sems}
        self.ops = {e: [] for e in self.ENGS}
        self.seen = {e: {} for e in self.ENGS}
        self.dma_pools = dma_pools
        self.dma_rr = {q: 0 for q in dma_pools}
        self.nwaits = 0

    def _wait(self, e, key, val):
        if val <= 0 or val <= self.seen[e].get(key, 0):
            return
        self.seen[e][key] = val
        sem = self.sems[key]
        self.ops[e].append(lambda eng, sem=sem, val=val: eng.wait_ge(sem, val))
        self.nwaits += 1

    def _sync(self, e, reads, writes, dma):
        rd = [r for r in reads if not r.excl]
        wr = list(writes) + [r for r in reads if r.excl]
        need = []
        for r in rd:
            if r.w is not None:
                need.append((r.w, True))
        for r in wr:
            if r.w is not None:
                need.append((r.w, False))
            for ev in r.r.values():
                need.append((ev, False))
        for (key, val, eng), raw in need:
            if eng == e and not dma:
                if e == "pe":
                    continue
            self._wait(e, key, val)
        return rd, wr

    def _update(self, rd, wr, ev):
        for r in rd:
            r.r[ev[0]] = ev
        for r in wr:
            r.w = ev
            r.r = {}

    def op(self, e, fn, reads=(), writes=()):
        rd, wr = self._sync(e, reads, writes, False)
        self.cnt[e] += 1
        sem = self.sems[e]
        self.ops[e].append(lambda eng, fn=fn, sem=sem: fn(eng).then_inc(sem, 1))
        self._update(rd, wr, (e, self.cnt[e], e))

    def dma(self, q, fn, reads=(), writes=()):
        pool = self.dma_pools[q]
        k = pool[self.dma_rr[q] % len(pool)]
        self.dma_rr[q] += 1
        self._wait(q, k, self.cnt[k])
        rd, wr = self._sync(q, reads, writes, True)
        self.cnt[k] += 16
        sem = self.sems[k]
        self.ops[q].append(lambda eng, fn=fn, sem=sem: fn(eng).then_inc(sem, 16))
        self._update(rd, wr, (k, self.cnt[k], None))

    def barrier(self, engines=None):
        for e in (engines or self.ENGS):
            for k, v in self.cnt.items():
                if k != e:
                    self._wait(e, k, v)


    def mm(self, out, lhsT, rhs, start, stop, reads, writes, skip=False):
        if skip:
            self.op("pe", lambda e: e.matmul(out, lhsT=lhsT, rhs=rhs, start=start, stop=stop, skip_group_check=True),
                    reads, writes)
        else:
            self.op("pe", lambda e: e.matmul(out, lhsT=lhsT, rhs=rhs, start=start, stop=stop), reads, writes)

    def tr(self, out, in_, ident, reads, writes):
        self.op("pe", lambda e: e.transpose(out=out, in_=in_, identity=ident), reads, writes)

    def act(self, out, in_, func, reads, writes, **kw):
        self.op("act", lambda e: e.activation(out=out, in_=in_, func=func, **kw), reads, writes)

    def tt(self, eng, out, in0, in1, op, reads, writes):
        self.op(eng, lambda e: e.tensor_tensor(out=out, in0=in0, in1=in1, op=op), reads, writes)

    def stt(self, out, in0, scalar, in1, op0, op1, reads, writes):
        self.op("dve", lambda e: e.scalar_tensor_tensor(out=out, in0=in0, scalar=scalar, in1=in1, op0=op0, op1=op1),
                reads, writes)

    def ts(self, eng, out, in0, s1, s2, op0, op1, reads, writes):
        if op1 is None:
            self.op(eng, lambda e: e.tensor_scalar(out=out, in0=in0, scalar1=s1, scalar2=None, op0=op0), reads, writes)
        else:
            self.op(eng, lambda e: e.tensor_scalar(out=out, in0=in0, scalar1=s1, scalar2=s2, op0=op0, op1=op1),
                    reads, writes)

    def cp(self, eng, out, in_, reads, writes):
        if eng == "act":
            self.op("act", lambda e: e.copy(out=out, in_=in_), reads, writes)
        else:
            self.op(eng, lambda e: e.tensor_copy(out=out, in_=in_), reads, writes)

    def ld(self, q, out, in_, reads=(), writes=()):
        self.dma(q, lambda e: e.dma_start(out=out, in_=in_), reads, writes)

    def emit(self, block):
        def mk(e):
            def body(eng):
                for f in self.ops[e]:
                    f(eng)
            return body
        block.tensor(mk("pe"))
        block.scalar(mk("act"))
        block.vector(mk("dve"))
        block.gpsimd(mk("pool"))
        block.sync(mk("sp"))


def make_consts():
    j = np.arange(128)[:, None]
    s = np.arange(128)[None, :]
    ident = (j == s).astype(np.float32)
    tinc = -(j >= s).astype(np.float32)
    tcar = -(j < s).astype(np.float32)
    ones = np.ones((128, 128), np.float32)
    cbf = np.concatenate([ident, tinc, tcar, ones], axis=1)
    mstrict = (s > j).astype(np.float32)
    mincl = (s >= j).astype(np.float32)
    half = 32
    inv_freq = (10000.0 ** (-np.arange(half, dtype=np.float32) / half)).astype(np.float32)
    pos = np.arange(S_LEN, dtype=np.float32)
    ang = (pos[:, None] * inv_freq[None, :]).astype(np.float32)
    cos = np.cos(ang).astype(np.float32)
    sin = np.sin(ang).astype(np.float32)
    cosT = cos.reshape(NT, 128, 32).transpose(1, 0, 2).reshape(128, NT * 32)
    sinT = sin.reshape(NT, 128, 32).transpose(1, 0, 2).reshape(128, NT * 32)
    slopes = (2.0 ** (-8.0 * np.arange(1, 17, dtype=np.float32) / 16)).astype(np.float32)
    NEG = -30000.0
    bias = np.zeros((128, 2, 16, 128), np.float32)
    for kbi in range(2):
        rel = (s + 128 - (j + 128 * kbi)).astype(np.float32)
        valid = (rel >= 0) & (rel < 128)
        for h in range(16):
            bias[:, kbi, h, :] = np.where(valid, -slopes[h] * rel, NEG)
    cf = np.concatenate([mstrict, mincl, cosT, sinT, bias.reshape(128, -1)], axis=1).astype(np.float32)
    return cbf.astype(np.float32), cf


CF_MSTRICT = 0
CF_MINCL = 128
CF_COS = 256
CF_SIN = 256 + NT * 32
CF_BIAS = 256 + 2 * NT * 32
CF_TOTAL = CF_BIAS + 2 * 16 * 128


class Ctx:
    pass


def build_program(layers, dbg=None):
    nc = bass.Bass("TRN2", target_bir_lowering=False)
    x_in = nc.dram_tensor("x", [S_LEN, D], F32, kind="ExternalInput").ap()
    y_out = nc.dram_tensor("y", [S_LEN, D], F32, kind="ExternalOutput").ap()
    cbf_d = nc.dram_tensor("cbf", [128, 512], F32, kind="ExternalInput").ap()
    cf_d = nc.dram_tensor("cf", [128, CF_TOTAL], F32, kind="ExternalInput").ap()
    W = {}
    for li in layers:
        for n in LAYER_WEIGHTS[li]:
            W[n] = nc.dram_tensor(n, WSHAPES[n], F32, kind="ExternalInput").ap()
    xs = [x_in]
    for i in range(len(layers) - 1):
        xs.append(nc.dram_tensor(f"xmid{i}", [S_LEN, D], F32, kind="Internal").ap())
    xs.append(y_out)

    from contextlib import ExitStack
    with ExitStack() as st:
        def sb(name, shape, dt):
            return st.enter_context(nc.sbuf_tensor(name, shape, dt))
        c = Ctx()
        c.nc = nc
        c.xnT = sb("xnT", [128, 8 * S_LEN], BF16)
        c.ogT = sb("ogT", [128, 8, S_LEN], BF16)
        c.hbuf = sb("hbuf", [128, 3 * S_LEN], BF16)
        c.wbuf = sb("wbuf", [128, 8192], BF16)
        c.scr = sb("scr", [128, 6144], F32)
        c.gbc = sb("gbc", [128, 1024], F32)
        c.cbf = sb("cbfs", [128, 512], BF16)
        c.cmask = sb("cmask", [128, 256], F32)
        c.small = sb("small", [128, 512], F32)
        c.ps = st.enter_context(nc.psum_tensor("ps", [128, 8, 512], F32))
        sem_names = list(Sched.ENGS) + [f"d{i}" for i in range(20)]
        sems = {k: st.enter_context(nc.semaphore(f"s_{k}")) for k in sem_names}
        S = Sched(sems, {"sp": [f"d{i}" for i in range(0, 8)],
                         "pool": [f"d{i}" for i in range(8, 16)],
                         "act": [f"d{i}" for i in range(16, 20)]})
        c.S = S
        c.W = W
        c.cf_d = cf_d
        c.dbg = dbg
        c.bank = [Res(f"bank{b}", excl=True) for b in range(8)]
        c.ident = c.cbf[:, 0:128]
        c.tinc = c.cbf[:, 128:256]
        c.tcar = c.cbf[:, 256:384]
        c.ones = c.cbf[:, 384:512]
        c.mstrict = c.cmask[:, 0:128]
        c.mincl = c.cmask[:, 128:256]
        c.Rconst = Res("const")
        S.ld("pool", c.cbf[:], cbf_d[:, :], writes=[c.Rconst])
        S.ld("sp", c.cmask[:], cf_d[:, 0:256], writes=[c.Rconst])
        S.barrier()

        block = st.enter_context(nc.Block())
        for idx, li in enumerate(layers):
            kind = LAYER_KIND[li]
            pre = f"l{li}_"
            if idx == 0:
                phase_norm(c, xs[idx], W[pre + "norm"])
                S.barrier()
            if kind == "sb":
                layer_sb(c, W[pre + "w_in"])
            elif kind == "mla":
                layer_mla(c, {k[3:]: v for k, v in W.items() if k.startswith(pre)})
            else:
                layer_swa(c, {k[3:]: v for k, v in W.items() if k.startswith(pre)})
            S.barrier()
            nxt = W[f"l{layers[idx + 1]}_norm"] if idx + 1 < len(layers) else None
            phase_out(c, xs[idx], xs[idx + 1], W[pre + "w_out"], nxt)
            S.barrier()
        S.emit(block)
    return nc


def norm_stages(c, Rg, src_of, Rsrc_of, tag):
    S = c.S
    xnT3 = c.xnT[:, :].rearrange("p (c t) -> p c t", c=8)
    junk = c.scr[:, 0:1024]
    Rjunk = Res("junk" + tag)
    xn = [c.scr[:, 1024 + 512 * j:1536 + 512 * j].bitcast(BF16) for j in range(2)]
    Rxn = [Res(f"xn{j}" + tag) for j in range(2)]
    NSTAT = 4
    Rst = [Res(f"nst{j}" + tag) for j in range(NSTAT)]
    c.RxnT = [Res(f"xnT{i}" + tag) for i in range(NT)]

    def cols(i):
        k = i % NSTAT
        return c.small[:, 4 * k:4 * k + 1], c.small[:, 4 * k + 1:4 * k + 2], c.small[:, 4 * k + 2:4 * k + 3], k

    def n_stat(i):
        ss, lg, rs, k = cols(i)
        S.act(junk, src_of(i), AF.Square, [Rsrc_of(i)], [Rjunk, Rst[k]], accum_out=ss)
        S.act(lg, ss, AF.Ln, [Rst[k]], [Rst[k]], scale=1.0 / D, bias=EPS)
        S.act(rs, lg, AF.Exp, [Rst[k]], [Rst[k]], scale=-0.5)

    def n_xn(i):
        ss, lg, rs, k = cols(i)
        b = i % 2
        S.stt(xn[b], src_of(i), rs, c.gbc[:], ALU.mult, ALU.mult, [Rsrc_of(i), Rst[k], Rg], [Rxn[b]])

    def n_tr(i):
        b = i % 2
        bank = 4 + b
        psT = c.ps[:, bank, :].bitcast(BF16)
        for ch in range(8):
            S.tr(psT[:, ch * 128:(ch + 1) * 128], xn[b][:, ch * 128:(ch + 1) * 128], c.ident,
                 [Rxn[b], c.Rconst], [c.bank[bank]])
        dst = xnT3[:, :, i * 128:(i + 1) * 128]
        src = psT.rearrange("p (c t) -> p c t", c=8)
        S.cp("act" if b == 0 else "dve", dst, src, [c.bank[bank]], [c.RxnT[i]])

    return n_stat, n_xn, n_tr


def phase_norm(c, x_d, g_d):
    S = c.S
    Rg = Res("gbc")
    S.ld("sp", c.gbc[:], g_d.partition_broadcast(128), writes=[Rg])
    hb = c.hbuf[:, :].bitcast(F32)
    NX = 4
    xin = [hb[:, 1024 * j:1024 * (j + 1)] for j in range(NX)]
    Rxin = [Res(f"xin{j}") for j in range(NX)]
    n_stat, n_xn, n_tr = norm_stages(c, Rg, lambda i: xin[i % NX], lambda i: Rxin[i % NX], "A")

    def n_ld(i):
        S.ld("sp", xin[i % NX], x_d[i * 128:(i + 1) * 128, :], writes=[Rxin[i % NX]])

    stages = [(n_tr, 4), (n_xn, 3), (n_stat, 2), (n_ld, 0)]
    for n in range(NT + 4):
        for fn, d in stages:
            i = n - d
            if 0 <= i < NT:
                fn(i)


def phase_out(c, x_d, y_d, wout_d, next_g=None):
    S = c.S
    wo = c.wbuf[:, :].rearrange("p (c n) -> p c n", c=8)
    Rwo = Res("wo")
    wv = wout_d.rearrange("(c p) n -> p c n", p=128)
    for h in range(2):
        S.ld("pool", wo[:, 4 * h:4 * h + 4, :], wv[:, 4 * h:4 * h + 4, :], writes=[Rwo])
    Rg = Res("gbcC")
    if next_g is not None:
        S.ld("sp", c.gbc[:], next_g.partition_broadcast(128), writes=[Rg])
    hb = c.hbuf[:, :].bitcast(F32)
    NX = 3
    xin = [hb[:, 1024 * j:1024 * (j + 1)] for j in range(NX)]
    yo = [hb[:, 3072 + 1024 * j:3072 + 1024 * (j + 1)] for j in range(NX)]
    Rxin = [Res(f"cxin{j}") for j in range(NX)]
    Ryo = [Res(f"yo{j}") for j in range(NX)]
    Rog = c.Rog

    def c_ld(i):
        S.ld("sp", xin[i % NX], x_d[i * 128:(i + 1) * 128, :], writes=[Rxin[i % NX]])

    def c_mm(i):
        for h in range(2):
            bank = 2 * (i % 2) + h
            for ch in range(8):
                S.mm(c.ps[:, bank, :], c.ogT[:, ch, i * 128:(i + 1) * 128], wo[:, ch, h * 512:(h + 1) * 512],
                     ch == 0, ch == 7, [Rwo, Rog[ch][i // 4]], [c.bank[bank]])

    def c_add(i):
        k = i % NX
        for h in range(2):
            bank = 2 * (i % 2) + h
            S.tt("dve", yo[k][:, h * 512:(h + 1) * 512], c.ps[:, bank, :], xin[k][:, h * 512:(h + 1) * 512],
                 ALU.add, [c.bank[bank], Rxin[k]], [Ryo[k]])
        S.ld("pool", y_d[i * 128:(i + 1) * 128, :], yo[k], reads=[Ryo[k]])

    stages = [(c_add, 3), (c_mm, 2), (c_ld, 0)]
    tail = 3
    if next_g is not None:
        n_stat, n_xn, n_tr = norm_stages(c, Rg, lambda i: yo[i % NX], lambda i: Ryo[i % NX], "C")
        stages = [(n_tr, 6), (n_xn, 5), (n_stat, 4)] + stages
        tail = 6
    for n in range(NT + tail):
        for fn, d in stages:
            i = n - d
            if 0 <= i < NT:
                fn(i)


def gate_phase(c, w_in, col0):
    S = c.S
    xnT3 = c.xnT[:, :].rearrange("p (c t) -> p c t", c=8)
    wv = w_in.rearrange("(c p) n -> p c n", p=128)
    wg = c.wbuf[:, :].rearrange("p (c n) -> p c n", c=8)
    Rwg = [Res("wgA"), Res("wgB")]
    for h in range(2):
        S.ld("pool", wg[:, 4 * h:4 * h + 4, :], wv[:, 4 * h:4 * h + 4, col0:col0 + 1024], writes=[Rwg[h]])
    c.Rog = [[Res(f"og{ch}_{tg}") for tg in range(NG)] for ch in range(8)]
    k = 0
    for ch in range(8):
        for tg in range(NG):
            bank = k % 4
            k += 1
            for cc in range(8):
                S.mm(c.ps[:, bank, :], wg[:, cc, ch * 128:(ch + 1) * 128], xnT3[:, cc, tg * 512:(tg + 1) * 512],
                     cc == 0, cc == 7, [Rwg[cc // 4]] + c.RxnT[4 * tg:4 * tg + 4], [c.bank[bank]])
            S.act(c.ogT[:, ch, tg * 512:(tg + 1) * 512], c.ps[:, bank, :], AF.Silu, [c.bank[bank]], [c.Rog[ch][tg]])


def layer_sb(c, w_in):
    S = c.S
    gate_phase(c, w_in, 3072)
    S.barrier()
    xnT3 = c.xnT[:, :].rearrange("p (c t) -> p c t", c=8)
    wv = w_in.rearrange("(c p) n -> p c n", p=128)
    qT = c.hbuf[:, 0:S_LEN]
    kT = c.hbuf[:, S_LEN:2 * S_LEN]
    v = c.hbuf[:, 2 * S_LEN:3 * S_LEN].rearrange("p (i f) -> p i f", f=128)
    RqT = [Res(f"qT{g}") for g in range(NG)]
    RkT = [Res(f"kT{g}") for g in range(NG)]
    Rv = [Res(f"v{g}") for g in range(NG)]
    wsl = [c.wbuf[:, 3072 * b:3072 * (b + 1)].rearrange("p (s c n) -> p s c n", s=3, c=8) for b in range(2)]
    Rw = [Res("wsl0"), Res("wsl1")]
    NE, NL, NW, NA = 3, 4, 2, 2
    off = 0
    E, L, Wb, Ab = [], [], [], []
    for i in range(NE):
        E.append(c.scr[:, off:off + 1024].rearrange("p (h t) -> p h t", h=2)); off += 1024
    for i in range(NL):
        L.append(c.scr[:, off:off + 512].bitcast(BF16).rearrange("p (h t) -> p h t", h=2)); off += 512
    for i in range(NW):
        Wb.append(c.scr[:, off:off + 512].bitcast(BF16).rearrange("p (h t) -> p h t", h=2)); off += 512
    assert off <= 6144
    for i in range(NA):
        Ab.append(c.wbuf[:, 6144 + 1024 * i:6144 + 1024 * (i + 1)].rearrange("p (h t) -> p h t", h=2))
    RE = [Res(f"E{i}") for i in range(NE)]
    RL = [Res(f"L{i}") for i in range(NL)]
    RW = [Res(f"W{i}") for i in range(NW)]
    RA = [Res(f"A{i}") for i in range(NA)]
    AB = [4, 5]
    OBK = 6
    gcount = 0
    mask2 = c.mstrict.unsqueeze(1).to_broadcast([128, 2, 128])

    def load_w(hp):
        b = hp % 2
        for s in range(3):
            col = s * 1024 + hp * 128
            S.ld("pool", wsl[b][:, s, :, :], wv[:, :, col:col + 128], writes=[Rw[b]])

    def proj_groups(hp, banks):
        b = hp % 2
        k = 0
        for tg in reversed(range(NG)):
            for s_, (dst, Rd) in enumerate(((qT, RqT), (kT, RkT))):
                bank = banks[k % len(banks)]
                k += 1

                def g_qk(s_=s_, dst=dst, Rd=Rd, bank=bank, tg=tg):
                    for cc in range(8):
                        S.mm(c.ps[:, bank, :], wsl[b][:, s_, cc, :], xnT3[:, cc, tg * 512:(tg + 1) * 512], cc == 0, cc == 7,
                             [Rw[b]] + c.RxnT[4 * tg:4 * tg + 4], [c.bank[bank]])
                    S.cp("dve", dst[:, tg * 512:(tg + 1) * 512], c.ps[:, bank, :], [c.bank[bank]], [Rd[tg]])
                yield tg, g_qk
            bank = banks[k % len(banks)]
            k += 1
            for j in range(4):
                def g_v(j=j, bank=bank, tg=tg):
                    i = 4 * tg + j
                    for cc in range(8):
                        S.mm(c.ps[:, bank, j * 128:(j + 1) * 128], xnT3[:, cc, i * 128:(i + 1) * 128], wsl[b][:, 2, cc, :],
                             cc == 0, cc == 7, [Rw[b], c.RxnT[i]], [c.bank[bank]])
                    if j == 3:
                        S.cp("dve", v[:, 4 * tg:4 * tg + 4, :], c.ps[:, bank, :].rearrange("p (j f) -> p j f", f=128),
                             [c.bank[bank]], [Rv[tg]])
                yield tg, g_v

    load_w(0)
    load_w(1)
    for _, g_ in proj_groups(0, [0, 1, 2, 3]):
        g_()
    for hp in range(8):
        if 1 <= hp and hp + 1 < 8:
            load_w(hp + 1)
        pending = list(proj_groups(hp + 1, [7])) if hp + 1 < 8 else []
        G = []
        sweep_end = {}
        for qg in reversed(range(NG)):
            for kb in reversed(range(4 * qg + 4)):
                G.append((qg, kb))
            sweep_end[qg] = len(G) - 1
        n_g = len(G)

        def info(gi):
            qg, kb = G[gi]
            r = kb - 4 * qg
            c0 = r * 128 if r >= 0 else 0
            return qg, kb, r, c0, kb == 4 * qg + 3, kb == 0, gcount + gi

        def st_z(gi):
            qg, kb, r, c0, first, last, gid = info(gi)
            zp = 2 * (gid % 2)
            for hd in range(2):
                rows = slice(hd * 64, hd * 64 + 64)
                S.mm(c.ps[:, zp + hd, c0:512], kT[rows, kb * 128:(kb + 1) * 128], qT[rows, qg * 512 + c0:(qg + 1) * 512],
                     True, True, [RkT[kb // 4], RqT[qg]], [c.bank[zp + hd]])

        def st_e(gi):
            qg, kb, r, c0, first, last, gid = info(gi)
            zp = 2 * (gid % 2)
            eb = gid % NE
            S.act(E[eb][:, :, c0:512], c.ps[:, zp:zp + 2, c0:512], AF.Exp, [c.bank[zp], c.bank[zp + 1]], [RE[eb]],
                  scale=0.125)
            if r >= 0:
                S.tt("dve", E[eb][:, :, c0:c0 + 128], E[eb][:, :, c0:c0 + 128], mask2, ALU.mult,
                     [RE[eb], c.Rconst], [RE[eb]])

        def st_l(gi):
            qg, kb, r, c0, first, last, gid = info(gi)
            eb = gid % NE
            lb = gid % NL
            S.act(L[lb][:, :, c0:512], E[eb][:, :, c0:512], AF.Ln, [RE[eb]], [RL[lb]], bias=1.0, scale=1.0)

        def st_cum(gi):
            qg, kb, r, c0, first, last, gid = info(gi)
            lb = gid % NL
            for hd in range(2):
                S.mm(c.ps[:, AB[hd], c0:512], c.tinc, L[lb][:, hd, c0:512], first, False, [RL[lb], c.Rconst],
                     [c.bank[AB[hd]]], skip=True)

        def st_w(gi):
            qg, kb, r, c0, first, last, gid = info(gi)
            wb = gid % NW
            eb = gid % NE
            a_i = gid % NA
            S.act(Wb[wb][:, :, c0:512], c.ps[:, 4:6, c0:512], AF.Exp, [c.bank[4], c.bank[5]], [RW[wb]])
            S.tt("dve", Ab[a_i][:, :, c0:512], E[eb][:, :, c0:512], Wb[wb][:, :, c0:512], ALU.mult,
                 [RE[eb], RW[wb]], [RA[a_i]])

        def st_car(gi):
            qg, kb, r, c0, first, last, gid = info(gi)
            lb = gid % NL
            if not last:
                for hd in range(2):
                    S.mm(c.ps[:, AB[hd], c0:512], c.tcar, L[lb][:, hd, c0:512], False, False, [RL[lb], c.Rconst],
                         [c.bank[AB[hd]]], skip=True)

        def st_av(gi):
            qg, kb, r, c0, first, last, gid = info(gi)
            a_i = gid % NA
            for hd in range(2):
                rows = slice(hd * 64, hd * 64 + 64)
                S.mm(c.ps[rows, OBK, c0:512], v[:, kb, hd * 64:(hd + 1) * 64], Ab[a_i][:, hd, c0:512], first, last,
                     [RA[a_i], Rv[kb // 4]], [c.bank[OBK]], skip=True)
            if last:
                og = c.ogT[:, hp, qg * 512:(qg + 1) * 512]
                S.tt("dve", og, c.ps[:, OBK, :], og, ALU.mult, [c.bank[OBK], c.Rog[hp][qg]], [c.Rog[hp][qg]])

        stages = [(st_car, 4), (st_cum, 3), (st_av, 5), (st_z, 0), (st_w, 3), (st_l, 2), (st_e, 1)]
        for n in range(n_g + 5):
            for fn, d in stages:
                gi = n - d
                if 0 <= gi < n_g:
                    fn(gi)
            if pending and n >= sweep_end[pending[0][0]] + 6:
                pending.pop(0)[1]()
        for _, g_ in pending:
            g_()
        gcount += n_g


def layer_mla(c, w):
    S = c.S
    w_in = w["w_in"]
    gate_phase(c, w_in, 448)
    S.barrier()
    xnT3 = c.xnT[:, :].rearrange("p (c t) -> p c t", c=8)
    wv = w_in.rearrange("(c p) n -> p c n", p=128)
    sm = c.small
    scr = c.scr
    Rgn = Res("mla_g")
    gqa = c.gbc[:, 0:256]
    gkva = c.gbc[:, 256:384]
    gq192 = c.gbc[:, 384:576]
    gk192 = c.gbc[:, 576:768]
    gkpe = c.gbc[:, 704:768]
    S.ld("sp", gqa, w["q_a_norm"].partition_broadcast(128), writes=[Rgn])
    S.ld("sp", gkva, w["kv_a_norm"].partition_broadcast(128), writes=[Rgn])
    S.ld("sp", gq192, w["q_head_norm"].partition_broadcast(128), writes=[Rgn])
    S.ld("sp", gk192, w["k_head_norm"].partition_broadcast(128), writes=[Rgn])
    S.tt("dve", gq192[:, 0:128], gq192[:, 0:128], gk192[:, 0:128], ALU.mult, [Rgn], [Rgn])
    cosT = scr[:, 0:1024].rearrange("p (i f) -> p i f", f=32)
    sinT = scr[:, 1024:2048].rearrange("p (i f) -> p i f", f=32)
    Rtab = Res("ropetab")
    S.ld("sp", scr[:, 0:2048], c.cf_d[:, CF_COS:CF_COS + 2048], writes=[Rtab])
    kpeT = scr[:, 2048:4096].bitcast(BF16)
    Rkpe_hi = Res("kpe_hi")
    S.op("dve", lambda e, o=scr[64:128, 2048:4096]: e.memset(o, 0.0), [], [Rkpe_hi])
    qlnT3 = c.hbuf[:, 0:2 * S_LEN].rearrange("p (a t) -> p a t", a=2)
    kvlnT = c.hbuf[:, 2 * S_LEN:3 * S_LEN]
    sskpe = sm[:, 128:160]
    rstdk = sm[:, 160:192]
    Rqln = [Res(f"qln{g}") for g in range(NG)]
    Rkvln = [Res(f"kvln{g}") for g in range(NG)]
    Rkpe = [Res(f"kpe{g}") for g in range(NG)]
    Rsskpe = [Res(f"sskpe{g}") for g in range(NG)]

    def rope(x, out, i, Rx, Rout, t1, t2, Rt):
        cb = cosT[:, i, :].unsqueeze(1).to_broadcast([128, 2, 32])
        S.tt("dve", t1.rearrange("p (a f) -> p a f", a=2), x.rearrange("p (a f) -> p a f", a=2), cb, ALU.mult,
             [Rx, Rtab], [Rt])
        S.tt("dve", t2[:, 0:32], x[:, 32:64], sinT[:, i, :], ALU.mult, [Rx, Rtab], [Rt])
        S.tt("dve", t2[:, 32:64], x[:, 0:32], sinT[:, i, :], ALU.mult, [Rx, Rtab], [Rt])
        S.tt("dve", out[:, 0:32], t1[:, 0:32], t2[:, 0:32], ALU.subtract, [Rt], [Rout])
        S.tt("dve", out[:, 32:64], t1[:, 32:64], t2[:, 32:64], ALU.add, [Rt], [Rout])

    wlat = c.wbuf[:, 0:3584].rearrange("p (c n) -> p c n", c=8)
    Rwlat = Res("wlat")
    S.ld("pool", wlat[:, 0:4, :], wv[:, 0:4, 0:448], writes=[Rwlat])
    S.ld("pool", wlat[:, 4:8, :], wv[:, 4:8, 0:448], writes=[Rwlat])
    o = 4096
    junk = [scr[:, o + 448 * j:o + 448 * (j + 1)] for j in range(2)]; o += 896
    NLN = 3
    lnb = [scr[:, o + 192 * j:o + 192 * (j + 1)].bitcast(BF16) for j in range(NLN)]; o += 192 * NLN
    kp = [scr[:, o + 64 * j:o + 64 * (j + 1)] for j in range(2)]; o += 128
    t1 = [scr[:, o + 64 * j:o + 64 * (j + 1)] for j in range(2)]; o += 128
    t2 = [scr[:, o + 64 * j:o + 64 * (j + 1)] for j in range(2)]; o += 128
    kr = [scr[:, o + 32 * j:o + 32 * (j + 1)].bitcast(BF16) for j in range(2)]; o += 64
    assert o <= 6144, o
    Rjunk = [Res("junk0"), Res("junk1")]
    Rkp = [Res("kp0"), Res("kp1")]
    Rt = [Res("ropet0"), Res("ropet1")]
    Rlnb = [Res(f"lnb{j}") for j in range(NLN)]
    Rkr = [Res("kr0"), Res("kr1")]
    NSB = 4
    Rst = [Res(f"mst{j}") for j in range(NSB)]
    LB = [0, 1, 2, 3, 6, 7]

    def binfo(i):
        k = i % NSB
        sb_ = 192 + 8 * k
        return i // 4, LB[i % len(LB)], sm[:, sb_:sb_ + 2], sm[:, sb_ + 2:sb_ + 4], sm[:, sb_ + 4:sb_ + 6], k, i % 2, i % NLN

    def b_mm(i):
        tg, bank, ss2, lg2, rs2, k, pb, li = binfo(i)
        for cc in range(8):
            S.mm(c.ps[:, bank, 0:448], xnT3[:, cc, i * 128:(i + 1) * 128], wlat[:, cc, :], cc == 0, cc == 7,
                 [Rwlat, c.RxnT[i]], [c.bank[bank]])

    def b_sq(i):
        tg, bank, ss2, lg2, rs2, k, pb, li = binfo(i)
        S.act(junk[pb][:, 0:256], c.ps[:, bank, 0:256], AF.Square, [c.bank[bank]], [Rjunk[pb], Rst[k]], accum_out=ss2[:, 0:1])
        S.act(junk[pb][:, 256:384], c.ps[:, bank, 256:384], AF.Square, [c.bank[bank]], [Rjunk[pb], Rst[k]],
              accum_out=ss2[:, 1:2])
        S.act(junk[pb][:, 384:448], c.ps[:, bank, 384:448], AF.Square, [c.bank[bank]], [Rjunk[pb], Rsskpe[tg]],
              accum_out=sskpe[:, i:i + 1])

    def b_rs(i):
        tg, bank, ss2, lg2, rs2, k, pb, li = binfo(i)
        S.act(lg2[:, 0:1], ss2[:, 0:1], AF.Ln, [Rst[k]], [Rst[k]], scale=1.0 / 256, bias=EPS)
        S.act(lg2[:, 1:2], ss2[:, 1:2], AF.Ln, [Rst[k]], [Rst[k]], scale=1.0 / 128, bias=EPS)
        S.act(rs2, lg2, AF.Exp, [Rst[k]], [Rst[k]], scale=-0.5)

    def b_ln(i):
        tg, bank, ss2, lg2, rs2, k, pb, li = binfo(i)
        S.stt(lnb[li][:, 0:256], c.ps[:, bank, 0:256], rs2[:, 0:1], gqa, ALU.mult, ALU.mult,
              [c.bank[bank], Rst[k], Rgn], [Rlnb[li]])
        S.stt(lnb[li][:, 256:384], c.ps[:, bank, 256:384], rs2[:, 1:2], gkva, ALU.mult, ALU.mult,
              [c.bank[bank], Rst[k], Rgn], [Rlnb[li]])
        S.tt("dve", kp[pb], c.ps[:, bank, 384:448], gkpe, ALU.mult, [c.bank[bank], Rgn], [Rkp[pb]])

    def b_rope(i):
        tg, bank, ss2, lg2, rs2, k, pb, li = binfo(i)
        rope(kp[pb], kr[pb], i, Rkp[pb], Rkr[pb], t1[pb], t2[pb], Rt[pb])

    def b_tr(i):
        tg, bank, ss2, lg2, rs2, k, pb, li = binfo(i)
        tbank = 4 + pb
        psT = c.ps[:, tbank, :].bitcast(BF16)
        for a in range(3):
            S.tr(psT[:, a * 128:(a + 1) * 128], lnb[li][:, a * 128:(a + 1) * 128], c.ident,
                 [Rlnb[li], c.Rconst], [c.bank[tbank]])
        S.tr(psT[0:64, 384:512], kr[pb], c.ident, [Rkr[pb], c.Rconst], [c.bank[tbank]])
        S.cp("act", qlnT3[:, :, i * 128:(i + 1) * 128], psT[:, 0:256].rearrange("p (a t) -> p a t", a=2),
             [c.bank[tbank]], [Rqln[tg]])
        S.cp("dve", kvlnT[:, i * 128:(i + 1) * 128], psT[:, 256:384], [c.bank[tbank]], [Rkvln[tg]])
        S.cp("dve", kpeT[0:64, i * 128:(i + 1) * 128], psT[0:64, 384:512], [c.bank[tbank]], [Rkpe[tg]])

    bstages = [(b_tr, 5), (b_rope, 4), (b_ln, 3), (b_rs, 2), (b_sq, 1), (b_mm, 0)]
    for n in range(NT + 5):
        for fn, d in bstages:
            i = n - d
            if 0 <= i < NT:
                fn(i)
    S.barrier()
    wuq = c.wbuf[:, 0:3072].rearrange("p (a n) -> p a n", a=2)
    wukv = c.wbuf[:, 3072:5120]
    Rwu = Res("wu")
    S.ld("pool", wuq, w["w_uq"].rearrange("(a p) n -> p a n", p=128), writes=[Rwu])
    S.ld("pool", wukv, w["w_ukv"], writes=[Rwu])
    X = c.xnT
    qTn = X[:, 0:4096]
    qTp = X[:, 4096:8192]
    RqTp_hi = Res("qTp_hi")
    S.op("dve", lambda e, o=X[64:128, 4096:8192]: e.memset(o, 0.0), [], [RqTp_hi])
    kTn = X[:, 8192:12288]
    vh = X[:, 12288:16384].rearrange("p (i f) -> p i f", f=128)
    NP = 3
    Pb = [X[:, 16384 + 512 * j:16384 + 512 * (j + 1)] for j in range(NP)]
    NQR = 3
    qr = [X[:, 18432 + 256 * j:18432 + 256 * j + 192] for j in range(NQR)]
    XF = X[:, 20480:24576].bitcast(F32)
    rcb = XF[:, 0:512]
    tbuf = XF[:, 512:1024]
    qpe = [XF[:, 1024 + 64 * j:1088 + 64 * j] for j in range(2)]
    u1 = [XF[:, 1152 + 64 * j:1216 + 64 * j] for j in range(2)]
    u2 = [XF[:, 1280 + 64 * j:1344 + 64 * j] for j in range(2)]
    junk2 = [XF[:, 1408 + 320 * j:1408 + 320 * (j + 1)] for j in range(2)]
    Rsum = [X[:, 24576 + 1024 * j:24576 + 1024 * (j + 1)].bitcast(F32) for j in range(2)]
    ones32 = X[:, 26624:26880].bitcast(F32)
    RRs = [Res("Rsum0"), Res("Rsum1")]
    Rones32 = Res("ones32")
    S.op("pool", lambda e, o=ones32: e.memset(o, 1.0), [], [Rones32])
    RqTn = [Res(f"qTn{g}") for g in range(NG)]
    RqTp = [Res(f"qTp{g}") for g in range(NG)]
    RkTn = [Res(f"kTn{g}") for g in range(NG)]
    Rvh = [Res(f"vh{g}") for g in range(NG)]
    Rrk = [Res(f"rstdk{g}") for g in range(NG)]
    RPb = [Res(f"mPb{j}") for j in range(NP)]
    Rqr = [Res(f"qr{j}") for j in range(NQR)]
    Rrc, Rtb = Res("rcb"), Res("tbuf")
    Rqpe = [Res("qpe0"), Res("qpe1")]
    Ru = [Res("u0"), Res("u1")]
    Rj2 = [Res("junk20"), Res("junk21")]
    NST = 4
    Rs2 = [Res(f"hst{j}") for j in range(NST)]
    QB = [0, 1, 2, 3, 6, 7]
    gid = 0
    sw = 0
    tcount = 0
    for h in range(8):
        for tg in range(NG):
            bank = 4 + tg % 2
            S.mm(c.ps[:, bank, :], wukv[:, h * 256:h * 256 + 128], kvlnT[:, tg * 512:(tg + 1) * 512], True, True,
                 [Rwu, Rkvln[tg]], [c.bank[bank]])
            S.cp("act", kTn[:, tg * 512:(tg + 1) * 512], c.ps[:, bank, :], [c.bank[bank]], [RkTn[tg]])

        def tinfo(i):
            t = tcount + i
            tg, j = divmod(i, 4)
            sb_ = 224 + 8 * (t % NST)
            return (t, tg, j, QB[t % len(QB)], slice(i * 128, (i + 1) * 128), sm[:, sb_:sb_ + 2], sm[:, sb_ + 2:sb_ + 4],
                    sm[:, sb_ + 4:sb_ + 5], t % NST, t % NQR, t % 2)

        def t_mm(i):
            t, tg, j, bank, tok, ss2, lg2, rsq, si, qi, pi = tinfo(i)
            S.mm(c.ps[:, bank, 0:192], qlnT3[:, 0, tok], wuq[:, 0, h * 192:(h + 1) * 192], True, False,
                 [Rwu, Rqln[tg]], [c.bank[bank]])
            S.mm(c.ps[:, bank, 0:192], qlnT3[:, 1, tok], wuq[:, 1, h * 192:(h + 1) * 192], False, True,
                 [Rwu, Rqln[tg]], [c.bank[bank]])
            S.mm(c.ps[:, bank, 192:448], kvlnT[:, tok], wukv[:, h * 256:(h + 1) * 256], True, True,
                 [Rwu, Rkvln[tg]], [c.bank[bank]])

        def t_sq(i):
            t, tg, j, bank, tok, ss2, lg2, rsq, si, qi, pi = tinfo(i)
            S.act(junk2[pi][:, 0:192], c.ps[:, bank, 0:192], AF.Square, [c.bank[bank]], [Rj2[pi], Rs2[si]],
                  accum_out=ss2[:, 0:1])
            S.act(junk2[pi][:, 192:320], c.ps[:, bank, 192:320], AF.Square, [c.bank[bank]], [Rj2[pi], Rs2[si]],
                  accum_out=ss2[:, 1:2])
            S.cp("act", vh[:, i, :], c.ps[:, bank, 320:448], [c.bank[bank]], [Rvh[tg]])

        def t_add(i):
            t, tg, j, bank, tok, ss2, lg2, rsq, si, qi, pi = tinfo(i)
            S.tt("dve", ss2[:, 1:2], ss2[:, 1:2], sskpe[:, i:i + 1], ALU.add, [Rs2[si], Rsskpe[tg]], [Rs2[si]])

        def t_rs(i):
            t, tg, j, bank, tok, ss2, lg2, rsq, si, qi, pi = tinfo(i)
            S.act(lg2, ss2, AF.Ln, [Rs2[si]], [Rs2[si]], scale=1.0 / 192, bias=EPS)
            S.act(rsq, lg2[:, 0:1], AF.Exp, [Rs2[si]], [Rs2[si]], scale=-0.5)
            S.act(rstdk[:, i:i + 1], lg2[:, 1:2], AF.Exp, [Rs2[si]], [Rrk[tg]], scale=-0.5, bias=-0.5 * math.log(192.0))

        def t_qn(i):
            t, tg, j, bank, tok, ss2, lg2, rsq, si, qi, pi = tinfo(i)
            S.stt(qr[qi][:, 0:128], c.ps[:, bank, 0:128], rsq, gq192[:, 0:128], ALU.mult, ALU.mult,
                  [c.bank[bank], Rs2[si], Rgn], [Rqr[qi]])
            S.stt(qpe[pi], c.ps[:, bank, 128:192], rsq, gq192[:, 128:192], ALU.mult, ALU.mult,
                  [c.bank[bank], Rs2[si], Rgn], [Rqpe[pi]])
            rope(qpe[pi], qr[qi][:, 128:192], i, Rqpe[pi], Rqr[qi], u1[pi], u2[pi], Ru[pi])

        def t_tr(i):
            t, tg, j, bank, tok, ss2, lg2, rsq, si, qi, pi = tinfo(i)
            tbank = 4 + tg % 2
            psT = c.ps[:, tbank, :].bitcast(BF16)
            S.tr(psT[:, j * 128:(j + 1) * 128], qr[qi][:, 0:128], c.ident, [Rqr[qi], c.Rconst], [c.bank[tbank]])
            S.tr(psT[0:64, 512 + j * 128:512 + (j + 1) * 128], qr[qi][:, 128:192], c.ident,
                 [Rqr[qi], c.Rconst], [c.bank[tbank]])
            if j == 3:
                S.cp("dve", qTn[:, tg * 512:(tg + 1) * 512], psT[:, 0:512], [c.bank[tbank]], [RqTn[tg]])
                S.cp("dve", qTp[0:64, tg * 512:(tg + 1) * 512], psT[0:64, 512:1024], [c.bank[tbank]], [RqTp[tg]])

        tstages = [(t_tr, 5), (t_qn, 4), (t_rs, 3), (t_add, 2), (t_sq, 1), (t_mm, 0)]
        for n in range(NT + 5):
            for fn, d in tstages:
                i = n - d
                if 0 <= i < NT:
                    fn(i)
        tcount += NT
        G = []
        for qg in range(NG):
            for kb in reversed(range(4 * qg + 4)):
                G.append((qg, kb))
        n_g = len(G)

        def info(gi):
            qg, kb = G[gi]
            r = kb - 4 * qg
            c0 = r * 128 if r >= 0 else 0
            return qg, kb, r, c0, kb == 4 * qg + 3, kb == 0, gid + gi, sw + qg

        def st_z(gi):
            qg, kb, r, c0, first, last, g_, sw_ = info(gi)
            zb = g_ % 4
            S.mm(c.ps[:, zb, c0:512], kTn[:, kb * 128:(kb + 1) * 128], qTn[:, qg * 512 + c0:(qg + 1) * 512], True, False,
                 [RkTn[kb // 4], RqTn[qg]], [c.bank[zb]])
            S.mm(c.ps[:, zb, c0:512], kpeT[:, kb * 128:(kb + 1) * 128], qTp[:, qg * 512 + c0:(qg + 1) * 512],
                 False, True, [Rkpe[kb // 4], RqTp[qg], Rkpe_hi, RqTp_hi], [c.bank[zb]])

        def st_e(gi):
            qg, kb, r, c0, first, last, g_, sw_ = info(gi)
            zb = g_ % 4
            pi = g_ % NP
            S.act(Pb[pi][:, c0:512], c.ps[:, zb, c0:512], AF.Exp, [c.bank[zb], Rrk[kb // 4]], [RPb[pi]],
                  scale=rstdk[:, kb:kb + 1])
            if r >= 0:
                S.tt("dve", Pb[pi][:, c0:c0 + 128], Pb[pi][:, c0:c0 + 128], c.mincl, ALU.mult,
                     [RPb[pi], c.Rconst], [RPb[pi]])

        def st_av(gi):
            qg, kb, r, c0, first, last, g_, sw_ = info(gi)
            pi = g_ % NP
            ob = 4 + sw_ % 2
            db = 6 + sw_ % 2
            ri = sw_ % 2
            S.mm(c.ps[:, ob, c0:512], vh[:, kb, :], Pb[pi][:, c0:512], first, last, [Rvh[kb // 4], RPb[pi]],
                 [c.bank[ob]], skip=True)
            S.mm(c.ps[:, db, c0:512], c.ones, Pb[pi][:, c0:512], first, last, [c.Rconst, RPb[pi]],
                 [c.bank[db]], skip=True)
            if last:
                S.act(rcb, c.ps[:, db, :], AF.Ln, [c.bank[db]], [Rrc])
                S.act(rcb, rcb, AF.Exp, [Rrc], [Rrc], scale=-1.0)
                S.tt("dve", tbuf, c.ps[:, ob, :], rcb, ALU.mult, [c.bank[ob], Rrc], [Rtb])
                og = c.ogT[:, h, qg * 512:(qg + 1) * 512]
                S.tt("dve", og, tbuf, og, ALU.mult, [Rtb, c.Rog[h][qg]], [c.Rog[h][qg]])

        stages = [(st_z, 0), (st_e, 1), (st_av, 2)]
        for n in range(n_g + 2):
            for fn, d in reversed(stages):
                gi = n - d
                if 0 <= gi < n_g:
                    fn(gi)
        gid += n_g
        sw += NG


def layer_swa(c, w):
    S = c.S
    w_in = w["w_in"]
    gate_phase(c, w_in, 1536)
    S.barrier()
    xnT3 = c.xnT[:, :].rearrange("p (c t) -> p c t", c=8)
    wv = w_in.rearrange("(c p) n -> p c n", p=128)
    qT3 = c.hbuf[:, 0:2 * S_LEN].rearrange("p (a t) -> p a t", a=2)
    kT2 = c.hbuf[:, 2 * S_LEN:3 * S_LEN]
    scr = c.scr
    off = 0
    vg = scr[:, off:off + 1024].bitcast(BF16).rearrange("p (i f) -> p i f", f=64); off += 1024
    junk = [scr[:, off + 320 * j:off + 320 * (j + 1)] for j in range(2)]; off += 640
    tmpq = [scr[:, off + 256 * j:off + 256 * (j + 1)] for j in range(2)]; off += 512
    NQN = 3
    qn = [scr[:, off + 128 * j:off + 128 * (j + 1)].bitcast(BF16) for j in range(NQN)]; off += 128 * NQN
    biasg = scr[:, off:off + 1024]; off += 1024
    Tb = scr[:, off:off + 1024]; off += 1024
    Pb = [scr[:, off:off + 512].bitcast(BF16), scr[:, off + 512:off + 1024].bitcast(BF16)]; off += 1024
    dnb = scr[:, off:off + 256]; off += 256
    gqk4 = scr[:, off:off + 256]; off += 256
    assert off <= 6144, off
    sm = c.small
    kscale = sm[:, 16:48]
    es16 = sm[:, 48:64]
    Rbias, RTb, Rdn, Rgqk, Res16 = (Res(n) for n in ("biasg", "Tb", "dnb", "gqk", "es16"))
    Rjunk = [Res("junk0"), Res("junk1")]
    Rtmpq = [Res("tmpq0"), Res("tmpq1")]
    Rqn = [Res(f"qn{j}") for j in range(NQN)]
    RPb = [Res("Pb0"), Res("Pb1")]
    NST = 4
    Rst = [Res(f"sst{j}") for j in range(NST)]
    RqT = [Res(f"qT{g}") for g in range(NG)]
    RkT = [Res(f"kT{g}") for g in range(NG)]
    Rvg = [Res(f"vg{g}") for g in range(NG)]
    Rks = [Res(f"ks{g}") for g in range(NG)]
    wtm = [c.wbuf[:, 4096 * b:4096 * b + 3072].rearrange("p (c n) -> p c n", c=8) for b in range(2)]
    wk2 = [c.wbuf[:, 4096 * b + 3072:4096 * (b + 1)].rearrange("p (c n) -> p c n", c=8) for b in range(2)]
    Rw = [Res("swaw0"), Res("swaw1")]
    gk4 = junk[0][:, 0:256]
    for j in range(4):
        S.ld("sp", gqk4[:, j * 64:(j + 1) * 64], w["q_head_norm"].partition_broadcast(128), writes=[Rgqk])
        S.ld("sp", gk4[:, j * 64:(j + 1) * 64], w["k_head_norm"].partition_broadcast(128), writes=[Rjunk[0]])
    S.tt("dve", gqk4, gqk4, gk4, ALU.mult, [Rjunk[0], Rgqk], [Rgqk])
    S.ld("sp", es16, w["sinks"].partition_broadcast(128), writes=[Res16])
    S.act(es16, es16, AF.Exp, [Res16], [Res16])

    def load_w(g):
        b = g % 2
        S.ld("pool", wtm[b][:, :, 0:256], wv[:, :, g * 256:(g + 1) * 256], writes=[Rw[b]])
        S.ld("pool", wtm[b][:, :, 256:320], wv[:, :, 1024 + g * 64:1024 + (g + 1) * 64], writes=[Rw[b]])
        S.ld("pool", wtm[b][:, :, 320:384], wv[:, :, 1280 + g * 64:1280 + (g + 1) * 64], writes=[Rw[b]])
        for d in range(2):
            S.ld("pool", wk2[b][:, :, d * 64:(d + 1) * 64], wv[:, :, 1024 + g * 64:1024 + (g + 1) * 64], writes=[Rw[b]])

    load_w(0)
    TB = [0, 1, 2, 3, 6, 7]
    tcount = 0
    acount = 0
    for g in range(4):
        b = g % 2
        if g + 1 < 4:
            load_w(g + 1)
        for kbi in range(2):
            S.ld("sp", biasg[:, kbi * 512:(kbi + 1) * 512],
                 c.cf_d[:, CF_BIAS + kbi * 2048 + g * 512:CF_BIAS + kbi * 2048 + (g + 1) * 512], writes=[Rbias])
        for tg in range(NG):
            bank = 4 + tg % 2
            for cc in range(8):
                S.mm(c.ps[:, bank, :], wk2[b][:, cc, :], xnT3[:, cc, tg * 512:(tg + 1) * 512], cc == 0, cc == 7,
                     [Rw[b]] + c.RxnT[4 * tg:4 * tg + 4], [c.bank[bank]])
            S.cp("act", kT2[:, tg * 512:(tg + 1) * 512], c.ps[:, bank, :], [c.bank[bank]], [RkT[tg]])

        def tinfo(i):
            t = tcount + i
            tg, j = divmod(i, 4)
            sb_ = 64 + 16 * (t % NST)
            return (t, tg, j, TB[t % len(TB)], sm[:, sb_:sb_ + 5], sm[:, sb_ + 5:sb_ + 10], sm[:, sb_ + 10:sb_ + 14],
                    t % NST, t % 2, t % NQN)

        def t_mm(i):
            t, tg, j, bank, ss5, lg5, rs4, si, pi, qi = tinfo(i)
            for cc in range(8):
                S.mm(c.ps[:, bank, 0:384], xnT3[:, cc, i * 128:(i + 1) * 128], wtm[b][:, cc, :], cc == 0, cc == 7,
                     [Rw[b], c.RxnT[i]], [c.bank[bank]])

        def t_sq(i):
            t, tg, j, bank, ss5, lg5, rs4, si, pi, qi = tinfo(i)
            S.act(junk[pi], c.ps[:, bank, 0:320], AF.Square, [c.bank[bank]], [Rjunk[pi]])
            S.cp("act", vg[:, i, :], c.ps[:, bank, 320:384], [c.bank[bank]], [Rvg[tg]])

        def t_red(i):
            t, tg, j, bank, ss5, lg5, rs4, si, pi, qi = tinfo(i)
            S.op("dve", lambda e, o=ss5, i_=junk[pi].rearrange("p (h f) -> p h f", f=64): e.reduce_sum(out=o, in_=i_, axis=AX.X),
                 [Rjunk[pi]], [Rst[si]])

        def t_rs(i):
            t, tg, j, bank, ss5, lg5, rs4, si, pi, qi = tinfo(i)
            S.act(lg5, ss5, AF.Ln, [Rst[si]], [Rst[si]], scale=1.0 / 64, bias=EPS)
            S.act(rs4, lg5[:, 0:4], AF.Exp, [Rst[si]], [Rst[si]], scale=-0.5)
            S.act(kscale[:, i:i + 1], lg5[:, 4:5], AF.Exp, [Rst[si]], [Rks[tg]], scale=-0.5, bias=math.log(0.125))

        def t_qn(i):
            t, tg, j, bank, ss5, lg5, rs4, si, pi, qi = tinfo(i)
            S.tt("dve", tmpq[pi].rearrange("p (h f) -> p h f", f=64),
                 c.ps[:, bank, 0:256].rearrange("p (h f) -> p h f", f=64),
                 rs4.unsqueeze(2).to_broadcast([128, 4, 64]), ALU.mult, [c.bank[bank], Rst[si]], [Rtmpq[pi]])
            S.tt("dve", qn[qi], tmpq[pi], gqk4, ALU.mult, [Rtmpq[pi], Rgqk], [Rqn[qi]])

        def t_tr(i):
            t, tg, j, bank, ss5, lg5, rs4, si, pi, qi = tinfo(i)
            tbank = 4 + tg % 2
            psT = c.ps[:, tbank, :].bitcast(BF16)
            for a in range(2):
                S.tr(psT[:, a * 512 + j * 128:a * 512 + (j + 1) * 128], qn[qi][:, a * 128:(a + 1) * 128], c.ident,
                     [Rqn[qi], c.Rconst], [c.bank[tbank]])
            if j == 3:
                S.cp("dve", qT3[:, :, tg * 512:(tg + 1) * 512], psT.rearrange("p (a t) -> p a t", a=2),
                     [c.bank[tbank]], [RqT[tg]])

        tstages = [(t_tr, 5), (t_qn, 4), (t_rs, 3), (t_red, 2), (t_sq, 1), (t_mm, 0)]
        for n in range(NT + 5):
            for fn, d in tstages:
                i = n - d
                if 0 <= i < NT:
                    fn(i)
        tcount += NT

        def ainfo(qb):
            a_ = acount + qb
            kbs = [(0, qb - 1), (1, qb)] if qb > 0 else [(1, qb)]
            return a_, kbs, 2 * (a_ % 2), a_ % 2, 4 + a_ % 2

        def a_qk(qb):
            a_, kbs, zb0, pbi, ob = ainfo(qb)
            for kbi, kb in kbs:
                for hq in range(4):
                    p = hq % 2
                    rows = slice(p * 64, p * 64 + 64)
                    col = (kbi * 2 + hq // 2) * 128
                    S.mm(c.ps[:, zb0 + p, col:col + 128], kT2[rows, kb * 128:(kb + 1) * 128],
                         qT3[rows, hq // 2, qb * 128:(qb + 1) * 128], True, True,
                         [RkT[kb // 4], RqT[qb // 4]], [c.bank[zb0 + p]])

        def a_bias(qb):
            a_, kbs, zb0, pbi, ob = ainfo(qb)
            for kbi, kb in kbs:
                for p in range(2):
                    src = c.ps[:, zb0 + p, kbi * 256:(kbi + 1) * 256].rearrange("q (a t) -> q a t", a=2)
                    dst = Tb[:, kbi * 512:(kbi + 1) * 512].rearrange("q (a p t) -> q p a t", a=2, p=2)[:, p]
                    bia = biasg[:, kbi * 512:(kbi + 1) * 512].rearrange("q (a p t) -> q p a t", a=2, p=2)[:, p]
                    S.stt(dst, src, kscale[:, kb:kb + 1], bia, ALU.mult, ALU.add,
                          [c.bank[zb0 + p], Rks[kb // 4], Rbias], [RTb])

        def a_exp(qb):
            a_, kbs, zb0, pbi, ob = ainfo(qb)
            lo = 0 if qb > 0 else 512
            S.act(Pb[pbi][:, lo:1024], Tb[:, lo:1024], AF.Exp, [RTb], [RPb[pbi]])

        def a_av(qb):
            a_, kbs, zb0, pbi, ob = ainfo(qb)
            for hq in range(4):
                rows = slice((hq % 2) * 64, (hq % 2) * 64 + 64)
                col = (hq // 2) * 128
                for n_, (kbi, kb) in enumerate(kbs):
                    S.mm(c.ps[rows, ob, col:col + 128], vg[:, kb, :], Pb[pbi][:, (kbi * 4 + hq) * 128:(kbi * 4 + hq + 1) * 128],
                         n_ == 0, n_ == len(kbs) - 1, [Rvg[kb // 4], RPb[pbi]], [c.bank[ob]])
                for n_, (kbi, kb) in enumerate(kbs):
                    S.mm(c.ps[rows, ob, 256 + col:256 + col + 128], c.ones[:, 0:64],
                         Pb[pbi][:, (kbi * 4 + hq) * 128:(kbi * 4 + hq + 1) * 128],
                         n_ == 0, n_ == len(kbs) - 1, [c.Rconst, RPb[pbi]], [c.bank[ob]])

        def a_out(qb):
            a_, kbs, zb0, pbi, ob = ainfo(qb)
            for hq in range(4):
                rows = slice((hq % 2) * 64, (hq % 2) * 64 + 64)
                col = (hq // 2) * 128
                h = 4 * g + hq
                S.ts("dve", dnb[rows, col:col + 128], c.ps[rows, ob, 256 + col:256 + col + 128], es16[rows, h:h + 1],
                     None, ALU.add, None, [c.bank[ob], Res16], [Rdn])
            S.act(dnb, dnb, AF.Ln, [Rdn], [Rdn])
            S.act(dnb, dnb, AF.Exp, [Rdn], [Rdn], scale=-1.0)
            S.tt("dve", dnb, c.ps[:, ob, 0:256], dnb, ALU.mult, [c.bank[ob], Rdn], [Rdn])
            og = c.ogT[:, 2 * g:2 * g + 2, qb * 128:(qb + 1) * 128]
            Rogs = [c.Rog[2 * g][qb // 4], c.Rog[2 * g + 1][qb // 4]]
            S.tt("dve", og, dnb.rearrange("p (a t) -> p a t", a=2), og, ALU.mult, [Rdn] + Rogs, Rogs)

        astages = [(a_out, 4), (a_av, 3), (a_exp, 2), (a_bias, 1), (a_qk, 0)]
        for n in range(NT + 4):
            for fn, d in astages:
                qb = n - d
                if 0 <= qb < NT:
                    fn(qb)
        acount += NT


LAUNCH_GROUPS = [[0, 1, 2, 3]]
_CONSTS = None


def run_layers(layers, xs, inputs):
    global _CONSTS
    if _CONSTS is None:
        _CONSTS = make_consts()
    cbf, cf = _CONSTS
    nc = build_program(layers)
    names = [n for li in layers for n in LAYER_WEIGHTS[li]]
    in_maps = []
    for b in range(len(xs)):
        m = {"x": np.ascontiguousarray(xs[b], dtype=np.float32), "cbf": cbf, "cf": cf}
        for n in names:
            m[n] = np.ascontiguousarray(inputs[n], dtype=np.float32)
        in_maps.append(m)
    res = run_bass_kernel_spmd(nc, in_maps, core_ids=list(range(len(xs))))
    return [r["y"] for r in res.results]


def kernel(**inputs):
    x = np.asarray(inputs["x"])
    xs = [x[b] for b in range(x.shape[0])]
    for grp in LAUNCH_GROUPS:
        xs = run_layers(grp, xs, inputs)
    return np.stack(xs, axis=0).astype(np.float32)
```

```python
import math
import numpy as np
import ml_dtypes
import concourse.bass as bass
import concourse.mybir as mybir
from concourse.bass_utils import run_bass_kernel_spmd

F32 = mybir.dt.float32
BF16 = mybir.dt.bfloat16
AF = mybir.ActivationFunctionType
ALU = mybir.AluOpType
AX = mybir.AxisListType

S_LEN = 4096
D = 1024
NT = S_LEN // 128
NG = S_LEN // 512
EPS = 1e-6

LAYER_KIND = ["sb", "mla", "swa", "sb"]
LAYER_WEIGHTS = [
    ["l0_norm", "l0_w_in", "l0_w_out"],
    ["l1_norm", "l1_w_in", "l1_q_a_norm", "l1_w_uq", "l1_kv_a_norm", "l1_w_ukv",
     "l1_q_head_norm", "l1_k_head_norm", "l1_w_out"],
    ["l2_norm", "l2_w_in", "l2_q_head_norm", "l2_k_head_norm", "l2_sinks", "l2_w_out"],
    ["l3_norm", "l3_w_in", "l3_w_out"],
]
WSHAPES = {
    "l0_norm": [1024], "l0_w_in": [1024, 4096], "l0_w_out": [1024, 1024],
    "l1_norm": [1024], "l1_w_in": [1024, 1472], "l1_q_a_norm": [256], "l1_w_uq": [256, 1536],
    "l1_kv_a_norm": [128], "l1_w_ukv": [128, 2048], "l1_q_head_norm": [192], "l1_k_head_norm": [192],
    "l1_w_out": [1024, 1024],
    "l2_norm": [1024], "l2_w_in": [1024, 2560], "l2_q_head_norm": [64], "l2_k_head_norm": [64],
    "l2_sinks": [16], "l2_w_out": [1024, 1024],
    "l3_norm": [1024], "l3_w_in": [1024, 4096], "l3_w_out": [1024, 1024],
}


class Res:
    __slots__ = ("name", "w", "r", "excl")

    def __init__(self, name, excl=False):
        self.name = name
        self.w = None
        self.r = {}
        self.excl = excl


class Sched:
    ENGS = ("pe", "act", "dve", "pool", "sp")

    def __init__(self, sems, dma_pools):
        self.sems = sems
        self.cnt = {k: 0 for k in sems}
        self.ops = {e: [] for e in self.ENGS}
        self.seen = {e: {} for e in self.ENGS}
        self.dma_pools = dma_pools
        self.dma_rr = {q: 0 for q in dma_pools}
        self.nwaits = 0

    def _wait(self, e, key, val):
        if val <= 0 or val <= self.seen[e].get(key, 0):
            return
        self.seen[e][key] = val
        sem = self.sems[key]
        self.ops[e].append(lambda eng, sem=sem, val=val: eng.wait_ge(sem, val))
        self.nwaits += 1

    def _sync(self, e, reads, writes, dma):
        rd = [r for r in reads if not r.excl]
        wr = list(writes) + [r for r in reads if r.excl]
        need = []
        for r in rd:
            if r.w is not None:
                need.append((r.w, True))
        for r in wr:
            if r.w is not None:
                need.append((r.w, False))
            for ev in r.r.values():
                need.append((ev, False))
        for (key, val, eng), raw in need:
            if eng == e and not dma:
                if e == "pe":
                    continue
            self._wait(e, key, val)
        return rd, wr

    def _update(self, rd, wr, ev):
        for r in rd:
            r.r[ev[0]] = ev
        for r in wr:
            r.w = ev
            r.r = {}

    def op(self, e, fn, reads=(), writes=()):
        rd, wr = self._sync(e, reads, writes, False)
        self.cnt[e] += 1
        sem = self.sems[e]
        self.ops[e].append(lambda eng, fn=fn, sem=sem: fn(eng).then_inc(sem, 1))
        self._update(rd, wr, (e, self.cnt[e], e))

    def dma(self, q, fn, reads=(), writes=()):
        pool = self.dma_pools[q]
        k = pool[self.dma_rr[q] % len(pool)]
        self.dma_rr[q] += 1
        self._wait(q, k, self.cnt[k])
        rd, wr = self._sync(q, reads, writes, True)
        self.cnt[k] += 16
        sem = self.sems[k]
        self.ops[q].append(lambda eng, fn=fn, sem=sem: fn(eng).then_inc(sem, 16))
        self._update(rd, wr, (k, self.cnt[k], None))

    def barrier(self, engines=None):
        for e in (engines or self.ENGS):
            for k, v in self.cnt.items():
                if k != e:
                    self._wait(e, k, v)


    def mm(self, out, lhsT, rhs, start, stop, reads, writes, skip=False):
        if skip:
            self.op("pe", lambda e: e.matmul(out, lhsT=lhsT, rhs=rhs, start=start, stop=stop, skip_group_check=True),
                    reads, writes)
        else:
            self.op("pe", lambda e: e.matmul(out, lhsT=lhsT, rhs=rhs, start=start, stop=stop), reads, writes)

    def tr(self, out, in_, ident, reads, writes):
        self.op("pe", lambda e: e.transpose(out=out, in_=in_, identity=ident), reads, writes)

    def act(self, out, in_, func, reads, writes, **kw):
        self.op("act", lambda e: e.activation(out=out, in_=in_, func=func, **kw), reads, writes)

    def tt(self, eng, out, in0, in1, op, reads, writes):
        self.op(eng, lambda e: e.tensor_tensor(out=out, in0=in0, in1=in1, op=op), reads, writes)

    def stt(self, out, in0, scalar, in1, op0, op1, reads, writes):
        self.op("dve", lambda e: e.scalar_tensor_tensor(out=out, in0=in0, scalar=scalar, in1=in1, op0=op0, op1=op1),
                reads, writes)

    def ts(self, eng, out, in0, s1, s2, op0, op1, reads, writes):
        if op1 is None:
            self.op(eng, lambda e: e.tensor_scalar(out=out, in0=in0, scalar1=s1, scalar2=None, op0=op0), reads, writes)
        else:
            self.op(eng, lambda e: e.tensor_scalar(out=out, in0=in0, scalar1=s1, scalar2=s2, op0=op0, op1=op1),
                    reads, writes)

    def cp(self, eng, out, in_, reads, writes):
        if eng == "act":
            self.op("act", lambda e: e.copy(out=out, in_=in_), reads, writes)
        else:
            self.op(eng, lambda e: e.tensor_copy(out=out, in_=in_), reads, writes)

    def ld(self, q, out, in_, reads=(), writes=()):
        self.dma(q, lambda e: e.dma_start(out=out, in_=in_), reads, writes)

    def emit(self, block):
        def mk(e):
            def body(eng):
                for f in self.ops[e]:
                    f(eng)
            return body
        block.tensor(mk("pe"))
        block.scalar(mk("act"))
        block.vector(mk("dve"))
        block.gpsimd(mk("pool"))
        block.sync(mk("sp"))


def make_consts():
    j = np.arange(128)[:, None]
    s = np.arange(128)[None, :]
    ident = (j == s).astype(np.float32)
    tinc = -(j >= s).astype(np.float32)
    tcar = -(j < s).astype(np.float32)
    ones = np.ones((128, 128), np.float32)
    cbf = np.concatenate([ident, tinc, tcar, ones], axis=1)
    mstrict = (s > j).astype(np.float32)
    mincl = (s >= j).astype(np.float32)
    half = 32
    inv_freq = (10000.0 ** (-np.arange(half, dtype=np.float32) / half)).astype(np.float32)
    pos = np.arange(S_LEN, dtype=np.float32)
    ang = (pos[:, None] * inv_freq[None, :]).astype(np.float32)
    cos = np.cos(ang).astype(np.float32)
    sin = np.sin(ang).astype(np.float32)
    cosT = cos.reshape(NT, 128, 32).transpose(1, 0, 2).reshape(128, NT * 32)
    sinT = sin.reshape(NT, 128, 32).transpose(1, 0, 2).reshape(128, NT * 32)
    slopes = (2.0 ** (-8.0 * np.arange(1, 17, dtype=np.float32) / 16)).astype(np.float32)
    NEG = -30000.0
    bias = np.zeros((128, 2, 16, 128), np.float32)
    for kbi in range(2):
        rel = (s + 128 - (j + 128 * kbi)).astype(np.float32)
        valid = (rel >= 0) & (rel < 128)
        for h in range(16):
            bias[:, kbi, h, :] = np.where(valid, -slopes[h] * rel, NEG)
    cf = np.concatenate([mstrict, mincl, cosT, sinT, bias.reshape(128, -1)], axis=1).astype(np.float32)
    return cbf.astype(np.float32), cf


CF_MSTRICT = 0
CF_MINCL = 128
CF_COS = 256
CF_SIN = 256 + NT * 32
CF_BIAS = 256 + 2 * NT * 32
CF_TOTAL = CF_BIAS + 2 * 16 * 128


class Ctx:
    pass


def build_program(layers, dbg=None):
    nc = bass.Bass("TRN2", target_bir_lowering=False)
    x_in = nc.dram_tensor("x", [S_LEN, D], F32, kind="ExternalInput").ap()
    y_out = nc.dram_tensor("y", [S_LEN, D], F32, kind="ExternalOutput").ap()
    cbf_d = nc.dram_tensor("cbf", [128, 512], F32, kind="ExternalInput").ap()
    cf_d = nc.dram_tensor("cf", [128, CF_TOTAL], F32, kind="ExternalInput").ap()
    W = {}
    for li in layers:
        for n in LAYER_WEIGHTS[li]:
            W[n] = nc.dram_tensor(n, WSHAPES[n], F32, kind="ExternalInput").ap()
    xs = [x_in]
    for i in range(len(layers) - 1):
        xs.append(nc.dram_tensor(f"xmid{i}", [S_LEN, D], F32, kind="Internal").ap())
    xs.append(y_out)

    from contextlib import ExitStack
    with ExitStack() as st:
        def sb(name, shape, dt):
            return st.enter_context(nc.sbuf_tensor(name, shape, dt))
        c = Ctx()
        c.nc = nc
        c.xnT = sb("xnT", [128, 8 * S_LEN], BF16)
        c.ogT = sb("ogT", [128, 8, S_LEN], BF16)
        c.hbuf = sb("hbuf", [128, 3 * S_LEN], BF16)
        c.wbuf = sb("wbuf", [128, 8192], BF16)
        c.scr = sb("scr", [128, 6144], F32)
        c.gbc = sb("gbc", [128, 1024], F32)
        c.cbf = sb("cbfs", [128, 512], BF16)
        c.cmask = sb("cmask", [128, 256], F32)
        c.small = sb("small", [128, 512], F32)
        c.ps = st.enter_context(nc.psum_tensor("ps", [128, 8, 512], F32))
        sem_names = list(Sched.ENGS) + [f"d{i}" for i in range(20)]
        sems = {k: st.enter_context(nc.semaphore(f"s_{k}")) for k in sem_names}
        S = Sched(sems, {"sp": [f"d{i}" for i in range(0, 8)],
                         "pool": [f"d{i}" for i in range(8, 16)],
                         "act": [f"d{i}" for i in range(16, 20)]})
        c.S = S
        c.W = W
        c.cf_d = cf_d
        c.dbg = dbg
        c.bank = [Res(f"bank{b}", excl=True) for b in range(8)]
        c.ident = c.cbf[:, 0:128]
        c.tinc = c.cbf[:, 128:256]
        c.tcar = c.cbf[:, 256:384]
        c.ones = c.cbf[:, 384:512]
        c.mstrict = c.cmask[:, 0:128]
        c.mincl = c.cmask[:, 128:256]
        c.Rconst = Res("const")
        S.ld("pool", c.cbf[:], cbf_d[:, :], writes=[c.Rconst])
        S.ld("sp", c.cmask[:], cf_d[:, 0:256], writes=[c.Rconst])
        S.barrier()

        block = st.enter_context(nc.Block())
        for idx, li in enumerate(layers):
            kind = LAYER_KIND[li]
            pre = f"l{li}_"
            if idx == 0:
                phase_norm(c, xs[idx], W[pre + "norm"])
                S.barrier()
            if kind == "sb":
                layer_sb(c, W[pre + "w_in"])
            elif kind == "mla":
                layer_mla(c, {k[3:]: v for k, v in W.items() if k.startswith(pre)})
            else:
                layer_swa(c, {k[3:]: v for k, v in W.items() if k.startswith(pre)})
            S.barrier()
            nxt = W[f"l{layers[idx + 1]}_norm"] if idx + 1 < len(layers) else None
            phase_out(c, xs[idx], xs[idx + 1], W[pre + "w_out"], nxt)
            S.barrier()
        S.emit(block)
    return nc


def norm_stages(c, Rg, src_of, Rsrc_of, tag):
    S = c.S
    xnT3 = c.xnT[:, :].rearrange("p (c t) -> p c t", c=8)
    junk = c.scr[:, 0:1024]
    Rjunk = Res("junk" + tag)
    xn = [c.scr[:, 1024 + 512 * j:1536 + 512 * j].bitcast(BF16) for j in range(2)]
    Rxn = [Res(f"xn{j}" + tag) for j in range(2)]
    NSTAT = 4
    Rst = [Res(f"nst{j}" + tag) for j in range(NSTAT)]
    c.RxnT = [Res(f"xnT{i}" + tag) for i in range(NT)]

    def cols(i):
        k = i % NSTAT
        return c.small[:, 4 * k:4 * k + 1], c.small[:, 4 * k + 1:4 * k + 2], c.small[:, 4 * k + 2:4 * k + 3], k

    def n_stat(i):
        ss, lg, rs, k = cols(i)
        S.act(junk, src_of(i), AF.Square, [Rsrc_of(i)], [Rjunk, Rst[k]], accum_out=ss)
        S.act(lg, ss, AF.Ln, [Rst[k]], [Rst[k]], scale=1.0 / D, bias=EPS)
        S.act(rs, lg, AF.Exp, [Rst[k]], [Rst[k]], scale=-0.5)

    def n_xn(i):
        ss, lg, rs, k = cols(i)
        b = i % 2
        S.stt(xn[b], src_of(i), rs, c.gbc[:], ALU.mult, ALU.mult, [Rsrc_of(i), Rst[k], Rg], [Rxn[b]])

    def n_tr(i):
        b = i % 2
        bank = 4 + b
        psT = c.ps[:, bank, :].bitcast(BF16)
        for ch in range(8):
            S.tr(psT[:, ch * 128:(ch + 1) * 128], xn[b][:, ch * 128:(ch + 1) * 128], c.ident,
                 [Rxn[b], c.Rconst], [c.bank[bank]])
        dst = xnT3[:, :, i * 128:(i + 1) * 128]
        src = psT.rearrange("p (c t) -> p c t", c=8)
        S.cp("act" if b == 0 else "dve", dst, src, [c.bank[bank]], [c.RxnT[i]])

    return n_stat, n_xn, n_tr


def phase_norm(c, x_d, g_d):
    S = c.S
    Rg = Res("gbc")
    S.ld("sp", c.gbc[:], g_d.partition_broadcast(128), writes=[Rg])
    hb = c.hbuf[:, :].bitcast(F32)
    NX = 4
    xin = [hb[:, 1024 * j:1024 * (j + 1)] for j in range(NX)]
    Rxin = [Res(f"xin{j}") for j in range(NX)]
    n_stat, n_xn, n_tr = norm_stages(c, Rg, lambda i: xin[i % NX], lambda i: Rxin[i % NX], "A")

    def n_ld(i):
        S.ld("sp", xin[i % NX], x_d[i * 128:(i + 1) * 128, :], writes=[Rxin[i % NX]])

    stages = [(n_tr, 4), (n_xn, 3), (n_stat, 2), (n_ld, 0)]
    for n in range(NT + 4):
        for fn, d in stages:
            i = n - d
            if 0 <= i < NT:
                fn(i)


def phase_out(c, x_d, y_d, wout_d, next_g=None):
    S = c.S
    wo = c.wbuf[:, :].rearrange("p (c n) -> p c n", c=8)
    Rwo = Res("wo")
    wv = wout_d.rearrange("(c p) n -> p c n", p=128)
    for h in range(2):
        S.ld("pool", wo[:, 4 * h:4 * h + 4, :], wv[:, 4 * h:4 * h + 4, :], writes=[Rwo])
    Rg = Res("gbcC")
    if next_g is not None:
        S.ld("sp", c.gbc[:], next_g.partition_broadcast(128), writes=[Rg])
    hb = c.hbuf[:, :].bitcast(F32)
    NX = 3
    xin = [hb[:, 1024 * j:1024 * (j + 1)] for j in range(NX)]
    yo = [hb[:, 3072 + 1024 * j:3072 + 1024 * (j + 1)] for j in range(NX)]
    Rxin = [Res(f"cxin{j}") for j in range(NX)]
    Ryo = [Res(f"yo{j}") for j in range(NX)]
    Rog = c.Rog

    def c_ld(i):
        S.ld("sp", xin[i % NX], x_d[i * 128:(i + 1) * 128, :], writes=[Rxin[i % NX]])

    def c_mm(i):
        for h in range(2):
            bank = 2 * (i % 2) + h
            for ch in range(8):
                S.mm(c.ps[:, bank, :], c.ogT[:, ch, i * 128:(i + 1) * 128], wo[:, ch, h * 512:(h + 1) * 512],
                     ch == 0, ch == 7, [Rwo, Rog[ch][i // 4]], [c.bank[bank]])

    def c_add(i):
        k = i % NX
        for h in range(2):
            bank = 2 * (i % 2) + h
            S.tt("dve", yo[k][:, h * 512:(h + 1) * 512], c.ps[:, bank, :], xin[k][:, h * 512:(h + 1) * 512],
                 ALU.add, [c.bank[bank], Rxin[k]], [Ryo[k]])
        S.ld("sp", y_d[i * 128:(i + 1) * 128, :], yo[k], reads=[Ryo[k]])

    stages = [(c_add, 3), (c_mm, 2), (c_ld, 0)]
    tail = 3
    if next_g is not None:
        n_stat, n_xn, n_tr = norm_stages(c, Rg, lambda i: yo[i % NX], lambda i: Ryo[i % NX], "C")
        stages = [(n_tr, 6), (n_xn, 5), (n_stat, 4)] + stages
        tail = 6
    for n in range(NT + tail):
        for fn, d in stages:
            i = n - d
            if 0 <= i < NT:
                fn(i)


def gate_phase(c, w_in, col0):
    S = c.S
    xnT3 = c.xnT[:, :].rearrange("p (c t) -> p c t", c=8)
    wv = w_in.rearrange("(c p) n -> p c n", p=128)
    wg = [c.wbuf[:, 6144 + 1024 * b: 6144 + 1024 * (b + 1)].rearrange("p (c n) -> p c n", c=8) for b in range(2)]
    Rwg = [Res("wg0"), Res("wg1")]
    c.Rog = [[Res(f"og{ch}_{tg}") for tg in range(NG)] for ch in range(8)]
    k = 0
    for ch in range(8):
        b = ch % 2
        S.ld("pool", wg[b], wv[:, :, col0 + ch * 128: col0 + (ch + 1) * 128], writes=[Rwg[b]])
        for tg in range(NG):
            bank = k % 4
            k += 1
            for cc in range(8):
                S.mm(c.ps[:, bank, :], wg[b][:, cc, :], xnT3[:, cc, tg * 512:(tg + 1) * 512], cc == 0, cc == 7,
                     [Rwg[b]] + c.RxnT[4 * tg:4 * tg + 4], [c.bank[bank]])
            S.act(c.ogT[:, ch, tg * 512:(tg + 1) * 512], c.ps[:, bank, :], AF.Silu, [c.bank[bank]], [c.Rog[ch][tg]])


def layer_sb(c, w_in):
    S = c.S
    gate_phase(c, w_in, 3072)
    S.barrier()
    xnT3 = c.xnT[:, :].rearrange("p (c t) -> p c t", c=8)
    wv = w_in.rearrange("(c p) n -> p c n", p=128)
    qT = c.hbuf[:, 0:S_LEN]
    kT = c.hbuf[:, S_LEN:2 * S_LEN]
    v = c.hbuf[:, 2 * S_LEN:3 * S_LEN].rearrange("p (i f) -> p i f", f=128)
    RqT = [Res(f"qT{g}") for g in range(NG)]
    RkT = [Res(f"kT{g}") for g in range(NG)]
    Rv = [Res(f"v{g}") for g in range(NG)]
    wsl = [c.wbuf[:, 3072 * b:3072 * (b + 1)].rearrange("p (s c n) -> p s c n", s=3, c=8) for b in range(2)]
    Rw = [Res("wsl0"), Res("wsl1")]
    NE, NL, NW, NA = 3, 4, 2, 2
    off = 0
    E, L, Wb, Ab = [], [], [], []
    for i in range(NE):
        E.append(c.scr[:, off:off + 1024].rearrange("p (h t) -> p h t", h=2)); off += 1024
    for i in range(NL):
        L.append(c.scr[:, off:off + 512].bitcast(BF16).rearrange("p (h t) -> p h t", h=2)); off += 512
    for i in range(NW):
        Wb.append(c.scr[:, off:off + 512].bitcast(BF16).rearrange("p (h t) -> p h t", h=2)); off += 512
    assert off <= 6144
    for i in range(NA):
        Ab.append(c.wbuf[:, 6144 + 1024 * i:6144 + 1024 * (i + 1)].rearrange("p (h t) -> p h t", h=2))
    RE = [Res(f"E{i}") for i in range(NE)]
    RL = [Res(f"L{i}") for i in range(NL)]
    RW = [Res(f"W{i}") for i in range(NW)]
    RA = [Res(f"A{i}") for i in range(NA)]
    AB = [4, 5]
    OBK = 6
    gcount = 0
    mask2 = c.mstrict.unsqueeze(1).to_broadcast([128, 2, 128])

    def load_w(hp):
        b = hp % 2
        for s in range(3):
            col = s * 1024 + hp * 128
            S.ld("pool", wsl[b][:, s, :, :], wv[:, :, col:col + 128], writes=[Rw[b]])

    def proj_groups(hp, banks):
        b = hp % 2
        k = 0
        for tg in reversed(range(NG)):
            for s_, (dst, Rd) in enumerate(((qT, RqT), (kT, RkT))):
                bank = banks[k % len(banks)]
                k += 1

                def g_qk(s_=s_, dst=dst, Rd=Rd, bank=bank, tg=tg):
                    for cc in range(8):
                        S.mm(c.ps[:, bank, :], wsl[b][:, s_, cc, :], xnT3[:, cc, tg * 512:(tg + 1) * 512], cc == 0, cc == 7,
                             [Rw[b]] + c.RxnT[4 * tg:4 * tg + 4], [c.bank[bank]])
                    S.cp("dve", dst[:, tg * 512:(tg + 1) * 512], c.ps[:, bank, :], [c.bank[bank]], [Rd[tg]])
                yield tg, g_qk
            bank = banks[k % len(banks)]
            k += 1
            for j in range(4):
                def g_v(j=j, bank=bank, tg=tg):
                    i = 4 * tg + j
                    for cc in range(8):
                        S.mm(c.ps[:, bank, j * 128:(j + 1) * 128], xnT3[:, cc, i * 128:(i + 1) * 128], wsl[b][:, 2, cc, :],
                             cc == 0, cc == 7, [Rw[b], c.RxnT[i]], [c.bank[bank]])
                    if j == 3:
                        S.cp("dve", v[:, 4 * tg:4 * tg + 4, :], c.ps[:, bank, :].rearrange("p (j f) -> p j f", f=128),
                             [c.bank[bank]], [Rv[tg]])
                yield tg, g_v

    load_w(0)
    load_w(1)
    for _, g_ in proj_groups(0, [0, 1, 2, 3]):
        g_()
    for hp in range(8):
        if 1 <= hp and hp + 1 < 8:
            load_w(hp + 1)
        pending = list(proj_groups(hp + 1, [7])) if hp + 1 < 8 else []
        G = []
        sweep_end = {}
        for qg in reversed(range(NG)):
            for kb in reversed(range(4 * qg + 4)):
                G.append((qg, kb))
            sweep_end[qg] = len(G) - 1
        n_g = len(G)

        def info(gi):
            qg, kb = G[gi]
            r = kb - 4 * qg
            c0 = r * 128 if r >= 0 else 0
            return qg, kb, r, c0, kb == 4 * qg + 3, kb == 0, gcount + gi

        def st_z(gi):
            qg, kb, r, c0, first, last, gid = info(gi)
            zp = 2 * (gid % 2)
            for hd in range(2):
                rows = slice(hd * 64, hd * 64 + 64)
                S.mm(c.ps[:, zp + hd, c0:512], kT[rows, kb * 128:(kb + 1) * 128], qT[rows, qg * 512 + c0:(qg + 1) * 512],
                     True, True, [RkT[kb // 4], RqT[qg]], [c.bank[zp + hd]])

        def st_e(gi):
            qg, kb, r, c0, first, last, gid = info(gi)
            zp = 2 * (gid % 2)
            eb = gid % NE
            S.act(E[eb][:, :, c0:512], c.ps[:, zp:zp + 2, c0:512], AF.Exp, [c.bank[zp], c.bank[zp + 1]], [RE[eb]],
                  scale=0.125)
            if r >= 0:
                S.tt("dve", E[eb][:, :, c0:c0 + 128], E[eb][:, :, c0:c0 + 128], mask2, ALU.mult,
                     [RE[eb], c.Rconst], [RE[eb]])

        def st_l(gi):
            qg, kb, r, c0, first, last, gid = info(gi)
            eb = gid % NE
            lb = gid % NL
            S.act(L[lb][:, :, c0:512], E[eb][:, :, c0:512], AF.Ln, [RE[eb]], [RL[lb]], bias=1.0, scale=1.0)

        def st_cum(gi):
            qg, kb, r, c0, first, last, gid = info(gi)
            lb = gid % NL
            for hd in range(2):
                S.mm(c.ps[:, AB[hd], c0:512], c.tinc, L[lb][:, hd, c0:512], first, False, [RL[lb], c.Rconst],
                     [c.bank[AB[hd]]], skip=True)

        def st_w(gi):
            qg, kb, r, c0, first, last, gid = info(gi)
            wb = gid % NW
            eb = gid % NE
            a_i = gid % NA
            S.act(Wb[wb][:, :, c0:512], c.ps[:, 4:6, c0:512], AF.Exp, [c.bank[4], c.bank[5]], [RW[wb]])
            S.tt("dve", Ab[a_i][:, :, c0:512], E[eb][:, :, c0:512], Wb[wb][:, :, c0:512], ALU.mult,
                 [RE[eb], RW[wb]], [RA[a_i]])

        def st_car(gi):
            qg, kb, r, c0, first, last, gid = info(gi)
            lb = gid % NL
            if not last:
                for hd in range(2):
                    S.mm(c.ps[:, AB[hd], c0:512], c.tcar, L[lb][:, hd, c0:512], False, False, [RL[lb], c.Rconst],
                         [c.bank[AB[hd]]], skip=True)

        def st_av(gi):
            qg, kb, r, c0, first, last, gid = info(gi)
            a_i = gid % NA
            for hd in range(2):
                rows = slice(hd * 64, hd * 64 + 64)
                S.mm(c.ps[rows, OBK, c0:512], v[:, kb, hd * 64:(hd + 1) * 64], Ab[a_i][:, hd, c0:512], first, last,
                     [RA[a_i], Rv[kb // 4]], [c.bank[OBK]], skip=True)
            if last:
                og = c.ogT[:, hp, qg * 512:(qg + 1) * 512]
                S.tt("dve", og, c.ps[:, OBK, :], og, ALU.mult, [c.bank[OBK], c.Rog[hp][qg]], [c.Rog[hp][qg]])

        stages = [(st_car, 4), (st_cum, 3), (st_av, 5), (st_z, 0), (st_w, 3), (st_l, 2), (st_e, 1)]
        for n in range(n_g + 5):
            for fn, d in stages:
                gi = n - d
                if 0 <= gi < n_g:
                    fn(gi)
            if pending and n >= sweep_end[pending[0][0]] + 6:
                pending.pop(0)[1]()
        for _, g_ in pending:
            g_()
        gcount += n_g


def layer_mla(c, w):
    S = c.S
    w_in = w["w_in"]
    gate_phase(c, w_in, 448)
    S.barrier()
    xnT3 = c.xnT[:, :].rearrange("p (c t) -> p c t", c=8)
    wv = w_in.rearrange("(c p) n -> p c n", p=128)
    sm = c.small
    scr = c.scr
    Rgn = Res("mla_g")
    gqa = c.gbc[:, 0:256]
    gkva = c.gbc[:, 256:384]
    gq192 = c.gbc[:, 384:576]
    gk192 = c.gbc[:, 576:768]
    gkpe = c.gbc[:, 704:768]
    S.ld("sp", gqa, w["q_a_norm"].partition_broadcast(128), writes=[Rgn])
    S.ld("sp", gkva, w["kv_a_norm"].partition_broadcast(128), writes=[Rgn])
    S.ld("sp", gq192, w["q_head_norm"].partition_broadcast(128), writes=[Rgn])
    S.ld("sp", gk192, w["k_head_norm"].partition_broadcast(128), writes=[Rgn])
    S.tt("dve", gq192[:, 0:128], gq192[:, 0:128], gk192[:, 0:128], ALU.mult, [Rgn], [Rgn])
    cosT = scr[:, 0:1024].rearrange("p (i f) -> p i f", f=32)
    sinT = scr[:, 1024:2048].rearrange("p (i f) -> p i f", f=32)
    Rtab = Res("ropetab")
    S.ld("sp", scr[:, 0:2048], c.cf_d[:, CF_COS:CF_COS + 2048], writes=[Rtab])
    kpeT = scr[:, 2048:4096].bitcast(BF16)
    Rkpe_hi = Res("kpe_hi")
    S.op("dve", lambda e, o=scr[64:128, 2048:4096]: e.memset(o, 0.0), [], [Rkpe_hi])
    qlnT3 = c.hbuf[:, 0:2 * S_LEN].rearrange("p (a t) -> p a t", a=2)
    kvlnT = c.hbuf[:, 2 * S_LEN:3 * S_LEN]
    sskpe = sm[:, 128:160]
    rstdk = sm[:, 160:192]
    Rqln = [Res(f"qln{g}") for g in range(NG)]
    Rkvln = [Res(f"kvln{g}") for g in range(NG)]
    Rkpe = [Res(f"kpe{g}") for g in range(NG)]
    Rsskpe = [Res(f"sskpe{g}") for g in range(NG)]

    def rope(x, out, i, Rx, Rout, t1, t2, Rt):
        cb = cosT[:, i, :].unsqueeze(1).to_broadcast([128, 2, 32])
        S.tt("dve", t1.rearrange("p (a f) -> p a f", a=2), x.rearrange("p (a f) -> p a f", a=2), cb, ALU.mult,
             [Rx, Rtab], [Rt])
        S.tt("dve", t2[:, 0:32], x[:, 32:64], sinT[:, i, :], ALU.mult, [Rx, Rtab], [Rt])
        S.tt("dve", t2[:, 32:64], x[:, 0:32], sinT[:, i, :], ALU.mult, [Rx, Rtab], [Rt])
        S.tt("dve", out[:, 0:32], t1[:, 0:32], t2[:, 0:32], ALU.subtract, [Rt], [Rout])
        S.tt("dve", out[:, 32:64], t1[:, 32:64], t2[:, 32:64], ALU.add, [Rt], [Rout])

    wlat = c.wbuf[:, 0:3584].rearrange("p (c n) -> p c n", c=8)
    Rwlat = Res("wlat")
    S.ld("pool", wlat[:, 0:4, :], wv[:, 0:4, 0:448], writes=[Rwlat])
    S.ld("pool", wlat[:, 4:8, :], wv[:, 4:8, 0:448], writes=[Rwlat])
    o = 4096
    junk = [scr[:, o + 448 * j:o + 448 * (j + 1)] for j in range(2)]; o += 896
    NLN = 3
    lnb = [scr[:, o + 192 * j:o + 192 * (j + 1)].bitcast(BF16) for j in range(NLN)]; o += 192 * NLN
    kp = [scr[:, o + 64 * j:o + 64 * (j + 1)] for j in range(2)]; o += 128
    t1 = [scr[:, o + 64 * j:o + 64 * (j + 1)] for j in range(2)]; o += 128
    t2 = [scr[:, o + 64 * j:o + 64 * (j + 1)] for j in range(2)]; o += 128
    kr = [scr[:, o + 32 * j:o + 32 * (j + 1)].bitcast(BF16) for j in range(2)]; o += 64
    assert o <= 6144, o
    Rjunk = [Res("junk0"), Res("junk1")]
    Rkp = [Res("kp0"), Res("kp1")]
    Rt = [Res("ropet0"), Res("ropet1")]
    Rlnb = [Res(f"lnb{j}") for j in range(NLN)]
    Rkr = [Res("kr0"), Res("kr1")]
    NSB = 4
    Rst = [Res(f"mst{j}") for j in range(NSB)]
    LB = [0, 1, 2, 3, 6, 7]

    def binfo(i):
        k = i % NSB
        sb_ = 192 + 8 * k
        return i // 4, LB[i % len(LB)], sm[:, sb_:sb_ + 2], sm[:, sb_ + 2:sb_ + 4], sm[:, sb_ + 4:sb_ + 6], k, i % 2, i % NLN

    def b_mm(i):
        tg, bank, ss2, lg2, rs2, k, pb, li = binfo(i)
        for cc in range(8):
            S.mm(c.ps[:, bank, 0:448], xnT3[:, cc, i * 128:(i + 1) * 128], wlat[:, cc, :], cc == 0, cc == 7,
                 [Rwlat, c.RxnT[i]], [c.bank[bank]])

    def b_sq(i):
        tg, bank, ss2, lg2, rs2, k, pb, li = binfo(i)
        S.act(junk[pb][:, 0:256], c.ps[:, bank, 0:256], AF.Square, [c.bank[bank]], [Rjunk[pb], Rst[k]], accum_out=ss2[:, 0:1])
        S.act(junk[pb][:, 256:384], c.ps[:, bank, 256:384], AF.Square, [c.bank[bank]], [Rjunk[pb], Rst[k]],
              accum_out=ss2[:, 1:2])
        S.act(junk[pb][:, 384:448], c.ps[:, bank, 384:448], AF.Square, [c.bank[bank]], [Rjunk[pb], Rsskpe[tg]],
              accum_out=sskpe[:, i:i + 1])

    def b_rs(i):
        tg, bank, ss2, lg2, rs2, k, pb, li = binfo(i)
        S.act(lg2[:, 0:1], ss2[:, 0:1], AF.Ln, [Rst[k]], [Rst[k]], scale=1.0 / 256, bias=EPS)
        S.act(lg2[:, 1:2], ss2[:, 1:2], AF.Ln, [Rst[k]], [Rst[k]], scale=1.0 / 128, bias=EPS)
        S.act(rs2, lg2, AF.Exp, [Rst[k]], [Rst[k]], scale=-0.5)

    def b_ln(i):
        tg, bank, ss2, lg2, rs2, k, pb, li = binfo(i)
        S.stt(lnb[li][:, 0:256], c.ps[:, bank, 0:256], rs2[:, 0:1], gqa, ALU.mult, ALU.mult,
              [c.bank[bank], Rst[k], Rgn], [Rlnb[li]])
        S.stt(lnb[li][:, 256:384], c.ps[:, bank, 256:384], rs2[:, 1:2], gkva, ALU.mult, ALU.mult,
              [c.bank[bank], Rst[k], Rgn], [Rlnb[li]])
        S.tt("dve", kp[pb], c.ps[:, bank, 384:448], gkpe, ALU.mult, [c.bank[bank], Rgn], [Rkp[pb]])

    def b_rope(i):
        tg, bank, ss2, lg2, rs2, k, pb, li = binfo(i)
        rope(kp[pb], kr[pb], i, Rkp[pb], Rkr[pb], t1[pb], t2[pb], Rt[pb])

    def b_tr(i):
        tg, bank, ss2, lg2, rs2, k, pb, li = binfo(i)
        tbank = 4 + pb
        psT = c.ps[:, tbank, :].bitcast(BF16)
        for a in range(3):
            S.tr(psT[:, a * 128:(a + 1) * 128], lnb[li][:, a * 128:(a + 1) * 128], c.ident,
                 [Rlnb[li], c.Rconst], [c.bank[tbank]])
        S.tr(psT[0:64, 384:512], kr[pb], c.ident, [Rkr[pb], c.Rconst], [c.bank[tbank]])
        S.cp("act", qlnT3[:, :, i * 128:(i + 1) * 128], psT[:, 0:256].rearrange("p (a t) -> p a t", a=2),
             [c.bank[tbank]], [Rqln[tg]])
        S.cp("dve", kvlnT[:, i * 128:(i + 1) * 128], psT[:, 256:384], [c.bank[tbank]], [Rkvln[tg]])
        S.cp("dve", kpeT[0:64, i * 128:(i + 1) * 128], psT[0:64, 384:512], [c.bank[tbank]], [Rkpe[tg]])

    bstages = [(b_tr, 5), (b_rope, 4), (b_ln, 3), (b_rs, 2), (b_sq, 1), (b_mm, 0)]
    for n in range(NT + 5):
        for fn, d in bstages:
            i = n - d
            if 0 <= i < NT:
                fn(i)
    S.barrier()
    wuq = c.wbuf[:, 0:3072].rearrange("p (a n) -> p a n", a=2)
    wukv = c.wbuf[:, 3072:5120]
    Rwu = Res("wu")
    S.ld("pool", wuq, w["w_uq"].rearrange("(a p) n -> p a n", p=128), writes=[Rwu])
    S.ld("pool", wukv, w["w_ukv"], writes=[Rwu])
    X = c.xnT
    qTn = X[:, 0:4096]
    qTp = X[:, 4096:8192]
    RqTp_hi = Res("qTp_hi")
    S.op("dve", lambda e, o=X[64:128, 4096:8192]: e.memset(o, 0.0), [], [RqTp_hi])
    kTn = X[:, 8192:12288]
    vh = X[:, 12288:16384].rearrange("p (i f) -> p i f", f=128)
    NP = 3
    Pb = [X[:, 16384 + 512 * j:16384 + 512 * (j + 1)] for j in range(NP)]
    NQR = 3
    qr = [X[:, 18432 + 256 * j:18432 + 256 * j + 192] for j in range(NQR)]
    XF = X[:, 20480:24576].bitcast(F32)
    rcb = XF[:, 0:512]
    tbuf = XF[:, 512:1024]
    qpe = [XF[:, 1024 + 64 * j:1088 + 64 * j] for j in range(2)]
    u1 = [XF[:, 1152 + 64 * j:1216 + 64 * j] for j in range(2)]
    u2 = [XF[:, 1280 + 64 * j:1344 + 64 * j] for j in range(2)]
    junk2 = [XF[:, 1408 + 320 * j:1408 + 320 * (j + 1)] for j in range(2)]
    Rsum = [X[:, 24576 + 1024 * j:24576 + 1024 * (j + 1)].bitcast(F32) for j in range(2)]
    ones32 = X[:, 26624:26880].bitcast(F32)
    RRs = [Res("Rsum0"), Res("Rsum1")]
    Rones32 = Res("ones32")
    S.op("pool", lambda e, o=ones32: e.memset(o, 1.0), [], [Rones32])
    RqTn = [Res(f"qTn{g}") for g in range(NG)]
    RqTp = [Res(f"qTp{g}") for g in range(NG)]
    RkTn = [Res(f"kTn{g}") for g in range(NG)]
    Rvh = [Res(f"vh{g}") for g in range(NG)]
    Rrk = [Res(f"rstdk{g}") for g in range(NG)]
    RPb = [Res(f"mPb{j}") for j in range(NP)]
    Rqr = [Res(f"qr{j}") for j in range(NQR)]
    Rrc, Rtb = Res("rcb"), Res("tbuf")
    Rqpe = [Res("qpe0"), Res("qpe1")]
    Ru = [Res("u0"), Res("u1")]
    Rj2 = [Res("junk20"), Res("junk21")]
    NST = 4
    Rs2 = [Res(f"hst{j}") for j in range(NST)]
    QB = [0, 1, 2, 3, 6, 7]
    gid = 0
    sw = 0
    tcount = 0
    for h in range(8):
        for tg in range(NG):
            bank = 4 + tg % 2
            S.mm(c.ps[:, bank, :], wukv[:, h * 256:h * 256 + 128], kvlnT[:, tg * 512:(tg + 1) * 512], True, True,
                 [Rwu, Rkvln[tg]], [c.bank[bank]])
            S.cp("act", kTn[:, tg * 512:(tg + 1) * 512], c.ps[:, bank, :], [c.bank[bank]], [RkTn[tg]])

        def tinfo(i):
            t = tcount + i
            tg, j = divmod(i, 4)
            sb_ = 224 + 8 * (t % NST)
            return (t, tg, j, QB[t % len(QB)], slice(i * 128, (i + 1) * 128), sm[:, sb_:sb_ + 2], sm[:, sb_ + 2:sb_ + 4],
                    sm[:, sb_ + 4:sb_ + 5], t % NST, t % NQR, t % 2)

        def t_mm(i):
            t, tg, j, bank, tok, ss2, lg2, rsq, si, qi, pi = tinfo(i)
            S.mm(c.ps[:, bank, 0:192], qlnT3[:, 0, tok], wuq[:, 0, h * 192:(h + 1) * 192], True, False,
                 [Rwu, Rqln[tg]], [c.bank[bank]])
            S.mm(c.ps[:, bank, 0:192], qlnT3[:, 1, tok], wuq[:, 1, h * 192:(h + 1) * 192], False, True,
                 [Rwu, Rqln[tg]], [c.bank[bank]])
            S.mm(c.ps[:, bank, 192:448], kvlnT[:, tok], wukv[:, h * 256:(h + 1) * 256], True, True,
                 [Rwu, Rkvln[tg]], [c.bank[bank]])

        def t_sq(i):
            t, tg, j, bank, tok, ss2, lg2, rsq, si, qi, pi = tinfo(i)
            S.act(junk2[pi][:, 0:192], c.ps[:, bank, 0:192], AF.Square, [c.bank[bank]], [Rj2[pi], Rs2[si]],
                  accum_out=ss2[:, 0:1])
            S.act(junk2[pi][:, 192:320], c.ps[:, bank, 192:320], AF.Square, [c.bank[bank]], [Rj2[pi], Rs2[si]],
                  accum_out=ss2[:, 1:2])
            S.cp("act", vh[:, i, :], c.ps[:, bank, 320:448], [c.bank[bank]], [Rvh[tg]])

        def t_add(i):
            t, tg, j, bank, tok, ss2, lg2, rsq, si, qi, pi = tinfo(i)
            S.tt("dve", ss2[:, 1:2], ss2[:, 1:2], sskpe[:, i:i + 1], ALU.add, [Rs2[si], Rsskpe[tg]], [Rs2[si]])

        def t_rs(i):
            t, tg, j, bank, tok, ss2, lg2, rsq, si, qi, pi = tinfo(i)
            S.act(lg2, ss2, AF.Ln, [Rs2[si]], [Rs2[si]], scale=1.0 / 192, bias=EPS)
            S.act(rsq, lg2[:, 0:1], AF.Exp, [Rs2[si]], [Rs2[si]], scale=-0.5)
            S.act(rstdk[:, i:i + 1], lg2[:, 1:2], AF.Exp, [Rs2[si]], [Rrk[tg]], scale=-0.5, bias=-0.5 * math.log(192.0))

        def t_qn(i):
            t, tg, j, bank, tok, ss2, lg2, rsq, si, qi, pi = tinfo(i)
            S.stt(qr[qi][:, 0:128], c.ps[:, bank, 0:128], rsq, gq192[:, 0:128], ALU.mult, ALU.mult,
                  [c.bank[bank], Rs2[si], Rgn], [Rqr[qi]])
            S.stt(qpe[pi], c.ps[:, bank, 128:192], rsq, gq192[:, 128:192], ALU.mult, ALU.mult,
                  [c.bank[bank], Rs2[si], Rgn], [Rqpe[pi]])
            rope(qpe[pi], qr[qi][:, 128:192], i, Rqpe[pi], Rqr[qi], u1[pi], u2[pi], Ru[pi])

        def t_tr(i):
            t, tg, j, bank, tok, ss2, lg2, rsq, si, qi, pi = tinfo(i)
            tbank = 4 + tg % 2
            psT = c.ps[:, tbank, :].bitcast(BF16)
            S.tr(psT[:, j * 128:(j + 1) * 128], qr[qi][:, 0:128], c.ident, [Rqr[qi], c.Rconst], [c.bank[tbank]])
            S.tr(psT[0:64, 512 + j * 128:512 + (j + 1) * 128], qr[qi][:, 128:192], c.ident,
                 [Rqr[qi], c.Rconst], [c.bank[tbank]])
            if j == 3:
                S.cp("dve", qTn[:, tg * 512:(tg + 1) * 512], psT[:, 0:512], [c.bank[tbank]], [RqTn[tg]])
                S.cp("dve", qTp[0:64, tg * 512:(tg + 1) * 512], psT[0:64, 512:1024], [c.bank[tbank]], [RqTp[tg]])

        tstages = [(t_tr, 5), (t_qn, 4), (t_rs, 3), (t_add, 2), (t_sq, 1), (t_mm, 0)]
        for n in range(NT + 5):
            for fn, d in tstages:
                i = n - d
                if 0 <= i < NT:
                    fn(i)
        tcount += NT
        G = []
        for qg in range(NG):
            for kb in reversed(range(4 * qg + 4)):
                G.append((qg, kb))
        n_g = len(G)

        def info(gi):
            qg, kb = G[gi]
            r = kb - 4 * qg
            c0 = r * 128 if r >= 0 else 0
            return qg, kb, r, c0, kb == 4 * qg + 3, kb == 0, gid + gi, sw + qg

        def st_z(gi):
            qg, kb, r, c0, first, last, g_, sw_ = info(gi)
            zb = g_ % 4
            S.mm(c.ps[:, zb, c0:512], kTn[:, kb * 128:(kb + 1) * 128], qTn[:, qg * 512 + c0:(qg + 1) * 512], True, False,
                 [RkTn[kb // 4], RqTn[qg]], [c.bank[zb]])
            S.mm(c.ps[:, zb, c0:512], kpeT[:, kb * 128:(kb + 1) * 128], qTp[:, qg * 512 + c0:(qg + 1) * 512],
                 False, True, [Rkpe[kb // 4], RqTp[qg], Rkpe_hi, RqTp_hi], [c.bank[zb]])

        def st_e(gi):
            qg, kb, r, c0, first, last, g_, sw_ = info(gi)
            zb = g_ % 4
            pi = g_ % NP
            S.act(Pb[pi][:, c0:512], c.ps[:, zb, c0:512], AF.Exp, [c.bank[zb], Rrk[kb // 4]], [RPb[pi]],
                  scale=rstdk[:, kb:kb + 1])
            if r >= 0:
                S.tt("dve", Pb[pi][:, c0:c0 + 128], Pb[pi][:, c0:c0 + 128], c.mincl, ALU.mult,
                     [RPb[pi], c.Rconst], [RPb[pi]])

        def st_av(gi):
            qg, kb, r, c0, first, last, g_, sw_ = info(gi)
            pi = g_ % NP
            ob = 4 + sw_ % 2
            db = 6 + sw_ % 2
            ri = sw_ % 2
            S.mm(c.ps[:, ob, c0:512], vh[:, kb, :], Pb[pi][:, c0:512], first, last, [Rvh[kb // 4], RPb[pi]],
                 [c.bank[ob]], skip=True)
            S.mm(c.ps[:, db, c0:512], c.ones, Pb[pi][:, c0:512], first, last, [c.Rconst, RPb[pi]],
                 [c.bank[db]], skip=True)
            if last:
                S.act(rcb, c.ps[:, db, :], AF.Ln, [c.bank[db]], [Rrc])
                S.act(rcb, rcb, AF.Exp, [Rrc], [Rrc], scale=-1.0)
                S.tt("dve", tbuf, c.ps[:, ob, :], rcb, ALU.mult, [c.bank[ob], Rrc], [Rtb])
                og = c.ogT[:, h, qg * 512:(qg + 1) * 512]
                S.tt("dve", og, tbuf, og, ALU.mult, [Rtb, c.Rog[h][qg]], [c.Rog[h][qg]])

        stages = [(st_z, 0), (st_e, 1), (st_av, 2)]
        for n in range(n_g + 2):
            for fn, d in reversed(stages):
                gi = n - d
                if 0 <= gi < n_g:
                    fn(gi)
        gid += n_g
        sw += NG


def layer_swa(c, w):
    S = c.S
    w_in = w["w_in"]
    gate_phase(c, w_in, 1536)
    S.barrier()
    xnT3 = c.xnT[:, :].rearrange("p (c t) -> p c t", c=8)
    wv = w_in.rearrange("(c p) n -> p c n", p=128)
    qT3 = c.hbuf[:, 0:2 * S_LEN].rearrange("p (a t) -> p a t", a=2)
    kT2 = c.hbuf[:, 2 * S_LEN:3 * S_LEN]
    scr = c.scr
    off = 0
    vg = scr[:, off:off + 1024].bitcast(BF16).rearrange("p (i f) -> p i f", f=64); off += 1024
    junk = [scr[:, off + 320 * j:off + 320 * (j + 1)] for j in range(2)]; off += 640
    tmpq = [scr[:, off + 256 * j:off + 256 * (j + 1)] for j in range(2)]; off += 512
    NQN = 3
    qn = [scr[:, off + 128 * j:off + 128 * (j + 1)].bitcast(BF16) for j in range(NQN)]; off += 128 * NQN
    biasg = scr[:, off:off + 1024]; off += 1024
    Tb = scr[:, off:off + 1024]; off += 1024
    Pb = [scr[:, off:off + 512].bitcast(BF16), scr[:, off + 512:off + 1024].bitcast(BF16)]; off += 1024
    dnb = scr[:, off:off + 256]; off += 256
    gqk4 = scr[:, off:off + 256]; off += 256
    assert off <= 6144, off
    sm = c.small
    kscale = sm[:, 16:48]
    es16 = sm[:, 48:64]
    Rbias, RTb, Rdn, Rgqk, Res16 = (Res(n) for n in ("biasg", "Tb", "dnb", "gqk", "es16"))
    Rjunk = [Res("junk0"), Res("junk1")]
    Rtmpq = [Res("tmpq0"), Res("tmpq1")]
    Rqn = [Res(f"qn{j}") for j in range(NQN)]
    RPb = [Res("Pb0"), Res("Pb1")]
    NST = 4
    Rst = [Res(f"sst{j}") for j in range(NST)]
    RqT = [Res(f"qT{g}") for g in range(NG)]
    RkT = [Res(f"kT{g}") for g in range(NG)]
    Rvg = [Res(f"vg{g}") for g in range(NG)]
    Rks = [Res(f"ks{g}") for g in range(NG)]
    wtm = [c.wbuf[:, 4096 * b:4096 * b + 3072].rearrange("p (c n) -> p c n", c=8) for b in range(2)]
    wk2 = [c.wbuf[:, 4096 * b + 3072:4096 * (b + 1)].rearrange("p (c n) -> p c n", c=8) for b in range(2)]
    Rw = [Res("swaw0"), Res("swaw1")]
    gk4 = junk[0][:, 0:256]
    for j in range(4):
        S.ld("sp", gqk4[:, j * 64:(j + 1) * 64], w["q_head_norm"].partition_broadcast(128), writes=[Rgqk])
        S.ld("sp", gk4[:, j * 64:(j + 1) * 64], w["k_head_norm"].partition_broadcast(128), writes=[Rjunk[0]])
    S.tt("dve", gqk4, gqk4, gk4, ALU.mult, [Rjunk[0], Rgqk], [Rgqk])
    S.ld("sp", es16, w["sinks"].partition_broadcast(128), writes=[Res16])
    S.act(es16, es16, AF.Exp, [Res16], [Res16])

    def load_w(g):
        b = g % 2
        S.ld("pool", wtm[b][:, :, 0:256], wv[:, :, g * 256:(g + 1) * 256], writes=[Rw[b]])
        S.ld("pool", wtm[b][:, :, 256:320], wv[:, :, 1024 + g * 64:1024 + (g + 1) * 64], writes=[Rw[b]])
        S.ld("pool", wtm[b][:, :, 320:384], wv[:, :, 1280 + g * 64:1280 + (g + 1) * 64], writes=[Rw[b]])
        for d in range(2):
            S.ld("pool", wk2[b][:, :, d * 64:(d + 1) * 64], wv[:, :, 1024 + g * 64:1024 + (g + 1) * 64], writes=[Rw[b]])

    load_w(0)
    TB = [0, 1, 2, 3, 6, 7]
    tcount = 0
    acount = 0
    for g in range(4):
        b = g % 2
        if g + 1 < 4:
            load_w(g + 1)
        for kbi in range(2):
            S.ld("sp", biasg[:, kbi * 512:(kbi + 1) * 512],
                 c.cf_d[:, CF_BIAS + kbi * 2048 + g * 512:CF_BIAS + kbi * 2048 + (g + 1) * 512], writes=[Rbias])
        for tg in range(NG):
            bank = 4 + tg % 2
            for cc in range(8):
                S.mm(c.ps[:, bank, :], wk2[b][:, cc, :], xnT3[:, cc, tg * 512:(tg + 1) * 512], cc == 0, cc == 7,
                     [Rw[b]] + c.RxnT[4 * tg:4 * tg + 4], [c.bank[bank]])
            S.cp("act", kT2[:, tg * 512:(tg + 1) * 512], c.ps[:, bank, :], [c.bank[bank]], [RkT[tg]])

        def tinfo(i):
            t = tcount + i
            tg, j = divmod(i, 4)
            sb_ = 64 + 16 * (t % NST)
            return (t, tg, j, TB[t % len(TB)], sm[:, sb_:sb_ + 5], sm[:, sb_ + 5:sb_ + 10], sm[:, sb_ + 10:sb_ + 14],
                    t % NST, t % 2, t % NQN)

        def t_mm(i):
            t, tg, j, bank, ss5, lg5, rs4, si, pi, qi = tinfo(i)
            for cc in range(8):
                S.mm(c.ps[:, bank, 0:384], xnT3[:, cc, i * 128:(i + 1) * 128], wtm[b][:, cc, :], cc == 0, cc == 7,
                     [Rw[b], c.RxnT[i]], [c.bank[bank]])

        def t_sq(i):
            t, tg, j, bank, ss5, lg5, rs4, si, pi, qi = tinfo(i)
            S.act(junk[pi], c.ps[:, bank, 0:320], AF.Square, [c.bank[bank]], [Rjunk[pi]])
            S.cp("act", vg[:, i, :], c.ps[:, bank, 320:384], [c.bank[bank]], [Rvg[tg]])

        def t_red(i):
            t, tg, j, bank, ss5, lg5, rs4, si, pi, qi = tinfo(i)
            S.op("dve", lambda e, o=ss5, i_=junk[pi].rearrange("p (h f) -> p h f", f=64): e.reduce_sum(out=o, in_=i_, axis=AX.X),
                 [Rjunk[pi]], [Rst[si]])

        def t_rs(i):
            t, tg, j, bank, ss5, lg5, rs4, si, pi, qi = tinfo(i)
            S.act(lg5, ss5, AF.Ln, [Rst[si]], [Rst[si]], scale=1.0 / 64, bias=EPS)
            S.act(rs4, lg5[:, 0:4], AF.Exp, [Rst[si]], [Rst[si]], scale=-0.5)
            S.act(kscale[:, i:i + 1], lg5[:, 4:5], AF.Exp, [Rst[si]], [Rks[tg]], scale=-0.5, bias=math.log(0.125))

        def t_qn(i):
            t, tg, j, bank, ss5, lg5, rs4, si, pi, qi = tinfo(i)
            S.tt("dve", tmpq[pi].rearrange("p (h f) -> p h f", f=64),
                 c.ps[:, bank, 0:256].rearrange("p (h f) -> p h f", f=64),
                 rs4.unsqueeze(2).to_broadcast([128, 4, 64]), ALU.mult, [c.bank[bank], Rst[si]], [Rtmpq[pi]])
            S.tt("dve", qn[qi], tmpq[pi], gqk4, ALU.mult, [Rtmpq[pi], Rgqk], [Rqn[qi]])

        def t_tr(i):
            t, tg, j, bank, ss5, lg5, rs4, si, pi, qi = tinfo(i)
            tbank = 4 + tg % 2
            psT = c.ps[:, tbank, :].bitcast(BF16)
            for a in range(2):
                S.tr(psT[:, a * 512 + j * 128:a * 512 + (j + 1) * 128], qn[qi][:, a * 128:(a + 1) * 128], c.ident,
                     [Rqn[qi], c.Rconst], [c.bank[tbank]])
            if j == 3:
                S.cp("dve", qT3[:, :, tg * 512:(tg + 1) * 512], psT.rearrange("p (a t) -> p a t", a=2),
                     [c.bank[tbank]], [RqT[tg]])

        tstages = [(t_tr, 5), (t_qn, 4), (t_rs, 3), (t_red, 2), (t_sq, 1), (t_mm, 0)]
        for n in range(NT + 5):
            for fn, d in tstages:
                i = n - d
                if 0 <= i < NT:
                    fn(i)
        tcount += NT

        def ainfo(qb):
            a_ = acount + qb
            kbs = [(0, qb - 1), (1, qb)] if qb > 0 else [(1, qb)]
            return a_, kbs, 2 * (a_ % 2), a_ % 2, 4 + a_ % 2

        def a_qk(qb):
            a_, kbs, zb0, pbi, ob = ainfo(qb)
            for kbi, kb in kbs:
                for hq in range(4):
                    p = hq % 2
                    rows = slice(p * 64, p * 64 + 64)
                    col = (kbi * 2 + hq // 2) * 128
                    S.mm(c.ps[:, zb0 + p, col:col + 128], kT2[rows, kb * 128:(kb + 1) * 128],
                         qT3[rows, hq // 2, qb * 128:(qb + 1) * 128], True, True,
                         [RkT[kb // 4], RqT[qb // 4]], [c.bank[zb0 + p]])

        def a_bias(qb):
            a_, kbs, zb0, pbi, ob = ainfo(qb)
            for kbi, kb in kbs:
                for p in range(2):
                    src = c.ps[:, zb0 + p, kbi * 256:(kbi + 1) * 256].rearrange("q (a t) -> q a t", a=2)
                    dst = Tb[:, kbi * 512:(kbi + 1) * 512].rearrange("q (a p t) -> q p a t", a=2, p=2)[:, p]
                    bia = biasg[:, kbi * 512:(kbi + 1) * 512].rearrange("q (a p t) -> q p a t", a=2, p=2)[:, p]
                    S.stt(dst, src, kscale[:, kb:kb + 1], bia, ALU.mult, ALU.add,
                          [c.bank[zb0 + p], Rks[kb // 4], Rbias], [RTb])

        def a_exp(qb):
            a_, kbs, zb0, pbi, ob = ainfo(qb)
            lo = 0 if qb > 0 else 512
            S.act(Pb[pbi][:, lo:1024], Tb[:, lo:1024], AF.Exp, [RTb], [RPb[pbi]])

        def a_av(qb):
            a_, kbs, zb0, pbi, ob = ainfo(qb)
            for hq in range(4):
                rows = slice((hq % 2) * 64, (hq % 2) * 64 + 64)
                col = (hq // 2) * 128
                for n_, (kbi, kb) in enumerate(kbs):
                    S.mm(c.ps[rows, ob, col:col + 128], vg[:, kb, :], Pb[pbi][:, (kbi * 4 + hq) * 128:(kbi * 4 + hq + 1) * 128],
                         n_ == 0, n_ == len(kbs) - 1, [Rvg[kb // 4], RPb[pbi]], [c.bank[ob]])
                for n_, (kbi, kb) in enumerate(kbs):
                    S.mm(c.ps[rows, ob, 256 + col:256 + col + 128], c.ones[:, 0:64],
                         Pb[pbi][:, (kbi * 4 + hq) * 128:(kbi * 4 + hq + 1) * 128],
                         n_ == 0, n_ == len(kbs) - 1, [c.Rconst, RPb[pbi]], [c.bank[ob]])

        def a_out(qb):
            a_, kbs, zb0, pbi, ob = ainfo(qb)
            for hq in range(4):
                rows = slice((hq % 2) * 64, (hq % 2) * 64 + 64)
                col = (hq // 2) * 128
                h = 4 * g + hq
                S.ts("dve", dnb[rows, col:col + 128], c.ps[rows, ob, 256 + col:256 + col + 128], es16[rows, h:h + 1],
                     None, ALU.add, None, [c.bank[ob], Res16], [Rdn])
            S.act(dnb, dnb, AF.Ln, [Rdn], [Rdn])
            S.act(dnb, dnb, AF.Exp, [Rdn], [Rdn], scale=-1.0)
            S.tt("dve", dnb, c.ps[:, ob, 0:256], dnb, ALU.mult, [c.bank[ob], Rdn], [Rdn])
            og = c.ogT[:, 2 * g:2 * g + 2, qb * 128:(qb + 1) * 128]
            Rogs = [c.Rog[2 * g][qb // 4], c.Rog[2 * g + 1][qb // 4]]
            S.tt("dve", og, dnb.rearrange("p (a t) -> p a t", a=2), og, ALU.mult, [Rdn] + Rogs, Rogs)

        astages = [(a_out, 4), (a_av, 3), (a_exp, 2), (a_bias, 1), (a_qk, 0)]
        for n in range(NT + 4):
            for fn, d in astages:
                qb = n - d
                if 0 <= qb < NT:
                    fn(qb)
        acount += NT


LAUNCH_GROUPS = [[0, 1, 2, 3]]
_CONSTS = None


def run_layers(layers, xs, inputs):
    global _CONSTS
    if _CONSTS is None:
        _CONSTS = make_consts()
    cbf, cf = _CONSTS
    nc = build_program(layers)
    names = [n for li in layers for n in LAYER_WEIGHTS[li]]
    in_maps = []
    for b in range(len(xs)):
        m = {"x": np.ascontiguousarray(xs[b], dtype=np.float32), "cbf": cbf, "cf": cf}
        for n in names:
            m[n] = np.ascontiguousarray(inputs[n], dtype=np.float32)
        in_maps.append(m)
    res = run_bass_kernel_spmd(nc, in_maps, core_ids=list(range(len(xs))))
    return [r["y"] for r in res.results]


def kernel(**inputs):
    x = np.asarray(inputs["x"])
    xs = [x[b] for b in range(x.shape[0])]
    for grp in LAUNCH_GROUPS:
        xs = run_layers(grp, xs, inputs)
    return np.stack(xs, axis=0).astype(np.float32)
```
